# Optimizing a Trainium2 kernel written in Bass

```python
import jax
import jax.numpy as jnp
from jax import lax
import numpy as np

D_MODEL = 1024
BATCH = 2
SEQ = 16384
DEPTH = 2

HEAD_DIM = 64
ROPE_THETA = 500000.0
ROT_DIMS = HEAD_DIM // 4
NORM_EPS = 1e-6
D_FF = 2816
Q_BLOCK = 128
MASK_BIG = 1e30

NSA_HEADS = 8
NSA_KV_GROUPS = 2
NSA_HPG = NSA_HEADS // NSA_KV_GROUPS
CMP_LEN = 32
CMP_STRIDE = 16
CMP_HIDDEN = 2 * HEAD_DIM
SEL_LEN = 64
SEL_TOPN = 16
SEL_LOCAL = 2
WIN_LEN = 512

DIL_HEADS = 8
DIL_PAIRS = ((128, 1), (512, 4), (2048, 16))

MLA_HEADS = 16
MLA_Q_RANK = 256
MLA_KV_RANK = 128
MLA_NOPE = 64
MLA_ROPE = 32
MLA_V = 64

NSA_Q_W = NSA_HEADS * HEAD_DIM
NSA_KV_W = NSA_KV_GROUPS * HEAD_DIM
NSA_GATE_W = 3 * NSA_HEADS
DIL_W = DIL_HEADS * HEAD_DIM
EVEN_IN_SPLITS = (NSA_Q_W,) + (NSA_KV_W,) * 6 + (NSA_GATE_W,) + (DIL_W,) * 3
EVEN_IN_W = sum(EVEN_IN_SPLITS)
EVEN_OUT_W = NSA_Q_W + DIL_W
MLA_IN_W = MLA_Q_RANK + MLA_KV_RANK + MLA_ROPE
N_EVEN = (DEPTH + 1) // 2
N_ODD = DEPTH // 2

kernel_name = 'hybrid_nsa_dilated_mla_macaron'


def _rms_norm(x, g):
    xf = x.astype(jnp.float32)
    y = xf * lax.rsqrt(jnp.mean(xf * xf, axis=-1, keepdims=True) + NORM_EPS)
    return (y * g.astype(jnp.float32)).astype(x.dtype)


def _split(t, widths):
    cuts = [int(c) for c in np.cumsum(widths)[:-1]]
    return jnp.split(t, cuts, axis=-1)


def _rope_table(seq, dims):
    inv = ROPE_THETA ** (-jnp.arange(0, dims, 2, dtype=jnp.float32) / dims)
    ang = jnp.arange(seq, dtype=jnp.float32)[:, None] * inv[None, :]
    return jnp.cos(ang), jnp.sin(ang)


def _rope(x, cos, sin):
    half = x.shape[-1] // 2
    c = cos[:, None, :].astype(x.dtype)
    s = sin[:, None, :].astype(x.dtype)
    x1, x2 = x[..., :half], x[..., half:]
    return jnp.concatenate([x1 * c - x2 * s, x2 * c + x1 * s], axis=-1)


def _partial_rope(x, cos, sin):
    return jnp.concatenate([_rope(x[..., :ROT_DIMS], cos, sin), x[..., ROT_DIMS:]], axis=-1)


def _swiglu(x, w_gate, w_up, w_down):
    return (jax.nn.silu(x @ w_gate) * (x @ w_up)) @ w_down


def _probs(scores, mask):
    s = jnp.where(mask, scores.astype(jnp.float32), -MASK_BIG)
    m = jnp.max(s, axis=-1, keepdims=True)
    p = jnp.where(mask, jnp.exp(s - m), 0.0)
    l = jnp.sum(p, axis=-1, keepdims=True)
    l_safe = jnp.where(l > 0, l, 1.0)
    return p / l_safe, (m + jnp.log(l_safe))[..., 0]


def _banded_attention(q, k, v, span):
    n, L, h, d = q.shape
    blk = span
    nb = -(-L // blk)
    pad = nb * blk - L
    to_blocks = lambda t: jnp.pad(t, ((0, 0), (0, pad), (0, 0), (0, 0))).reshape(n, nb, blk, h, d)
    qb, kb, vb = to_blocks(q), to_blocks(k), to_blocks(v)
    with_prev = lambda t: jnp.concatenate([jnp.concatenate([jnp.zeros_like(t[:, :1]), t[:, :-1]], axis=1), t], axis=2)
    kk, vv = with_prev(kb), with_prev(vb)
    sc = jnp.einsum('nbqhd,nbkhd->nbhqk', qb, kk) * (d ** -0.5)
    qi = jnp.arange(blk)[:, None] + blk
    ki = jnp.arange(2 * blk)[None, :]
    dist = qi - ki
    band = (dist >= 0) & (dist <= span)
    has_prev = (jnp.arange(nb) > 0)[:, None, None] | (ki >= blk)[None]
    mask = band[None] & has_prev
    p, lse = _probs(sc, mask[None, :, None])
    o = jnp.einsum('nbhqk,nbkhd->nbqhd', p.astype(v.dtype), vv).reshape(n, nb * blk, h, d)[:, :L]
    lse = lse.transpose(0, 1, 3, 2).reshape(n, nb * blk, h)[:, :L]
    return o, lse


def _dilated_branch(q, k, v, window, dil):
    b, s, h, d = q.shape
    L = s // dil
    to_res = lambda t: t.reshape(b, L, dil, h, d).transpose(0, 2, 1, 3, 4).reshape(b * dil, L, h, d)
    o, lse = _banded_attention(to_res(q), to_res(k), to_res(v), window // dil)
    o = o.reshape(b, dil, L, h, d).transpose(0, 2, 1, 3, 4).reshape(b, s, h, d)
    lse = lse.reshape(b, dil, L, h).transpose(0, 2, 1, 3).reshape(b, s, h)
    return o, lse


def _dilated_mixture(q, k, v):
    b, s, h, d = q.shape
    outs, lses = [], []
    for window, dil in DIL_PAIRS:
        o, lse = _dilated_branch(q, k, v, window, dil)
        outs.append(o)
        lses.append(lse)
    w = jax.nn.softmax(jnp.stack(lses, axis=0), axis=0)
    o = jnp.sum(w[..., None] * jnp.stack(outs, axis=0), axis=0)
    return o.astype(q.dtype).reshape(b, s, h * d)


def _compress(t, pos_emb, w1, w2):
    b, s, g, d = t.shape
    r = CMP_LEN // CMP_STRIDE
    n_chunks = s // CMP_STRIDE
    n_cmp = n_chunks - r + 1
    ch = t.reshape(b, n_chunks, CMP_STRIDE, g, d)
    blocks = jnp.concatenate([ch[:, i:i + n_cmp] for i in range(r)], axis=2)
    blocks = blocks + pos_emb[None, None, :, None, :]
    flat = blocks.transpose(0, 1, 3, 2, 4).reshape(b, n_cmp, g, CMP_LEN * d)
    return jax.nn.silu(flat @ w1) @ w2


def _cmp_to_sel_matrix(n_cmp, n_sel):
    a = SEL_LEN // CMP_STRIDE
    c = CMP_LEN // CMP_STRIDE
    rel = jnp.arange(n_cmp)[:, None] - a * jnp.arange(n_sel)[None, :]
    return sum((rel == (m - n)).astype(jnp.float32) for m in range(a) for n in range(c))


def _nsa(q, kc_raw, vc_raw, ks, vs, kw, vw, gates, pos_k, pos_v, ck_w1, ck_w2, cv_w1, cv_w2):
    b, s, G, HPG, D = q.shape
    scale = D ** -0.5
    kc = _compress(kc_raw, pos_k, ck_w1, ck_w2)
    vc = _compress(vc_raw, pos_v, cv_w1, cv_w2)
    n_cmp = kc.shape[1]
    cmp_end = jnp.arange(n_cmp) * CMP_STRIDE + (CMP_LEN - 1)
    n_sel = s // SEL_LEN
    n_top = min(SEL_TOPN, n_sel)
    imp_map = _cmp_to_sel_matrix(n_cmp, n_sel)
    ks_blk = ks.reshape(b, n_sel, SEL_LEN, G, D).transpose(0, 3, 1, 2, 4)
    vs_blk = vs.reshape(b, n_sel, SEL_LEN, G, D).transpose(0, 3, 1, 2, 4)
    kw_pad = jnp.pad(kw, ((0, 0), (WIN_LEN, 0), (0, 0), (0, 0)))
    vw_pad = jnp.pad(vw, ((0, 0), (WIN_LEN, 0), (0, 0), (0, 0)))
    gather_blocks = jax.vmap(jax.vmap(lambda blk, ix: blk[ix]))
    sel_off = jnp.arange(SEL_LEN)
    blk_ids = jnp.arange(n_sel)
    nq = s // Q_BLOCK

    def chunk(args):
        ci, qc, gc = args
        t = ci * Q_BLOCK + jnp.arange(Q_BLOCK)
        s_c = jnp.einsum('bqghd,bngd->bghqn', qc, kc) * scale
        p_c, _ = _probs(s_c, cmp_end[None, :] <= t[:, None])
        o_c = jnp.einsum('bghqn,bngd->bqghd', p_c.astype(qc.dtype), vc)
        imp = jnp.einsum('bghqn,nj->bgqj', p_c, imp_map)
        rel = (t // SEL_LEN)[:, None] - blk_ids[None, :]
        forced = (blk_ids[None, :] == 0) | ((rel >= 0) & (rel < SEL_LOCAL))
        imp = jnp.where(forced, MASK_BIG, jnp.where(rel >= 0, imp, -MASK_BIG))
        _, idx = lax.top_k(imp, n_top)
        k_sel = gather_blocks(ks_blk, idx).reshape(b, G, Q_BLOCK, n_top * SEL_LEN, D)
        v_sel = gather_blocks(vs_blk, idx).reshape(b, G, Q_BLOCK, n_top * SEL_LEN, D)
        key_pos = (idx[..., None] * SEL_LEN + sel_off).reshape(b, G, Q_BLOCK, n_top * SEL_LEN)
        s_s = jnp.einsum('bqghd,bgqkd->bghqk', qc, k_sel) * scale
        p_s, _ = _probs(s_s, (key_pos <= t[:, None])[:, :, None])
        o_s = jnp.einsum('bghqk,bgqkd->bqghd', p_s.astype(qc.dtype), v_sel)
        k_win = lax.dynamic_slice_in_dim(kw_pad, ci * Q_BLOCK, Q_BLOCK + WIN_LEN, axis=1)
        v_win = lax.dynamic_slice_in_dim(vw_pad, ci * Q_BLOCK, Q_BLOCK + WIN_LEN, axis=1)
        pos_w = ci * Q_BLOCK - WIN_LEN + jnp.arange(Q_BLOCK + WIN_LEN)
        dist = t[:, None] - pos_w[None, :]
        mask_w = (pos_w[None, :] >= 0) & (dist >= 0) & (dist < WIN_LEN)
        s_w = jnp.einsum('bqghd,bkgd->bghqk', qc, k_win) * scale
        p_w, _ = _probs(s_w, mask_w)
        o_w = jnp.einsum('bghqk,bkgd->bqghd', p_w.astype(qc.dtype), v_win)
        return gc[..., 0:1] * o_c + gc[..., 1:2] * o_s + gc[..., 2:3] * o_w

    q_chunks = q.reshape(b, nq, Q_BLOCK, G, HPG, D).swapaxes(0, 1)
    g_chunks = gates.reshape(b, nq, Q_BLOCK, G, HPG, 3).swapaxes(0, 1)
    out = lax.map(chunk, (jnp.arange(nq), q_chunks, g_chunks))
    return out.swapaxes(0, 1).reshape(b, s, G * HPG * D)


def _even_mixer(h, w_in, w_out, pos_k, pos_v, ck_w1, ck_w2, cv_w1, cv_w2, cos_p, sin_p):
    b, s, _ = h.shape
    q_a, kc, vc, ks, vs, kw, vw, gates, q_b, k_b, v_b = _split(h @ w_in, EVEN_IN_SPLITS)
    heads = lambda t: t.reshape(b, s, -1, HEAD_DIM)
    q_a = _partial_rope(heads(q_a), cos_p, sin_p).reshape(b, s, NSA_KV_GROUPS, NSA_HPG, HEAD_DIM)
    kc, ks, kw = [_partial_rope(heads(t), cos_p, sin_p) for t in (kc, ks, kw)]
    gates = jax.nn.sigmoid(gates).reshape(b, s, NSA_KV_GROUPS, NSA_HPG, 3)
    o_a = _nsa(q_a, kc, heads(vc), ks, heads(vs), kw, heads(vw), gates,
               pos_k, pos_v, ck_w1, ck_w2, cv_w1, cv_w2)
    o_b = _dilated_mixture(_partial_rope(heads(q_b), cos_p, sin_p),
                           _partial_rope(heads(k_b), cos_p, sin_p), heads(v_b))
    return jnp.concatenate([o_a, o_b], axis=-1) @ w_out


def _mla_mixer(h, w_in, q_norm, kv_norm, w_uq, w_ukv, w_out, cos_m, sin_m):
    b, s, _ = h.shape
    H = MLA_HEADS
    cq, ckv, k_rope = _split(h @ w_in, (MLA_Q_RANK, MLA_KV_RANK, MLA_ROPE))
    q = (_rms_norm(cq, q_norm) @ w_uq).reshape(b, s, H, MLA_NOPE + MLA_ROPE)
    q_nope, q_rope = q[..., :MLA_NOPE], _rope(q[..., MLA_NOPE:], cos_m, sin_m)
    k_rope = _rope(k_rope[:, :, None, :], cos_m, sin_m)[:, :, 0]
    kv = (_rms_norm(ckv, kv_norm) @ w_ukv).reshape(b, s, H, MLA_NOPE + MLA_V)
    k_nope, v = kv[..., :MLA_NOPE], kv[..., MLA_NOPE:]
    scale = (MLA_NOPE + MLA_ROPE) ** -0.5
    nq = s // Q_BLOCK
    k_pos = jnp.arange(s)

    def chunk(args):
        ci, qn, qr = args
        t = ci * Q_BLOCK + jnp.arange(Q_BLOCK)
        sc = (jnp.einsum('bqhd,bkhd->bhqk', qn, k_nope) + jnp.einsum('bqhr,bkr->bhqk', qr, k_rope)) * scale
        p, _ = _probs(sc, k_pos[None, :] <= t[:, None])
        return jnp.einsum('bhqk,bkhd->bqhd', p.astype(v.dtype), v)

    qn_c = q_nope.reshape(b, nq, Q_BLOCK, H, MLA_NOPE).swapaxes(0, 1)
    qr_c = q_rope.reshape(b, nq, Q_BLOCK, H, MLA_ROPE).swapaxes(0, 1)
    out = lax.map(chunk, (jnp.arange(nq), qn_c, qr_c))
    return out.swapaxes(0, 1).reshape(b, s, H * MLA_V) @ w_out


def setup_inputs(seed: int = 0) -> dict:
    key = jax.random.key(seed)
    specs = [
        ('x', (BATCH, SEQ, D_MODEL), 'w', 1.0),
        ('ffn1_norm', (DEPTH, D_MODEL), 'g', 0.0),
        ('ffn1_w_gate', (DEPTH, D_MODEL, D_FF), 'w', D_MODEL ** -0.5),
        ('ffn1_w_up', (DEPTH, D_MODEL, D_FF), 'w', D_MODEL ** -0.5),
        ('ffn1_w_down', (DEPTH, D_FF, D_MODEL), 'w', D_FF ** -0.5),
        ('ffn2_norm', (DEPTH, D_MODEL), 'g', 0.0),
        ('ffn2_w_gate', (DEPTH, D_MODEL, D_FF), 'w', D_MODEL ** -0.5),
        ('ffn2_w_up', (DEPTH, D_MODEL, D_FF), 'w', D_MODEL ** -0.5),
        ('ffn2_w_down', (DEPTH, D_FF, D_MODEL), 'w', D_FF ** -0.5),
        ('mix_norm', (DEPTH, D_MODEL), 'g', 0.0),
        ('ev_w_in', (N_EVEN, D_MODEL, EVEN_IN_W), 'w', D_MODEL ** -0.5),
        ('ev_w_out', (N_EVEN, EVEN_OUT_W, D_MODEL), 'w', EVEN_OUT_W ** -0.5),
        ('nsa_cmp_pos_k', (N_EVEN, CMP_LEN, HEAD_DIM), 'w', 0.1),
        ('nsa_cmp_pos_v', (N_EVEN, CMP_LEN, HEAD_DIM), 'w', 0.1),
        ('nsa_cmp_k_w1', (N_EVEN, CMP_LEN * HEAD_DIM, CMP_HIDDEN), 'w', (CMP_LEN * HEAD_DIM) ** -0.5),
        ('nsa_cmp_k_w2', (N_EVEN, CMP_HIDDEN, HEAD_DIM), 'w', CMP_HIDDEN ** -0.5),
        ('nsa_cmp_v_w1', (N_EVEN, CMP_LEN * HEAD_DIM, CMP_HIDDEN), 'w', (CMP_LEN * HEAD_DIM) ** -0.5),
        ('nsa_cmp_v_w2', (N_EVEN, CMP_HIDDEN, HEAD_DIM), 'w', CMP_HIDDEN ** -0.5),
        ('mla_w_in', (N_ODD, D_MODEL, MLA_IN_W), 'w', D_MODEL ** -0.5),
        ('mla_q_norm', (N_ODD, MLA_Q_RANK), 'g', 0.0),
        ('mla_kv_norm', (N_ODD, MLA_KV_RANK), 'g', 0.0),
        ('mla_w_uq', (N_ODD, MLA_Q_RANK, MLA_HEADS * (MLA_NOPE + MLA_ROPE)), 'w', MLA_Q_RANK ** -0.5),
        ('mla_w_ukv', (N_ODD, MLA_KV_RANK, MLA_HEADS * (MLA_NOPE + MLA_V)), 'w', MLA_KV_RANK ** -0.5),
        ('mla_w_out', (N_ODD, MLA_HEADS * MLA_V, D_MODEL), 'w', (MLA_HEADS * MLA_V) ** -0.5),
        ('final_norm', (D_MODEL,), 'g', 0.0),
    ]
    keys = jax.random.split(key, len(specs))
    out = {}
    for k, (name, shape, kind, sc) in zip(keys, specs):
        z = jax.random.normal(k, shape, dtype=jnp.float32)
        out[name] = 1.0 + 0.02 * z if kind == 'g' else z * sc
    return out


def reference(x, ffn1_norm, ffn1_w_gate, ffn1_w_up, ffn1_w_down,
              ffn2_norm, ffn2_w_gate, ffn2_w_up, ffn2_w_down, mix_norm,
              ev_w_in, ev_w_out, nsa_cmp_pos_k, nsa_cmp_pos_v,
              nsa_cmp_k_w1, nsa_cmp_k_w2, nsa_cmp_v_w1, nsa_cmp_v_w2,
              mla_w_in, mla_q_norm, mla_kv_norm, mla_w_uq, mla_w_ukv, mla_w_out,
              final_norm):
    s = x.shape[1]
    cos_p, sin_p = _rope_table(s, ROT_DIMS)
    cos_m, sin_m = _rope_table(s, MLA_ROPE)
    for l in range(DEPTH):
        x = x + 0.5 * _swiglu(_rms_norm(x, ffn1_norm[l]), ffn1_w_gate[l], ffn1_w_up[l], ffn1_w_down[l])
        h = _rms_norm(x, mix_norm[l])
        if l % 2 == 0:
            e = l // 2
            x = x + _even_mixer(h, ev_w_in[e], ev_w_out[e], nsa_cmp_pos_k[e], nsa_cmp_pos_v[e],
                                nsa_cmp_k_w1[e], nsa_cmp_k_w2[e], nsa_cmp_v_w1[e], nsa_cmp_v_w2[e],
                                cos_p, sin_p)
        else:
            o = l // 2
            x = x + _mla_mixer(h, mla_w_in[o], mla_q_norm[o], mla_kv_norm[o], mla_w_uq[o],
                               mla_w_ukv[o], mla_w_out[o], cos_m, sin_m)
        x = x + 0.5 * _swiglu(_rms_norm(x, ffn2_norm[l]), ffn2_w_gate[l], ffn2_w_up[l], ffn2_w_down[l])
    return _rms_norm(x, final_norm)
```

```python
import numpy as np
import concourse.bass as bass
import concourse.mybir as mybir
from concourse.bass_utils import run_bass_kernel_spmd

F32 = mybir.dt.float32
BF16 = mybir.dt.bfloat16
AF = mybir.ActivationFunctionType
ALU = mybir.AluOpType
AX = mybir.AxisListType


class SemObj:
    def __init__(self, nc, name):
        self.sem = nc.alloc_semaphore(name)
        self.name = name
        self.val = 0


class EngState:
    def __init__(self, nc, eng, name):
        self.e = eng
        self.name = name
        self.so = SemObj(nc, "sE_" + name)
        self.waited = {}


class Tile:
    def __init__(self, ctx, ap, name, dma_target=False):
        self.ap = ap
        self.name = name
        self.w = None
        self.r = {}
        self.dso = None
        self.ctx = ctx

    def dsem(self):
        if self.dso is None:
            self.dso = SemObj(self.ctx.nc, "sD_" + self.name)
        return self.dso

    def __getitem__(self, idx):
        return self.ap[idx]


class Ctx:
    def __init__(self, nc):
        self.nc = nc
        self.E = {n: EngState(nc, getattr(nc, n), n) for n in ["tensor", "vector", "scalar", "gpsimd", "sync"]}
        self.ntile = 0
        self.ninst = 0

    def sb(self, name, shape, dtype):
        self.ntile += 1
        return Tile(self, self.nc.alloc_sbuf_tensor(name, list(shape), dtype).ap(), name)

    def ps(self, name, shape=(128, 512), dtype=F32):
        self.ntile += 1
        return Tile(self, self.nc.alloc_psum_tensor(name, list(shape), dtype).ap(), name)

    def dram(self, name, shape, dtype, kind):
        return Tile(self, self.nc.dram_tensor(name, list(shape), dtype, kind=kind).ap(), name)

    def _deps(self, E, reads, writes, skip_self_pe=True):
        needs = {}

        def need(dep):
            if dep is None:
                return
            so, v = dep
            if needs.get(so, 0) < v:
                needs[so] = v

        for t in reads:
            need(t.w)
        for t in writes:
            need(t.w)
            for d in t.r.values():
                need(d)
        for so, v in needs.items():
            if so is E.so and E.name == "tensor":
                continue
            if E.waited.get(so, 0) >= v:
                continue
            E.e.wait_ge(so.sem, v)
            E.waited[so] = v

    def op(self, eng, fn, reads=(), writes=()):
        E = self.E[eng]
        self._deps(E, reads, writes)
        ins = fn(E.e)
        E.so.val += 1
        ins.then_inc(E.so.sem, 1)
        me = (E.so, E.so.val)
        for t in reads:
            t.r[E.so] = me
        for t in writes:
            t.w = me
            t.r = {}
        self.ninst += 1
        return ins

    def dma(self, eng, out_t, out_ap, in_t, in_ap, **kw):
        E = self.E[eng]
        self._deps(E, [in_t], [out_t])
        so = out_t.dsem()
        ins = E.e.dma_start(out=out_ap, in_=in_ap, **kw)
        so.val += 16
        ins.then_inc(so.sem, 16)
        me = (so, so.val)
        in_t.r[so] = me
        out_t.w = me
        out_t.r = {}
        self.ninst += 1
        return ins

    def finish(self, out_tiles):
        E = self.E["sync"]
        for t in out_tiles:
            if t.w is not None:
                so, v = t.w
                E.e.wait_ge(so.sem, v)


NORM_EPS = 1e-6
D = 1024
KC = 8
FF = 2816
FC = 22
TT = 512


class Common:
    def __init__(self, ctx, norm=True):
        self.ctx = ctx
        self.psum = [ctx.ps(f"ps{i}") for i in range(8)]
        if norm:
            self.init_eps()
            self.ones = ctx.sb("ones_f32", (128, 128), F32)
            ctx.op("vector", lambda e: e.memset(self.ones[:], 1.0), writes=[self.ones])
            self.sq = [ctx.sb(f"sq{i}", (128, TT), F32) for i in range(2)]
            self.rstd = ctx.sb("rstd", (128, TT), F32)
        self.rr = 0

    def rmsnorm_T(self, xt, nk, gam, outT, n, pbank, width, out2=None):
        ctx = self.ctx
        ps = pbank
        for kc in range(nk):
            sq = self.sq[self.rr % 2]
            self.rr += 1
            ctx.op("scalar", lambda e, kc=kc, sq=sq: e.activation(out=sq[:, :n], in_=xt[:, kc, :n], func=AF.Square),
                   reads=[xt], writes=[sq])
            ctx.op("tensor", lambda e, kc=kc, sq=sq: e.matmul(ps[:, :n], lhsT=self.ones[:], rhs=sq[:, :n],
                                                             start=(kc == 0), stop=(kc == nk - 1)),
                   reads=[self.ones, sq], writes=[ps])
        rstd = self.rstd
        ctx.op("scalar", lambda e: e.activation(out=rstd[:, :n], in_=ps[:, :n], func=AF.Sqrt,
                                                 bias=self.eps_t(), scale=1.0 / width),
               reads=[ps, self.eps_tile], writes=[rstd])
        ctx.op("vector", lambda e: e.reciprocal(out=rstd[:, :n], in_=rstd[:, :n]), reads=[rstd], writes=[rstd])
        for kc in range(nk):
            ctx.op("vector", lambda e, kc=kc: e.scalar_tensor_tensor(
                out=outT[:, kc, :n], in0=xt[:, kc, :n], scalar=gam[:, kc:kc + 1], in1=rstd[:, :n],
                op0=ALU.mult, op1=ALU.mult), reads=[xt, gam, rstd], writes=[outT])
            if out2 is not None:
                ctx.op("vector", lambda e, kc=kc: e.scalar_tensor_tensor(
                    out=out2[:, kc, :n], in0=xt[:, kc, :n], scalar=gam[:, kc:kc + 1], in1=rstd[:, :n],
                    op0=ALU.mult, op1=ALU.mult), reads=[xt, gam, rstd], writes=[out2])

    def eps_t(self):
        return self.eps_tile[:, 0:1]

    def init_eps(self):
        ctx = self.ctx
        self.eps_tile = ctx.sb("eps", (128, 1), F32)
        ctx.op("vector", lambda e: e.memset(self.eps_tile[:], NORM_EPS), writes=[self.eps_tile])


class FFN:
    def __init__(self, ctx, cm):
        self.ctx = ctx
        self.cm = cm
        self.hT = [ctx.sb(f"ffn_hT{i}", (128, KC, TT), BF16) for i in range(2)]
        self.wgu = [ctx.sb(f"ffn_wgu{i}", (128, 2, KC, 128), BF16) for i in range(3)]
        self.wd = [ctx.sb(f"ffn_wd{i}", (128, FC, 128), BF16) for i in range(2)]
        self.act = [ctx.sb(f"ffn_act{j}", (128, TT), BF16) for j in range(FC)]
        self.sg = [ctx.sb(f"ffn_sg{i}", (128, TT), F32) for i in range(2)]
        self.n = 0
        self.nw = 0
        self.nd = 0

    def run(self, xt, gam, wgu_d, wd_d, pb):
        ctx, cm = self.ctx, self.cm
        hT = self.hT[self.n % 2]
        self.n += 1
        cm.rmsnorm_T(xt, KC, gam, hT, TT, pb[0], D)
        for j in range(FC):
            w = self.wgu[self.nw % 3]
            self.nw += 1
            ctx.dma("gpsimd", w, w[:], wgu_d, wgu_d[j], max_dma_last_dim=4096)
            pg = pb[1 + (j % 2)]
            pu = pb[3 + (j % 2)]
            for kc in range(KC):
                ctx.op("tensor", lambda e, kc=kc, w=w, pg=pg: e.matmul(pg[:], lhsT=w[:, 0, kc, :], rhs=hT[:, kc, :],
                                                                     start=(kc == 0), stop=(kc == KC - 1)),
                       reads=[w, hT], writes=[pg])
            for kc in range(KC):
                ctx.op("tensor", lambda e, kc=kc, w=w, pu=pu: e.matmul(pu[:], lhsT=w[:, 1, kc, :], rhs=hT[:, kc, :],
                                                                     start=(kc == 0), stop=(kc == KC - 1)),
                       reads=[w, hT], writes=[pu])
            sg = self.sg[j % 2]
            ctx.op("scalar", lambda e, sg=sg, pg=pg: e.activation(out=sg[:], in_=pg[:], func=AF.Silu),
                   reads=[pg], writes=[sg])
            a = self.act[j]
            ctx.op("vector", lambda e, sg=sg, pu=pu, a=a: e.tensor_tensor(out=a[:], in0=pu[:], in1=sg[:], op=ALU.mult),
                   reads=[pu, sg], writes=[a])
        for c in range(KC):
            w = self.wd[self.nd % 2]
            self.nd += 1
            ctx.dma("gpsimd", w, w[:], wd_d, wd_d[c], max_dma_last_dim=4096)
            po = pb[5 + (c % 2)]
            for j in range(FC):
                ctx.op("tensor", lambda e, j=j, w=w, po=po: e.matmul(po[:], lhsT=w[:, j, :], rhs=self.act[j][:],
                                                                     start=(j == 0), stop=(j == FC - 1)),
                       reads=[w, self.act[j]], writes=[po])
            ctx.op("vector", lambda e, c=c, po=po: e.scalar_tensor_tensor(
                out=xt[:, c, :], in0=po[:], scalar=0.5, in1=xt[:, c, :], op0=ALU.mult, op1=ALU.add),
                reads=[po, xt], writes=[xt])


def ffn_host_layout(wg, wu, wd):
    g = wg.reshape(KC, 128, FC, 128).transpose(2, 1, 0, 3)
    u = wu.reshape(KC, 128, FC, 128).transpose(2, 1, 0, 3)
    wgu = np.ascontiguousarray(np.stack([g, u], axis=2))
    wdt = np.ascontiguousarray(wd.reshape(FC, 128, KC, 128).transpose(2, 1, 0, 3))
    return wgu, wdt


def gain_layout(g, nk):
    return np.ascontiguousarray(g.reshape(nk, 128).T)

import ml_dtypes

NBF = ml_dtypes.bfloat16
NTOK = 4096
NSLOT = 8
S = 16384
ROPE_THETA = 500000.0


def wtile(W, nk):
    return np.ascontiguousarray(W.reshape(nk, 128, -1).transpose(1, 0, 2))


class Proj:
    def __init__(self, ctx, nslots=3):
        self.ctx = ctx
        self.w = [ctx.sb(f"pw{i}", (128, KC, 128), BF16) for i in range(nslots)]
        self.n = 0

    def mm(self, w_d, w_ap, nk, M, rhsT, ps, n=TT):
        ctx = self.ctx
        w = self.w[self.n % len(self.w)]
        self.n += 1
        ctx.dma("gpsimd", w, w[:, :nk, :M], w_d, w_ap)
        for kc in range(nk):
            ctx.op("tensor", lambda e, kc=kc: e.matmul(ps[:M, :n], lhsT=w[:, kc, :M], rhs=rhsT[:, kc, :n],
                                                       start=(kc == 0), stop=(kc == nk - 1)),
                   reads=[w, rhsT], writes=[ps])


def rope_combine(ctx, out_t, out_ap, p1, p2, ct, c_ap, st, s_ap, tmp, M, n=TT):
    t1, t2 = tmp
    ctx.op("vector", lambda e: e.tensor_tensor(out=t1[:M, :n], in0=p1[:M, :n], in1=c_ap, op=ALU.mult),
           reads=[p1, ct], writes=[t1])
    ctx.op("vector", lambda e: e.tensor_tensor(out=t2[:M, :n], in0=p2[:M, :n], in1=s_ap, op=ALU.mult),
           reads=[p2, st], writes=[t2])
    ctx.op("vector", lambda e: e.tensor_tensor(out=out_ap, in0=t1[:M, :n], in1=t2[:M, :n], op=ALU.add),
           reads=[t1, t2], writes=[out_t])


EV_ROPE = [True] * 4 + [True, False, True, False, True, False] + [True] * 4 + [True] * 4 + [False] * 4
EV_COLS = list(range(0, 1280, 128)) + list(range(1304, 2840, 128))


def build_stage(kind):
    nc = bass.Bass("TRN2", target_bir_lowering=False)
    ctx = Ctx(nc)
    cm = Common(ctx)
    ffn = FFN(ctx, cm)
    pj = Proj(ctx)
    pb = cm.psum
    D_ = {}

    def din(name, shape, dt=F32):
        D_[name] = ctx.dram(name, shape, dt, "ExternalInput")
        return D_[name]

    def dout(name, shape, dt=F32):
        D_[name] = ctx.dram(name, shape, dt, "ExternalOutput")
        return D_[name]

    xT = din("xT", (128, KC, NTOK))
    outs = []
    gams = {}

    def load_gam(name, nk=KC):
        d = din(name, (128, nk))
        t = ctx.sb("sb_" + name, (128, nk), F32)
        ctx.dma("sync", t, t[:], d, d[:])
        gams[name] = t
        return t

    def ffn_in(pref):
        return (load_gam(pref + "_g"), din(pref + "_wgu", (FC, 128, 2, KC, 128)), din(pref + "_wd", (KC, 128, FC, 128)))

    xts = [ctx.sb(f"xt{i}", (128, KC, TT), F32) for i in range(2)]
    tmp = [ctx.sb(f"tmp{i}", (128, TT), F32) for i in range(2)]
    hTs = [ctx.sb(f"hmix{i}", (128, KC, TT), BF16) for i in range(2)]
    if kind in ("L3", "L5"):
        aT = din("aT", (128, KC, NTOK), BF16)
        wo = din("wo", (KC, 128, KC, 128))
        ats = [ctx.sb(f"at{i}", (128, KC, TT), BF16) for i in range(2)]
    if kind == "L1":
        fa = ffn_in("fa")
        gm = load_gam("gmix")
        win = din("win", (22, 128, KC, 128))
        wsw = din("wsw", (22, 128, KC, 128))
        wgt = din("wgt", (128, KC, 24))
        ctab = din("ctab", (128, NTOK))
        stab = din("stab", (128, NTOK))
        x1T = dout("x1T", (128, KC, NTOK))
        pjo = dout("pj", (22, 128, NTOK), BF16)
        gto = dout("gates", (24, NTOK))
        outs = [x1T, pjo, gto]
        cts = [ctx.sb(f"ct{i}", (128, TT), F32) for i in range(2)]
        sts = [ctx.sb(f"st{i}", (128, TT), F32) for i in range(2)]
        obs = [ctx.sb(f"ob{i}", (128, TT), BF16) for i in range(3)]
        gos = [ctx.sb(f"go{i}", (24, TT), F32) for i in range(2)]
        hT32 = ctx.sb("hT32", (128, KC, TT), F32)
        w32 = [ctx.sb(f"w32_{i}", (128, KC, 128), F32) for i in range(2)]
        ob32s = [ctx.sb(f"ob32_{i}", (128, TT), F32) for i in range(2)]
        q32o = dout("q32", (5, 128, NTOK), F32)
        outs.append(q32o)
        n32 = [0]

        def mm32(w_d, w_ap, ps):
            w = w32[n32[0] % 2]
            n32[0] += 1
            ctx.dma("sync", w, w[:], w_d, w_ap)
            for kc in range(KC):
                ctx.op("tensor", lambda e, kc=kc: e.matmul(ps[:], lhsT=w[:, kc, :], rhs=hT32[:, kc, :],
                                                           start=(kc == 0), stop=(kc == KC - 1)),
                       reads=[w, hT32], writes=[ps])
    if kind == "L3":
        fa = ffn_in("fa")
        fb = ffn_in("fb")
        gm = load_gam("gmix")
        gq = load_gam("gq", 2)
        gkv = load_gam("gkv", 1)
        wmi = din("wmi", (3, 128, KC, 128))
        wkr = din("wkr", (2, 128, KC, 32))
        wuq = din("wuq", (16, 128, 2, 96))
        wuqs = din("wuqs", (16, 128, 2, 96))
        wukv = din("wukv", (16, 128, 1, 128))
        cq_t = din("cq_t", (96, NTOK))
        sq_t = din("sq_t", (96, NTOK))
        ck_t = din("ck_t", (32, NTOK))
        sk_t = din("sk_t", (32, NTOK))
        x3T = dout("x3T", (128, KC, NTOK))
        qTo = dout("qT", (16, 96, NTOK), BF16)
        kvo = dout("kvT", (16, 128, NTOK), BF16)
        kro = dout("krT", (32, NTOK), BF16)
        outs = [x3T, qTo, kvo, kro]
        cts = [ctx.sb(f"ct{i}", (96, TT), F32) for i in range(2)]
        sts = [ctx.sb(f"st{i}", (96, TT), F32) for i in range(2)]
        ckts = [ctx.sb(f"ckt{i}", (32, TT), F32) for i in range(2)]
        skts = [ctx.sb(f"skt{i}", (32, TT), F32) for i in range(2)]
        obs = [ctx.sb(f"ob{i}", (128, TT), BF16) for i in range(3)]
        cqT = [ctx.sb(f"cqT{i}", (128, 2, TT), F32) for i in range(2)]
        ckvT = [ctx.sb(f"ckvT{i}", (128, 1, TT), F32) for i in range(2)]
        cqn = [ctx.sb(f"cqn{i}", (128, 2, TT), BF16) for i in range(2)]
        ckvn = [ctx.sb(f"ckvn{i}", (128, 1, TT), BF16) for i in range(2)]
    if kind == "L5":
        fa = ffn_in("fa")
        gf = load_gam("gfin")
        yT = dout("yT", (128, KC, NTOK))
        outs = [yT]
        yts = [ctx.sb(f"yt{i}", (128, KC, TT), F32) for i in range(2)]

    nob = 0
    for t in range(NSLOT):
        ts = slice(t * TT, (t + 1) * TT)
        xt = xts[t % 2]
        ctx.dma("sync", xt, xt[:], xT, xT[:, :, ts])
        if kind in ("L3", "L5"):
            at = ats[t % 2]
            ctx.dma("sync", at, at[:], aT, aT[:, :, ts])
            for c in range(KC):
                ps = pb[5 + (c % 2)]
                pj.mm(wo, wo[c], KC, 128, at, ps)
                ctx.op("vector", lambda e, c=c, ps=ps: e.tensor_tensor(out=xt[:, c, :], in0=ps[:], in1=xt[:, c, :], op=ALU.add),
                       reads=[ps, xt], writes=[xt])
        if kind == "L1":
            ffn.run(xt, fa[0], fa[1], fa[2], pb[0:7])
            ctx.dma("sync", x1T, x1T[:, :, ts], xt, xt[:])
            hT = hTs[t % 2]
            cm.rmsnorm_T(xt, KC, gm, hT, TT, pb[0], D, out2=hT32)
            ct, st = cts[t % 2], sts[t % 2]
            ctx.dma("sync", ct, ct[:], ctab, ctab[:, ts])
            ctx.dma("sync", st, st[:], stab, stab[:, ts])
            for c in range(22):
                p1 = pb[1 + (c % 2)]
                ob = obs[nob % 3]
                nob += 1
                if c < 5:
                    p2 = pb[3 + (c % 2)]
                    mm32(win, win[c], p1)
                    mm32(wsw, wsw[c], p2)
                    ob32 = ob32s[c % 2]
                    rope_combine(ctx, ob32, ob32[:], p1, p2, ct, ct[:], st, st[:], tmp, 128)
                    ctx.op("scalar", lambda e, ob=ob, ob32=ob32: e.activation(out=ob[:], in_=ob32[:], func=AF.Copy),
                           reads=[ob32], writes=[ob])
                    ctx.dma("sync", q32o, q32o[c, :, ts], ob32, ob32[:])
                    ctx.dma("sync", pjo, pjo[c, :, ts], ob, ob[:])
                    continue
                pj.mm(win, win[c], KC, 128, hT, p1)
                if EV_ROPE[c]:
                    p2 = pb[3 + (c % 2)]
                    pj.mm(wsw, wsw[c], KC, 128, hT, p2)
                    rope_combine(ctx, ob, ob[:], p1, p2, ct, ct[:], st, st[:], tmp, 128)
                else:
                    ctx.op("scalar", lambda e, p1=p1, ob=ob: e.activation(out=ob[:], in_=p1[:], func=AF.Copy),
                           reads=[p1], writes=[ob])
                ctx.dma("sync", pjo, pjo[c, :, ts], ob, ob[:])
            p1 = pb[7]
            pj.mm(wgt, wgt[:], KC, 24, hT, p1)
            go = gos[t % 2]
            ctx.op("scalar", lambda e, p1=p1, go=go: e.activation(out=go[:], in_=p1[:24, :], func=AF.Sigmoid),
                   reads=[p1], writes=[go])
            ctx.dma("sync", gto, gto[:, ts], go, go[:])
        if kind == "L3":
            ffn.run(xt, fa[0], fa[1], fa[2], pb[0:7])
            ffn.run(xt, fb[0], fb[1], fb[2], pb[0:7])
            ctx.dma("sync", x3T, x3T[:, :, ts], xt, xt[:])
            hT = hTs[t % 2]
            cm.rmsnorm_T(xt, KC, gm, hT, TT, pb[0], D)
            cq, ckv, cqn_, ckvn_ = cqT[t % 2], ckvT[t % 2], cqn[t % 2], ckvn[t % 2]
            for i in range(3):
                p1 = pb[1 + (i % 2)]
                pj.mm(wmi, wmi[i], KC, 128, hT, p1)
                dst_t, dst = (cq, cq[:, i, :]) if i < 2 else (ckv, ckv[:, 0, :])
                ctx.op("scalar", lambda e, p1=p1, dst=dst: e.activation(out=dst, in_=p1[:], func=AF.Copy),
                       reads=[p1], writes=[dst_t])
            ckt, skt = ckts[t % 2], skts[t % 2]
            ctx.dma("sync", ckt, ckt[:], ck_t, ck_t[:, ts])
            ctx.dma("sync", skt, skt[:], sk_t, sk_t[:, ts])
            p1, p2 = pb[3], pb[4]
            pj.mm(wkr, wkr[0], KC, 32, hT, p1)
            pj.mm(wkr, wkr[1], KC, 32, hT, p2)
            ob = obs[nob % 3]
            nob += 1
            rope_combine(ctx, ob, ob[:32, :], p1, p2, ckt, ckt[:], skt, skt[:], tmp, 32)
            ctx.dma("sync", kro, kro[:, ts], ob, ob[:32, :])
            cm.rmsnorm_T(cq, 2, gq, cqn_, TT, pb[0], 256)
            cm.rmsnorm_T(ckv, 1, gkv, ckvn_, TT, pb[0], 128)
            ct, st = cts[t % 2], sts[t % 2]
            ctx.dma("sync", ct, ct[:], cq_t, cq_t[:, ts])
            ctx.dma("sync", st, st[:], sq_t, sq_t[:, ts])
            for h in range(16):
                p1 = pb[1 + (h % 2)]
                p2 = pb[3 + (h % 2)]
                pj.mm(wuq, wuq[h], 2, 96, cqn_, p1)
                pj.mm(wuqs, wuqs[h], 2, 96, cqn_, p2)
                ob = obs[nob % 3]
                nob += 1
                rope_combine(ctx, ob, ob[:96, :], p1, p2, ct, ct[:], st, st[:], tmp, 96)
                ctx.dma("sync", qTo, qTo[h, :, ts], ob, ob[:96, :])
            for h in range(16):
                p1 = pb[5 + (h % 2)]
                pj.mm(wukv, wukv[h], 1, 128, ckvn_, p1)
                ob = obs[nob % 3]
                nob += 1
                ctx.op("scalar", lambda e, p1=p1, ob=ob: e.activation(out=ob[:], in_=p1[:], func=AF.Copy),
                       reads=[p1], writes=[ob])
                ctx.dma("sync", kvo, kvo[h, :, ts], ob, ob[:])
        if kind == "L5":
            ffn.run(xt, fa[0], fa[1], fa[2], pb[0:7])
            yt = yts[t % 2]
            cm.rmsnorm_T(xt, KC, gf, yt, TT, pb[0], D)
            ctx.dma("sync", yT, yT[:, :, ts], yt, yt[:])
    ctx.finish(outs)
    return nc, ctx


NEG = -30000.0
BIG = 1e30
NQB = 32


class Attn:
    def __init__(self, ctx, cm, scale, consts, ns3=False, sbanks=None, lbanks=None):
        self.ctx, self.cm, self.scale = ctx, cm, scale
        self.S = sbanks if sbanks else [cm.psum[0], cm.psum[1]] + ([cm.psum[7]] if ns3 else [])
        self.lag = len(self.S) - 1
        self.pending = []
        self.LB = lbanks if lbanks else [cm.psum[5], cm.psum[6]]
        self.P = [ctx.sb(f"P{i}", (128, 512), BF16) for i in range(4)]
        self.OL = [ctx.sb(f"OL{i}", (128, 512), F32) for i in range(2)]
        self.rl = [ctx.sb(f"rl{i}", (64, 512), F32) for i in range(2)]
        self.ident = ctx.sb("sb_ident", (128, 128), BF16)
        self.sel = ctx.sb("sb_sel", (128, 64), F32)
        ctx.dma("sync", self.ident, self.ident[:], consts["ident"], consts["ident"][:])
        ctx.dma("sync", self.sel, self.sel[:], consts["sel"], consts["sel"][:])
        self.i = 0
        self.j = 0

    def step(self, nk, c0, c1, mains, masks, pvs):
        ctx = self.ctx
        S = self.S[self.i % len(self.S)]
        P = self.P[self.i % 4]
        self.i += 1
        allm = list(mains) + list(masks)
        n = len(allm)
        for k, (lt, lap, rt, rap, cs) in enumerate(allm):
            ctx.op("tensor", lambda e, lap=lap, rap=rap, cs=cs, k=k: e.matmul(
                S[:nk, cs], lhsT=lap, rhs=rap, start=(k == 0), stop=(k == n - 1), skip_group_check=True),
                reads=[lt, rt], writes=[S])
        ctx.op("scalar", lambda e: e.activation(out=P[:nk, c0:c1], in_=S[:nk, c0:c1], func=AF.Exp, scale=self.scale),
               reads=[S], writes=[P])
        self.pending.append((nk, P, pvs))
        while len(self.pending) > self.lag:
            self._flush_one()

    def _flush_one(self):
        ctx = self.ctx
        nk, P, pvs = self.pending.pop(0)
        for (vt, vap, pcs, acc, acc_ap, start) in pvs:
            ctx.op("tensor", lambda e, vap=vap, pcs=pcs, acc_ap=acc_ap, start=start: e.matmul(
                acc_ap, lhsT=vap, rhs=P[:nk, pcs], start=start, stop=True, skip_group_check=True),
                reads=[vt, P], writes=[acc])

    def flush(self):
        while self.pending:
            self._flush_one()

    def finish(self, acc):
        self.flush()
        ctx = self.ctx
        OL = self.OL[self.j % 2]
        rl = self.rl[self.j % 2]
        LB = self.LB[self.j % len(self.LB)]
        self.j += 1
        ctx.op("scalar", lambda e: e.activation(out=OL[:], in_=acc[:], func=AF.Copy), reads=[acc], writes=[OL])
        ctx.op("tensor", lambda e: e.matmul(LB[:64, :], lhsT=self.sel[:], rhs=OL[:], start=True, stop=True),
               reads=[self.sel, OL], writes=[LB])
        ctx.op("vector", lambda e: e.tensor_scalar(out=rl[:], in0=LB[:64, :], scalar1=1e-30, scalar2=None, op0=ALU.max),
               reads=[LB], writes=[rl])
        ctx.op("vector", lambda e: e.reciprocal(out=rl[:], in_=rl[:]), reads=[rl], writes=[rl])
        return OL, rl


def attn_consts_np():
    ident = np.eye(128, dtype=np.float32).astype(NBF)
    sel = np.zeros((128, 64), np.float32)
    sel[64 + np.arange(64), np.arange(64)] = 1.0
    kl = np.arange(128)[:, None]
    ql = np.arange(512)[None, :]
    mc = np.stack([np.where(ql >= kl + o, 0.0, NEG) for o in (0, 128, 256, 384)], axis=1)
    return {"ident": ident, "sel": sel, "mcausal": mc.astype(NBF)}


def build_mla():
    nc = bass.Bass("TRN2", target_bir_lowering=False)
    ctx = Ctx(nc)
    cm = Common(ctx, norm=False)
    NU = 4
    qT = ctx.dram("qT", (NU, 96, S), BF16, "ExternalInput")
    kT = ctx.dram("kT", (NU, 96, S), BF16, "ExternalInput")
    v1 = ctx.dram("v1", (NU, 128, 128, 128), BF16, "ExternalInput")
    cd = {"ident": ctx.dram("ident", (128, 128), BF16, "ExternalInput"),
          "sel": ctx.dram("sel", (128, 64), F32, "ExternalInput")}
    mcd = ctx.dram("mcausal", (128, 4, 512), BF16, "ExternalInput")
    oT = ctx.dram("oT", (NU, 64, S), BF16, "ExternalOutput")
    at = Attn(ctx, cm, 96 ** -0.5, cd, ns3=True)
    mc = ctx.sb("mc", (128, 4, 512), BF16)
    ctx.dma("sync", mc, mc[:], mcd, mcd[:])
    Kb = [ctx.sb(f"Kb{i}", (96, S), BF16) for i in range(2)]
    Vb = [ctx.sb(f"Vb{i}", (128, 128, 128), BF16) for i in range(2)]
    Qb = [ctx.sb(f"Qb{i}", (96, 512), BF16) for i in range(3)]
    Ob = [ctx.sb(f"Ob{i}", (64, 512), BF16) for i in range(2)]
    acc = [cm.psum[2], cm.psum[3]]
    n = 0
    for u in range(NU):
        K, V = Kb[u % 2], Vb[u % 2]
        ctx.dma("sync", K, K[:], kT, kT[u])
        ctx.dma("sync", V, V[:], v1, v1[u])
        for qb in range(NQB):
            Q = Qb[n % 3]
            A = acc[n % 2]
            O = Ob[n % 2]
            n += 1
            ctx.dma("sync", Q, Q[:], qT, qT[u, :, qb * 512:(qb + 1) * 512])
            nkt = 4 * qb + 4
            for kt in range(nkt):
                d = kt - 4 * qb
                c0 = d * 128 if d > 0 else 0
                mains = [(K, K[:, kt * 128:(kt + 1) * 128], Q, Q[:, c0:512], slice(c0, 512))]
                masks = []
                if d >= 0:
                    masks = [(at.ident, at.ident[:], mc, mc[:, d, c0:512], slice(c0, 512))]
                pvs = [(V, V[:, kt, :], slice(c0, 512), A, A[:, c0:512], kt == 0)]
                at.step(128, c0, 512, mains, masks, pvs)
            OL, rl = at.finish(A)
            ctx.op("vector", lambda e, OL=OL, rl=rl, O=O: e.tensor_tensor(out=O[:], in0=OL[:64, :], in1=rl[:], op=ALU.mult),
                   reads=[OL, rl], writes=[O])
            ctx.dma("sync", oT, oT[u, :, qb * 512:(qb + 1) * 512], O, O[:])
    ctx.finish([oT])
    return nc, ctx


def nsa_consts_np():
    c = attn_consts_np()
    kl = np.arange(128)[:, None]
    ql = np.arange(512)[None, :]
    d = ql - kl
    c["mwin"] = np.stack([np.where((d - o >= 0) & (d - o < 512), 0.0, NEG) for o in range(-512, 512, 128)], 1).astype(NBF)
    c["mcmp"] = np.stack([np.where(ql - 16 * kl >= 31 - 512 * dl, 0.0, NEG) for dl in range(5)], 1).astype(NBF)
    E = np.zeros((128, 64, 128), np.float32)
    for m in range(64):
        for k in range(128):
            E[2 * m + k // 64, m, k] = 30000.0
    c["E"] = E.astype(NBF)
    q = np.arange(128)[:, None]
    npr = np.arange(-1, 8)[None, :]
    c["mtm"] = np.where(q >= 31 + 16 * npr, 0.0, NEG).astype(NBF)
    lo = (np.arange(128) < 64)[:, None]
    c["mul3"] = np.where(lo, np.array([[0., 0., 0.]]), np.array([[1., 0., 0.]])).astype(np.float32)
    c["add3"] = np.where(lo, np.array([[BIG, BIG, -BIG]]), np.array([[0., BIG, BIG]])).astype(np.float32)
    c["identf"] = np.eye(128, dtype=np.float32)
    sg = np.zeros((6, 6, 64), np.float32)
    for r in range(6):
        sg[r, r, :] = 1.0
    c["selg"] = sg
    return c


NSA_CONST_SHAPES = {"ident": ((128, 128), BF16), "sel": ((128, 64), F32), "mcausal": ((128, 4, 512), BF16),
                    "mwin": ((128, 8, 512), BF16), "mcmp": ((128, 5, 512), BF16), "E": ((128, 64, 128), BF16),
                    "mtm": ((128, 9), BF16), "mul3": ((128, 3), F32), "add3": ((128, 3), F32),
                    "identf": ((128, 128), F32), "selg": ((6, 6, 64), F32)}


USE32 = True
DBG_SKIP = set()


def build_nsa(nqb=NQB):
    nc = bass.Bass("TRN2", target_bir_lowering=False)
    ctx = Ctx(nc)
    cm = Common(ctx, norm=False)
    pb = cm.psum
    din = lambda n, s, dt=BF16: ctx.dram(n, s, dt, "ExternalInput")
    qg = din("qg", (128, 2, S), F32 if USE32 else BF16)
    qmy = din("qmy", (128, S))
    kcraw = din("kcraw", (64, 16, 1024), F32)
    vcraw = din("vcraw", (64, S))
    w1k = din("w1k", (128, 32, 128), F32)
    w1v = din("w1v", (64, 32, 128), F32)
    posk = din("posk", (128, 32, 8), F32)
    posv = din("posv", (64, 32), F32)
    w2k = din("w2k", (128, 128), F32)
    w2v = din("w2v", (128, 64), F32)
    ksT = din("ksT", (128, S))
    vs1 = din("vs1", (128, 128, 128))
    kwT = din("kwT", (128, 512 + S))
    vw1 = din("vw1", (128, 132, 128))
    gat = din("gat", (6, S), F32)
    cd = {k: din(k, s, dt) for k, (s, dt) in NSA_CONST_SHAPES.items()}
    oA = ctx.dram("oA", (2, 64, S), BF16, "ExternalOutput")
    at = Attn(ctx, cm, 0.125, cd, sbanks=[pb[0], pb[1], pb[5]], lbanks=[pb[6]])

    def cload(name, eng="sync"):
        s, dt = NSA_CONST_SHAPES[name]
        t = ctx.sb("c_" + name, s, dt)
        ctx.dma(eng, t, t[:], cd[name], cd[name][:])
        return t
    mc, mwin, mcmp, E, mtm, mul3, add3, identf = [cload(n) for n in
                                                  ("mcausal", "mwin", "mcmp", "E", "mtm", "mul3", "add3", "identf")]
    selg = ctx.sb("c_selg", (6, 6 * 64), F32)
    ctx.dma("sync", selg, selg[:], cd["selg"], cd["selg"].ap.rearrange("a b c -> a (b c)"))
    bigA = ctx.sb("bigA", (128, S), BF16)
    bigB = ctx.sb("bigB", (128, S), BF16)
    bigB3 = bigB.ap.rearrange("p (t c) -> p t c", c=128)
    kcT = ctx.sb("kcT", (128, 1024), BF16)
    vc1 = ctx.sb("vc1", (128, 8, 128), BF16)
    kcT32 = ctx.sb("kcT32", (128, 1024), F32)
    ctx.op("gpsimd", lambda e: e.memset(bigA[:], 0.0), writes=[bigA])
    bigA32 = bigA.ap.bitcast(F32)
    k32buf = bigA32[:, 0:2080].rearrange("p (j m) -> p j m", m=130)
    w1k32 = bigA32[:, 2080:2080 + 4096].rearrange("p (l h) -> p l h", h=128)
    pos32 = ctx.sb("pos32", (128, 32, 8), F32)
    w2k32 = ctx.sb("w2k32", (128, 128), F32)
    hid32 = [ctx.sb(f"hid32_{i}", (128, 128), F32) for i in range(2)]
    posb = ctx.sb("posb", (128, 1), F32)
    ctx.dma("sync", bigA, w1k32, w1k, w1k[:])
    ctx.dma("sync", pos32, pos32[:], posk, posk[:])
    ctx.dma("sync", w2k32, w2k32[:], w2k, w2k[:])
    ps = pb[7]
    for l in range(32 if "posb" not in DBG_SKIP else 1):
        ctx.op("tensor", lambda e, l=l: e.matmul(ps[:, 0:8], lhsT=w1k32[:, l, :], rhs=pos32[:, l, :],
                                                 start=(l == 0), stop=(l == 31)), reads=[bigA, pos32], writes=[ps])
    ctx.op("vector", lambda e: e.tensor_copy(out=posb[:], in_=ps[:, 0:1]), reads=[ps], writes=[posb])
    for p in range(8 if "kpath" not in DBG_SKIP else 0):
        nm = min(130, 1024 - 128 * p)
        ctx.dma("sync", bigA, k32buf[0:64, :, 0:nm], kcraw, kcraw[:, :, 128 * p:128 * p + nm])
        ps = pb[p % 2]
        for l in range(32):
            a_, j_ = l // 16, l % 16
            ctx.op("tensor", lambda e, l=l: e.matmul(ps[:, 0:128], lhsT=w1k32[:, l, :], rhs=k32buf[:, j_, a_:a_ + 128],
                                                     start=(l == 0), stop=(l == 31)), reads=[bigA], writes=[ps])
        h32 = hid32[p % 2]
        ctx.op("scalar", lambda e: e.activation(out=h32[:], in_=ps[:, 0:128], func=AF.Silu, bias=posb[:, 0:1]),
               reads=[ps, posb], writes=[h32])
        p2 = pb[2 + (p % 2)]
        ctx.op("tensor", lambda e: e.matmul(p2[:, 0:128], lhsT=w2k32[:], rhs=h32[:], start=True, stop=True),
               reads=[w2k32, h32], writes=[p2])
        ctx.op("vector", lambda e: e.tensor_copy(out=kcT32[:, p * 128:(p + 1) * 128], in_=p2[:, 0:128]), reads=[p2], writes=[kcT32])
        ctx.op("scalar", lambda e: e.activation(out=kcT[:, p * 128:(p + 1) * 128], in_=kcT32[:, p * 128:(p + 1) * 128], func=AF.Copy), reads=[kcT32], writes=[kcT])
    ctx.dma("sync", bigB, bigB[0:64, :], vcraw, vcraw[:])
    w1s_ap = bigA[0:64, 0:4096].rearrange("p (l h) -> p l h", h=128)
    poss = ctx.sb("poss", (64, 32), BF16)
    w2vs = ctx.sb("w2vs", (128, 64), BF16)
    ctx.dma("gpsimd", w2vs, w2vs[:], w2v, w2v[:])
    hid = [ctx.sb(f"hid{i}", (128, 512), BF16) for i in range(2)]
    ctx.op("vector", lambda e: e.memset(hid[1][:], 0.0), writes=[hid[1]])
    ctx.op("vector", lambda e: e.memset(vc1[:], 1.0), writes=[vc1])
    ctx.dma("gpsimd", bigA, w1s_ap, w1v, w1v[:])
    ctx.dma("gpsimd", poss, poss[:], posv, posv[:])
    ps = pb[7]
    for l in range(32):
        ctx.op("tensor", lambda e, l=l: e.matmul(ps[:, 0:1], lhsT=w1s_ap[:, l, :], rhs=poss[:, l:l + 1],
                                                 start=(l == 0), stop=(l == 31)), reads=[bigA, poss], writes=[ps])
    posbv = ctx.sb("posbv", (128, 1), F32)
    ctx.op("vector", lambda e: e.tensor_copy(out=posbv[:], in_=ps[:, 0:1]), reads=[ps], writes=[posbv])
    for nt in range(2 if "vpath" not in DBG_SKIP else 0):
        ncol = 512 if nt == 0 else 511
        ps = pb[nt]
        for l in range(32):
            st_ = nt * 8192 + l
            en = min(S, st_ + 16 * ncol)
            ctx.op("tensor", lambda e, l=l: e.matmul(ps[:, 0:ncol], lhsT=w1s_ap[:, l, :], rhs=bigB[0:64, st_:en:16],
                                                     start=(l == 0), stop=(l == 31)), reads=[bigA, bigB], writes=[ps])
        ctx.op("scalar", lambda e: e.activation(out=hid[nt][:, 0:ncol], in_=ps[:, 0:ncol], func=AF.Silu, bias=posbv[:, 0:1]),
               reads=[ps, posbv], writes=[hid[nt]])
        for j in range(4):
            p2 = pb[2 + (j % 2)]
            ctx.op("tensor", lambda e: e.matmul(p2[:, 0:64], lhsT=hid[nt][:, j * 128:(j + 1) * 128], rhs=w2vs[:], start=True, stop=True),
                   reads=[w2vs, hid[nt]], writes=[p2])
            ctx.op("vector", lambda e: e.tensor_copy(out=vc1[:, nt * 4 + j, 0:64], in_=p2[:, 0:64]), reads=[p2], writes=[vc1])
    ctx.dma("sync", bigA, bigA[:], ksT, ksT[:])
    ctx.dma("sync", bigB, bigB[:], vs1, vs1.ap.rearrange("p t c -> p (t c)"))
    QDT = F32 if USE32 else BF16
    Qg1 = ctx.sb("Qg0", (128, 4, 512), QDT)
    Qgs = [Qg1, Qg1]
    Qms = [[ctx.sb(f"Qm{i}_{h}", (128, 512), BF16) for h in range(2)] for i in range(2)]
    ctx.op("gpsimd", lambda e: e.memset(Qg1[:], 0.0), writes=[Qg1])
    for i in range(2):
        for h in range(2):
            ctx.op("gpsimd", lambda e: e.memset(Qms[i][h][:], 0.0), writes=[Qms[i][h]])
    wKs = [ctx.sb(f"wK{i}", (128, 1024), BF16) for i in range(2)]
    wVs = [ctx.sb(f"wV{i}", (128, 8, 128), BF16) for i in range(2)]
    gts = [ctx.sb(f"gt{i}", (6, 512), F32) for i in range(2)]
    NE = 4
    es = [ctx.sb(f"e{i}", (128, 512), F32) for i in range(NE)]
    lp = [ctx.sb(f"lp{i}", (128, 2), F32) for i in range(2)]
    rlh = [ctx.sb(f"rlh{i}", (128, 1), F32) for i in range(2)]
    Aim = ctx.sb("Aim", (128, 1024), F32)
    I1 = ctx.sb("I1", (128, 256), F32)
    I2 = ctx.sb("I2", (128, 256), F32)
    m8 = ctx.sb("m8", (128, 16), F32)
    negms = [ctx.sb(f"negm{i}", (128, 256), F32) for i in range(4)]
    selTs = [[ctx.sb(f"selT{k}_{i}", (128, 512), BF16) for i in range(2)] for k in range(2)]
    fg = ctx.sb("fg", (64, 512), F32)
    tmpc = ctx.sb("tmpc", (64, 512), F32)
    accsb = ctx.sb("accsb", (64, 512), F32)
    Ob = [ctx.sb(f"Ob{i}", (64, 512), BF16) for i in range(2)]
    accC, accS, accW, GB = pb[2], pb[3], pb[4], pb[7]
    kq = kcT32 if USE32 else kcT
    st = {"ne": 0, "no": 0}

    def load(qb):
        T0 = qb * 512
        if qb >= 2:
            for hh in range(4):
                r0 = (hh % 2) * 64
                ctx.dma("sync", Qgs[qb % 2], Qgs[qb % 2][r0:r0 + 64, hh, :], qg, qg[r0:r0 + 64, hh // 2, T0:T0 + 512])
        for h in range(2):
            ctx.dma("sync", Qms[qb % 2][h], Qms[qb % 2][h][h * 64:(h + 1) * 64, :], qmy, qmy[h * 64:(h + 1) * 64, T0:T0 + 512])
        ctx.dma("sync", wKs[qb % 2], wKs[qb % 2][:], kwT, kwT[:, T0:T0 + 1024])
        ctx.dma("sync", wVs[qb % 2], wVs[qb % 2][:], vw1, vw1[:, 4 * qb:4 * qb + 8, :])
        ctx.dma("sync", gts[qb % 2], gts[qb % 2][:], gat, gat[:, T0:T0 + 512])

    def phase1a(qb, subs=(0, 1, 2, 3)):
        if qb < 2:
            return
        Qg = Qgs[qb % 2]
        for qsl in subs:
            qs = 4 * qb + qsl
            ncols = 8 * qs + 8
            nj = 2 * qs + 2
            negm = negms[qsl]
            halves = [(lo, min(ncols, lo + 512)) for lo in (0, 512) if lo < ncols]
            for hh in range(4):
                ch, r0 = hh // 2, (hh % 2) * 64
                lpt = lp[hh % 2]
                rl1 = rlh[hh % 2]
                ehs = []
                for hi_, (lo, hi) in enumerate(halves):
                    w = hi - lo
                    Sb = at.S[at.i % len(at.S)]
                    at.i += 1
                    a, b2 = max(lo, ncols - 9, 0), min(hi, ncols)
                    hasm = b2 > a
                    ctx.op("tensor", lambda e: e.matmul(
                        Sb[:, 0:w], lhsT=Qg[:, hh, qsl * 128:(qsl + 1) * 128], rhs=kq[:, lo:hi],
                        start=True, stop=(not hasm), skip_group_check=True), reads=[Qg, kq], writes=[Sb])
                    if hasm:
                        ctx.op("tensor", lambda e: e.matmul(
                            Sb[:, a - lo:b2 - lo], lhsT=at.ident[:], rhs=mtm[:, a - (ncols - 9):b2 - (ncols - 9)],
                            start=False, stop=True, skip_group_check=True), reads=[at.ident, mtm], writes=[Sb])
                    et = es[st["ne"] % NE]
                    st["ne"] += 1
                    ctx.op("scalar", lambda e: e.activation(
                        out=et[:, 0:w], in_=Sb[:, 0:w], func=AF.Exp, scale=0.125, accum_out=lpt[:, hi_:hi_ + 1]),
                        reads=[Sb], writes=[et, lpt])
                    ehs.append((et, lo, hi))
                if len(halves) == 2:
                    ctx.op("vector", lambda e: e.tensor_tensor(out=lpt[:, 0:1], in0=lpt[:, 0:1], in1=lpt[:, 1:2], op=ALU.add),
                           reads=[lpt], writes=[lpt])
                ctx.op("vector", lambda e: e.tensor_scalar(out=rl1[:], in0=lpt[:, 0:1], scalar1=1e-30, scalar2=None, op0=ALU.max),
                       reads=[lpt], writes=[rl1])
                ctx.op("vector", lambda e: e.reciprocal(out=rl1[:], in_=rl1[:]), reads=[rl1], writes=[rl1])
                for (et, lo, hi) in ehs:
                    w = hi - lo
                    if hh == 0:
                        ctx.op("vector", lambda e: e.tensor_scalar(
                            out=Aim[:, lo:hi], in0=et[:, 0:w], scalar1=rl1[:, 0:1], scalar2=None, op0=ALU.mult),
                            reads=[et, rl1], writes=[Aim])
                    else:
                        ctx.op("vector", lambda e: e.scalar_tensor_tensor(
                            out=Aim[:, lo:hi], in0=et[:, 0:w], scalar=rl1[:, 0:1], in1=Aim[:, lo:hi], op0=ALU.mult, op1=ALU.add),
                            reads=[et, rl1, Aim], writes=[Aim])
            n4 = 4 * nj
            tt = lambda o, a_, b_, op: ctx.op("vector", lambda e: e.tensor_tensor(out=o, in0=a_, in1=b_, op=op),
                                              reads=[Aim, I1, mul3, add3], writes=[I1])
            tt(I1[:, 0:nj], Aim[:, 0:n4:4], Aim[:, 1:n4:4], ALU.add)
            tt(I1[:, 0:nj], I1[:, 0:nj], Aim[:, 2:n4:4], ALU.add)
            ctx.op("vector", lambda e: e.scalar_tensor_tensor(out=I1[:, 0:nj], in0=I1[:, 0:nj], scalar=2.0, in1=Aim[:, 3:n4:4],
                                                              op0=ALU.mult, op1=ALU.add), reads=[Aim, I1], writes=[I1])
            tt(I1[:, 1:nj], I1[:, 1:nj], Aim[:, 3:n4 - 4:4], ALU.add)
            tt(I1[:, nj - 3:nj], I1[:, nj - 3:nj], mul3[:, :], ALU.mult)
            tt(I1[:, nj - 3:nj], I1[:, nj - 3:nj], add3[:, :], ALU.add)
            ctx.op("vector", lambda e: e.memset(I1[:, 0:1], BIG), writes=[I1])
            ctx.op("gpsimd", lambda e: e.memset(negm[:], 0.0), writes=[negm])
            ctx.op("vector", lambda e: e.max(out=m8[:, 0:8], in_=I1[:, 0:nj]), reads=[I1], writes=[m8])
            ctx.op("vector", lambda e: e.match_replace(out=I2[:, 0:nj], in_to_replace=m8[:, 0:8], in_values=I1[:, 0:nj],
                                                       imm_value=-BIG), reads=[I1, m8], writes=[I2])
            ctx.op("vector", lambda e: e.max(out=m8[:, 8:16], in_=I2[:, 0:nj]), reads=[I2], writes=[m8])
            ctx.op("vector", lambda e: e.tensor_scalar(out=negm[:, 0:nj], in0=I1[:, 0:nj], scalar1=m8[:, 15:16], scalar2=-1.0,
                                                       op0=ALU.is_ge, op1=ALU.add), reads=[I1, m8], writes=[negm])

    def phase1b(qb):
        if qb < 2:
            return
        selT = selTs[qb % 2]
        njc = 1 if qb < 16 else 2
        for qsl in range(4):
            negm = negms[qsl]
            for jc in range(njc):
                ctx.op("tensor", lambda e: e.transpose(out=GB[:, 0:128], in_=negm[:, jc * 128:(jc + 1) * 128], identity=identf[:]),
                       reads=[negm, identf], writes=[GB])
                ctx.op("vector", lambda e: e.tensor_copy(out=selT[jc][:, qsl * 128:(qsl + 1) * 128], in_=GB[:, 0:128]),
                       reads=[GB], writes=[selT[jc]])

    def phase2(qb, todo):
        T0 = qb * 512
        wK, wV, gt = wKs[qb % 2], wVs[qb % 2], gts[qb % 2]
        selT = selTs[qb % 2]
        use_sel = qb >= 2
        for hl in range(2):
            r0 = hl * 64
            Qm = Qms[qb % 2][hl]
            for m in range(qb // 4 + 1):
                dl = qb - 4 * m
                masks = [(at.ident, at.ident[:], mcmp, mcmp[:, dl, :], slice(0, 512))] if dl <= 4 else []
                todo.append(lambda Qm=Qm, m=m, masks=masks: at.step(128, 0, 512, [(kcT, kcT[:, m * 128:(m + 1) * 128], Qm, Qm[:, 0:512], slice(0, 512))], masks,
                        [(vc1, vc1[:, m, :], slice(0, 512), accC, accC[:, :], m == 0)]))
            for kt in range(8):
                c0 = max(0, (kt - 4) * 128)
                c1 = min(512, 128 * kt + 128)
                todo.append(lambda Qm=Qm, kt=kt, c0=c0, c1=c1: at.step(128, c0, c1, [(wK, wK[:, kt * 128:(kt + 1) * 128], Qm, Qm[:, c0:c1], slice(c0, c1))],
                        [(at.ident, at.ident[:], mwin, mwin[:, kt, c0:c1], slice(c0, c1))],
                        [(wV, wV[:, kt, :], slice(c0, c1), accW, accW[:, c0:c1], kt == 0)]))
            for kt in range(4 * qb + 4):
                d = kt - 4 * qb
                c0 = d * 128 if d > 0 else 0
                masks = []
                if use_sel:
                    masks.append((E, E[:, kt % 64, :], selT[kt // 64], selT[kt // 64][:, c0:512], slice(c0, 512)))
                if d >= 0:
                    masks.append((at.ident, at.ident[:], mc, mc[:, d, c0:512], slice(c0, 512)))
                todo.append(lambda Qm=Qm, kt=kt, c0=c0, masks=masks: at.step(128, c0, 512, [(bigA, bigA[:, kt * 128:(kt + 1) * 128], Qm, Qm[:, c0:512], slice(c0, 512))], masks,
                        [(bigB, bigB3[:, kt, :], slice(c0, 512), accS, accS[:, c0:512], kt == 0)]))
            todo.append(lambda hl=hl: epilogue(qb, hl))

    def epilogue(qb, hl):
        T0 = qb * 512
        gt = gts[qb % 2]
        for br, acc in enumerate((accC, accS, accW)):
            OL, rl = at.finish(acc)
            r = hl * 3 + br
            ctx.op("tensor", lambda e: e.matmul(GB[:64, :], lhsT=selg[:, r * 64:(r + 1) * 64], rhs=gt[:, :], start=True, stop=True),
                   reads=[selg, gt], writes=[GB])
            ctx.op("vector", lambda e: e.tensor_tensor(out=fg[:], in0=GB[:64, :], in1=rl[:], op=ALU.mult),
                   reads=[GB, rl], writes=[fg])
            if br == 0:
                ctx.op("vector", lambda e: e.tensor_tensor(out=accsb[:], in0=OL[:64, :], in1=fg[:], op=ALU.mult),
                       reads=[OL, fg], writes=[accsb])
            else:
                ctx.op("vector", lambda e: e.tensor_tensor(out=tmpc[:], in0=OL[:64, :], in1=fg[:], op=ALU.mult),
                       reads=[OL, fg], writes=[tmpc])
                ctx.op("vector", lambda e: e.tensor_tensor(out=accsb[:], in0=accsb[:], in1=tmpc[:], op=ALU.add),
                       reads=[accsb, tmpc], writes=[accsb])
        O = Ob[st["no"] % 2]
        st["no"] += 1
        ctx.op("scalar", lambda e: e.activation(out=O[:], in_=accsb[:], func=AF.Copy), reads=[accsb], writes=[O])
        ctx.dma("sync", oA, oA[hl, :, T0:T0 + 512], O, O[:])

    load(0)
    phase1a(0)
    for qb in range(nqb):
        phase1b(qb)
        todo = []
        phase2(qb, todo)
        nxt = qb + 1 < nqb
        if nxt:
            load(qb + 1)
        n = len(todo)
        cuts = {(n * k) // 4: k for k in range(4)}
        for i, fn in enumerate(todo):
            if nxt and i in cuts:
                phase1a(qb + 1, (cuts[i],))
            fn()
    ctx.finish([oA])
    return nc, ctx


def dil_consts_np():
    c = attn_consts_np()
    kl = np.arange(128)[:, None]
    ql = np.arange(512)[None, :]
    d = ql - kl
    c["md1"] = np.stack([np.where((d - o >= 0) & (d - o <= 128), 0.0, NEG) for o in range(-128, 512, 128)], 1).astype(NBF)
    i4 = ql % 128
    c["md4"] = np.stack([np.where(i4 <= kl, 0.0, NEG), np.where(i4 >= kl, 0.0, NEG)], 1).astype(NBF)
    i16 = ql % 32
    c["md16"] = np.stack([np.where(i16 <= kl, 0.0, NEG), np.where(i16 >= kl, 0.0, NEG)], 1).astype(NBF)
    del c["mcausal"]
    return c


DIL_CONST_SHAPES = {"ident": ((128, 128), BF16), "sel": ((128, 64), F32), "md1": ((128, 5, 512), BF16),
                    "md4": ((128, 2, 512), BF16), "md16": ((128, 2, 512), BF16)}


def build_dil(nqb=NQB):
    nc = bass.Bass("TRN2", target_bir_lowering=False)
    ctx = Ctx(nc)
    cm = Common(ctx, norm=False)
    pb = cm.psum
    din = lambda n, s, dt=BF16: ctx.dram(n, s, dt, "ExternalInput")
    qd = din("qd", (128, S))
    kb1T = din("kb1T", (128, 128 + S))
    vb1 = din("vb1", (2, 128, 129, 128))
    kb4T = din("kb4T", (128, 4, 128 + 4096))
    vb4 = din("vb4", (2, 128, 4, 33, 128))
    kb16T = din("kb16T", (128, 16, 128 + 1024))
    vb16 = din("vb16", (2, 16, 1152, 128))
    cd = {k: din(k, s, dt) for k, (s, dt) in DIL_CONST_SHAPES.items()}
    oB = ctx.dram("oB", (2, 64, S), BF16, "ExternalOutput")
    at = Attn(ctx, cm, 0.125, cd, ns3=True)
    ms = {}
    for name in ("md1", "md4", "md16"):
        s, dt = DIL_CONST_SHAPES[name]
        ms[name] = ctx.sb("c_" + name, s, dt)
        ctx.dma("sync", ms[name], ms[name][:], cd[name], cd[name][:])
    md1, md4, md16 = ms["md1"], ms["md4"], ms["md16"]
    Qs = [[ctx.sb(f"Qd{i}_{h}", (128, 512), BF16) for h in range(2)] for i in range(2)]
    for i in range(2):
        for h in range(2):
            ctx.op("gpsimd", lambda e: e.memset(Qs[i][h][:], 0.0), writes=[Qs[i][h]])
    K1 = [ctx.sb(f"K1_{i}", (128, 640), BF16) for i in range(2)]
    V1 = [[ctx.sb(f"V1_{i}_{h}", (128, 5, 128), BF16) for h in range(2)] for i in range(2)]
    K4 = [ctx.sb(f"K4_{i}", (128, 4, 256), BF16) for i in range(2)]
    V4 = [[ctx.sb(f"V4_{i}_{h}", (128, 4, 2, 128), BF16) for h in range(2)] for i in range(2)]
    K16 = [ctx.sb(f"K16_{i}", (128, 16, 160), BF16) for i in range(2)]
    V16A = [[ctx.sb(f"V16A_{i}_{h}", (128, 16, 128), BF16) for h in range(2)] for i in range(2)]
    V16B = [[ctx.sb(f"V16B_{i}_{h}", (32, 16, 128), BF16) for h in range(2)] for i in range(2)]
    Ob = [ctx.sb(f"Ob{i}", (64, 512), BF16) for i in range(2)]
    accs = [pb[2], pb[3]]
    n = 0
    for qb in range(nqb):
        T0 = qb * 512
        i = qb % 2
        k1, k4, k16 = K1[i], K4[i], K16[i]
        for h in range(2):
            ctx.dma("sync", Qs[i][h], Qs[i][h][h * 64:(h + 1) * 64, :], qd, qd[h * 64:(h + 1) * 64, T0:T0 + 512])
        ctx.dma("sync", k1, k1[:], kb1T, kb1T[:, T0:T0 + 640])
        ctx.dma("sync", k4, k4[:], kb4T, kb4T[:, :, 128 * qb:128 * qb + 256])
        ctx.dma("sync", k16, k16[:], kb16T, kb16T[:, :, 32 * qb:32 * qb + 160])
        for h in range(2):
            ctx.dma("sync", V1[i][h], V1[i][h][:], vb1, vb1[h, :, 4 * qb:4 * qb + 5, :])
            ctx.dma("sync", V4[i][h], V4[i][h][:], vb4, vb4[h, :, :, qb:qb + 2, :])
            ctx.dma("gpsimd", V16A[i][h], V16A[i][h][:], vb16,
                    vb16[h, :, 32 * qb:32 * qb + 128, :].rearrange("r p c -> p r c"))
            ctx.dma("gpsimd", V16B[i][h], V16B[i][h][:], vb16,
                    vb16[h, :, 32 * qb + 128:32 * qb + 160, :].rearrange("r p c -> p r c"))
        for h in range(2):
            r0 = h * 64
            Q = Qs[i][h]
            A = accs[n % 2]
            O = Ob[n % 2]
            n += 1
            v1, v4, va, vb = V1[i][h], V4[i][h], V16A[i][h], V16B[i][h]
            for kt in range(5):
                o = -128 + 128 * kt
                c0, c1 = max(0, o), min(512, 128 * kt + 128)
                at.step(128, c0, c1, [(k1, k1[:, kt * 128:(kt + 1) * 128], Q, Q[:, c0:c1], slice(c0, c1))],
                        [(at.ident, at.ident[:], md1, md1[:, kt, c0:c1], slice(c0, c1))],
                        [(v1, v1[:, kt, :], slice(c0, c1), A, A[:, c0:c1], kt == 0)])
            for kt in range(2):
                mains = [(k4, k4[:, r, kt * 128:(kt + 1) * 128], Q, Q[:, r:512:4], slice(r * 128, (r + 1) * 128))
                         for r in range(4)]
                pvs = [(v4, v4[:, r, kt, :], slice(r * 128, (r + 1) * 128), A, A[:, r:512:4], False) for r in range(4)]
                at.step(128, 0, 512, mains, [(at.ident, at.ident[:], md4, md4[:, kt, :], slice(0, 512))], pvs)
            mains = [(k16, k16[:, r, 0:128], Q, Q[:, r:512:16], slice(r * 32, (r + 1) * 32)) for r in range(16)]
            pvs = [(va, va[:, r, :], slice(r * 32, (r + 1) * 32), A, A[:, r:512:16], False) for r in range(16)]
            at.step(128, 0, 512, mains, [(at.ident, at.ident[:], md16, md16[:, 0, :], slice(0, 512))], pvs)
            mains = [(k16, k16[:, r, 128:160], Q, Q[:, r:512:16], slice(r * 32, (r + 1) * 32)) for r in range(16)]
            pvs = [(vb, vb[:, r, :], slice(r * 32, (r + 1) * 32), A, A[:, r:512:16], False) for r in range(16)]
            at.step(32, 0, 512, mains, [(at.ident, at.ident[0:32, 0:32], md16, md16[0:32, 1, :], slice(0, 512))], pvs)
            OL, rl = at.finish(A)
            ctx.op("vector", lambda e, OL=OL, rl=rl, O=O: e.tensor_tensor(out=O[:], in0=OL[:64, :], in1=rl[:], op=ALU.mult),
                   reads=[OL, rl], writes=[O])
            ctx.dma("sync", oB, oB[h, :, T0:T0 + 512], O, O[:])
    ctx.finish([oB])
    return nc, ctx


NCORE = 8
DBG = {}


def _run(nc, in_maps):
    res = run_bass_kernel_spmd(nc, in_maps, core_ids=list(range(NCORE)))
    return res.results


def _tokT(a):
    R = a.shape[1]
    return np.ascontiguousarray(a.T.reshape(R // 128, 128, a.shape[0]).transpose(1, 0, 2))


def _v1_tiles(vT, pad_rows=0):
    L = vT.shape[1]
    a = np.zeros((pad_rows + L, 128), NBF)
    a[pad_rows:, :64] = vT.T
    a[pad_rows:, 64:] = 1
    nt = (pad_rows + L) // 128
    return np.ascontiguousarray(a.reshape(nt, 128, 128).transpose(1, 0, 2))


def _padfront(a, n):
    z = np.zeros(a.shape[:-1] + (n,), a.dtype)
    return np.concatenate([z, a], axis=-1)


def _rope_tabs(pos, dims):
    inv = (np.float32(ROPE_THETA) ** (-np.arange(0, dims, 2, dtype=np.float32) / np.float32(dims))).astype(np.float32)
    ang = pos.astype(np.float32)[:, None] * inv[None, :]
    return np.cos(ang).astype(np.float32).T, np.sin(ang).astype(np.float32).T


def kernel(x, ffn1_norm, ffn1_w_gate, ffn1_w_up, ffn1_w_down, ffn2_norm, ffn2_w_gate, ffn2_w_up, ffn2_w_down, mix_norm,
           ev_w_in, ev_w_out, nsa_cmp_pos_k, nsa_cmp_pos_v, nsa_cmp_k_w1, nsa_cmp_k_w2, nsa_cmp_v_w1, nsa_cmp_v_w2,
           mla_w_in, mla_q_norm, mla_kv_norm, mla_w_uq, mla_w_ukv, mla_w_out, final_norm):
    f32 = lambda a: np.asarray(a, dtype=np.float32)
    x = f32(x)
    B = 2
    xf = x.reshape(B * S, D)
    pos_of_core = [(c % 4) * NTOK + np.arange(NTOK) for c in range(NCORE)]

    def ffn_maps(pref, g, wg, wu, wd):
        wgu, wdt = ffn_host_layout(f32(wg), f32(wu), f32(wd))
        return {pref + "_g": gain_layout(f32(g), KC), pref + "_wgu": wgu, pref + "_wd": wdt}

    W = f32(ev_w_in[0])
    perm64 = np.concatenate([np.arange(8, 16), np.arange(0, 8), np.arange(16, 64)])
    perm128 = np.concatenate([perm64, 64 + perm64])
    win = np.stack([wtile(W[:, c0:c0 + 128], KC) for c0 in EV_COLS])
    wsw = np.stack([wtile(W[:, c0 + perm128], KC) for c0 in EV_COLS])
    wgt = wtile(W[:, 1280:1304], KC)
    common = dict(win=win, wsw=wsw, wgt=wgt, gmix=gain_layout(f32(mix_norm[0]), KC))
    common.update(ffn_maps("fa", ffn1_norm[0], ffn1_w_gate[0], ffn1_w_up[0], ffn1_w_down[0]))
    in_maps = []
    for c in range(NCORE):
        cs, sn = _rope_tabs(pos_of_core[c], 16)
        ct = np.ones((64, NTOK), np.float32)
        st = np.zeros((64, NTOK), np.float32)
        ct[0:8], ct[8:16] = cs, cs
        st[0:8], st[8:16] = -sn, sn
        m = dict(common)
        m.update(xT=_tokT(xf[c * NTOK:(c + 1) * NTOK]), ctab=np.concatenate([ct, ct]), stab=np.concatenate([st, st]))
        in_maps.append(m)
    nc, _ = build_stage("L1")
    r1 = _run(nc, in_maps)
    x1T = [r["x1T"] for r in r1]
    PJ = np.concatenate([r["pj"] for r in r1], axis=2).reshape(22, 128, B, S)
    GT = np.concatenate([r["gates"] for r in r1], axis=1).reshape(24, B, S)
    Q32 = np.concatenate([r["q32"] for r in r1], axis=2).reshape(5, 128, B, S)
    DBG.update(x1T=x1T, PJ=PJ, GT=GT)
    cn = nsa_consts_np()
    w1k = np.zeros((128, 32, 128), np.float32)
    w1k[:64] = f32(nsa_cmp_k_w1[0]).reshape(32, 64, 128).transpose(1, 0, 2)
    posk8 = np.zeros((128, 32, 8), np.float32)
    posk8[:64] = np.repeat(f32(nsa_cmp_pos_k[0]).T[:, :, None], 8, axis=2)
    w1v = np.ascontiguousarray(f32(nsa_cmp_v_w1[0]).reshape(32, 64, 128).transpose(1, 0, 2))
    w2k = f32(nsa_cmp_k_w2[0])
    in_maps = []
    for c in range(NCORE):
        b, g, pr = c // 4, (c % 4) // 2, c % 2
        gs = slice(g * 64, (g + 1) * 64)
        ks = PJ[6][gs, b]
        kw = PJ[8][gs, b]
        h0 = 4 * g + 2 * pr
        m = dict(cn)
        m.update(qg=np.ascontiguousarray(np.stack([Q32[2 * g][:, b], Q32[2 * g + 1][:, b]], axis=1)),
                 qmy=np.ascontiguousarray(PJ[2 * g + pr][:, b]),
                 kcraw=np.ascontiguousarray(Q32[4][gs, b].reshape(64, 1024, 16).transpose(0, 2, 1)), vcraw=np.ascontiguousarray(PJ[5][gs, b]),
                 w1k=w1k, w1v=w1v, posk=posk8,
                 posv=np.ascontiguousarray(f32(nsa_cmp_pos_v[0]).T),
                 w2k=np.concatenate([w2k, w2k], axis=1), w2v=f32(nsa_cmp_v_w2[0]),
                 ksT=np.concatenate([ks, ks], axis=0), vs1=_v1_tiles(PJ[7][gs, b]),
                 kwT=_padfront(np.concatenate([kw, kw], axis=0), 512), vw1=_v1_tiles(PJ[9][gs, b], 512),
                 gat=np.ascontiguousarray(GT[h0 * 3:h0 * 3 + 6, b]))
        in_maps.append(m)
    nc, _ = build_nsa()
    rA = _run(nc, in_maps)
    AT = np.zeros((1024, B, S), NBF)
    for c in range(NCORE):
        b, g, pr = c // 4, (c % 4) // 2, c % 2
        h0 = 4 * g + 2 * pr
        AT[h0 * 64:(h0 + 2) * 64, b] = rA[c]["oA"].reshape(128, S)
    cdl = dil_consts_np()
    in_maps = []
    for c in range(NCORE):
        b, cc = c // 4, c % 4
        k = PJ[14 + cc][:, b]
        v = PJ[18 + cc][:, b]
        m = dict(cdl)
        vh = [v[0:64], v[64:128]]
        m.update(qd=np.ascontiguousarray(PJ[10 + cc][:, b]), kb1T=_padfront(k, 128),
                 vb1=np.stack([_v1_tiles(vv, 128) for vv in vh]),
                 kb4T=np.ascontiguousarray(np.stack([_padfront(k[:, r::4], 128) for r in range(4)], axis=1)),
                 vb4=np.stack([np.stack([_v1_tiles(vv[:, r::4], 128) for r in range(4)], axis=1) for vv in vh]),
                 kb16T=np.ascontiguousarray(np.stack([_padfront(k[:, r::16], 128) for r in range(16)], axis=1)),
                 vb16=np.stack([np.stack([_v1_tiles(vv[:, r::16], 128).transpose(1, 0, 2).reshape(1152, 128)
                                          for r in range(16)]) for vv in vh]))
        in_maps.append(m)
    nc, _ = build_dil()
    rB = _run(nc, in_maps)
    for c in range(NCORE):
        b, cc = c // 4, c % 4
        AT[512 + cc * 128:512 + (cc + 1) * 128, b] = rB[c]["oB"].reshape(128, S)
    DBG.update(AT=AT)
    ATf = AT.reshape(1024, B * S)
    Wo = f32(ev_w_out[0])
    Wm = f32(mla_w_in[0])
    Wq = f32(mla_w_uq[0])
    Wkv = f32(mla_w_ukv[0])
    permq = np.concatenate([np.arange(64), np.arange(80, 96), np.arange(64, 80)])
    permk = np.concatenate([np.arange(16, 32), np.arange(0, 16)])
    common = dict(wo=np.stack([wtile(Wo[:, c0:c0 + 128], KC) for c0 in range(0, 1024, 128)]),
                  gmix=gain_layout(f32(mix_norm[1]), KC), gq=gain_layout(f32(mla_q_norm[0]), 2),
                  gkv=gain_layout(f32(mla_kv_norm[0]), 1),
                  wmi=np.stack([wtile(Wm[:, i * 128:(i + 1) * 128], KC) for i in range(3)]),
                  wkr=np.stack([wtile(Wm[:, 384:416], KC), wtile(Wm[:, 384 + permk], KC)]),
                  wuq=np.stack([wtile(Wq[:, h * 96:(h + 1) * 96], 2) for h in range(16)]),
                  wuqs=np.stack([wtile(Wq[:, h * 96 + permq], 2) for h in range(16)]),
                  wukv=np.stack([wtile(Wkv[:, h * 128:(h + 1) * 128], 1) for h in range(16)]))
    common.update(ffn_maps("fa", ffn2_norm[0], ffn2_w_gate[0], ffn2_w_up[0], ffn2_w_down[0]))
    common.update(ffn_maps("fb", ffn1_norm[1], ffn1_w_gate[1], ffn1_w_up[1], ffn1_w_down[1]))
    in_maps = []
    for c in range(NCORE):
        cs, sn = _rope_tabs(pos_of_core[c], 32)
        cq = np.ones((96, NTOK), np.float32)
        sq = np.zeros((96, NTOK), np.float32)
        cq[64:80], cq[80:96] = cs, cs
        sq[64:80], sq[80:96] = -sn, sn
        m = dict(common)
        m.update(xT=x1T[c], aT=_tokT(ATf[:, c * NTOK:(c + 1) * NTOK].T), cq_t=cq, sq_t=sq,
                 ck_t=np.concatenate([cs, cs]), sk_t=np.concatenate([-sn, sn]))
        in_maps.append(m)
    nc, _ = build_stage("L3")
    r3 = _run(nc, in_maps)
    x3T = [r["x3T"] for r in r3]
    QT = np.concatenate([r["qT"] for r in r3], axis=2).reshape(16, 96, B, S)
    KV = np.concatenate([r["kvT"] for r in r3], axis=2).reshape(16, 128, B, S)
    KR = np.concatenate([r["krT"] for r in r3], axis=1).reshape(32, B, S)
    DBG.update(x3T=x3T, QT=QT, KV=KV, KR=KR)
    ca = attn_consts_np()
    in_maps = []
    for c in range(NCORE):
        b = c // 4
        hs = [4 * (c % 4) + u for u in range(4)]
        m = dict(ca)
        m.update(qT=np.ascontiguousarray(np.stack([QT[h, :, b] for h in hs])),
                 kT=np.stack([np.concatenate([KV[h, 0:64, b], KR[:, b]], axis=0) for h in hs]),
                 v1=np.stack([_v1_tiles(KV[h, 64:128, b]) for h in hs]))
        in_maps.append(m)
    nc, _ = build_mla()
    r4 = _run(nc, in_maps)
    AT2 = np.zeros((1024, B, S), NBF)
    for c in range(NCORE):
        b = c // 4
        AT2[(c % 4) * 256:(c % 4 + 1) * 256, b] = r4[c]["oT"].reshape(256, S)
    DBG.update(AT2=AT2)
    AT2f = AT2.reshape(1024, B * S)
    Wo2 = f32(mla_w_out[0])
    common = dict(wo=np.stack([wtile(Wo2[:, c0:c0 + 128], KC) for c0 in range(0, 1024, 128)]),
                  gfin=gain_layout(f32(final_norm), KC))
    common.update(ffn_maps("fa", ffn2_norm[1], ffn2_w_gate[1], ffn2_w_up[1], ffn2_w_down[1]))
    in_maps = []
    for c in range(NCORE):
        m = dict(common)
        m.update(xT=x3T[c], aT=_tokT(AT2f[:, c * NTOK:(c + 1) * NTOK].T))
        in_maps.append(m)
    nc, _ = build_stage("L5")
    r5 = _run(nc, in_maps)
    out = np.zeros((B * S, D), np.float32)
    for c in range(NCORE):
        out[c * NTOK:(c + 1) * NTOK] = r5[c]["yT"].transpose(1, 0, 2).reshape(D, NTOK).T
    return out.reshape(B, S, D)
```

```python
import numpy as np
import concourse.bass as bass
import concourse.mybir as mybir
from concourse.bass_utils import run_bass_kernel_spmd

F32 = mybir.dt.float32
BF16 = mybir.dt.bfloat16
AF = mybir.ActivationFunctionType
ALU = mybir.AluOpType
AX = mybir.AxisListType


class SemObj:
    def __init__(self, nc, name):
        self.sem = nc.alloc_semaphore(name)
        self.name = name
        self.val = 0


class EngState:
    def __init__(self, nc, eng, name):
        self.e = eng
        self.name = name
        self.so = SemObj(nc, "sE_" + name)
        self.waited = {}


class Tile:
    def __init__(self, ctx, ap, name, dma_target=False):
        self.ap = ap
        self.name = name
        self.w = None
        self.r = {}
        self.dso = None
        self.ctx = ctx

    def dsem(self):
        if self.dso is None:
            self.dso = SemObj(self.ctx.nc, "sD_" + self.name)
        return self.dso

    def __getitem__(self, idx):
        return self.ap[idx]


class Ctx:
    def __init__(self, nc):
        self.nc = nc
        self.E = {n: EngState(nc, getattr(nc, n), n) for n in ["tensor", "vector", "scalar", "gpsimd", "sync"]}
        self.ntile = 0
        self.ninst = 0

    def sb(self, name, shape, dtype):
        self.ntile += 1
        return Tile(self, self.nc.alloc_sbuf_tensor(name, list(shape), dtype).ap(), name)

    def ps(self, name, shape=(128, 512), dtype=F32):
        self.ntile += 1
        return Tile(self, self.nc.alloc_psum_tensor(name, list(shape), dtype).ap(), name)

    def dram(self, name, shape, dtype, kind):
        return Tile(self, self.nc.dram_tensor(name, list(shape), dtype, kind=kind).ap(), name)

    def _deps(self, E, reads, writes, skip_self_pe=True):
        needs = {}

        def need(dep):
            if dep is None:
                return
            so, v = dep
            if needs.get(so, 0) < v:
                needs[so] = v

        for t in reads:
            need(t.w)
        for t in writes:
            need(t.w)
            for d in t.r.values():
                need(d)
        for so, v in needs.items():
            if so is E.so and E.name == "tensor":
                continue
            if E.waited.get(so, 0) >= v:
                continue
            E.e.wait_ge(so.sem, v)
            E.waited[so] = v

    def op(self, eng, fn, reads=(), writes=()):
        E = self.E[eng]
        self._deps(E, reads, writes)
        ins = fn(E.e)
        E.so.val += 1
        ins.then_inc(E.so.sem, 1)
        me = (E.so, E.so.val)
        for t in reads:
            t.r[E.so] = me
        for t in writes:
            t.w = me
            t.r = {}
        self.ninst += 1
        return ins

    def dma(self, eng, out_t, out_ap, in_t, in_ap, **kw):
        E = self.E[eng]
        self._deps(E, [in_t], [out_t])
        so = out_t.dsem()
        ins = E.e.dma_start(out=out_ap, in_=in_ap, **kw)
        so.val += 16
        ins.then_inc(so.sem, 16)
        me = (so, so.val)
        in_t.r[so] = me
        out_t.w = me
        out_t.r = {}
        self.ninst += 1
        return ins

    def finish(self, out_tiles):
        E = self.E["sync"]
        for t in out_tiles:
            if t.w is not None:
                so, v = t.w
                E.e.wait_ge(so.sem, v)


NORM_EPS = 1e-6
D = 1024
KC = 8
FF = 2816
FC = 22
TT = 512


class Common:
    def __init__(self, ctx, norm=True):
        self.ctx = ctx
        self.psum = [ctx.ps(f"ps{i}") for i in range(8)]
        if norm:
            self.init_eps()
            self.ones = ctx.sb("ones_f32", (128, 128), F32)
            ctx.op("vector", lambda e: e.memset(self.ones[:], 1.0), writes=[self.ones])
            self.sq = [ctx.sb(f"sq{i}", (128, TT), F32) for i in range(2)]
            self.rstd = ctx.sb("rstd", (128, TT), F32)
        self.rr = 0

    def rmsnorm_T(self, xt, nk, gam, outT, n, pbank, width, out2=None):
        ctx = self.ctx
        ps = pbank
        for kc in range(nk):
            sq = self.sq[self.rr % 2]
            self.rr += 1
            ctx.op("scalar", lambda e, kc=kc, sq=sq: e.activation(out=sq[:, :n], in_=xt[:, kc, :n], func=AF.Square),
                   reads=[xt], writes=[sq])
            ctx.op("tensor", lambda e, kc=kc, sq=sq: e.matmul(ps[:, :n], lhsT=self.ones[:], rhs=sq[:, :n],
                                                             start=(kc == 0), stop=(kc == nk - 1)),
                   reads=[self.ones, sq], writes=[ps])
        rstd = self.rstd
        ctx.op("scalar", lambda e: e.activation(out=rstd[:, :n], in_=ps[:, :n], func=AF.Sqrt,
                                                 bias=self.eps_t(), scale=1.0 / width),
               reads=[ps, self.eps_tile], writes=[rstd])
        ctx.op("vector", lambda e: e.reciprocal(out=rstd[:, :n], in_=rstd[:, :n]), reads=[rstd], writes=[rstd])
        for kc in range(nk):
            ctx.op("vector", lambda e, kc=kc: e.scalar_tensor_tensor(
                out=outT[:, kc, :n], in0=xt[:, kc, :n], scalar=gam[:, kc:kc + 1], in1=rstd[:, :n],
                op0=ALU.mult, op1=ALU.mult), reads=[xt, gam, rstd], writes=[outT])
            if out2 is not None:
                ctx.op("vector", lambda e, kc=kc: e.scalar_tensor_tensor(
                    out=out2[:, kc, :n], in0=xt[:, kc, :n], scalar=gam[:, kc:kc + 1], in1=rstd[:, :n],
                    op0=ALU.mult, op1=ALU.mult), reads=[xt, gam, rstd], writes=[out2])

    def eps_t(self):
        return self.eps_tile[:, 0:1]

    def init_eps(self):
        ctx = self.ctx
        self.eps_tile = ctx.sb("eps", (128, 1), F32)
        ctx.op("vector", lambda e: e.memset(self.eps_tile[:], NORM_EPS), writes=[self.eps_tile])


class FFN:
    def __init__(self, ctx, cm):
        self.ctx = ctx
        self.cm = cm
        self.hT = [ctx.sb(f"ffn_hT{i}", (128, KC, TT), BF16) for i in range(2)]
        self.wgu = [ctx.sb(f"ffn_wgu{i}", (128, 2, KC, 128), BF16) for i in range(3)]
        self.wd = [ctx.sb(f"ffn_wd{i}", (128, FC, 128), BF16) for i in range(2)]
        self.act = [ctx.sb(f"ffn_act{j}", (128, TT), BF16) for j in range(FC)]
        self.sg = [ctx.sb(f"ffn_sg{i}", (128, TT), F32) for i in range(2)]
        self.n = 0
        self.nw = 0
        self.nd = 0

    def run(self, xt, gam, wgu_d, wd_d, pb):
        ctx, cm = self.ctx, self.cm
        hT = self.hT[self.n % 2]
        self.n += 1
        cm.rmsnorm_T(xt, KC, gam, hT, TT, pb[0], D)
        for j in range(FC):
            w = self.wgu[self.nw % 3]
            self.nw += 1
            ctx.dma("gpsimd", w, w[:], wgu_d, wgu_d[j], max_dma_last_dim=4096)
            pg = pb[1 + (j % 2)]
            pu = pb[3 + (j % 2)]
            for kc in range(KC):
                ctx.op("tensor", lambda e, kc=kc, w=w, pg=pg: e.matmul(pg[:], lhsT=w[:, 0, kc, :], rhs=hT[:, kc, :],
                                                                     start=(kc == 0), stop=(kc == KC - 1)),
                       reads=[w, hT], writes=[pg])
            for kc in range(KC):
                ctx.op("tensor", lambda e, kc=kc, w=w, pu=pu: e.matmul(pu[:], lhsT=w[:, 1, kc, :], rhs=hT[:, kc, :],
                                                                     start=(kc == 0), stop=(kc == KC - 1)),
                       reads=[w, hT], writes=[pu])
            sg = self.sg[j % 2]
            ctx.op("scalar", lambda e, sg=sg, pg=pg: e.activation(out=sg[:], in_=pg[:], func=AF.Silu),
                   reads=[pg], writes=[sg])
            a = self.act[j]
            ctx.op("vector", lambda e, sg=sg, pu=pu, a=a: e.tensor_tensor(out=a[:], in0=pu[:], in1=sg[:], op=ALU.mult),
                   reads=[pu, sg], writes=[a])
        for c in range(KC):
            w = self.wd[self.nd % 2]
            self.nd += 1
            ctx.dma("gpsimd", w, w[:], wd_d, wd_d[c], max_dma_last_dim=4096)
            po = pb[5 + (c % 2)]
            for j in range(FC):
                ctx.op("tensor", lambda e, j=j, w=w, po=po: e.matmul(po[:], lhsT=w[:, j, :], rhs=self.act[j][:],
                                                                     start=(j == 0), stop=(j == FC - 1)),
                       reads=[w, self.act[j]], writes=[po])
            ctx.op("vector", lambda e, c=c, po=po: e.scalar_tensor_tensor(
                out=xt[:, c, :], in0=po[:], scalar=0.5, in1=xt[:, c, :], op0=ALU.mult, op1=ALU.add),
                reads=[po, xt], writes=[xt])


def ffn_host_layout(wg, wu, wd):
    g = wg.reshape(KC, 128, FC, 128).transpose(2, 1, 0, 3)
    u = wu.reshape(KC, 128, FC, 128).transpose(2, 1, 0, 3)
    wgu = np.ascontiguousarray(np.stack([g, u], axis=2))
    wdt = np.ascontiguousarray(wd.reshape(FC, 128, KC, 128).transpose(2, 1, 0, 3))
    return wgu, wdt


def gain_layout(g, nk):
    return np.ascontiguousarray(g.reshape(nk, 128).T)

import ml_dtypes

NBF = ml_dtypes.bfloat16
NTOK = 4096
NSLOT = 8
S = 16384
ROPE_THETA = 500000.0


def wtile(W, nk):
    return np.ascontiguousarray(W.reshape(nk, 128, -1).transpose(1, 0, 2))


class Proj:
    def __init__(self, ctx, nslots=3):
        self.ctx = ctx
        self.w = [ctx.sb(f"pw{i}", (128, KC, 128), BF16) for i in range(nslots)]
        self.n = 0

    def mm(self, w_d, w_ap, nk, M, rhsT, ps, n=TT):
        ctx = self.ctx
        w = self.w[self.n % len(self.w)]
        self.n += 1
        ctx.dma("gpsimd", w, w[:, :nk, :M], w_d, w_ap)
        for kc in range(nk):
            ctx.op("tensor", lambda e, kc=kc: e.matmul(ps[:M, :n], lhsT=w[:, kc, :M], rhs=rhsT[:, kc, :n],
                                                       start=(kc == 0), stop=(kc == nk - 1)),
                   reads=[w, rhsT], writes=[ps])


def rope_combine(ctx, out_t, out_ap, p1, p2, ct, c_ap, st, s_ap, tmp, M, n=TT):
    t1, t2 = tmp
    ctx.op("vector", lambda e: e.tensor_tensor(out=t1[:M, :n], in0=p1[:M, :n], in1=c_ap, op=ALU.mult),
           reads=[p1, ct], writes=[t1])
    ctx.op("vector", lambda e: e.tensor_tensor(out=t2[:M, :n], in0=p2[:M, :n], in1=s_ap, op=ALU.mult),
           reads=[p2, st], writes=[t2])
    ctx.op("vector", lambda e: e.tensor_tensor(out=out_ap, in0=t1[:M, :n], in1=t2[:M, :n], op=ALU.add),
           reads=[t1, t2], writes=[out_t])


EV_ROPE = [True] * 4 + [True, False, True, False, True, False] + [True] * 4 + [True] * 4 + [False] * 4
EV_COLS = list(range(0, 1280, 128)) + list(range(1304, 2840, 128))


def build_stage(kind):
    nc = bass.Bass("TRN2", target_bir_lowering=False)
    ctx = Ctx(nc)
    cm = Common(ctx)
    ffn = FFN(ctx, cm)
    pj = Proj(ctx)
    pb = cm.psum
    D_ = {}

    def din(name, shape, dt=F32):
        D_[name] = ctx.dram(name, shape, dt, "ExternalInput")
        return D_[name]

    def dout(name, shape, dt=F32):
        D_[name] = ctx.dram(name, shape, dt, "ExternalOutput")
        return D_[name]

    xT = din("xT", (128, KC, NTOK))
    outs = []
    gams = {}

    def load_gam(name, nk=KC):
        d = din(name, (128, nk))
        t = ctx.sb("sb_" + name, (128, nk), F32)
        ctx.dma("sync", t, t[:], d, d[:])
        gams[name] = t
        return t

    def ffn_in(pref):
        return (load_gam(pref + "_g"), din(pref + "_wgu", (FC, 128, 2, KC, 128)), din(pref + "_wd", (KC, 128, FC, 128)))

    xts = [ctx.sb(f"xt{i}", (128, KC, TT), F32) for i in range(2)]
    tmp = [ctx.sb(f"tmp{i}", (128, TT), F32) for i in range(2)]
    hTs = [ctx.sb(f"hmix{i}", (128, KC, TT), BF16) for i in range(2)]
    if kind in ("L3", "L5"):
        aT = din("aT", (128, KC, NTOK), BF16)
        wo = din("wo", (KC, 128, KC, 128))
        ats = [ctx.sb(f"at{i}", (128, KC, TT), BF16) for i in range(2)]
    if kind == "L1":
        fa = ffn_in("fa")
        gm = load_gam("gmix")
        win = din("win", (22, 128, KC, 128))
        wsw = din("wsw", (22, 128, KC, 128))
        wgt = din("wgt", (128, KC, 24))
        ctab = din("ctab", (128, NTOK))
        stab = din("stab", (128, NTOK))
        x1T = dout("x1T", (128, KC, NTOK))
        pjo = dout("pj", (22, 128, NTOK), BF16)
        gto = dout("gates", (24, NTOK))
        outs = [x1T, pjo, gto]
        cts = [ctx.sb(f"ct{i}", (128, TT), F32) for i in range(2)]
        sts = [ctx.sb(f"st{i}", (128, TT), F32) for i in range(2)]
        obs = [ctx.sb(f"ob{i}", (128, TT), BF16) for i in range(3)]
        gos = [ctx.sb(f"go{i}", (24, TT), F32) for i in range(2)]
        hT32 = ctx.sb("hT32", (128, KC, TT), F32)
        w32 = [ctx.sb(f"w32_{i}", (128, KC, 128), F32) for i in range(2)]
        ob32s = [ctx.sb(f"ob32_{i}", (128, TT), F32) for i in range(2)]
        q32o = dout("q32", (5, 128, NTOK), F32)
        outs.append(q32o)
        n32 = [0]

        def mm32(w_d, w_ap, ps):
            w = w32[n32[0] % 2]
            n32[0] += 1
            ctx.dma("sync", w, w[:], w_d, w_ap)
            for kc in range(KC):
                ctx.op("tensor", lambda e, kc=kc: e.matmul(ps[:], lhsT=w[:, kc, :], rhs=hT32[:, kc, :],
                                                           start=(kc == 0), stop=(kc == KC - 1)),
                       reads=[w, hT32], writes=[ps])
    if kind == "L3":
        fa = ffn_in("fa")
        fb = ffn_in("fb")
        gm = load_gam("gmix")
        gq = load_gam("gq", 2)
        gkv = load_gam("gkv", 1)
        wmi = din("wmi", (3, 128, KC, 128))
        wkr = din("wkr", (2, 128, KC, 32))
        wuq = din("wuq", (16, 128, 2, 96))
        wuqs = din("wuqs", (16, 128, 2, 96))
        wukv = din("wukv", (16, 128, 1, 128))
        cq_t = din("cq_t", (96, NTOK))
        sq_t = din("sq_t", (96, NTOK))
        ck_t = din("ck_t", (32, NTOK))
        sk_t = din("sk_t", (32, NTOK))
        x3T = dout("x3T", (128, KC, NTOK))
        qTo = dout("qT", (16, 96, NTOK), BF16)
        kvo = dout("kvT", (16, 128, NTOK), BF16)
        kro = dout("krT", (32, NTOK), BF16)
        outs = [x3T, qTo, kvo, kro]
        cts = [ctx.sb(f"ct{i}", (96, TT), F32) for i in range(2)]
        sts = [ctx.sb(f"st{i}", (96, TT), F32) for i in range(2)]
        ckts = [ctx.sb(f"ckt{i}", (32, TT), F32) for i in range(2)]
        skts = [ctx.sb(f"skt{i}", (32, TT), F32) for i in range(2)]
        obs = [ctx.sb(f"ob{i}", (128, TT), BF16) for i in range(3)]
        cqT = [ctx.sb(f"cqT{i}", (128, 2, TT), F32) for i in range(2)]
        ckvT = [ctx.sb(f"ckvT{i}", (128, 1, TT), F32) for i in range(2)]
        cqn = [ctx.sb(f"cqn{i}", (128, 2, TT), BF16) for i in range(2)]
        ckvn = [ctx.sb(f"ckvn{i}", (128, 1, TT), BF16) for i in range(2)]
    if kind == "L5":
        fa = ffn_in("fa")
        gf = load_gam("gfin")
        yT = dout("yT", (128, KC, NTOK))
        outs = [yT]
        yts = [ctx.sb(f"yt{i}", (128, KC, TT), F32) for i in range(2)]

    nob = 0
    for t in range(NSLOT):
        ts = slice(t * TT, (t + 1) * TT)
        xt = xts[t % 2]
        ctx.dma("sync", xt, xt[:], xT, xT[:, :, ts])
        if kind in ("L3", "L5"):
            at = ats[t % 2]
            ctx.dma("sync", at, at[:], aT, aT[:, :, ts])
            for c in range(KC):
                ps = pb[5 + (c % 2)]
                pj.mm(wo, wo[c], KC, 128, at, ps)
                ctx.op("vector", lambda e, c=c, ps=ps: e.tensor_tensor(out=xt[:, c, :], in0=ps[:], in1=xt[:, c, :], op=ALU.add),
                       reads=[ps, xt], writes=[xt])
        if kind == "L1":
            ffn.run(xt, fa[0], fa[1], fa[2], pb[0:7])
            ctx.dma("sync", x1T, x1T[:, :, ts], xt, xt[:])
            hT = hTs[t % 2]
            cm.rmsnorm_T(xt, KC, gm, hT, TT, pb[0], D, out2=hT32)
            ct, st = cts[t % 2], sts[t % 2]
            ctx.dma("sync", ct, ct[:], ctab, ctab[:, ts])
            ctx.dma("sync", st, st[:], stab, stab[:, ts])
            for c in range(22):
                p1 = pb[1 + (c % 2)]
                ob = obs[nob % 3]
                nob += 1
                if c < 5:
                    p2 = pb[3 + (c % 2)]
                    mm32(win, win[c], p1)
                    mm32(wsw, wsw[c], p2)
                    ob32 = ob32s[c % 2]
                    rope_combine(ctx, ob32, ob32[:], p1, p2, ct, ct[:], st, st[:], tmp, 128)
                    ctx.op("scalar", lambda e, ob=ob, ob32=ob32: e.activation(out=ob[:], in_=ob32[:], func=AF.Copy),
                           reads=[ob32], writes=[ob])
                    ctx.dma("sync", q32o, q32o[c, :, ts], ob32, ob32[:])
                    ctx.dma("sync", pjo, pjo[c, :, ts], ob, ob[:])
                    continue
                pj.mm(win, win[c], KC, 128, hT, p1)
                if EV_ROPE[c]:
                    p2 = pb[3 + (c % 2)]
                    pj.mm(wsw, wsw[c], KC, 128, hT, p2)
                    rope_combine(ctx, ob, ob[:], p1, p2, ct, ct[:], st, st[:], tmp, 128)
                else:
                    ctx.op("scalar", lambda e, p1=p1, ob=ob: e.activation(out=ob[:], in_=p1[:], func=AF.Copy),
                           reads=[p1], writes=[ob])
                ctx.dma("sync", pjo, pjo[c, :, ts], ob, ob[:])
            p1 = pb[7]
            pj.mm(wgt, wgt[:], KC, 24, hT, p1)
            go = gos[t % 2]
            ctx.op("scalar", lambda e, p1=p1, go=go: e.activation(out=go[:], in_=p1[:24, :], func=AF.Sigmoid),
                   reads=[p1], writes=[go])
            ctx.dma("sync", gto, gto[:, ts], go, go[:])
        if kind == "L3":
            ffn.run(xt, fa[0], fa[1], fa[2], pb[0:7])
            ffn.run(xt, fb[0], fb[1], fb[2], pb[0:7])
            ctx.dma("sync", x3T, x3T[:, :, ts], xt, xt[:])
            hT = hTs[t % 2]
            cm.rmsnorm_T(xt, KC, gm, hT, TT, pb[0], D)
            cq, ckv, cqn_, ckvn_ = cqT[t % 2], ckvT[t % 2], cqn[t % 2], ckvn[t % 2]
            for i in range(3):
                p1 = pb[1 + (i % 2)]
                pj.mm(wmi, wmi[i], KC, 128, hT, p1)
                dst_t, dst = (cq, cq[:, i, :]) if i < 2 else (ckv, ckv[:, 0, :])
                ctx.op("scalar", lambda e, p1=p1, dst=dst: e.activation(out=dst, in_=p1[:], func=AF.Copy),
                       reads=[p1], writes=[dst_t])
            ckt, skt = ckts[t % 2], skts[t % 2]
            ctx.dma("sync", ckt, ckt[:], ck_t, ck_t[:, ts])
            ctx.dma("sync", skt, skt[:], sk_t, sk_t[:, ts])
            p1, p2 = pb[3], pb[4]
            pj.mm(wkr, wkr[0], KC, 32, hT, p1)
            pj.mm(wkr, wkr[1], KC, 32, hT, p2)
            ob = obs[nob % 3]
            nob += 1
            rope_combine(ctx, ob, ob[:32, :], p1, p2, ckt, ckt[:], skt, skt[:], tmp, 32)
            ctx.dma("sync", kro, kro[:, ts], ob, ob[:32, :])
            cm.rmsnorm_T(cq, 2, gq, cqn_, TT, pb[0], 256)
            cm.rmsnorm_T(ckv, 1, gkv, ckvn_, TT, pb[0], 128)
            ct, st = cts[t % 2], sts[t % 2]
            ctx.dma("sync", ct, ct[:], cq_t, cq_t[:, ts])
            ctx.dma("sync", st, st[:], sq_t, sq_t[:, ts])
            for h in range(16):
                p1 = pb[1 + (h % 2)]
                p2 = pb[3 + (h % 2)]
                pj.mm(wuq, wuq[h], 2, 96, cqn_, p1)
                pj.mm(wuqs, wuqs[h], 2, 96, cqn_, p2)
                ob = obs[nob % 3]
                nob += 1
                rope_combine(ctx, ob, ob[:96, :], p1, p2, ct, ct[:], st, st[:], tmp, 96)
                ctx.dma("sync", qTo, qTo[h, :, ts], ob, ob[:96, :])
            for h in range(16):
                p1 = pb[5 + (h % 2)]
                pj.mm(wukv, wukv[h], 1, 128, ckvn_, p1)
                ob = obs[nob % 3]
                nob += 1
                ctx.op("scalar", lambda e, p1=p1, ob=ob: e.activation(out=ob[:], in_=p1[:], func=AF.Copy),
                       reads=[p1], writes=[ob])
                ctx.dma("sync", kvo, kvo[h, :, ts], ob, ob[:])
        if kind == "L5":
            ffn.run(xt, fa[0], fa[1], fa[2], pb[0:7])
            yt = yts[t % 2]
            cm.rmsnorm_T(xt, KC, gf, yt, TT, pb[0], D)
            ctx.dma("sync", yT, yT[:, :, ts], yt, yt[:])
    ctx.finish(outs)
    return nc, ctx


NEG = -30000.0
BIG = 1e30
NQB = 32


class Attn:
    def __init__(self, ctx, cm, scale, consts, ns3=False, sbanks=None, lbanks=None):
        self.ctx, self.cm, self.scale = ctx, cm, scale
        self.S = sbanks if sbanks else [cm.psum[0], cm.psum[1]] + ([cm.psum[7]] if ns3 else [])
        self.lag = len(self.S) - 1
        self.pending = []
        self.LB = lbanks if lbanks else [cm.psum[5], cm.psum[6]]
        self.P = [ctx.sb(f"P{i}", (128, 512), BF16) for i in range(4)]
        self.OL = [ctx.sb(f"OL{i}", (128, 512), F32) for i in range(2)]
        self.rl = [ctx.sb(f"rl{i}", (64, 512), F32) for i in range(2)]
        self.ident = ctx.sb("sb_ident", (128, 128), BF16)
        self.sel = ctx.sb("sb_sel", (128, 64), F32)
        ctx.dma("sync", self.ident, self.ident[:], consts["ident"], consts["ident"][:])
        ctx.dma("sync", self.sel, self.sel[:], consts["sel"], consts["sel"][:])
        self.i = 0
        self.j = 0

    def step(self, nk, c0, c1, mains, masks, pvs):
        ctx = self.ctx
        S = self.S[self.i % len(self.S)]
        P = self.P[self.i % 4]
        self.i += 1
        allm = list(mains) + list(masks)
        n = len(allm)
        for k, (lt, lap, rt, rap, cs) in enumerate(allm):
            ctx.op("tensor", lambda e, lap=lap, rap=rap, cs=cs, k=k: e.matmul(
                S[:nk, cs], lhsT=lap, rhs=rap, start=(k == 0), stop=(k == n - 1), skip_group_check=True),
                reads=[lt, rt], writes=[S])
        ctx.op("scalar", lambda e: e.activation(out=P[:nk, c0:c1], in_=S[:nk, c0:c1], func=AF.Exp, scale=self.scale),
               reads=[S], writes=[P])
        self.pending.append((nk, P, pvs))
        while len(self.pending) > self.lag:
            self._flush_one()

    def _flush_one(self):
        ctx = self.ctx
        nk, P, pvs = self.pending.pop(0)
        for (vt, vap, pcs, acc, acc_ap, start) in pvs:
            ctx.op("tensor", lambda e, vap=vap, pcs=pcs, acc_ap=acc_ap, start=start: e.matmul(
                acc_ap, lhsT=vap, rhs=P[:nk, pcs], start=start, stop=True, skip_group_check=True),
                reads=[vt, P], writes=[acc])

    def flush(self):
        while self.pending:
            self._flush_one()

    def finish(self, acc):
        self.flush()
        ctx = self.ctx
        OL = self.OL[self.j % 2]
        rl = self.rl[self.j % 2]
        LB = self.LB[self.j % len(self.LB)]
        self.j += 1
        ctx.op("scalar", lambda e: e.activation(out=OL[:], in_=acc[:], func=AF.Copy), reads=[acc], writes=[OL])
        ctx.op("tensor", lambda e: e.matmul(LB[:64, :], lhsT=self.sel[:], rhs=OL[:], start=True, stop=True),
               reads=[self.sel, OL], writes=[LB])
        ctx.op("vector", lambda e: e.tensor_scalar(out=rl[:], in0=LB[:64, :], scalar1=1e-30, scalar2=None, op0=ALU.max),
               reads=[LB], writes=[rl])
        ctx.op("vector", lambda e: e.reciprocal(out=rl[:], in_=rl[:]), reads=[rl], writes=[rl])
        return OL, rl


def attn_consts_np():
    ident = np.eye(128, dtype=np.float32).astype(NBF)
    sel = np.zeros((128, 64), np.float32)
    sel[64 + np.arange(64), np.arange(64)] = 1.0
    kl = np.arange(128)[:, None]
    ql = np.arange(512)[None, :]
    mc = np.stack([np.where(ql >= kl + o, 0.0, NEG) for o in (0, 128, 256, 384)], axis=1)
    return {"ident": ident, "sel": sel, "mcausal": mc.astype(NBF)}


def build_mla():
    nc = bass.Bass("TRN2", target_bir_lowering=False)
    ctx = Ctx(nc)
    cm = Common(ctx, norm=False)
    NU = 4
    qT = ctx.dram("qT", (NU, 96, S), BF16, "ExternalInput")
    kT = ctx.dram("kT", (NU, 96, S), BF16, "ExternalInput")
    v1 = ctx.dram("v1", (NU, 128, 128, 128), BF16, "ExternalInput")
    cd = {"ident": ctx.dram("ident", (128, 128), BF16, "ExternalInput"),
          "sel": ctx.dram("sel", (128, 64), F32, "ExternalInput")}
    mcd = ctx.dram("mcausal", (128, 4, 512), BF16, "ExternalInput")
    oT = ctx.dram("oT", (NU, 64, S), BF16, "ExternalOutput")
    at = Attn(ctx, cm, 96 ** -0.5, cd, ns3=True)
    mc = ctx.sb("mc", (128, 4, 512), BF16)
    ctx.dma("sync", mc, mc[:], mcd, mcd[:])
    Kb = [ctx.sb(f"Kb{i}", (96, S), BF16) for i in range(2)]
    Vb = [ctx.sb(f"Vb{i}", (128, 128, 128), BF16) for i in range(2)]
    Qb = [ctx.sb(f"Qb{i}", (96, 512), BF16) for i in range(3)]
    Ob = [ctx.sb(f"Ob{i}", (64, 512), BF16) for i in range(2)]
    acc = [cm.psum[2], cm.psum[3]]
    n = 0
    for u in range(NU):
        K, V = Kb[u % 2], Vb[u % 2]
        ctx.dma("sync", K, K[:], kT, kT[u])
        ctx.dma("sync", V, V[:], v1, v1[u])
        for qb in range(NQB):
            Q = Qb[n % 3]
            A = acc[n % 2]
            O = Ob[n % 2]
            n += 1
            ctx.dma("sync", Q, Q[:], qT, qT[u, :, qb * 512:(qb + 1) * 512])
            nkt = 4 * qb + 4
            for kt in range(nkt):
                d = kt - 4 * qb
                c0 = d * 128 if d > 0 else 0
                mains = [(K, K[:, kt * 128:(kt + 1) * 128], Q, Q[:, c0:512], slice(c0, 512))]
                masks = []
                if d >= 0:
                    masks = [(at.ident, at.ident[:], mc, mc[:, d, c0:512], slice(c0, 512))]
                pvs = [(V, V[:, kt, :], slice(c0, 512), A, A[:, c0:512], kt == 0)]
                at.step(128, c0, 512, mains, masks, pvs)
            OL, rl = at.finish(A)
            ctx.op("vector", lambda e, OL=OL, rl=rl, O=O: e.tensor_tensor(out=O[:], in0=OL[:64, :], in1=rl[:], op=ALU.mult),
                   reads=[OL, rl], writes=[O])
            ctx.dma("sync", oT, oT[u, :, qb * 512:(qb + 1) * 512], O, O[:])
    ctx.finish([oT])
    return nc, ctx


def nsa_consts_np():
    c = attn_consts_np()
    kl = np.arange(128)[:, None]
    ql = np.arange(512)[None, :]
    d = ql - kl
    c["mwin"] = np.stack([np.where((d - o >= 0) & (d - o < 512), 0.0, NEG) for o in range(-512, 512, 128)], 1).astype(NBF)
    c["mcmp"] = np.stack([np.where(ql - 16 * kl >= 31 - 512 * dl, 0.0, NEG) for dl in range(5)], 1).astype(NBF)
    q = np.arange(128)[:, None]
    npr = np.arange(-1, 8)[None, :]
    c["mtm"] = np.where(q >= 31 + 16 * npr, 0.0, NEG).astype(NBF)
    lo = (np.arange(128) < 64)[:, None]
    c["mul3"] = np.where(lo, np.array([[0., 0., 0.]]), np.array([[1., 0., 0.]])).astype(np.float32)
    c["add3"] = np.where(lo, np.array([[BIG, BIG, -BIG]]), np.array([[0., BIG, BIG]])).astype(np.float32)
    c["identf"] = np.eye(128, dtype=np.float32)
    sg = np.zeros((6, 6, 64), np.float32)
    for r in range(6):
        sg[r, r, :] = 1.0
    c["selg"] = sg
    return c


NSA_CONST_SHAPES = {"ident": ((128, 128), BF16), "sel": ((128, 64), F32), "mcausal": ((128, 4, 512), BF16),
                    "mwin": ((128, 8, 512), BF16), "mcmp": ((128, 5, 512), BF16),
                    "mtm": ((128, 9), BF16), "mul3": ((128, 3), F32), "add3": ((128, 3), F32),
                    "identf": ((128, 128), F32), "selg": ((6, 6, 64), F32)}


USE32 = True
DBG_SKIP = set()


def build_nsa(nqb=NQB):
    nc = bass.Bass("TRN2", target_bir_lowering=False)
    ctx = Ctx(nc)
    cm = Common(ctx, norm=False)
    pb = cm.psum
    din = lambda n, s, dt=BF16: ctx.dram(n, s, dt, "ExternalInput")
    qg = din("qg", (128, 2, S), F32 if USE32 else BF16)
    qmy = din("qmy", (128, S))
    kcraw = din("kcraw", (64, 16, 1024), F32)
    vcraw = din("vcraw", (64, S))
    w1k = din("w1k", (128, 32, 128), F32)
    w1v = din("w1v", (64, 32, 128), F32)
    posk = din("posk", (128, 32, 8), F32)
    posv = din("posv", (64, 32), F32)
    w2k = din("w2k", (128, 128), F32)
    w2v = din("w2v", (128, 64), F32)
    ksT = din("ksT", (128, S))
    vs1 = din("vs1", (128, 128, 128))
    kwT = din("kwT", (128, 512 + S))
    vw1 = din("vw1", (128, 132, 128))
    gat = din("gat", (6, S), F32)
    cd = {k: din(k, s, dt) for k, (s, dt) in NSA_CONST_SHAPES.items()}
    oA = ctx.dram("oA", (2, 64, S), BF16, "ExternalOutput")
    at = Attn(ctx, cm, 0.125, cd, sbanks=[pb[0], pb[1], pb[5]], lbanks=[pb[6]])

    def cload(name, eng="sync"):
        s, dt = NSA_CONST_SHAPES[name]
        t = ctx.sb("c_" + name, s, dt)
        ctx.dma(eng, t, t[:], cd[name], cd[name][:])
        return t
    mc, mwin, mcmp, mtm, mul3, add3, identf = [cload(n) for n in
                                               ("mcausal", "mwin", "mcmp", "mtm", "mul3", "add3", "identf")]
    selg = ctx.sb("c_selg", (6, 6 * 64), F32)
    ctx.dma("sync", selg, selg[:], cd["selg"], cd["selg"].ap.rearrange("a b c -> a (b c)"))
    bigA = ctx.sb("bigA", (128, S), BF16)
    bigB = ctx.sb("bigB", (128, S), BF16)
    bigB3 = bigB.ap.rearrange("p (t c) -> p t c", c=128)
    kcT = ctx.sb("kcT", (128, 1024), BF16)
    vc1 = ctx.sb("vc1", (128, 8, 128), BF16)
    kcT32 = ctx.sb("kcT32", (128, 1024), F32)
    ctx.op("gpsimd", lambda e: e.memset(bigA[:], 0.0), writes=[bigA])
    bigA32 = bigA.ap.bitcast(F32)
    k32buf = bigA32[:, 0:2080].rearrange("p (j m) -> p j m", m=130)
    w1k32 = bigA32[:, 2080:2080 + 4096].rearrange("p (l h) -> p l h", h=128)
    pos32 = ctx.sb("pos32", (128, 32, 8), F32)
    w2k32 = ctx.sb("w2k32", (128, 128), F32)
    hid32 = [ctx.sb(f"hid32_{i}", (128, 128), F32) for i in range(2)]
    posb = ctx.sb("posb", (128, 1), F32)
    ctx.dma("sync", bigA, w1k32, w1k, w1k[:])
    ctx.dma("sync", pos32, pos32[:], posk, posk[:])
    ctx.dma("sync", w2k32, w2k32[:], w2k, w2k[:])
    ps = pb[7]
    for l in range(32 if "posb" not in DBG_SKIP else 1):
        ctx.op("tensor", lambda e, l=l: e.matmul(ps[:, 0:8], lhsT=w1k32[:, l, :], rhs=pos32[:, l, :],
                                                 start=(l == 0), stop=(l == 31)), reads=[bigA, pos32], writes=[ps])
    ctx.op("vector", lambda e: e.tensor_copy(out=posb[:], in_=ps[:, 0:1]), reads=[ps], writes=[posb])
    for p in range(8 if "kpath" not in DBG_SKIP else 0):
        nm = min(130, 1024 - 128 * p)
        ctx.dma("sync", bigA, k32buf[0:64, :, 0:nm], kcraw, kcraw[:, :, 128 * p:128 * p + nm])
        ps = pb[p % 2]
        for l in range(32):
            a_, j_ = l // 16, l % 16
            ctx.op("tensor", lambda e, l=l: e.matmul(ps[:, 0:128], lhsT=w1k32[:, l, :], rhs=k32buf[:, j_, a_:a_ + 128],
                                                     start=(l == 0), stop=(l == 31)), reads=[bigA], writes=[ps])
        h32 = hid32[p % 2]
        ctx.op("scalar", lambda e: e.activation(out=h32[:], in_=ps[:, 0:128], func=AF.Silu, bias=posb[:, 0:1]),
               reads=[ps, posb], writes=[h32])
        p2 = pb[2 + (p % 2)]
        ctx.op("tensor", lambda e: e.matmul(p2[:, 0:128], lhsT=w2k32[:], rhs=h32[:], start=True, stop=True),
               reads=[w2k32, h32], writes=[p2])
        ctx.op("vector", lambda e: e.tensor_copy(out=kcT32[:, p * 128:(p + 1) * 128], in_=p2[:, 0:128]), reads=[p2], writes=[kcT32])
        ctx.op("scalar", lambda e: e.activation(out=kcT[:, p * 128:(p + 1) * 128], in_=kcT32[:, p * 128:(p + 1) * 128], func=AF.Copy), reads=[kcT32], writes=[kcT])
    ctx.dma("sync", bigB, bigB[0:64, :], vcraw, vcraw[:])
    w1s_ap = bigA[0:64, 0:4096].rearrange("p (l h) -> p l h", h=128)
    poss = ctx.sb("poss", (64, 32), BF16)
    w2vs = ctx.sb("w2vs", (128, 64), BF16)
    ctx.dma("gpsimd", w2vs, w2vs[:], w2v, w2v[:])
    hid = [ctx.sb(f"hid{i}", (128, 512), BF16) for i in range(2)]
    ctx.op("vector", lambda e: e.memset(hid[1][:], 0.0), writes=[hid[1]])
    ctx.op("vector", lambda e: e.memset(vc1[:], 1.0), writes=[vc1])
    ctx.dma("gpsimd", bigA, w1s_ap, w1v, w1v[:])
    ctx.dma("gpsimd", poss, poss[:], posv, posv[:])
    ps = pb[7]
    for l in range(32):
        ctx.op("tensor", lambda e, l=l: e.matmul(ps[:, 0:1], lhsT=w1s_ap[:, l, :], rhs=poss[:, l:l + 1],
                                                 start=(l == 0), stop=(l == 31)), reads=[bigA, poss], writes=[ps])
    posbv = ctx.sb("posbv", (128, 1), F32)
    ctx.op("vector", lambda e: e.tensor_copy(out=posbv[:], in_=ps[:, 0:1]), reads=[ps], writes=[posbv])
    for nt in range(2 if "vpath" not in DBG_SKIP else 0):
        ncol = 512 if nt == 0 else 511
        ps = pb[nt]
        for l in range(32):
            st_ = nt * 8192 + l
            en = min(S, st_ + 16 * ncol)
            ctx.op("tensor", lambda e, l=l: e.matmul(ps[:, 0:ncol], lhsT=w1s_ap[:, l, :], rhs=bigB[0:64, st_:en:16],
                                                     start=(l == 0), stop=(l == 31)), reads=[bigA, bigB], writes=[ps])
        ctx.op("scalar", lambda e: e.activation(out=hid[nt][:, 0:ncol], in_=ps[:, 0:ncol], func=AF.Silu, bias=posbv[:, 0:1]),
               reads=[ps, posbv], writes=[hid[nt]])
        for j in range(4):
            p2 = pb[2 + (j % 2)]
            ctx.op("tensor", lambda e: e.matmul(p2[:, 0:64], lhsT=hid[nt][:, j * 128:(j + 1) * 128], rhs=w2vs[:], start=True, stop=True),
                   reads=[w2vs, hid[nt]], writes=[p2])
            ctx.op("vector", lambda e: e.tensor_copy(out=vc1[:, nt * 4 + j, 0:64], in_=p2[:, 0:64]), reads=[p2], writes=[vc1])
    ctx.dma("sync", bigA, bigA[:], ksT, ksT[:])
    ctx.dma("sync", bigB, bigB[:], vs1, vs1.ap.rearrange("p t c -> p (t c)"))
    QDT = F32 if USE32 else BF16
    Qg1 = ctx.sb("Qg0", (128, 4, 512), QDT)
    Qgs = [Qg1, Qg1]
    Qms = [[[ctx.sb(f"QY{i}_{h}_{c}", (128, 512), BF16) for c in range(4)] for h in range(2)] for i in range(2)]
    ctx.op("gpsimd", lambda e: e.memset(Qg1[:], 0.0), writes=[Qg1])
    for i in range(2):
        for h in range(2):
            for c in range(4):
                ctx.op("gpsimd", lambda e: e.memset(Qms[i][h][c][:], 0.0), writes=[Qms[i][h][c]])
    wKs = [ctx.sb(f"wK{i}", (128, 1024), BF16) for i in range(2)]
    wVs = [ctx.sb(f"wV{i}", (128, 8, 128), BF16) for i in range(2)]
    gts = [ctx.sb(f"gt{i}", (6, 512), F32) for i in range(2)]
    NE = 4
    es = [ctx.sb(f"e{i}", (128, 512), F32) for i in range(NE)]
    lp = [ctx.sb(f"lp{i}", (128, 2), F32) for i in range(2)]
    rlh = [ctx.sb(f"rlh{i}", (128, 1), F32) for i in range(2)]
    Aim = ctx.sb("Aim", (128, 1024), F32)
    I1 = ctx.sb("I1", (128, 256), F32)
    I2 = ctx.sb("I2", (128, 256), F32)
    m8 = ctx.sb("m8", (128, 16), F32)
    negms = [ctx.sb(f"negm{i}", (128, 320), F32) for i in range(4)]
    fg = ctx.sb("fg", (64, 512), F32)
    tmpc = ctx.sb("tmpc", (64, 512), F32)
    accsb = ctx.sb("accsb", (64, 512), F32)
    Ob = [ctx.sb(f"Ob{i}", (64, 512), BF16) for i in range(2)]
    accC, accS, accW, GB = pb[2], pb[3], pb[4], pb[7]
    kq = kcT32 if USE32 else kcT
    st = {"ne": 0, "no": 0}

    def load(qb):
        T0 = qb * 512
        if qb >= 2:
            for hh in range(4):
                r0 = (hh % 2) * 64
                ctx.dma("sync", Qgs[qb % 2], Qgs[qb % 2][0:64, hh, :], qg, qg[r0:r0 + 64, hh // 2, T0:T0 + 512])
        for h in range(2):
            for c in range(qb // 8 + 1):
                t_ = Qms[qb % 2][h][c]
                ctx.dma("sync", t_, t_[0:64, :], qmy, qmy[h * 64:(h + 1) * 64, T0:T0 + 512])
        ctx.dma("sync", wKs[qb % 2], wKs[qb % 2][:], kwT, kwT[:, T0:T0 + 1024])
        ctx.dma("sync", wVs[qb % 2], wVs[qb % 2][:], vw1, vw1[:, 4 * qb:4 * qb + 8, :])
        ctx.dma("sync", gts[qb % 2], gts[qb % 2][:], gat, gat[:, T0:T0 + 512])

    def phase1a(qb, subs=(0, 1, 2, 3)):
        if qb < 2:
            return
        Qg = Qgs[qb % 2]
        for qsl in subs:
            qs = 4 * qb + qsl
            ncols = 8 * qs + 8
            nj = 2 * qs + 2
            negm = negms[qsl]
            halves = [(lo, min(ncols, lo + 512)) for lo in (0, 512) if lo < ncols]
            for hh in range(4):
                ch, r0 = hh // 2, (hh % 2) * 64
                lpt = lp[hh % 2]
                rl1 = rlh[hh % 2]
                ehs = []
                for hi_, (lo, hi) in enumerate(halves):
                    w = hi - lo
                    Sb = at.S[at.i % len(at.S)]
                    at.i += 1
                    a, b2 = max(lo, ncols - 9, 0), min(hi, ncols)
                    hasm = b2 > a
                    ctx.op("tensor", lambda e: e.matmul(
                        Sb[:, 0:w], lhsT=Qg[:, hh, qsl * 128:(qsl + 1) * 128], rhs=kq[:, lo:hi],
                        start=True, stop=(not hasm), skip_group_check=True), reads=[Qg, kq], writes=[Sb])
                    if hasm:
                        ctx.op("tensor", lambda e: e.matmul(
                            Sb[:, a - lo:b2 - lo], lhsT=at.ident[:], rhs=mtm[:, a - (ncols - 9):b2 - (ncols - 9)],
                            start=False, stop=True, skip_group_check=True), reads=[at.ident, mtm], writes=[Sb])
                    et = es[st["ne"] % NE]
                    st["ne"] += 1
                    ctx.op("scalar", lambda e: e.activation(
                        out=et[:, 0:w], in_=Sb[:, 0:w], func=AF.Exp, scale=0.125, accum_out=lpt[:, hi_:hi_ + 1]),
                        reads=[Sb], writes=[et, lpt])
                    ehs.append((et, lo, hi))
                if len(halves) == 2:
                    ctx.op("vector", lambda e: e.tensor_tensor(out=lpt[:, 0:1], in0=lpt[:, 0:1], in1=lpt[:, 1:2], op=ALU.add),
                           reads=[lpt], writes=[lpt])
                ctx.op("vector", lambda e: e.tensor_scalar(out=rl1[:], in0=lpt[:, 0:1], scalar1=1e-30, scalar2=None, op0=ALU.max),
                       reads=[lpt], writes=[rl1])
                ctx.op("vector", lambda e: e.reciprocal(out=rl1[:], in_=rl1[:]), reads=[rl1], writes=[rl1])
                for (et, lo, hi) in ehs:
                    w = hi - lo
                    if hh == 0:
                        ctx.op("vector", lambda e: e.tensor_scalar(
                            out=Aim[:, lo:hi], in0=et[:, 0:w], scalar1=rl1[:, 0:1], scalar2=None, op0=ALU.mult),
                            reads=[et, rl1], writes=[Aim])
                    else:
                        ctx.op("vector", lambda e: e.scalar_tensor_tensor(
                            out=Aim[:, lo:hi], in0=et[:, 0:w], scalar=rl1[:, 0:1], in1=Aim[:, lo:hi], op0=ALU.mult, op1=ALU.add),
                            reads=[et, rl1, Aim], writes=[Aim])
            n4 = 4 * nj
            tt = lambda o, a_, b_, op: ctx.op("vector", lambda e: e.tensor_tensor(out=o, in0=a_, in1=b_, op=op),
                                              reads=[Aim, I1, mul3, add3], writes=[I1])
            tt(I1[:, 0:nj], Aim[:, 0:n4:4], Aim[:, 1:n4:4], ALU.add)
            tt(I1[:, 0:nj], I1[:, 0:nj], Aim[:, 2:n4:4], ALU.add)
            ctx.op("vector", lambda e: e.scalar_tensor_tensor(out=I1[:, 0:nj], in0=I1[:, 0:nj], scalar=2.0, in1=Aim[:, 3:n4:4],
                                                              op0=ALU.mult, op1=ALU.add), reads=[Aim, I1], writes=[I1])
            tt(I1[:, 1:nj], I1[:, 1:nj], Aim[:, 3:n4 - 4:4], ALU.add)
            tt(I1[:, nj - 3:nj], I1[:, nj - 3:nj], mul3[:, :], ALU.mult)
            tt(I1[:, nj - 3:nj], I1[:, nj - 3:nj], add3[:, :], ALU.add)
            ctx.op("vector", lambda e: e.memset(I1[:, 0:1], BIG), writes=[I1])
            ctx.op("gpsimd", lambda e: e.memset(negm[:], 0.0), writes=[negm])
            ctx.op("vector", lambda e: e.max(out=m8[:, 0:8], in_=I1[:, 0:nj]), reads=[I1], writes=[m8])
            ctx.op("vector", lambda e: e.match_replace(out=I2[:, 0:nj], in_to_replace=m8[:, 0:8], in_values=I1[:, 0:nj],
                                                       imm_value=-BIG), reads=[I1, m8], writes=[I2])
            ctx.op("vector", lambda e: e.max(out=m8[:, 8:16], in_=I2[:, 0:nj]), reads=[I2], writes=[m8])
            ctx.op("vector", lambda e: e.tensor_scalar(out=negm[:, 64:64 + nj], in0=I1[:, 0:nj], scalar1=m8[:, 15:16], scalar2=-1.0,
                                                       op0=ALU.is_ge, op1=ALU.add), reads=[I1, m8], writes=[negm])

    def phase1b(qb):
        if qb < 2:
            return
        for qsl in range(4):
            negm = negms[qsl]
            for c in range(qb // 8 + 1):
                ctx.op("tensor", lambda e: e.transpose(out=GB[:, 0:128], in_=negm[:, 64 * c:64 * c + 128], identity=identf[:]),
                       reads=[negm, identf], writes=[GB])
                for h in range(2):
                    t_ = Qms[qb % 2][h][c]
                    ctx.op("vector", lambda e: e.tensor_copy(out=t_[64:128, qsl * 128:(qsl + 1) * 128], in_=GB[64:128, 0:128]),
                           reads=[GB], writes=[t_])

    def phase2(qb, todo):
        T0 = qb * 512
        wK, wV, gt = wKs[qb % 2], wVs[qb % 2], gts[qb % 2]
        use_sel = qb >= 2
        for hl in range(2):
            r0 = hl * 64
            Qm = Qms[qb % 2][hl][0]
            for m in range(qb // 4 + 1):
                dl = qb - 4 * m
                masks = [(at.ident, at.ident[:], mcmp, mcmp[:, dl, :], slice(0, 512))] if dl <= 4 else []
                todo.append(lambda Qm=Qm, m=m, masks=masks: at.step(128, 0, 512, [(kcT, kcT[:, m * 128:(m + 1) * 128], Qm, Qm[:, 0:512], slice(0, 512))], masks,
                        [(vc1, vc1[:, m, :], slice(0, 512), accC, accC[:, :], m == 0)]))
            for kt in range(8):
                c0 = max(0, (kt - 4) * 128)
                c1 = min(512, 128 * kt + 128)
                todo.append(lambda Qm=Qm, kt=kt, c0=c0, c1=c1: at.step(128, c0, c1, [(wK, wK[:, kt * 128:(kt + 1) * 128], Qm, Qm[:, c0:c1], slice(c0, c1))],
                        [(at.ident, at.ident[:], mwin, mwin[:, kt, c0:c1], slice(c0, c1))],
                        [(wV, wV[:, kt, :], slice(c0, c1), accW, accW[:, c0:c1], kt == 0)]))
            for kt in range(4 * qb + 4):
                d = kt - 4 * qb
                c0 = d * 128 if d > 0 else 0
                masks = []
                Qm = Qms[qb % 2][hl][kt // 32]
                if d >= 0:
                    masks.append((at.ident, at.ident[:], mc, mc[:, d, c0:512], slice(c0, 512)))
                todo.append(lambda Qm=Qm, kt=kt, c0=c0, masks=masks: at.step(128, c0, 512, [(bigA, bigA[:, kt * 128:(kt + 1) * 128], Qm, Qm[:, c0:512], slice(c0, 512))], masks,
                        [(bigB, bigB3[:, kt, :], slice(c0, 512), accS, accS[:, c0:512], kt == 0)]))
            todo.append(lambda hl=hl: epilogue(qb, hl))

    def epilogue(qb, hl):
        T0 = qb * 512
        gt = gts[qb % 2]
        for br, acc in enumerate((accC, accS, accW)):
            OL, rl = at.finish(acc)
            r = hl * 3 + br
            ctx.op("tensor", lambda e: e.matmul(GB[:64, :], lhsT=selg[:, r * 64:(r + 1) * 64], rhs=gt[:, :], start=True, stop=True),
                   reads=[selg, gt], writes=[GB])
            ctx.op("vector", lambda e: e.tensor_tensor(out=fg[:], in0=GB[:64, :], in1=rl[:], op=ALU.mult),
                   reads=[GB, rl], writes=[fg])
            if br == 0:
                ctx.op("vector", lambda e: e.tensor_tensor(out=accsb[:], in0=OL[:64, :], in1=fg[:], op=ALU.mult),
                       reads=[OL, fg], writes=[accsb])
            else:
                ctx.op("vector", lambda e: e.tensor_tensor(out=tmpc[:], in0=OL[:64, :], in1=fg[:], op=ALU.mult),
                       reads=[OL, fg], writes=[tmpc])
                ctx.op("vector", lambda e: e.tensor_tensor(out=accsb[:], in0=accsb[:], in1=tmpc[:], op=ALU.add),
                       reads=[accsb, tmpc], writes=[accsb])
        O = Ob[st["no"] % 2]
        st["no"] += 1
        ctx.op("scalar", lambda e: e.activation(out=O[:], in_=accsb[:], func=AF.Copy), reads=[accsb], writes=[O])
        ctx.dma("sync", oA, oA[hl, :, T0:T0 + 512], O, O[:])

    load(0)
    phase1a(0)
    for qb in range(nqb):
        phase1b(qb)
        todo = []
        phase2(qb, todo)
        nxt = qb + 1 < nqb
        if nxt:
            load(qb + 1)
        n = len(todo)
        cuts = {(n * k) // 4: k for k in range(4)}
        for i, fn in enumerate(todo):
            if nxt and i in cuts:
                phase1a(qb + 1, (cuts[i],))
            fn()
    ctx.finish([oA])
    return nc, ctx


def dil_consts_np():
    c = attn_consts_np()
    kl = np.arange(128)[:, None]
    ql = np.arange(512)[None, :]
    d = ql - kl
    c["md1"] = np.stack([np.where((d - o >= 0) & (d - o <= 128), 0.0, NEG) for o in range(-128, 512, 128)], 1).astype(NBF)
    i4 = ql % 128
    c["md4"] = np.stack([np.where(i4 <= kl, 0.0, NEG), np.where(i4 >= kl, 0.0, NEG)], 1).astype(NBF)
    i16 = ql % 32
    c["md16"] = np.stack([np.where(i16 <= kl, 0.0, NEG), np.where(i16 >= kl, 0.0, NEG)], 1).astype(NBF)
    del c["mcausal"]
    return c


DIL_CONST_SHAPES = {"ident": ((128, 128), BF16), "sel": ((128, 64), F32), "md1": ((128, 5, 512), BF16),
                    "md4": ((128, 2, 512), BF16), "md16": ((128, 2, 512), BF16)}


def build_dil(nqb=NQB):
    nc = bass.Bass("TRN2", target_bir_lowering=False)
    ctx = Ctx(nc)
    cm = Common(ctx, norm=False)
    pb = cm.psum
    din = lambda n, s, dt=BF16: ctx.dram(n, s, dt, "ExternalInput")
    qd = din("qd", (128, S))
    kb1T = din("kb1T", (128, 128 + S))
    vb1 = din("vb1", (2, 128, 129, 128))
    kb4T = din("kb4T", (128, 4, 128 + 4096))
    vb4 = din("vb4", (2, 128, 4, 33, 128))
    kb16T = din("kb16T", (128, 16, 128 + 1024))
    vb16 = din("vb16", (2, 16, 1152, 128))
    cd = {k: din(k, s, dt) for k, (s, dt) in DIL_CONST_SHAPES.items()}
    oB = ctx.dram("oB", (2, 64, S), BF16, "ExternalOutput")
    at = Attn(ctx, cm, 0.125, cd, ns3=True)
    ms = {}
    for name in ("md1", "md4", "md16"):
        s, dt = DIL_CONST_SHAPES[name]
        ms[name] = ctx.sb("c_" + name, s, dt)
        ctx.dma("sync", ms[name], ms[name][:], cd[name], cd[name][:])
    md1, md4, md16 = ms["md1"], ms["md4"], ms["md16"]
    Qs = [[ctx.sb(f"Qd{i}_{h}", (128, 512), BF16) for h in range(2)] for i in range(2)]
    for i in range(2):
        for h in range(2):
            ctx.op("gpsimd", lambda e: e.memset(Qs[i][h][:], 0.0), writes=[Qs[i][h]])
    K1 = [ctx.sb(f"K1_{i}", (128, 640), BF16) for i in range(2)]
    V1 = [[ctx.sb(f"V1_{i}_{h}", (128, 5, 128), BF16) for h in range(2)] for i in range(2)]
    K4 = [ctx.sb(f"K4_{i}", (128, 4, 256), BF16) for i in range(2)]
    V4 = [[ctx.sb(f"V4_{i}_{h}", (128, 4, 2, 128), BF16) for h in range(2)] for i in range(2)]
    K16 = [ctx.sb(f"K16_{i}", (128, 16, 160), BF16) for i in range(2)]
    V16A = [[ctx.sb(f"V16A_{i}_{h}", (128, 16, 128), BF16) for h in range(2)] for i in range(2)]
    V16B = [[ctx.sb(f"V16B_{i}_{h}", (32, 16, 128), BF16) for h in range(2)] for i in range(2)]
    Ob = [ctx.sb(f"Ob{i}", (64, 512), BF16) for i in range(2)]
    accs = [pb[2], pb[3]]
    n = 0
    for qb in range(nqb):
        T0 = qb * 512
        i = qb % 2
        k1, k4, k16 = K1[i], K4[i], K16[i]
        for h in range(2):
            ctx.dma("sync", Qs[i][h], Qs[i][h][h * 64:(h + 1) * 64, :], qd, qd[h * 64:(h + 1) * 64, T0:T0 + 512])
        ctx.dma("sync", k1, k1[:], kb1T, kb1T[:, T0:T0 + 640])
        ctx.dma("sync", k4, k4[:], kb4T, kb4T[:, :, 128 * qb:128 * qb + 256])
        ctx.dma("sync", k16, k16[:], kb16T, kb16T[:, :, 32 * qb:32 * qb + 160])
        for h in range(2):
            ctx.dma("sync", V1[i][h], V1[i][h][:], vb1, vb1[h, :, 4 * qb:4 * qb + 5, :])
            ctx.dma("sync", V4[i][h], V4[i][h][:], vb4, vb4[h, :, :, qb:qb + 2, :])
            ctx.dma("gpsimd", V16A[i][h], V16A[i][h][:], vb16,
                    vb16[h, :, 32 * qb:32 * qb + 128, :].rearrange("r p c -> p r c"))
            ctx.dma("gpsimd", V16B[i][h], V16B[i][h][:], vb16,
                    vb16[h, :, 32 * qb + 128:32 * qb + 160, :].rearrange("r p c -> p r c"))
        for h in range(2):
            r0 = h * 64
            Q = Qs[i][h]
            A = accs[n % 2]
            O = Ob[n % 2]
            n += 1
            v1, v4, va, vb = V1[i][h], V4[i][h], V16A[i][h], V16B[i][h]
            for kt in range(5):
                o = -128 + 128 * kt
                c0, c1 = max(0, o), min(512, 128 * kt + 128)
                at.step(128, c0, c1, [(k1, k1[:, kt * 128:(kt + 1) * 128], Q, Q[:, c0:c1], slice(c0, c1))],
                        [(at.ident, at.ident[:], md1, md1[:, kt, c0:c1], slice(c0, c1))],
                        [(v1, v1[:, kt, :], slice(c0, c1), A, A[:, c0:c1], kt == 0)])
            for kt in range(2):
                mains = [(k4, k4[:, r, kt * 128:(kt + 1) * 128], Q, Q[:, r:512:4], slice(r * 128, (r + 1) * 128))
                         for r in range(4)]
                pvs = [(v4, v4[:, r, kt, :], slice(r * 128, (r + 1) * 128), A, A[:, r:512:4], False) for r in range(4)]
                at.step(128, 0, 512, mains, [(at.ident, at.ident[:], md4, md4[:, kt, :], slice(0, 512))], pvs)
            mains = [(k16, k16[:, r, 0:128], Q, Q[:, r:512:16], slice(r * 32, (r + 1) * 32)) for r in range(16)]
            pvs = [(va, va[:, r, :], slice(r * 32, (r + 1) * 32), A, A[:, r:512:16], False) for r in range(16)]
            at.step(128, 0, 512, mains, [(at.ident, at.ident[:], md16, md16[:, 0, :], slice(0, 512))], pvs)
            mains = [(k16, k16[:, r, 128:160], Q, Q[:, r:512:16], slice(r * 32, (r + 1) * 32)) for r in range(16)]
            pvs = [(vb, vb[:, r, :], slice(r * 32, (r + 1) * 32), A, A[:, r:512:16], False) for r in range(16)]
            at.step(32, 0, 512, mains, [(at.ident, at.ident[0:32, 0:32], md16, md16[0:32, 1, :], slice(0, 512))], pvs)
            OL, rl = at.finish(A)
            ctx.op("vector", lambda e, OL=OL, rl=rl, O=O: e.tensor_tensor(out=O[:], in0=OL[:64, :], in1=rl[:], op=ALU.mult),
                   reads=[OL, rl], writes=[O])
            ctx.dma("sync", oB, oB[h, :, T0:T0 + 512], O, O[:])
    ctx.finish([oB])
    return nc, ctx


NCORE = 8
DBG = {}


def _run(nc, in_maps):
    res = run_bass_kernel_spmd(nc, in_maps, core_ids=list(range(NCORE)))
    return res.results


def _tokT(a):
    R = a.shape[1]
    return np.ascontiguousarray(a.T.reshape(R // 128, 128, a.shape[0]).transpose(1, 0, 2))


def _v1_tiles(vT, pad_rows=0):
    L = vT.shape[1]
    a = np.zeros((pad_rows + L, 128), NBF)
    a[pad_rows:, :64] = vT.T
    a[pad_rows:, 64:] = 1
    nt = (pad_rows + L) // 128
    return np.ascontiguousarray(a.reshape(nt, 128, 128).transpose(1, 0, 2))


def _padfront(a, n):
    z = np.zeros(a.shape[:-1] + (n,), a.dtype)
    return np.concatenate([z, a], axis=-1)


def _rope_tabs(pos, dims):
    inv = (np.float32(ROPE_THETA) ** (-np.arange(0, dims, 2, dtype=np.float32) / np.float32(dims))).astype(np.float32)
    ang = pos.astype(np.float32)[:, None] * inv[None, :]
    return np.cos(ang).astype(np.float32).T, np.sin(ang).astype(np.float32).T


def kernel(x, ffn1_norm, ffn1_w_gate, ffn1_w_up, ffn1_w_down, ffn2_norm, ffn2_w_gate, ffn2_w_up, ffn2_w_down, mix_norm,
           ev_w_in, ev_w_out, nsa_cmp_pos_k, nsa_cmp_pos_v, nsa_cmp_k_w1, nsa_cmp_k_w2, nsa_cmp_v_w1, nsa_cmp_v_w2,
           mla_w_in, mla_q_norm, mla_kv_norm, mla_w_uq, mla_w_ukv, mla_w_out, final_norm):
    f32 = lambda a: np.asarray(a, dtype=np.float32)
    x = f32(x)
    B = 2
    xf = x.reshape(B * S, D)
    pos_of_core = [(c % 4) * NTOK + np.arange(NTOK) for c in range(NCORE)]

    def ffn_maps(pref, g, wg, wu, wd):
        wgu, wdt = ffn_host_layout(f32(wg), f32(wu), f32(wd))
        return {pref + "_g": gain_layout(f32(g), KC), pref + "_wgu": wgu, pref + "_wd": wdt}

    W = f32(ev_w_in[0])
    perm64 = np.concatenate([np.arange(8, 16), np.arange(0, 8), np.arange(16, 64)])
    perm128 = np.concatenate([perm64, 64 + perm64])
    win = np.stack([wtile(W[:, c0:c0 + 128], KC) for c0 in EV_COLS])
    wsw = np.stack([wtile(W[:, c0 + perm128], KC) for c0 in EV_COLS])
    wgt = wtile(W[:, 1280:1304], KC)
    common = dict(win=win, wsw=wsw, wgt=wgt, gmix=gain_layout(f32(mix_norm[0]), KC))
    common.update(ffn_maps("fa", ffn1_norm[0], ffn1_w_gate[0], ffn1_w_up[0], ffn1_w_down[0]))
    in_maps = []
    for c in range(NCORE):
        cs, sn = _rope_tabs(pos_of_core[c], 16)
        ct = np.ones((64, NTOK), np.float32)
        st = np.zeros((64, NTOK), np.float32)
        ct[0:8], ct[8:16] = cs, cs
        st[0:8], st[8:16] = -sn, sn
        m = dict(common)
        m.update(xT=_tokT(xf[c * NTOK:(c + 1) * NTOK]), ctab=np.concatenate([ct, ct]), stab=np.concatenate([st, st]))
        in_maps.append(m)
    nc, _ = build_stage("L1")
    r1 = _run(nc, in_maps)
    x1T = [r["x1T"] for r in r1]
    PJ = np.concatenate([r["pj"] for r in r1], axis=2).reshape(22, 128, B, S)
    GT = np.concatenate([r["gates"] for r in r1], axis=1).reshape(24, B, S)
    Q32 = np.concatenate([r["q32"] for r in r1], axis=2).reshape(5, 128, B, S)
    DBG.update(x1T=x1T, PJ=PJ, GT=GT)
    cn = nsa_consts_np()
    w1k = np.zeros((128, 32, 128), np.float32)
    w1k[:64] = f32(nsa_cmp_k_w1[0]).reshape(32, 64, 128).transpose(1, 0, 2)
    posk8 = np.zeros((128, 32, 8), np.float32)
    posk8[:64] = np.repeat(f32(nsa_cmp_pos_k[0]).T[:, :, None], 8, axis=2)
    w1v = np.ascontiguousarray(f32(nsa_cmp_v_w1[0]).reshape(32, 64, 128).transpose(1, 0, 2))
    w2k = f32(nsa_cmp_k_w2[0])
    XM = np.zeros((64, 128, 128), NBF)
    for kt in range(128):
        for half in range(2):
            XM[2 * (kt % 32) + half, kt, half * 64:(half + 1) * 64] = 30000.0
    XM = XM.reshape(64, S)
    in_maps = []
    for c in range(NCORE):
        b, g, pr = c // 4, (c % 4) // 2, c % 2
        gs = slice(g * 64, (g + 1) * 64)
        ks = PJ[6][gs, b]
        kw = PJ[8][gs, b]
        h0 = 4 * g + 2 * pr
        m = dict(cn)
        m.update(qg=np.ascontiguousarray(np.stack([Q32[2 * g][:, b], Q32[2 * g + 1][:, b]], axis=1)),
                 qmy=np.ascontiguousarray(PJ[2 * g + pr][:, b]),
                 kcraw=np.ascontiguousarray(Q32[4][gs, b].reshape(64, 1024, 16).transpose(0, 2, 1)), vcraw=np.ascontiguousarray(PJ[5][gs, b]),
                 w1k=w1k, w1v=w1v, posk=posk8,
                 posv=np.ascontiguousarray(f32(nsa_cmp_pos_v[0]).T),
                 w2k=np.concatenate([w2k, np.zeros_like(w2k)], axis=1), w2v=f32(nsa_cmp_v_w2[0]),
                 ksT=np.concatenate([ks, XM], axis=0), vs1=_v1_tiles(PJ[7][gs, b]),
                 kwT=_padfront(np.concatenate([kw, np.zeros_like(kw)], axis=0), 512), vw1=_v1_tiles(PJ[9][gs, b], 512),
                 gat=np.ascontiguousarray(GT[h0 * 3:h0 * 3 + 6, b]))
        in_maps.append(m)
    nc, _ = build_nsa()
    rA = _run(nc, in_maps)
    AT = np.zeros((1024, B, S), NBF)
    for c in range(NCORE):
        b, g, pr = c // 4, (c % 4) // 2, c % 2
        h0 = 4 * g + 2 * pr
        AT[h0 * 64:(h0 + 2) * 64, b] = rA[c]["oA"].reshape(128, S)
    cdl = dil_consts_np()
    in_maps = []
    for c in range(NCORE):
        b, cc = c // 4, c % 4
        k = PJ[14 + cc][:, b]
        v = PJ[18 + cc][:, b]
        m = dict(cdl)
        vh = [v[0:64], v[64:128]]
        m.update(qd=np.ascontiguousarray(PJ[10 + cc][:, b]), kb1T=_padfront(k, 128),
                 vb1=np.stack([_v1_tiles(vv, 128) for vv in vh]),
                 kb4T=np.ascontiguousarray(np.stack([_padfront(k[:, r::4], 128) for r in range(4)], axis=1)),
                 vb4=np.stack([np.stack([_v1_tiles(vv[:, r::4], 128) for r in range(4)], axis=1) for vv in vh]),
                 kb16T=np.ascontiguousarray(np.stack([_padfront(k[:, r::16], 128) for r in range(16)], axis=1)),
                 vb16=np.stack([np.stack([_v1_tiles(vv[:, r::16], 128).transpose(1, 0, 2).reshape(1152, 128)
                                          for r in range(16)]) for vv in vh]))
        in_maps.append(m)
    nc, _ = build_dil()
    rB = _run(nc, in_maps)
    for c in range(NCORE):
        b, cc = c // 4, c % 4
        AT[512 + cc * 128:512 + (cc + 1) * 128, b] = rB[c]["oB"].reshape(128, S)
    DBG.update(AT=AT)
    ATf = AT.reshape(1024, B * S)
    Wo = f32(ev_w_out[0])
    Wm = f32(mla_w_in[0])
    Wq = f32(mla_w_uq[0])
    Wkv = f32(mla_w_ukv[0])
    permq = np.concatenate([np.arange(64), np.arange(80, 96), np.arange(64, 80)])
    permk = np.concatenate([np.arange(16, 32), np.arange(0, 16)])
    common = dict(wo=np.stack([wtile(Wo[:, c0:c0 + 128], KC) for c0 in range(0, 1024, 128)]),
                  gmix=gain_layout(f32(mix_norm[1]), KC), gq=gain_layout(f32(mla_q_norm[0]), 2),
                  gkv=gain_layout(f32(mla_kv_norm[0]), 1),
                  wmi=np.stack([wtile(Wm[:, i * 128:(i + 1) * 128], KC) for i in range(3)]),
                  wkr=np.stack([wtile(Wm[:, 384:416], KC), wtile(Wm[:, 384 + permk], KC)]),
                  wuq=np.stack([wtile(Wq[:, h * 96:(h + 1) * 96], 2) for h in range(16)]),
                  wuqs=np.stack([wtile(Wq[:, h * 96 + permq], 2) for h in range(16)]),
                  wukv=np.stack([wtile(Wkv[:, h * 128:(h + 1) * 128], 1) for h in range(16)]))
    common.update(ffn_maps("fa", ffn2_norm[0], ffn2_w_gate[0], ffn2_w_up[0], ffn2_w_down[0]))
    common.update(ffn_maps("fb", ffn1_norm[1], ffn1_w_gate[1], ffn1_w_up[1], ffn1_w_down[1]))
    in_maps = []
    for c in range(NCORE):
        cs, sn = _rope_tabs(pos_of_core[c], 32)
        cq = np.ones((96, NTOK), np.float32)
        sq = np.zeros((96, NTOK), np.float32)
        cq[64:80], cq[80:96] = cs, cs
        sq[64:80], sq[80:96] = -sn, sn
        m = dict(common)
        m.update(xT=x1T[c], aT=_tokT(ATf[:, c * NTOK:(c + 1) * NTOK].T), cq_t=cq, sq_t=sq,
                 ck_t=np.concatenate([cs, cs]), sk_t=np.concatenate([-sn, sn]))
        in_maps.append(m)
    nc, _ = build_stage("L3")
    r3 = _run(nc, in_maps)
    x3T = [r["x3T"] for r in r3]
    QT = np.concatenate([r["qT"] for r in r3], axis=2).reshape(16, 96, B, S)
    KV = np.concatenate([r["kvT"] for r in r3], axis=2).reshape(16, 128, B, S)
    KR = np.concatenate([r["krT"] for r in r3], axis=1).reshape(32, B, S)
    DBG.update(x3T=x3T, QT=QT, KV=KV, KR=KR)
    ca = attn_consts_np()
    in_maps = []
    for c in range(NCORE):
        b = c // 4
        hs = [4 * (c % 4) + u for u in range(4)]
        m = dict(ca)
        m.update(qT=np.ascontiguousarray(np.stack([QT[h, :, b] for h in hs])),
                 kT=np.stack([np.concatenate([KV[h, 0:64, b], KR[:, b]], axis=0) for h in hs]),
                 v1=np.stack([_v1_tiles(KV[h, 64:128, b]) for h in hs]))
        in_maps.append(m)
    nc, _ = build_mla()
    r4 = _run(nc, in_maps)
    AT2 = np.zeros((1024, B, S), NBF)
    for c in range(NCORE):
        b = c // 4
        AT2[(c % 4) * 256:(c % 4 + 1) * 256, b] = r4[c]["oT"].reshape(256, S)
    DBG.update(AT2=AT2)
    AT2f = AT2.reshape(1024, B * S)
    Wo2 = f32(mla_w_out[0])
    common = dict(wo=np.stack([wtile(Wo2[:, c0:c0 + 128], KC) for c0 in range(0, 1024, 128)]),
                  gfin=gain_layout(f32(final_norm), KC))
    common.update(ffn_maps("fa", ffn2_norm[1], ffn2_w_gate[1], ffn2_w_up[1], ffn2_w_down[1]))
    in_maps = []
    for c in range(NCORE):
        m = dict(common)
        m.update(xT=x3T[c], aT=_tokT(AT2f[:, c * NTOK:(c + 1) * NTOK].T))
        in_maps.append(m)
    nc, _ = build_stage("L5")
    r5 = _run(nc, in_maps)
    out = np.zeros((B * S, D), np.float32)
    for c in range(NCORE):
        out[c * NTOK:(c + 1) * NTOK] = r5[c]["yT"].transpose(1, 0, 2).reshape(D, NTOK).T
    return out.reshape(B, S, D)
```

```python
import numpy as np
import concourse.bass as bass
import concourse.mybir as mybir
from concourse.bass_utils import run_bass_kernel_spmd

F32 = mybir.dt.float32
BF16 = mybir.dt.bfloat16
AF = mybir.ActivationFunctionType
ALU = mybir.AluOpType
AX = mybir.AxisListType


class SemObj:
    def __init__(self, nc, name):
        self.sem = nc.alloc_semaphore(name)
        self.name = name
        self.val = 0


class EngState:
    def __init__(self, nc, eng, name):
        self.e = eng
        self.name = name
        self.so = SemObj(nc, "sE_" + name)
        self.waited = {}


class Tile:
    def __init__(self, ctx, ap, name, dma_target=False):
        self.ap = ap
        self.name = name
        self.w = None
        self.r = {}
        self.dso = None
        self.ctx = ctx

    def dsem(self):
        if self.dso is None:
            self.dso = SemObj(self.ctx.nc, "sD_" + self.name)
        return self.dso

    def __getitem__(self, idx):
        return self.ap[idx]


class Ctx:
    def __init__(self, nc):
        self.nc = nc
        self.E = {n: EngState(nc, getattr(nc, n), n) for n in ["tensor", "vector", "scalar", "gpsimd", "sync"]}
        self.ntile = 0
        self.ninst = 0

    def sb(self, name, shape, dtype):
        self.ntile += 1
        return Tile(self, self.nc.alloc_sbuf_tensor(name, list(shape), dtype).ap(), name)

    def ps(self, name, shape=(128, 512), dtype=F32):
        self.ntile += 1
        return Tile(self, self.nc.alloc_psum_tensor(name, list(shape), dtype).ap(), name)

    def dram(self, name, shape, dtype, kind):
        t = Tile(self, self.nc.dram_tensor(name, list(shape), dtype, kind=kind).ap(), name)
        t.shape = tuple(shape)
        return t

    def _deps(self, E, reads, writes, waw=True):
        needs = {}

        def need(dep):
            if dep is None:
                return
            so, v = dep
            if needs.get(so, 0) < v:
                needs[so] = v

        for t in reads:
            need(t.w)
        for t in writes:
            if waw:
                need(t.w)
            for d in t.r.values():
                need(d)
        for so, v in needs.items():
            if so is E.so and E.name == "tensor":
                continue
            if E.waited.get(so, 0) >= v:
                continue
            E.e.wait_ge(so.sem, v)
            E.waited[so] = v

    def op(self, eng, fn, reads=(), writes=()):
        E = self.E[eng]
        self._deps(E, reads, writes)
        ins = fn(E.e)
        E.so.val += 1
        ins.then_inc(E.so.sem, 1)
        me = (E.so, E.so.val)
        for t in reads:
            t.r[E.so] = me
        for t in writes:
            t.w = me
            t.r = {}
        self.ninst += 1
        return ins

    def dma(self, eng, out_t, out_ap, in_t, in_ap, waw=True, **kw):
        E = self.E[eng]
        self._deps(E, [in_t], [out_t], waw=waw)
        so = out_t.dsem()
        ins = E.e.dma_start(out=out_ap, in_=in_ap, **kw)
        so.val += 16
        ins.then_inc(so.sem, 16)
        me = (so, so.val)
        in_t.r[so] = me
        out_t.w = me
        out_t.r = {}
        self.ninst += 1
        return ins

    def finish(self, out_tiles):
        E = self.E["sync"]
        for t in out_tiles:
            if t.w is not None:
                so, v = t.w
                E.e.wait_ge(so.sem, v)


NORM_EPS = 1e-6
D = 1024
KC = 8
FF = 2816
FC = 22
TT = 512


class Common:
    def __init__(self, ctx, norm=True):
        self.ctx = ctx
        self.psum = [ctx.ps(f"ps{i}") for i in range(8)]
        if norm:
            self.init_eps()
            self.ones = ctx.sb("ones_f32", (128, 128), F32)
            ctx.op("vector", lambda e: e.memset(self.ones[:], 1.0), writes=[self.ones])
            self.sq = [ctx.sb(f"sq{i}", (128, TT), F32) for i in range(2)]
            self.rstd = ctx.sb("rstd", (128, TT), F32)
        self.rr = 0

    def rmsnorm_T(self, xt, nk, gam, outT, n, pbank, width, out2=None):
        ctx = self.ctx
        ps = pbank
        for kc in range(nk):
            sq = self.sq[self.rr % 2]
            self.rr += 1
            ctx.op("scalar", lambda e, kc=kc, sq=sq: e.activation(out=sq[:, :n], in_=xt[:, kc, :n], func=AF.Square),
                   reads=[xt], writes=[sq])
            ctx.op("tensor", lambda e, kc=kc, sq=sq: e.matmul(ps[:, :n], lhsT=self.ones[:], rhs=sq[:, :n],
                                                             start=(kc == 0), stop=(kc == nk - 1)),
                   reads=[self.ones, sq], writes=[ps])
        rstd = self.rstd
        ctx.op("scalar", lambda e: e.activation(out=rstd[:, :n], in_=ps[:, :n], func=AF.Sqrt,
                                                 bias=self.eps_t(), scale=1.0 / width),
               reads=[ps, self.eps_tile], writes=[rstd])
        ctx.op("vector", lambda e: e.reciprocal(out=rstd[:, :n], in_=rstd[:, :n]), reads=[rstd], writes=[rstd])
        for kc in range(nk):
            ctx.op("vector", lambda e, kc=kc: e.scalar_tensor_tensor(
                out=outT[:, kc, :n], in0=xt[:, kc, :n], scalar=gam[:, kc:kc + 1], in1=rstd[:, :n],
                op0=ALU.mult, op1=ALU.mult), reads=[xt, gam, rstd], writes=[outT])
            if out2 is not None:
                ctx.op("vector", lambda e, kc=kc: e.scalar_tensor_tensor(
                    out=out2[:, kc, :n], in0=xt[:, kc, :n], scalar=gam[:, kc:kc + 1], in1=rstd[:, :n],
                    op0=ALU.mult, op1=ALU.mult), reads=[xt, gam, rstd], writes=[out2])

    def eps_t(self):
        return self.eps_tile[:, 0:1]

    def init_eps(self):
        ctx = self.ctx
        self.eps_tile = ctx.sb("eps", (128, 1), F32)
        ctx.op("vector", lambda e: e.memset(self.eps_tile[:], NORM_EPS), writes=[self.eps_tile])


class FFN:
    def __init__(self, ctx, cm):
        self.ctx = ctx
        self.cm = cm
        self.hT = [ctx.sb(f"ffn_hT{i}", (128, KC, TT), BF16) for i in range(2)]
        self.wgu = [ctx.sb(f"ffn_wgu{i}", (128, 2, KC, 128), BF16) for i in range(3)]
        self.wd = [ctx.sb(f"ffn_wd{i}", (128, FC, 128), BF16) for i in range(2)]
        self.act = [ctx.sb(f"ffn_act{j}", (128, TT), BF16) for j in range(FC)]
        self.sg = [ctx.sb(f"ffn_sg{i}", (128, TT), F32) for i in range(2)]
        self.n = 0
        self.nw = 0
        self.nd = 0

    def run(self, xt, gam, wgu_d, wd_d, pb, sc=None, first=True):
        ctx, cm = self.ctx, self.cm
        hT = self.hT[self.n % 2]
        self.n += 1
        cm.rmsnorm_T(xt, KC, gam, hT, TT, pb[0], D)
        for j in range(FC):
            w = self.wgu[self.nw % 3]
            self.nw += 1
            if sc is None or first:
                ctx.dma("gpsimd", w, w[:], wgu_d, wgu_d[j], max_dma_last_dim=4096)
                if sc is not None:
                    ctx.dma("sync", sc[0], sc[0][j], w, w[:], waw=False, max_dma_last_dim=4096)
            else:
                ctx.dma("gpsimd", w, w[:], sc[0], sc[0][j], max_dma_last_dim=4096)
            pg = pb[1 + (j % 2)]
            pu = pb[3 + (j % 2)]
            for kc in range(KC):
                ctx.op("tensor", lambda e, kc=kc, w=w, pg=pg: e.matmul(pg[:], lhsT=w[:, 0, kc, :], rhs=hT[:, kc, :],
                                                                     start=(kc == 0), stop=(kc == KC - 1)),
                       reads=[w, hT], writes=[pg])
            for kc in range(KC):
                ctx.op("tensor", lambda e, kc=kc, w=w, pu=pu: e.matmul(pu[:], lhsT=w[:, 1, kc, :], rhs=hT[:, kc, :],
                                                                     start=(kc == 0), stop=(kc == KC - 1)),
                       reads=[w, hT], writes=[pu])
            sg = self.sg[j % 2]
            ctx.op("scalar", lambda e, sg=sg, pg=pg: e.activation(out=sg[:], in_=pg[:], func=AF.Silu),
                   reads=[pg], writes=[sg])
            a = self.act[j]
            ctx.op("vector", lambda e, sg=sg, pu=pu, a=a: e.tensor_tensor(out=a[:], in0=pu[:], in1=sg[:], op=ALU.mult),
                   reads=[pu, sg], writes=[a])
        for c in range(KC):
            w = self.wd[self.nd % 2]
            self.nd += 1
            if sc is None or first:
                ctx.dma("gpsimd", w, w[:], wd_d, wd_d[c], max_dma_last_dim=4096)
                if sc is not None:
                    ctx.dma("sync", sc[1], sc[1][c], w, w[:], waw=False, max_dma_last_dim=4096)
            else:
                ctx.dma("gpsimd", w, w[:], sc[1], sc[1][c], max_dma_last_dim=4096)
            po = pb[5 + (c % 2)]
            for j in range(FC):
                ctx.op("tensor", lambda e, j=j, w=w, po=po: e.matmul(po[:], lhsT=w[:, j, :], rhs=self.act[j][:],
                                                                     start=(j == 0), stop=(j == FC - 1)),
                       reads=[w, self.act[j]], writes=[po])
            ctx.op("vector", lambda e, c=c, po=po: e.scalar_tensor_tensor(
                out=xt[:, c, :], in0=po[:], scalar=0.5, in1=xt[:, c, :], op0=ALU.mult, op1=ALU.add),
                reads=[po, xt], writes=[xt])


def ffn_host_layout(wg, wu, wd):
    g = wg.reshape(KC, 128, FC, 128).transpose(2, 1, 0, 3)
    u = wu.reshape(KC, 128, FC, 128).transpose(2, 1, 0, 3)
    wgu = np.ascontiguousarray(np.stack([g, u], axis=2))
    wdt = np.ascontiguousarray(wd.reshape(FC, 128, KC, 128).transpose(2, 1, 0, 3))
    return wgu, wdt


def gain_layout(g, nk):
    return np.ascontiguousarray(g.reshape(nk, 128).T)

import ml_dtypes

NBF = ml_dtypes.bfloat16
NTOK = 4096
NSLOT = 8
S = 16384
ROPE_THETA = 500000.0


def wtile(W, nk):
    return np.ascontiguousarray(W.reshape(nk, 128, -1).transpose(1, 0, 2))


class Proj:
    def __init__(self, ctx, nslots=3):
        self.ctx = ctx
        self.w = [ctx.sb(f"pw{i}", (128, KC, 128), BF16) for i in range(nslots)]
        self.n = 0
        self.sc = {}
        self.first = True

    def mm(self, w_d, idx, nk, M, rhsT, ps, n=TT):
        ctx = self.ctx
        w = self.w[self.n % len(self.w)]
        self.n += 1
        sc = self.sc.get(w_d.name)
        if sc is None:
            sc = self.sc[w_d.name] = ctx.dram("sc_" + w_d.name, w_d.shape, BF16, "Internal")
        if self.first:
            ctx.dma("gpsimd", w, w[:, :nk, :M], w_d, w_d[idx])
            ctx.dma("sync", sc, sc[idx], w, w[:, :nk, :M], waw=False)
        else:
            ctx.dma("gpsimd", w, w[:, :nk, :M], sc, sc[idx])
        for kc in range(nk):
            ctx.op("tensor", lambda e, kc=kc: e.matmul(ps[:M, :n], lhsT=w[:, kc, :M], rhs=rhsT[:, kc, :n],
                                                       start=(kc == 0), stop=(kc == nk - 1)),
                   reads=[w, rhsT], writes=[ps])


def rope_combine(ctx, out_t, out_ap, p1, p2, ct, c_ap, st, s_ap, tmp, M, n=TT):
    t1, t2 = tmp
    ctx.op("vector", lambda e: e.tensor_tensor(out=t1[:M, :n], in0=p1[:M, :n], in1=c_ap, op=ALU.mult),
           reads=[p1, ct], writes=[t1])
    ctx.op("vector", lambda e: e.tensor_tensor(out=t2[:M, :n], in0=p2[:M, :n], in1=s_ap, op=ALU.mult),
           reads=[p2, st], writes=[t2])
    ctx.op("vector", lambda e: e.tensor_tensor(out=out_ap, in0=t1[:M, :n], in1=t2[:M, :n], op=ALU.add),
           reads=[t1, t2], writes=[out_t])


EV_ROPE = [True] * 4 + [True, False, True, False, True, False] + [True] * 4 + [True] * 4 + [False] * 4
EV_COLS = list(range(0, 1280, 128)) + list(range(1304, 2840, 128))


def build_stage(kind):
    nc = bass.Bass("TRN2", target_bir_lowering=False)
    ctx = Ctx(nc)
    cm = Common(ctx)
    ffn = FFN(ctx, cm)
    pj = Proj(ctx)
    pb = cm.psum
    D_ = {}

    def din(name, shape, dt=F32):
        D_[name] = ctx.dram(name, shape, dt, "ExternalInput")
        return D_[name]

    def dout(name, shape, dt=F32):
        D_[name] = ctx.dram(name, shape, dt, "ExternalOutput")
        return D_[name]

    xT = din("xT", (128, KC, NTOK))
    outs = []
    gams = {}

    def load_gam(name, nk=KC):
        d = din(name, (128, nk))
        t = ctx.sb("sb_" + name, (128, nk), F32)
        ctx.dma("sync", t, t[:], d, d[:])
        gams[name] = t
        return t

    def ffn_in(pref):
        return (load_gam(pref + "_g"), din(pref + "_wgu", (FC, 128, 2, KC, 128)), din(pref + "_wd", (KC, 128, FC, 128)),
                (ctx.dram("sc_" + pref + "_wgu", (FC, 128, 2, KC, 128), BF16, "Internal"),
                 ctx.dram("sc_" + pref + "_wd", (KC, 128, FC, 128), BF16, "Internal")))

    xts = [ctx.sb(f"xt{i}", (128, KC, TT), F32) for i in range(2)]
    tmp = [ctx.sb(f"tmp{i}", (128, TT), F32) for i in range(2)]
    hTs = [ctx.sb(f"hmix{i}", (128, KC, TT), BF16) for i in range(2)]
    if kind in ("L3", "L5"):
        aT = din("aT", (128, KC, NTOK), BF16)
        wo = din("wo", (KC, 128, KC, 128))
        ats = [ctx.sb(f"at{i}", (128, KC, TT), BF16) for i in range(2)]
    if kind == "L1":
        fa = ffn_in("fa")
        gm = load_gam("gmix")
        win = din("win", (22, 128, KC, 128))
        wsw = din("wsw", (22, 128, KC, 128))
        wgt = din("wgt", (128, KC, 24))
        ctab = din("ctab", (128, NTOK))
        stab = din("stab", (128, NTOK))
        x1T = dout("x1T", (128, KC, NTOK))
        pjo = dout("pj", (22, 128, NTOK), BF16)
        gto = dout("gates", (24, NTOK))
        outs = [x1T, pjo, gto]
        cts = [ctx.sb(f"ct{i}", (128, TT), F32) for i in range(2)]
        sts = [ctx.sb(f"st{i}", (128, TT), F32) for i in range(2)]
        obs = [ctx.sb(f"ob{i}", (128, TT), BF16) for i in range(3)]
        gos = [ctx.sb(f"go{i}", (24, TT), F32) for i in range(2)]
        hT32 = ctx.sb("hT32", (128, KC, TT), F32)
        w32 = [ctx.sb(f"w32_{i}", (128, KC, 128), F32) for i in range(2)]
        ob32s = [ctx.sb(f"ob32_{i}", (128, TT), F32) for i in range(2)]
        q32o = dout("q32", (5, 128, NTOK), F32)
        outs.append(q32o)
        n32 = [0]
        permd = din("permT", (128, 128))
        permT = ctx.sb("sb_permT", (128, 128), F32)
        ctx.dma("sync", permT, permT[:], permd, permd[:])
        p1sb = [ctx.sb(f"p1sb{i}", (128, TT), F32) for i in range(2)]

        def mm32(w_d, w_ap, ps):
            w = w32[n32[0] % 2]
            n32[0] += 1
            ctx.dma("sync", w, w[:], w_d, w_ap)
            for kc in range(KC):
                ctx.op("tensor", lambda e, kc=kc: e.matmul(ps[:], lhsT=w[:, kc, :], rhs=hT32[:, kc, :],
                                                           start=(kc == 0), stop=(kc == KC - 1)),
                       reads=[w, hT32], writes=[ps])
    if kind == "L3":
        fa = ffn_in("fa")
        fb = ffn_in("fb")
        gm = load_gam("gmix")
        gq = load_gam("gq", 2)
        gkv = load_gam("gkv", 1)
        wmi = din("wmi", (3, 128, KC, 128))
        wkr = din("wkr", (2, 128, KC, 32))
        wuq = din("wuq", (16, 128, 2, 96))
        wuqs = din("wuqs", (16, 128, 2, 96))
        wukv = din("wukv", (16, 128, 1, 128))
        cq_t = din("cq_t", (96, NTOK))
        sq_t = din("sq_t", (96, NTOK))
        ck_t = din("ck_t", (32, NTOK))
        sk_t = din("sk_t", (32, NTOK))
        x3T = dout("x3T", (128, KC, NTOK))
        qTo = dout("qT", (16, 96, NTOK), BF16)
        kvo = dout("kvT", (16, 128, NTOK), BF16)
        kro = dout("krT", (32, NTOK), BF16)
        outs = [x3T, qTo, kvo, kro]
        cts = [ctx.sb(f"ct{i}", (96, TT), F32) for i in range(2)]
        sts = [ctx.sb(f"st{i}", (96, TT), F32) for i in range(2)]
        ckts = [ctx.sb(f"ckt{i}", (32, TT), F32) for i in range(2)]
        skts = [ctx.sb(f"skt{i}", (32, TT), F32) for i in range(2)]
        obs = [ctx.sb(f"ob{i}", (128, TT), BF16) for i in range(3)]
        cqT = [ctx.sb(f"cqT{i}", (128, 2, TT), F32) for i in range(2)]
        ckvT = [ctx.sb(f"ckvT{i}", (128, 1, TT), F32) for i in range(2)]
        cqn = [ctx.sb(f"cqn{i}", (128, 2, TT), BF16) for i in range(2)]
        ckvn = [ctx.sb(f"ckvn{i}", (128, 1, TT), BF16) for i in range(2)]
    if kind == "L5":
        fa = ffn_in("fa")
        gf = load_gam("gfin")
        yT = dout("yT", (128, KC, NTOK))
        outs = [yT]
        yts = [ctx.sb(f"yt{i}", (128, KC, TT), F32) for i in range(2)]

    nob = 0
    for t in range(NSLOT):
        ts = slice(t * TT, (t + 1) * TT)
        xt = xts[t % 2]
        pj.first = (t == 0)
        ctx.dma("sync", xt, xt[:], xT, xT[:, :, ts])
        if kind in ("L3", "L5"):
            at = ats[t % 2]
            ctx.dma("sync", at, at[:], aT, aT[:, :, ts])
            for c in range(KC):
                ps = pb[5 + (c % 2)]
                pj.mm(wo, c, KC, 128, at, ps)
                ctx.op("vector", lambda e, c=c, ps=ps: e.tensor_tensor(out=xt[:, c, :], in0=ps[:], in1=xt[:, c, :], op=ALU.add),
                       reads=[ps, xt], writes=[xt])
        if kind == "L1":
            ffn.run(xt, fa[0], fa[1], fa[2], pb[0:7], sc=fa[3], first=(t == 0))
            ctx.dma("sync", x1T, x1T[:, :, ts], xt, xt[:])
            hT = hTs[t % 2]
            cm.rmsnorm_T(xt, KC, gm, hT, TT, pb[0], D, out2=hT32)
            ct, st = cts[t % 2], sts[t % 2]
            ctx.dma("sync", ct, ct[:], ctab, ctab[:, ts])
            ctx.dma("sync", st, st[:], stab, stab[:, ts])
            for c in range(22):
                p1 = pb[1 + (c % 2)]
                ob = obs[nob % 3]
                nob += 1
                if c < 5:
                    p2 = pb[3 + (c % 2)]
                    mm32(win, win[c], p1)
                    p1s = p1sb[c % 2]
                    ctx.op("scalar", lambda e, p1=p1, p1s=p1s: e.activation(out=p1s[:], in_=p1[:], func=AF.Copy),
                           reads=[p1], writes=[p1s])
                    ctx.op("tensor", lambda e, p2=p2, p1s=p1s: e.matmul(p2[:], lhsT=permT[:], rhs=p1s[:], start=True, stop=True),
                           reads=[permT, p1s], writes=[p2])
                    ob32 = ob32s[c % 2]
                    rope_combine(ctx, ob32, ob32[:], p1s, p2, ct, ct[:], st, st[:], tmp, 128)
                    ctx.op("scalar", lambda e, ob=ob, ob32=ob32: e.activation(out=ob[:], in_=ob32[:], func=AF.Copy),
                           reads=[ob32], writes=[ob])
                    ctx.dma("sync", q32o, q32o[c, :, ts], ob32, ob32[:])
                    ctx.dma("sync", pjo, pjo[c, :, ts], ob, ob[:])
                    continue
                pj.mm(win, c, KC, 128, hT, p1)
                if EV_ROPE[c]:
                    p2 = pb[3 + (c % 2)]
                    pj.mm(wsw, c, KC, 128, hT, p2)
                    rope_combine(ctx, ob, ob[:], p1, p2, ct, ct[:], st, st[:], tmp, 128)
                else:
                    ctx.op("scalar", lambda e, p1=p1, ob=ob: e.activation(out=ob[:], in_=p1[:], func=AF.Copy),
                           reads=[p1], writes=[ob])
                ctx.dma("sync", pjo, pjo[c, :, ts], ob, ob[:])
            p1 = pb[7]
            pj.mm(wgt, slice(None), KC, 24, hT, p1)
            go = gos[t % 2]
            ctx.op("scalar", lambda e, p1=p1, go=go: e.activation(out=go[:], in_=p1[:24, :], func=AF.Sigmoid),
                   reads=[p1], writes=[go])
            ctx.dma("sync", gto, gto[:, ts], go, go[:])
        if kind == "L3":
            ffn.run(xt, fa[0], fa[1], fa[2], pb[0:7], sc=fa[3], first=(t == 0))
            ffn.run(xt, fb[0], fb[1], fb[2], pb[0:7], sc=fb[3], first=(t == 0))
            ctx.dma("sync", x3T, x3T[:, :, ts], xt, xt[:])
            hT = hTs[t % 2]
            cm.rmsnorm_T(xt, KC, gm, hT, TT, pb[0], D)
            cq, ckv, cqn_, ckvn_ = cqT[t % 2], ckvT[t % 2], cqn[t % 2], ckvn[t % 2]
            for i in range(3):
                p1 = pb[1 + (i % 2)]
                pj.mm(wmi, i, KC, 128, hT, p1)
                dst_t, dst = (cq, cq[:, i, :]) if i < 2 else (ckv, ckv[:, 0, :])
                ctx.op("scalar", lambda e, p1=p1, dst=dst: e.activation(out=dst, in_=p1[:], func=AF.Copy),
                       reads=[p1], writes=[dst_t])
            ckt, skt = ckts[t % 2], skts[t % 2]
            ctx.dma("sync", ckt, ckt[:], ck_t, ck_t[:, ts])
            ctx.dma("sync", skt, skt[:], sk_t, sk_t[:, ts])
            p1, p2 = pb[3], pb[4]
            pj.mm(wkr, 0, KC, 32, hT, p1)
            pj.mm(wkr, 1, KC, 32, hT, p2)
            ob = obs[nob % 3]
            nob += 1
            rope_combine(ctx, ob, ob[:32, :], p1, p2, ckt, ckt[:], skt, skt[:], tmp, 32)
            ctx.dma("sync", kro, kro[:, ts], ob, ob[:32, :])
            cm.rmsnorm_T(cq, 2, gq, cqn_, TT, pb[0], 256)
            cm.rmsnorm_T(ckv, 1, gkv, ckvn_, TT, pb[0], 128)
            ct, st = cts[t % 2], sts[t % 2]
            ctx.dma("sync", ct, ct[:], cq_t, cq_t[:, ts])
            ctx.dma("sync", st, st[:], sq_t, sq_t[:, ts])
            for h in range(16):
                p1 = pb[1 + (h % 2)]
                p2 = pb[3 + (h % 2)]
                pj.mm(wuq, h, 2, 96, cqn_, p1)
                pj.mm(wuqs, h, 2, 96, cqn_, p2)
                ob = obs[nob % 3]
                nob += 1
                rope_combine(ctx, ob, ob[:96, :], p1, p2, ct, ct[:], st, st[:], tmp, 96)
                ctx.dma("sync", qTo, qTo[h, :, ts], ob, ob[:96, :])
            for h in range(16):
                p1 = pb[5 + (h % 2)]
                pj.mm(wukv, h, 1, 128, ckvn_, p1)
                ob = obs[nob % 3]
                nob += 1
                ctx.op("scalar", lambda e, p1=p1, ob=ob: e.activation(out=ob[:], in_=p1[:], func=AF.Copy),
                       reads=[p1], writes=[ob])
                ctx.dma("sync", kvo, kvo[h, :, ts], ob, ob[:])
        if kind == "L5":
            ffn.run(xt, fa[0], fa[1], fa[2], pb[0:7], sc=fa[3], first=(t == 0))
            yt = yts[t % 2]
            cm.rmsnorm_T(xt, KC, gf, yt, TT, pb[0], D)
            ctx.dma("sync", yT, yT[:, :, ts], yt, yt[:])
    ctx.finish(outs)
    return nc, ctx


NEG = -30000.0
BIG = 1e30
NQB = 32


class Attn:
    def __init__(self, ctx, cm, scale, consts, ns3=False, sbanks=None, lbanks=None):
        self.ctx, self.cm, self.scale = ctx, cm, scale
        self.S = sbanks if sbanks else [cm.psum[0], cm.psum[1]] + ([cm.psum[7]] if ns3 else [])
        self.lag = len(self.S) - 1
        self.pending = []
        self.LB = lbanks if lbanks else [cm.psum[5], cm.psum[6]]
        self.P = [ctx.sb(f"P{i}", (128, 512), BF16) for i in range(4)]
        self.OL = [ctx.sb(f"OL{i}", (128, 512), F32) for i in range(2)]
        self.rl = [ctx.sb(f"rl{i}", (64, 512), F32) for i in range(2)]
        self.ident = ctx.sb("sb_ident", (128, 128), BF16)
        self.sel = ctx.sb("sb_sel", (128, 64), F32)
        ctx.dma("sync", self.ident, self.ident[:], consts["ident"], consts["ident"][:])
        ctx.dma("sync", self.sel, self.sel[:], consts["sel"], consts["sel"][:])
        self.i = 0
        self.j = 0

    def step(self, nk, c0, c1, mains, masks, pvs):
        ctx = self.ctx
        S = self.S[self.i % len(self.S)]
        P = self.P[self.i % 4]
        self.i += 1
        allm = list(mains) + list(masks)
        n = len(allm)
        for k, (lt, lap, rt, rap, cs) in enumerate(allm):
            ctx.op("tensor", lambda e, lap=lap, rap=rap, cs=cs, k=k: e.matmul(
                S[:nk, cs], lhsT=lap, rhs=rap, start=(k == 0), stop=(k == n - 1), skip_group_check=True),
                reads=[lt, rt], writes=[S])
        ctx.op("scalar", lambda e: e.activation(out=P[:nk, c0:c1], in_=S[:nk, c0:c1], func=AF.Exp, scale=self.scale),
               reads=[S], writes=[P])
        self.pending.append((nk, P, pvs))
        while len(self.pending) > self.lag:
            self._flush_one()

    def _flush_one(self):
        ctx = self.ctx
        nk, P, pvs = self.pending.pop(0)
        for (vt, vap, pcs, acc, acc_ap, start) in pvs:
            ctx.op("tensor", lambda e, vap=vap, pcs=pcs, acc_ap=acc_ap, start=start: e.matmul(
                acc_ap, lhsT=vap, rhs=P[:nk, pcs], start=start, stop=True, skip_group_check=True),
                reads=[vt, P], writes=[acc])

    def flush(self):
        while self.pending:
            self._flush_one()

    def finish(self, acc):
        self.flush()
        ctx = self.ctx
        OL = self.OL[self.j % 2]
        rl = self.rl[self.j % 2]
        LB = self.LB[self.j % len(self.LB)]
        self.j += 1
        ctx.op("scalar", lambda e: e.activation(out=OL[:], in_=acc[:], func=AF.Copy), reads=[acc], writes=[OL])
        ctx.op("tensor", lambda e: e.matmul(LB[:64, :], lhsT=self.sel[:], rhs=OL[:], start=True, stop=True),
               reads=[self.sel, OL], writes=[LB])
        ctx.op("vector", lambda e: e.tensor_scalar(out=rl[:], in0=LB[:64, :], scalar1=1e-30, scalar2=None, op0=ALU.max),
               reads=[LB], writes=[rl])
        ctx.op("vector", lambda e: e.reciprocal(out=rl[:], in_=rl[:]), reads=[rl], writes=[rl])
        return OL, rl


def attn_consts_np():
    ident = np.eye(128, dtype=np.float32).astype(NBF)
    sel = np.zeros((128, 64), np.float32)
    sel[64 + np.arange(64), np.arange(64)] = 1.0
    kl = np.arange(128)[:, None]
    ql = np.arange(512)[None, :]
    mc = np.stack([np.where(ql >= kl + o, 0.0, NEG) for o in (0, 128, 256, 384)], axis=1)
    return {"ident": ident, "sel": sel, "mcausal": mc.astype(NBF)}


def build_mla():
    nc = bass.Bass("TRN2", target_bir_lowering=False)
    ctx = Ctx(nc)
    cm = Common(ctx, norm=False)
    NU = 4
    qT = ctx.dram("qT", (NU, 96, S), BF16, "ExternalInput")
    kT = ctx.dram("kT", (NU, 96, S), BF16, "ExternalInput")
    v1 = ctx.dram("v1", (NU, 128, 128, 128), BF16, "ExternalInput")
    cd = {"ident": ctx.dram("ident", (128, 128), BF16, "ExternalInput"),
          "sel": ctx.dram("sel", (128, 64), F32, "ExternalInput")}
    mcd = ctx.dram("mcausal", (128, 4, 512), BF16, "ExternalInput")
    oT = ctx.dram("oT", (NU, 64, S), BF16, "ExternalOutput")
    at = Attn(ctx, cm, 96 ** -0.5, cd, ns3=True)
    mc = ctx.sb("mc", (128, 4, 512), BF16)
    ctx.dma("sync", mc, mc[:], mcd, mcd[:])
    Kb = [ctx.sb(f"Kb{i}", (96, S), BF16) for i in range(2)]
    Vb = [ctx.sb(f"Vb{i}", (128, 128, 128), BF16) for i in range(2)]
    Qb = [ctx.sb(f"Qb{i}", (96, 512), BF16) for i in range(3)]
    Ob = [ctx.sb(f"Ob{i}", (64, 512), BF16) for i in range(2)]
    acc = [cm.psum[2], cm.psum[3]]
    n = 0
    for u in range(NU):
        K, V = Kb[u % 2], Vb[u % 2]
        ctx.dma("sync", K, K[:], kT, kT[u])
        ctx.dma("sync", V, V[:], v1, v1[u])
        for qb in range(NQB):
            Q = Qb[n % 3]
            A = acc[n % 2]
            O = Ob[n % 2]
            n += 1
            ctx.dma("sync", Q, Q[:], qT, qT[u, :, qb * 512:(qb + 1) * 512])
            nkt = 4 * qb + 4
            for kt in range(nkt):
                d = kt - 4 * qb
                c0 = d * 128 if d > 0 else 0
                mains = [(K, K[:, kt * 128:(kt + 1) * 128], Q, Q[:, c0:512], slice(c0, 512))]
                masks = []
                if d >= 0:
                    masks = [(at.ident, at.ident[:], mc, mc[:, d, c0:512], slice(c0, 512))]
                pvs = [(V, V[:, kt, :], slice(c0, 512), A, A[:, c0:512], kt == 0)]
                at.step(128, c0, 512, mains, masks, pvs)
            OL, rl = at.finish(A)
            ctx.op("vector", lambda e, OL=OL, rl=rl, O=O: e.tensor_tensor(out=O[:], in0=OL[:64, :], in1=rl[:], op=ALU.mult),
                   reads=[OL, rl], writes=[O])
            ctx.dma("sync", oT, oT[u, :, qb * 512:(qb + 1) * 512], O, O[:])
    ctx.finish([oT])
    return nc, ctx


def nsa_consts_np():
    c = attn_consts_np()
    kl = np.arange(128)[:, None]
    ql = np.arange(512)[None, :]
    d = ql - kl
    c["mwin"] = np.stack([np.where((d - o >= 0) & (d - o < 512), 0.0, NEG) for o in range(-512, 512, 128)], 1).astype(NBF)
    c["mcmp"] = np.stack([np.where(ql - 16 * kl >= 31 - 512 * dl, 0.0, NEG) for dl in range(5)], 1).astype(NBF)
    q = np.arange(128)[:, None]
    npr = np.arange(-1, 8)[None, :]
    c["mtm"] = np.where(q >= 31 + 16 * npr, 0.0, NEG).astype(NBF)
    lo = (np.arange(128) < 64)[:, None]
    c["mul3"] = np.where(lo, np.array([[0., 0., 0.]]), np.array([[1., 0., 0.]])).astype(np.float32)
    c["add3"] = np.where(lo, np.array([[BIG, BIG, -BIG]]), np.array([[0., BIG, BIG]])).astype(np.float32)
    c["identf"] = np.eye(128, dtype=np.float32)
    sg = np.zeros((6, 6, 64), np.float32)
    for r in range(6):
        sg[r, r, :] = 1.0
    c["selg"] = sg
    return c


NSA_CONST_SHAPES = {"ident": ((128, 128), BF16), "sel": ((128, 64), F32), "mcausal": ((128, 4, 512), BF16),
                    "mwin": ((128, 8, 512), BF16), "mcmp": ((128, 5, 512), BF16),
                    "mtm": ((128, 9), BF16), "mul3": ((128, 3), F32), "add3": ((128, 3), F32),
                    "identf": ((128, 128), F32), "selg": ((6, 6, 64), F32)}


USE32 = True
DBG_SKIP = set()


def build_nsa(nqb=NQB):
    nc = bass.Bass("TRN2", target_bir_lowering=False)
    ctx = Ctx(nc)
    cm = Common(ctx, norm=False)
    pb = cm.psum
    din = lambda n, s, dt=BF16: ctx.dram(n, s, dt, "ExternalInput")
    qg = din("qg", (128, 2, S), F32 if USE32 else BF16)
    qmy = din("qmy", (128, S))
    kcraw = din("kcraw", (64, 16, 1024), F32)
    vcraw = din("vcraw", (64, S))
    w1k = din("w1k", (128, 32, 128), F32)
    w1v = din("w1v", (64, 32, 128), F32)
    posk = din("posk", (128, 32, 8), F32)
    posv = din("posv", (64, 32), F32)
    w2k = din("w2k", (128, 128), F32)
    w2v = din("w2v", (128, 64), F32)
    ksT = din("ksT", (128, S))
    vs1 = din("vs1", (128, 128, 128))
    kwT = din("kwT", (128, 512 + S))
    vw1 = din("vw1", (128, 132, 128))
    gat = din("gat", (6, S), F32)
    cd = {k: din(k, s, dt) for k, (s, dt) in NSA_CONST_SHAPES.items()}
    oA = ctx.dram("oA", (2, 64, S), BF16, "ExternalOutput")
    at = Attn(ctx, cm, 0.125, cd, sbanks=[pb[0], pb[1], pb[5]], lbanks=[pb[6]])

    def cload(name, eng="sync"):
        s, dt = NSA_CONST_SHAPES[name]
        t = ctx.sb("c_" + name, s, dt)
        ctx.dma(eng, t, t[:], cd[name], cd[name][:])
        return t
    mc, mwin, mcmp, mtm, mul3, add3, identf = [cload(n) for n in
                                               ("mcausal", "mwin", "mcmp", "mtm", "mul3", "add3", "identf")]
    selg = ctx.sb("c_selg", (6, 6 * 64), F32)
    ctx.dma("sync", selg, selg[:], cd["selg"], cd["selg"].ap.rearrange("a b c -> a (b c)"))
    bigA = ctx.sb("bigA", (128, S), BF16)
    bigB = ctx.sb("bigB", (128, S), BF16)
    bigB3 = bigB.ap.rearrange("p (t c) -> p t c", c=128)
    kcT = ctx.sb("kcT", (128, 1024), BF16)
    vc1 = ctx.sb("vc1", (128, 8, 128), BF16)
    kcT32 = ctx.sb("kcT32", (128, 1024), F32)
    ctx.op("gpsimd", lambda e: e.memset(bigA[:], 0.0), writes=[bigA])
    bigA32 = bigA.ap.bitcast(F32)
    k32buf = bigA32[:, 0:2080].rearrange("p (j m) -> p j m", m=130)
    w1k32 = bigA32[:, 2080:2080 + 4096].rearrange("p (l h) -> p l h", h=128)
    pos32 = ctx.sb("pos32", (128, 32, 8), F32)
    w2k32 = ctx.sb("w2k32", (128, 128), F32)
    hid32 = [ctx.sb(f"hid32_{i}", (128, 128), F32) for i in range(2)]
    posb = ctx.sb("posb", (128, 1), F32)
    ctx.dma("sync", bigA, w1k32, w1k, w1k[:])
    ctx.dma("sync", pos32, pos32[:], posk, posk[:])
    ctx.dma("sync", w2k32, w2k32[:], w2k, w2k[:])
    ps = pb[7]
    for l in range(32 if "posb" not in DBG_SKIP else 1):
        ctx.op("tensor", lambda e, l=l: e.matmul(ps[:, 0:8], lhsT=w1k32[:, l, :], rhs=pos32[:, l, :],
                                                 start=(l == 0), stop=(l == 31)), reads=[bigA, pos32], writes=[ps])
    ctx.op("vector", lambda e: e.tensor_copy(out=posb[:], in_=ps[:, 0:1]), reads=[ps], writes=[posb])
    for p in range(8 if "kpath" not in DBG_SKIP else 0):
        nm = min(130, 1024 - 128 * p)
        ctx.dma("sync", bigA, k32buf[0:64, :, 0:nm], kcraw, kcraw[:, :, 128 * p:128 * p + nm])
        ps = pb[p % 2]
        for l in range(32):
            a_, j_ = l // 16, l % 16
            ctx.op("tensor", lambda e, l=l: e.matmul(ps[:, 0:128], lhsT=w1k32[:, l, :], rhs=k32buf[:, j_, a_:a_ + 128],
                                                     start=(l == 0), stop=(l == 31)), reads=[bigA], writes=[ps])
        h32 = hid32[p % 2]
        ctx.op("scalar", lambda e: e.activation(out=h32[:], in_=ps[:, 0:128], func=AF.Silu, bias=posb[:, 0:1]),
               reads=[ps, posb], writes=[h32])
        p2 = pb[2 + (p % 2)]
        ctx.op("tensor", lambda e: e.matmul(p2[:, 0:128], lhsT=w2k32[:], rhs=h32[:], start=True, stop=True),
               reads=[w2k32, h32], writes=[p2])
        ctx.op("vector", lambda e: e.tensor_copy(out=kcT32[:, p * 128:(p + 1) * 128], in_=p2[:, 0:128]), reads=[p2], writes=[kcT32])
        ctx.op("scalar", lambda e: e.activation(out=kcT[:, p * 128:(p + 1) * 128], in_=kcT32[:, p * 128:(p + 1) * 128], func=AF.Copy), reads=[kcT32], writes=[kcT])
    ctx.dma("sync", bigB, bigB[0:64, :], vcraw, vcraw[:])
    w1s_ap = bigA[0:64, 0:4096].rearrange("p (l h) -> p l h", h=128)
    poss = ctx.sb("poss", (64, 32), BF16)
    w2vs = ctx.sb("w2vs", (128, 64), BF16)
    ctx.dma("gpsimd", w2vs, w2vs[:], w2v, w2v[:])
    hid = [ctx.sb(f"hid{i}", (128, 512), BF16) for i in range(2)]
    ctx.op("vector", lambda e: e.memset(hid[1][:], 0.0), writes=[hid[1]])
    ctx.op("vector", lambda e: e.memset(vc1[:], 1.0), writes=[vc1])
    ctx.dma("gpsimd", bigA, w1s_ap, w1v, w1v[:])
    ctx.dma("gpsimd", poss, poss[:], posv, posv[:])
    ps = pb[7]
    for l in range(32):
        ctx.op("tensor", lambda e, l=l: e.matmul(ps[:, 0:1], lhsT=w1s_ap[:, l, :], rhs=poss[:, l:l + 1],
                                                 start=(l == 0), stop=(l == 31)), reads=[bigA, poss], writes=[ps])
    posbv = ctx.sb("posbv", (128, 1), F32)
    ctx.op("vector", lambda e: e.tensor_copy(out=posbv[:], in_=ps[:, 0:1]), reads=[ps], writes=[posbv])
    for nt in range(2 if "vpath" not in DBG_SKIP else 0):
        ncol = 512 if nt == 0 else 511
        ps = pb[nt]
        for l in range(32):
            st_ = nt * 8192 + l
            en = min(S, st_ + 16 * ncol)
            ctx.op("tensor", lambda e, l=l: e.matmul(ps[:, 0:ncol], lhsT=w1s_ap[:, l, :], rhs=bigB[0:64, st_:en:16],
                                                     start=(l == 0), stop=(l == 31)), reads=[bigA, bigB], writes=[ps])
        ctx.op("scalar", lambda e: e.activation(out=hid[nt][:, 0:ncol], in_=ps[:, 0:ncol], func=AF.Silu, bias=posbv[:, 0:1]),
               reads=[ps, posbv], writes=[hid[nt]])
        for j in range(4):
            p2 = pb[2 + (j % 2)]
            ctx.op("tensor", lambda e: e.matmul(p2[:, 0:64], lhsT=hid[nt][:, j * 128:(j + 1) * 128], rhs=w2vs[:], start=True, stop=True),
                   reads=[w2vs, hid[nt]], writes=[p2])
            ctx.op("vector", lambda e: e.tensor_copy(out=vc1[:, nt * 4 + j, 0:64], in_=p2[:, 0:64]), reads=[p2], writes=[vc1])
    ctx.dma("sync", bigA, bigA[:], ksT, ksT[:])
    ctx.dma("sync", bigB, bigB[:], vs1, vs1.ap.rearrange("p t c -> p (t c)"))
    QDT = F32 if USE32 else BF16
    Qg1 = ctx.sb("Qg0", (128, 4, 512), QDT)
    Qgs = [Qg1, Qg1]
    Qms = [[[ctx.sb(f"QY{i}_{h}_{c}", (128, 512), BF16) for c in range(4)] for h in range(2)] for i in range(2)]
    ctx.op("gpsimd", lambda e: e.memset(Qg1[:], 0.0), writes=[Qg1])
    for i in range(2):
        for h in range(2):
            for c in range(4):
                ctx.op("gpsimd", lambda e: e.memset(Qms[i][h][c][:], 0.0), writes=[Qms[i][h][c]])
    wKs = [ctx.sb(f"wK{i}", (128, 1024), BF16) for i in range(2)]
    wVs = [ctx.sb(f"wV{i}", (128, 8, 128), BF16) for i in range(2)]
    gts = [ctx.sb(f"gt{i}", (6, 512), F32) for i in range(2)]
    NE = 4
    es = [ctx.sb(f"e{i}", (128, 512), F32) for i in range(NE)]
    lp = [ctx.sb(f"lp{i}", (128, 2), F32) for i in range(2)]
    rlh = [ctx.sb(f"rlh{i}", (128, 1), F32) for i in range(2)]
    Aim = ctx.sb("Aim", (128, 1024), F32)
    I1 = ctx.sb("I1", (128, 256), F32)
    I2 = ctx.sb("I2", (128, 256), F32)
    m8 = ctx.sb("m8", (128, 16), F32)
    negms = [ctx.sb(f"negm{i}", (128, 320), F32) for i in range(4)]
    fg = ctx.sb("fg", (64, 512), F32)
    tmpc = ctx.sb("tmpc", (64, 512), F32)
    accsb = ctx.sb("accsb", (64, 512), F32)
    Ob = [ctx.sb(f"Ob{i}", (64, 512), BF16) for i in range(2)]
    accC, accS, accW, GB = pb[2], pb[3], pb[4], pb[7]
    kq = kcT32 if USE32 else kcT
    st = {"ne": 0, "no": 0}

    def load(qb):
        T0 = qb * 512
        if qb >= 2:
            for hh in range(4):
                r0 = (hh % 2) * 64
                ctx.dma("sync", Qgs[qb % 2], Qgs[qb % 2][0:64, hh, :], qg, qg[r0:r0 + 64, hh // 2, T0:T0 + 512])
        for h in range(2):
            for c in range(qb // 8 + 1):
                t_ = Qms[qb % 2][h][c]
                ctx.dma("sync", t_, t_[0:64, :], qmy, qmy[h * 64:(h + 1) * 64, T0:T0 + 512])
        ctx.dma("sync", wKs[qb % 2], wKs[qb % 2][:], kwT, kwT[:, T0:T0 + 1024])
        ctx.dma("sync", wVs[qb % 2], wVs[qb % 2][:], vw1, vw1[:, 4 * qb:4 * qb + 8, :])
        ctx.dma("sync", gts[qb % 2], gts[qb % 2][:], gat, gat[:, T0:T0 + 512])

    def phase1a(qb, subs=(0, 1, 2, 3)):
        if qb < 2:
            return
        Qg = Qgs[qb % 2]
        for qsl in subs:
            qs = 4 * qb + qsl
            ncols = 8 * qs + 8
            nj = 2 * qs + 2
            negm = negms[qsl]
            halves = [(lo, min(ncols, lo + 512)) for lo in (0, 512) if lo < ncols]
            for hh in range(4):
                ch, r0 = hh // 2, (hh % 2) * 64
                lpt = lp[hh % 2]
                rl1 = rlh[hh % 2]
                ehs = []
                for hi_, (lo, hi) in enumerate(halves):
                    w = hi - lo
                    Sb = at.S[at.i % len(at.S)]
                    at.i += 1
                    a, b2 = max(lo, ncols - 9, 0), min(hi, ncols)
                    hasm = b2 > a
                    ctx.op("tensor", lambda e: e.matmul(
                        Sb[:, 0:w], lhsT=Qg[:, hh, qsl * 128:(qsl + 1) * 128], rhs=kq[:, lo:hi],
                        start=True, stop=(not hasm), skip_group_check=True), reads=[Qg, kq], writes=[Sb])
                    if hasm:
                        ctx.op("tensor", lambda e: e.matmul(
                            Sb[:, a - lo:b2 - lo], lhsT=at.ident[:], rhs=mtm[:, a - (ncols - 9):b2 - (ncols - 9)],
                            start=False, stop=True, skip_group_check=True), reads=[at.ident, mtm], writes=[Sb])
                    et = es[st["ne"] % NE]
                    st["ne"] += 1
                    ctx.op("scalar", lambda e: e.activation(
                        out=et[:, 0:w], in_=Sb[:, 0:w], func=AF.Exp, scale=0.125, accum_out=lpt[:, hi_:hi_ + 1]),
                        reads=[Sb], writes=[et, lpt])
                    ehs.append((et, lo, hi))
                if len(halves) == 2:
                    ctx.op("vector", lambda e: e.tensor_tensor(out=lpt[:, 0:1], in0=lpt[:, 0:1], in1=lpt[:, 1:2], op=ALU.add),
                           reads=[lpt], writes=[lpt])
                ctx.op("vector", lambda e: e.tensor_scalar(out=rl1[:], in0=lpt[:, 0:1], scalar1=1e-30, scalar2=None, op0=ALU.max),
                       reads=[lpt], writes=[rl1])
                ctx.op("vector", lambda e: e.reciprocal(out=rl1[:], in_=rl1[:]), reads=[rl1], writes=[rl1])
                for (et, lo, hi) in ehs:
                    w = hi - lo
                    if hh == 0:
                        ctx.op("vector", lambda e: e.tensor_scalar(
                            out=Aim[:, lo:hi], in0=et[:, 0:w], scalar1=rl1[:, 0:1], scalar2=None, op0=ALU.mult),
                            reads=[et, rl1], writes=[Aim])
                    else:
                        ctx.op("vector", lambda e: e.scalar_tensor_tensor(
                            out=Aim[:, lo:hi], in0=et[:, 0:w], scalar=rl1[:, 0:1], in1=Aim[:, lo:hi], op0=ALU.mult, op1=ALU.add),
                            reads=[et, rl1, Aim], writes=[Aim])
            n4 = 4 * nj
            tt = lambda o, a_, b_, op: ctx.op("vector", lambda e: e.tensor_tensor(out=o, in0=a_, in1=b_, op=op),
                                              reads=[Aim, I1, mul3, add3], writes=[I1])
            tt(I1[:, 0:nj], Aim[:, 0:n4:4], Aim[:, 1:n4:4], ALU.add)
            tt(I1[:, 0:nj], I1[:, 0:nj], Aim[:, 2:n4:4], ALU.add)
            ctx.op("vector", lambda e: e.scalar_tensor_tensor(out=I1[:, 0:nj], in0=I1[:, 0:nj], scalar=2.0, in1=Aim[:, 3:n4:4],
                                                              op0=ALU.mult, op1=ALU.add), reads=[Aim, I1], writes=[I1])
            tt(I1[:, 1:nj], I1[:, 1:nj], Aim[:, 3:n4 - 4:4], ALU.add)
            tt(I1[:, nj - 3:nj], I1[:, nj - 3:nj], mul3[:, :], ALU.mult)
            tt(I1[:, nj - 3:nj], I1[:, nj - 3:nj], add3[:, :], ALU.add)
            ctx.op("vector", lambda e: e.memset(I1[:, 0:1], BIG), writes=[I1])
            ctx.op("gpsimd", lambda e: e.memset(negm[:], 0.0), writes=[negm])
            ctx.op("vector", lambda e: e.max(out=m8[:, 0:8], in_=I1[:, 0:nj]), reads=[I1], writes=[m8])
            ctx.op("vector", lambda e: e.match_replace(out=I2[:, 0:nj], in_to_replace=m8[:, 0:8], in_values=I1[:, 0:nj],
                                                       imm_value=-BIG), reads=[I1, m8], writes=[I2])
            ctx.op("vector", lambda e: e.max(out=m8[:, 8:16], in_=I2[:, 0:nj]), reads=[I2], writes=[m8])
            ctx.op("vector", lambda e: e.tensor_scalar(out=negm[:, 64:64 + nj], in0=I1[:, 0:nj], scalar1=m8[:, 15:16], scalar2=-1.0,
                                                       op0=ALU.is_ge, op1=ALU.add), reads=[I1, m8], writes=[negm])

    def phase1b(qb):
        if qb < 2:
            return
        for qsl in range(4):
            negm = negms[qsl]
            for c in range(qb // 8 + 1):
                ctx.op("tensor", lambda e: e.transpose(out=GB[:, 0:128], in_=negm[:, 64 * c:64 * c + 128], identity=identf[:]),
                       reads=[negm, identf], writes=[GB])
                for h in range(2):
                    t_ = Qms[qb % 2][h][c]
                    ctx.op("vector", lambda e: e.tensor_copy(out=t_[64:128, qsl * 128:(qsl + 1) * 128], in_=GB[64:128, 0:128]),
                           reads=[GB], writes=[t_])

    def phase2(qb, todo):
        T0 = qb * 512
        wK, wV, gt = wKs[qb % 2], wVs[qb % 2], gts[qb % 2]
        use_sel = qb >= 2
        for hl in range(2):
            r0 = hl * 64
            Qm = Qms[qb % 2][hl][0]
            for m in range(qb // 4 + 1):
                dl = qb - 4 * m
                masks = [(at.ident, at.ident[:], mcmp, mcmp[:, dl, :], slice(0, 512))] if dl <= 4 else []
                todo.append(lambda Qm=Qm, m=m, masks=masks: at.step(128, 0, 512, [(kcT, kcT[:, m * 128:(m + 1) * 128], Qm, Qm[:, 0:512], slice(0, 512))], masks,
                        [(vc1, vc1[:, m, :], slice(0, 512), accC, accC[:, :], m == 0)]))
            for kt in range(8):
                c0 = max(0, (kt - 4) * 128)
                c1 = min(512, 128 * kt + 128)
                todo.append(lambda Qm=Qm, kt=kt, c0=c0, c1=c1: at.step(128, c0, c1, [(wK, wK[:, kt * 128:(kt + 1) * 128], Qm, Qm[:, c0:c1], slice(c0, c1))],
                        [(at.ident, at.ident[:], mwin, mwin[:, kt, c0:c1], slice(c0, c1))],
                        [(wV, wV[:, kt, :], slice(c0, c1), accW, accW[:, c0:c1], kt == 0)]))
            for kt in range(4 * qb + 4):
                d = kt - 4 * qb
                c0 = d * 128 if d > 0 else 0
                masks = []
                Qm = Qms[qb % 2][hl][kt // 32]
                if d >= 0:
                    masks.append((at.ident, at.ident[:], mc, mc[:, d, c0:512], slice(c0, 512)))
                todo.append(lambda Qm=Qm, kt=kt, c0=c0, masks=masks: at.step(128, c0, 512, [(bigA, bigA[:, kt * 128:(kt + 1) * 128], Qm, Qm[:, c0:512], slice(c0, 512))], masks,
                        [(bigB, bigB3[:, kt, :], slice(c0, 512), accS, accS[:, c0:512], kt == 0)]))
            todo.append(lambda hl=hl: epilogue(qb, hl))

    def epilogue(qb, hl):
        T0 = qb * 512
        gt = gts[qb % 2]
        for br, acc in enumerate((accC, accS, accW)):
            OL, rl = at.finish(acc)
            r = hl * 3 + br
            ctx.op("tensor", lambda e: e.matmul(GB[:64, :], lhsT=selg[:, r * 64:(r + 1) * 64], rhs=gt[:, :], start=True, stop=True),
                   reads=[selg, gt], writes=[GB])
            ctx.op("vector", lambda e: e.tensor_tensor(out=fg[:], in0=GB[:64, :], in1=rl[:], op=ALU.mult),
                   reads=[GB, rl], writes=[fg])
            if br == 0:
                ctx.op("vector", lambda e: e.tensor_tensor(out=accsb[:], in0=OL[:64, :], in1=fg[:], op=ALU.mult),
                       reads=[OL, fg], writes=[accsb])
            else:
                ctx.op("vector", lambda e: e.tensor_tensor(out=tmpc[:], in0=OL[:64, :], in1=fg[:], op=ALU.mult),
                       reads=[OL, fg], writes=[tmpc])
                ctx.op("vector", lambda e: e.tensor_tensor(out=accsb[:], in0=accsb[:], in1=tmpc[:], op=ALU.add),
                       reads=[accsb, tmpc], writes=[accsb])
        O = Ob[st["no"] % 2]
        st["no"] += 1
        ctx.op("scalar", lambda e: e.activation(out=O[:], in_=accsb[:], func=AF.Copy), reads=[accsb], writes=[O])
        ctx.dma("sync", oA, oA[hl, :, T0:T0 + 512], O, O[:])

    load(0)
    phase1a(0)
    for qb in range(nqb):
        phase1b(qb)
        todo = []
        phase2(qb, todo)
        nxt = qb + 1 < nqb
        if nxt:
            load(qb + 1)
        n = len(todo)
        cuts = {(n * k) // 4: k for k in range(4)}
        for i, fn in enumerate(todo):
            if nxt and i in cuts:
                phase1a(qb + 1, (cuts[i],))
            fn()
    ctx.finish([oA])
    return nc, ctx


def dil_consts_np():
    c = attn_consts_np()
    kl = np.arange(128)[:, None]
    ql = np.arange(512)[None, :]
    d = ql - kl
    c["md1"] = np.stack([np.where((d - o >= 0) & (d - o <= 128), 0.0, NEG) for o in range(-128, 512, 128)], 1).astype(NBF)
    i4 = ql % 128
    c["md4"] = np.stack([np.where(i4 <= kl, 0.0, NEG), np.where(i4 >= kl, 0.0, NEG)], 1).astype(NBF)
    i16 = ql % 32
    c["md16"] = np.stack([np.where(i16 <= kl, 0.0, NEG), np.where(i16 >= kl, 0.0, NEG)], 1).astype(NBF)
    del c["mcausal"]
    return c


DIL_CONST_SHAPES = {"ident": ((128, 128), BF16), "sel": ((128, 64), F32), "md1": ((128, 5, 512), BF16),
                    "md4": ((128, 2, 512), BF16), "md16": ((128, 2, 512), BF16)}


def build_dil(nqb=NQB):
    nc = bass.Bass("TRN2", target_bir_lowering=False)
    ctx = Ctx(nc)
    cm = Common(ctx, norm=False)
    pb = cm.psum
    din = lambda n, s, dt=BF16: ctx.dram(n, s, dt, "ExternalInput")
    qd = din("qd", (128, S))
    kb1T = din("kb1T", (128, 128 + S))
    vb1 = din("vb1", (2, 128, 129, 128))
    kb4T = din("kb4T", (128, 4, 128 + 4096))
    vb4 = din("vb4", (2, 128, 4, 33, 128))
    kb16T = din("kb16T", (128, 16, 128 + 1024))
    vb16 = din("vb16", (2, 16, 1152, 128))
    cd = {k: din(k, s, dt) for k, (s, dt) in DIL_CONST_SHAPES.items()}
    oB = ctx.dram("oB", (2, 64, S), BF16, "ExternalOutput")
    at = Attn(ctx, cm, 0.125, cd, ns3=True)
    ms = {}
    for name in ("md1", "md4", "md16"):
        s, dt = DIL_CONST_SHAPES[name]
        ms[name] = ctx.sb("c_" + name, s, dt)
        ctx.dma("sync", ms[name], ms[name][:], cd[name], cd[name][:])
    md1, md4, md16 = ms["md1"], ms["md4"], ms["md16"]
    Qs = [[ctx.sb(f"Qd{i}_{h}", (128, 512), BF16) for h in range(2)] for i in range(2)]
    for i in range(2):
        for h in range(2):
            ctx.op("gpsimd", lambda e: e.memset(Qs[i][h][:], 0.0), writes=[Qs[i][h]])
    K1 = [ctx.sb(f"K1_{i}", (128, 640), BF16) for i in range(2)]
    V1 = [[ctx.sb(f"V1_{i}_{h}", (128, 5, 128), BF16) for h in range(2)] for i in range(2)]
    K4 = [ctx.sb(f"K4_{i}", (128, 4, 256), BF16) for i in range(2)]
    V4 = [[ctx.sb(f"V4_{i}_{h}", (128, 4, 2, 128), BF16) for h in range(2)] for i in range(2)]
    K16 = [ctx.sb(f"K16_{i}", (128, 16, 160), BF16) for i in range(2)]
    V16A = [[ctx.sb(f"V16A_{i}_{h}", (128, 16, 128), BF16) for h in range(2)] for i in range(2)]
    V16B = [[ctx.sb(f"V16B_{i}_{h}", (32, 16, 128), BF16) for h in range(2)] for i in range(2)]
    Ob = [ctx.sb(f"Ob{i}", (64, 512), BF16) for i in range(2)]
    accs = [pb[2], pb[3]]
    n = 0
    for qb in range(nqb):
        T0 = qb * 512
        i = qb % 2
        k1, k4, k16 = K1[i], K4[i], K16[i]
        for h in range(2):
            ctx.dma("sync", Qs[i][h], Qs[i][h][h * 64:(h + 1) * 64, :], qd, qd[h * 64:(h + 1) * 64, T0:T0 + 512])
        ctx.dma("sync", k1, k1[:], kb1T, kb1T[:, T0:T0 + 640])
        ctx.dma("sync", k4, k4[:], kb4T, kb4T[:, :, 128 * qb:128 * qb + 256])
        ctx.dma("sync", k16, k16[:], kb16T, kb16T[:, :, 32 * qb:32 * qb + 160])
        for h in range(2):
            ctx.dma("sync", V1[i][h], V1[i][h][:], vb1, vb1[h, :, 4 * qb:4 * qb + 5, :])
            ctx.dma("sync", V4[i][h], V4[i][h][:], vb4, vb4[h, :, :, qb:qb + 2, :])
            ctx.dma("gpsimd", V16A[i][h], V16A[i][h][:], vb16,
                    vb16[h, :, 32 * qb:32 * qb + 128, :].rearrange("r p c -> p r c"))
            ctx.dma("gpsimd", V16B[i][h], V16B[i][h][:], vb16,
                    vb16[h, :, 32 * qb + 128:32 * qb + 160, :].rearrange("r p c -> p r c"))
        for h in range(2):
            r0 = h * 64
            Q = Qs[i][h]
            A = accs[n % 2]
            O = Ob[n % 2]
            n += 1
            v1, v4, va, vb = V1[i][h], V4[i][h], V16A[i][h], V16B[i][h]
            for kt in range(5):
                o = -128 + 128 * kt
                c0, c1 = max(0, o), min(512, 128 * kt + 128)
                at.step(128, c0, c1, [(k1, k1[:, kt * 128:(kt + 1) * 128], Q, Q[:, c0:c1], slice(c0, c1))],
                        [(at.ident, at.ident[:], md1, md1[:, kt, c0:c1], slice(c0, c1))],
                        [(v1, v1[:, kt, :], slice(c0, c1), A, A[:, c0:c1], kt == 0)])
            for kt in range(2):
                mains = [(k4, k4[:, r, kt * 128:(kt + 1) * 128], Q, Q[:, r:512:4], slice(r * 128, (r + 1) * 128))
                         for r in range(4)]
                pvs = [(v4, v4[:, r, kt, :], slice(r * 128, (r + 1) * 128), A, A[:, r:512:4], False) for r in range(4)]
                at.step(128, 0, 512, mains, [(at.ident, at.ident[:], md4, md4[:, kt, :], slice(0, 512))], pvs)
            mains = [(k16, k16[:, r, 0:128], Q, Q[:, r:512:16], slice(r * 32, (r + 1) * 32)) for r in range(16)]
            pvs = [(va, va[:, r, :], slice(r * 32, (r + 1) * 32), A, A[:, r:512:16], False) for r in range(16)]
            at.step(128, 0, 512, mains, [(at.ident, at.ident[:], md16, md16[:, 0, :], slice(0, 512))], pvs)
            mains = [(k16, k16[:, r, 128:160], Q, Q[:, r:512:16], slice(r * 32, (r + 1) * 32)) for r in range(16)]
            pvs = [(vb, vb[:, r, :], slice(r * 32, (r + 1) * 32), A, A[:, r:512:16], False) for r in range(16)]
            at.step(32, 0, 512, mains, [(at.ident, at.ident[0:32, 0:32], md16, md16[0:32, 1, :], slice(0, 512))], pvs)
            OL, rl = at.finish(A)
            ctx.op("vector", lambda e, OL=OL, rl=rl, O=O: e.tensor_tensor(out=O[:], in0=OL[:64, :], in1=rl[:], op=ALU.mult),
                   reads=[OL, rl], writes=[O])
            ctx.dma("sync", oB, oB[h, :, T0:T0 + 512], O, O[:])
    ctx.finish([oB])
    return nc, ctx


NCORE = 8
DBG = {}


def _run(nc, in_maps):
    res = run_bass_kernel_spmd(nc, in_maps, core_ids=list(range(NCORE)))
    et = getattr(res, "exec_time_ns", None)
    if et is not None:
        print(f"[launch] exec_time_ns={et}", flush=True)
    return res.results


def _tokT(a):
    R = a.shape[1]
    return np.ascontiguousarray(a.T.reshape(R // 128, 128, a.shape[0]).transpose(1, 0, 2))


def _v1_tiles(vT, pad_rows=0):
    L = vT.shape[1]
    a = np.zeros((pad_rows + L, 128), NBF)
    a[pad_rows:, :64] = vT.T
    a[pad_rows:, 64:] = 1
    nt = (pad_rows + L) // 128
    return np.ascontiguousarray(a.reshape(nt, 128, 128).transpose(1, 0, 2))


def _padfront(a, n):
    z = np.zeros(a.shape[:-1] + (n,), a.dtype)
    return np.concatenate([z, a], axis=-1)


def _rope_tabs(pos, dims):
    inv = (np.float32(ROPE_THETA) ** (-np.arange(0, dims, 2, dtype=np.float32) / np.float32(dims))).astype(np.float32)
    ang = pos.astype(np.float32)[:, None] * inv[None, :]
    return np.cos(ang).astype(np.float32).T, np.sin(ang).astype(np.float32).T


def kernel(x, ffn1_norm, ffn1_w_gate, ffn1_w_up, ffn1_w_down, ffn2_norm, ffn2_w_gate, ffn2_w_up, ffn2_w_down, mix_norm,
           ev_w_in, ev_w_out, nsa_cmp_pos_k, nsa_cmp_pos_v, nsa_cmp_k_w1, nsa_cmp_k_w2, nsa_cmp_v_w1, nsa_cmp_v_w2,
           mla_w_in, mla_q_norm, mla_kv_norm, mla_w_uq, mla_w_ukv, mla_w_out, final_norm):
    f32 = lambda a: np.asarray(a, dtype=np.float32)
    x = f32(x)
    B = 2
    xf = x.reshape(B * S, D)
    pos_of_core = [(c % 4) * NTOK + np.arange(NTOK) for c in range(NCORE)]

    def ffn_maps(pref, g, wg, wu, wd):
        wgu, wdt = ffn_host_layout(f32(wg), f32(wu), f32(wd))
        return {pref + "_g": gain_layout(f32(g), KC), pref + "_wgu": wgu, pref + "_wd": wdt}

    W = f32(ev_w_in[0])
    perm64 = np.concatenate([np.arange(8, 16), np.arange(0, 8), np.arange(16, 64)])
    perm128 = np.concatenate([perm64, 64 + perm64])
    win = np.stack([wtile(W[:, c0:c0 + 128], KC) for c0 in EV_COLS])
    wsw = np.stack([wtile(W[:, c0 + perm128], KC) for c0 in EV_COLS])
    wgt = wtile(W[:, 1280:1304], KC)
    permT = np.zeros((128, 128), np.float32)
    permT[perm128, np.arange(128)] = 1.0
    common = dict(win=win, wsw=wsw, wgt=wgt, gmix=gain_layout(f32(mix_norm[0]), KC), permT=permT)
    common.update(ffn_maps("fa", ffn1_norm[0], ffn1_w_gate[0], ffn1_w_up[0], ffn1_w_down[0]))
    in_maps = []
    for c in range(NCORE):
        cs, sn = _rope_tabs(pos_of_core[c], 16)
        ct = np.ones((64, NTOK), np.float32)
        st = np.zeros((64, NTOK), np.float32)
        ct[0:8], ct[8:16] = cs, cs
        st[0:8], st[8:16] = -sn, sn
        m = dict(common)
        m.update(xT=_tokT(xf[c * NTOK:(c + 1) * NTOK]), ctab=np.concatenate([ct, ct]), stab=np.concatenate([st, st]))
        in_maps.append(m)
    nc, _ = build_stage("L1")
    r1 = _run(nc, in_maps)
    x1T = [r["x1T"] for r in r1]
    PJ = np.concatenate([r["pj"] for r in r1], axis=2).reshape(22, 128, B, S)
    GT = np.concatenate([r["gates"] for r in r1], axis=1).reshape(24, B, S)
    Q32 = np.concatenate([r["q32"] for r in r1], axis=2).reshape(5, 128, B, S)
    DBG.update(x1T=x1T, PJ=PJ, GT=GT)
    cn = nsa_consts_np()
    w1k = np.zeros((128, 32, 128), np.float32)
    w1k[:64] = f32(nsa_cmp_k_w1[0]).reshape(32, 64, 128).transpose(1, 0, 2)
    posk8 = np.zeros((128, 32, 8), np.float32)
    posk8[:64] = np.repeat(f32(nsa_cmp_pos_k[0]).T[:, :, None], 8, axis=2)
    w1v = np.ascontiguousarray(f32(nsa_cmp_v_w1[0]).reshape(32, 64, 128).transpose(1, 0, 2))
    w2k = f32(nsa_cmp_k_w2[0])
    XM = np.zeros((64, 128, 128), NBF)
    for kt in range(128):
        for half in range(2):
            XM[2 * (kt % 32) + half, kt, half * 64:(half + 1) * 64] = 30000.0
    XM = XM.reshape(64, S)
    in_maps = []
    for c in range(NCORE):
        b, g, pr = c // 4, (c % 4) // 2, c % 2
        gs = slice(g * 64, (g + 1) * 64)
        ks = PJ[6][gs, b]
        kw = PJ[8][gs, b]
        h0 = 4 * g + 2 * pr
        m = dict(cn)
        m.update(qg=np.ascontiguousarray(np.stack([Q32[2 * g][:, b], Q32[2 * g + 1][:, b]], axis=1)),
                 qmy=np.ascontiguousarray(PJ[2 * g + pr][:, b]),
                 kcraw=np.ascontiguousarray(Q32[4][gs, b].reshape(64, 1024, 16).transpose(0, 2, 1)), vcraw=np.ascontiguousarray(PJ[5][gs, b]),
                 w1k=w1k, w1v=w1v, posk=posk8,
                 posv=np.ascontiguousarray(f32(nsa_cmp_pos_v[0]).T),
                 w2k=np.concatenate([w2k, np.zeros_like(w2k)], axis=1), w2v=f32(nsa_cmp_v_w2[0]),
                 ksT=np.concatenate([ks, XM], axis=0), vs1=_v1_tiles(PJ[7][gs, b]),
                 kwT=_padfront(np.concatenate([kw, np.zeros_like(kw)], axis=0), 512), vw1=_v1_tiles(PJ[9][gs, b], 512),
                 gat=np.ascontiguousarray(GT[h0 * 3:h0 * 3 + 6, b]))
        in_maps.append(m)
    nc, _ = build_nsa()
    rA = _run(nc, in_maps)
    AT = np.zeros((1024, B, S), NBF)
    for c in range(NCORE):
        b, g, pr = c // 4, (c % 4) // 2, c % 2
        h0 = 4 * g + 2 * pr
        AT[h0 * 64:(h0 + 2) * 64, b] = rA[c]["oA"].reshape(128, S)
    cdl = dil_consts_np()
    in_maps = []
    for c in range(NCORE):
        b, cc = c // 4, c % 4
        k = PJ[14 + cc][:, b]
        v = PJ[18 + cc][:, b]
        m = dict(cdl)
        vh = [v[0:64], v[64:128]]
        m.update(qd=np.ascontiguousarray(PJ[10 + cc][:, b]), kb1T=_padfront(k, 128),
                 vb1=np.stack([_v1_tiles(vv, 128) for vv in vh]),
                 kb4T=np.ascontiguousarray(np.stack([_padfront(k[:, r::4], 128) for r in range(4)], axis=1)),
                 vb4=np.stack([np.stack([_v1_tiles(vv[:, r::4], 128) for r in range(4)], axis=1) for vv in vh]),
                 kb16T=np.ascontiguousarray(np.stack([_padfront(k[:, r::16], 128) for r in range(16)], axis=1)),
                 vb16=np.stack([np.stack([_v1_tiles(vv[:, r::16], 128).transpose(1, 0, 2).reshape(1152, 128)
                                          for r in range(16)]) for vv in vh]))
        in_maps.append(m)
    nc, _ = build_dil()
    rB = _run(nc, in_maps)
    for c in range(NCORE):
        b, cc = c // 4, c % 4
        AT[512 + cc * 128:512 + (cc + 1) * 128, b] = rB[c]["oB"].reshape(128, S)
    DBG.update(AT=AT)
    ATf = AT.reshape(1024, B * S)
    Wo = f32(ev_w_out[0])
    Wm = f32(mla_w_in[0])
    Wq = f32(mla_w_uq[0])
    Wkv = f32(mla_w_ukv[0])
    permq = np.concatenate([np.arange(64), np.arange(80, 96), np.arange(64, 80)])
    permk = np.concatenate([np.arange(16, 32), np.arange(0, 16)])
    common = dict(wo=np.stack([wtile(Wo[:, c0:c0 + 128], KC) for c0 in range(0, 1024, 128)]),
                  gmix=gain_layout(f32(mix_norm[1]), KC), gq=gain_layout(f32(mla_q_norm[0]), 2),
                  gkv=gain_layout(f32(mla_kv_norm[0]), 1),
                  wmi=np.stack([wtile(Wm[:, i * 128:(i + 1) * 128], KC) for i in range(3)]),
                  wkr=np.stack([wtile(Wm[:, 384:416], KC), wtile(Wm[:, 384 + permk], KC)]),
                  wuq=np.stack([wtile(Wq[:, h * 96:(h + 1) * 96], 2) for h in range(16)]),
                  wuqs=np.stack([wtile(Wq[:, h * 96 + permq], 2) for h in range(16)]),
                  wukv=np.stack([wtile(Wkv[:, h * 128:(h + 1) * 128], 1) for h in range(16)]))
    common.update(ffn_maps("fa", ffn2_norm[0], ffn2_w_gate[0], ffn2_w_up[0], ffn2_w_down[0]))
    common.update(ffn_maps("fb", ffn1_norm[1], ffn1_w_gate[1], ffn1_w_up[1], ffn1_w_down[1]))
    in_maps = []
    for c in range(NCORE):
        cs, sn = _rope_tabs(pos_of_core[c], 32)
        cq = np.ones((96, NTOK), np.float32)
        sq = np.zeros((96, NTOK), np.float32)
        cq[64:80], cq[80:96] = cs, cs
        sq[64:80], sq[80:96] = -sn, sn
        m = dict(common)
        m.update(xT=x1T[c], aT=_tokT(ATf[:, c * NTOK:(c + 1) * NTOK].T), cq_t=cq, sq_t=sq,
                 ck_t=np.concatenate([cs, cs]), sk_t=np.concatenate([-sn, sn]))
        in_maps.append(m)
    nc, _ = build_stage("L3")
    r3 = _run(nc, in_maps)
    x3T = [r["x3T"] for r in r3]
    QT = np.concatenate([r["qT"] for r in r3], axis=2).reshape(16, 96, B, S)
    KV = np.concatenate([r["kvT"] for r in r3], axis=2).reshape(16, 128, B, S)
    KR = np.concatenate([r["krT"] for r in r3], axis=1).reshape(32, B, S)
    DBG.update(x3T=x3T, QT=QT, KV=KV, KR=KR)
    ca = attn_consts_np()
    in_maps = []
    for c in range(NCORE):
        b = c // 4
        hs = [4 * (c % 4) + u for u in range(4)]
        m = dict(ca)
        m.update(qT=np.ascontiguousarray(np.stack([QT[h, :, b] for h in hs])),
                 kT=np.stack([np.concatenate([KV[h, 0:64, b], KR[:, b]], axis=0) for h in hs]),
                 v1=np.stack([_v1_tiles(KV[h, 64:128, b]) for h in hs]))
        in_maps.append(m)
    nc, _ = build_mla()
    r4 = _run(nc, in_maps)
    AT2 = np.zeros((1024, B, S), NBF)
    for c in range(NCORE):
        b = c // 4
        AT2[(c % 4) * 256:(c % 4 + 1) * 256, b] = r4[c]["oT"].reshape(256, S)
    DBG.update(AT2=AT2)
    AT2f = AT2.reshape(1024, B * S)
    Wo2 = f32(mla_w_out[0])
    common = dict(wo=np.stack([wtile(Wo2[:, c0:c0 + 128], KC) for c0 in range(0, 1024, 128)]),
                  gfin=gain_layout(f32(final_norm), KC))
    common.update(ffn_maps("fa", ffn2_norm[1], ffn2_w_gate[1], ffn2_w_up[1], ffn2_w_down[1]))
    in_maps = []
    for c in range(NCORE):
        m = dict(common)
        m.update(xT=x3T[c], aT=_tokT(AT2f[:, c * NTOK:(c + 1) * NTOK].T))
        in_maps.append(m)
    nc, _ = build_stage("L5")
    r5 = _run(nc, in_maps)
    out = np.zeros((B * S, D), np.float32)
    for c in range(NCORE):
        out[c * NTOK:(c + 1) * NTOK] = r5[c]["yT"].transpose(1, 0, 2).reshape(D, NTOK).T
    return out.reshape(B, S, D)
```

```python
import numpy as np
import concourse.bass as bass
import concourse.mybir as mybir
from concourse.bass_utils import run_bass_kernel_spmd

F32 = mybir.dt.float32
BF16 = mybir.dt.bfloat16
AF = mybir.ActivationFunctionType
ALU = mybir.AluOpType
AX = mybir.AxisListType


class SemObj:
    def __init__(self, nc, name):
        self.sem = nc.alloc_semaphore(name)
        self.name = name
        self.val = 0


class EngState:
    def __init__(self, nc, eng, name):
        self.e = eng
        self.name = name
        self.so = SemObj(nc, "sE_" + name)
        self.waited = {}


class Tile:
    def __init__(self, ctx, ap, name, dma_target=False):
        self.ap = ap
        self.name = name
        self.w = None
        self.r = {}
        self.dso = None
        self.ctx = ctx

    def dsem(self):
        if self.dso is None:
            self.dso = SemObj(self.ctx.nc, "sD_" + self.name)
        return self.dso

    def __getitem__(self, idx):
        return self.ap[idx]


class Ctx:
    def __init__(self, nc):
        self.nc = nc
        self.E = {n: EngState(nc, getattr(nc, n), n) for n in ["tensor", "vector", "scalar", "gpsimd", "sync"]}
        self.ntile = 0
        self.ninst = 0

    def sb(self, name, shape, dtype):
        self.ntile += 1
        return Tile(self, self.nc.alloc_sbuf_tensor(name, list(shape), dtype).ap(), name)

    def ps(self, name, shape=(128, 512), dtype=F32):
        self.ntile += 1
        return Tile(self, self.nc.alloc_psum_tensor(name, list(shape), dtype).ap(), name)

    def dram(self, name, shape, dtype, kind):
        t = Tile(self, self.nc.dram_tensor(name, list(shape), dtype, kind=kind).ap(), name)
        t.shape = tuple(shape)
        return t

    def _deps(self, E, reads, writes, waw=True):
        needs = {}

        def need(dep):
            if dep is None:
                return
            so, v = dep
            if needs.get(so, 0) < v:
                needs[so] = v

        for t in reads:
            need(t.w)
        for t in writes:
            if waw:
                need(t.w)
            for d in t.r.values():
                need(d)
        for so, v in needs.items():
            if so is E.so and E.name == "tensor":
                continue
            if E.waited.get(so, 0) >= v:
                continue
            E.e.wait_ge(so.sem, v)
            E.waited[so] = v

    def op(self, eng, fn, reads=(), writes=()):
        E = self.E[eng]
        self._deps(E, reads, writes)
        ins = fn(E.e)
        E.so.val += 1
        ins.then_inc(E.so.sem, 1)
        me = (E.so, E.so.val)
        for t in reads:
            t.r[E.so] = me
        for t in writes:
            t.w = me
            t.r = {}
        self.ninst += 1
        return ins

    def dma(self, eng, out_t, out_ap, in_t, in_ap, waw=True, **kw):
        E = self.E[eng]
        self._deps(E, [in_t], [out_t], waw=waw)
        so = out_t.dsem()
        ins = E.e.dma_start(out=out_ap, in_=in_ap, **kw)
        so.val += 16
        ins.then_inc(so.sem, 16)
        me = (so, so.val)
        in_t.r[so] = me
        out_t.w = me
        out_t.r = {}
        self.ninst += 1
        return ins

    def finish(self, out_tiles):
        E = self.E["sync"]
        for t in out_tiles:
            if t.w is not None:
                so, v = t.w
                E.e.wait_ge(so.sem, v)


NORM_EPS = 1e-6
D = 1024
KC = 8
FF = 2816
FC = 22
TT = 512


class Common:
    def __init__(self, ctx, norm=True):
        self.ctx = ctx
        self.psum = [ctx.ps(f"ps{i}") for i in range(8)]
        if norm:
            self.init_eps()
            self.ones = ctx.sb("ones_f32", (128, 128), F32)
            ctx.op("vector", lambda e: e.memset(self.ones[:], 1.0), writes=[self.ones])
            self.sq = [ctx.sb(f"sq{i}", (128, TT), F32) for i in range(2)]
            self.rstd = ctx.sb("rstd", (128, TT), F32)
        self.rr = 0

    def rmsnorm_T(self, xt, nk, gam, outT, n, pbank, width, out2=None):
        ctx = self.ctx
        ps = pbank
        for kc in range(nk):
            sq = self.sq[self.rr % 2]
            self.rr += 1
            ctx.op("scalar", lambda e, kc=kc, sq=sq: e.activation(out=sq[:, :n], in_=xt[:, kc, :n], func=AF.Square),
                   reads=[xt], writes=[sq])
            ctx.op("tensor", lambda e, kc=kc, sq=sq: e.matmul(ps[:, :n], lhsT=self.ones[:], rhs=sq[:, :n],
                                                             start=(kc == 0), stop=(kc == nk - 1)),
                   reads=[self.ones, sq], writes=[ps])
        rstd = self.rstd
        ctx.op("scalar", lambda e: e.activation(out=rstd[:, :n], in_=ps[:, :n], func=AF.Sqrt,
                                                 bias=self.eps_t(), scale=1.0 / width),
               reads=[ps, self.eps_tile], writes=[rstd])
        ctx.op("vector", lambda e: e.reciprocal(out=rstd[:, :n], in_=rstd[:, :n]), reads=[rstd], writes=[rstd])
        for kc in range(nk):
            ctx.op("vector", lambda e, kc=kc: e.scalar_tensor_tensor(
                out=outT[:, kc, :n], in0=xt[:, kc, :n], scalar=gam[:, kc:kc + 1], in1=rstd[:, :n],
                op0=ALU.mult, op1=ALU.mult), reads=[xt, gam, rstd], writes=[outT])
            if out2 is not None:
                ctx.op("vector", lambda e, kc=kc: e.scalar_tensor_tensor(
                    out=out2[:, kc, :n], in0=xt[:, kc, :n], scalar=gam[:, kc:kc + 1], in1=rstd[:, :n],
                    op0=ALU.mult, op1=ALU.mult), reads=[xt, gam, rstd], writes=[out2])

    def eps_t(self):
        return self.eps_tile[:, 0:1]

    def init_eps(self):
        ctx = self.ctx
        self.eps_tile = ctx.sb("eps", (128, 1), F32)
        ctx.op("vector", lambda e: e.memset(self.eps_tile[:], NORM_EPS), writes=[self.eps_tile])


class FFN:
    def __init__(self, ctx, cm):
        self.ctx = ctx
        self.cm = cm
        self.hT = [ctx.sb(f"ffn_hT{i}", (128, KC, TT), BF16) for i in range(2)]
        self.wgu = [ctx.sb(f"ffn_wgu{i}", (128, 2, KC, 128), BF16) for i in range(3)]
        self.wd = [ctx.sb(f"ffn_wd{i}", (128, FC, 128), BF16) for i in range(2)]
        self.act = [ctx.sb(f"ffn_act{j}", (128, TT), BF16) for j in range(FC)]
        self.sg = [ctx.sb(f"ffn_sg{i}", (128, TT), F32) for i in range(2)]
        self.n = 0
        self.nw = 0
        self.nd = 0

    def run(self, xt, gam, wgu_d, wd_d, pb, sc=None, first=True):
        ctx, cm = self.ctx, self.cm
        hT = self.hT[self.n % 2]
        self.n += 1
        cm.rmsnorm_T(xt, KC, gam, hT, TT, pb[0], D)
        for j in range(FC):
            w = self.wgu[self.nw % 3]
            self.nw += 1
            if sc is None or first:
                ctx.dma("gpsimd", w, w[:], wgu_d, wgu_d[j], max_dma_last_dim=4096)
                if sc is not None:
                    ctx.dma("sync", sc[0], sc[0][j], w, w[:], waw=False, max_dma_last_dim=4096)
            else:
                ctx.dma("gpsimd", w, w[:], sc[0], sc[0][j], max_dma_last_dim=4096)
            pg = pb[1 + (j % 2)]
            pu = pb[3 + (j % 2)]
            for kc in range(KC):
                ctx.op("tensor", lambda e, kc=kc, w=w, pg=pg: e.matmul(pg[:], lhsT=w[:, 0, kc, :], rhs=hT[:, kc, :],
                                                                     start=(kc == 0), stop=(kc == KC - 1)),
                       reads=[w, hT], writes=[pg])
            for kc in range(KC):
                ctx.op("tensor", lambda e, kc=kc, w=w, pu=pu: e.matmul(pu[:], lhsT=w[:, 1, kc, :], rhs=hT[:, kc, :],
                                                                     start=(kc == 0), stop=(kc == KC - 1)),
                       reads=[w, hT], writes=[pu])
            sg = self.sg[j % 2]
            ctx.op("scalar", lambda e, sg=sg, pg=pg: e.activation(out=sg[:], in_=pg[:], func=AF.Silu),
                   reads=[pg], writes=[sg])
            a = self.act[j]
            ctx.op("vector", lambda e, sg=sg, pu=pu, a=a: e.tensor_tensor(out=a[:], in0=pu[:], in1=sg[:], op=ALU.mult),
                   reads=[pu, sg], writes=[a])
        for c in range(KC):
            w = self.wd[self.nd % 2]
            self.nd += 1
            if sc is None or first:
                ctx.dma("gpsimd", w, w[:], wd_d, wd_d[c], max_dma_last_dim=4096)
                if sc is not None:
                    ctx.dma("sync", sc[1], sc[1][c], w, w[:], waw=False, max_dma_last_dim=4096)
            else:
                ctx.dma("gpsimd", w, w[:], sc[1], sc[1][c], max_dma_last_dim=4096)
            po = pb[5 + (c % 2)]
            for j in range(FC):
                ctx.op("tensor", lambda e, j=j, w=w, po=po: e.matmul(po[:], lhsT=w[:, j, :], rhs=self.act[j][:],
                                                                     start=(j == 0), stop=(j == FC - 1)),
                       reads=[w, self.act[j]], writes=[po])
            ctx.op("vector", lambda e, c=c, po=po: e.scalar_tensor_tensor(
                out=xt[:, c, :], in0=po[:], scalar=0.5, in1=xt[:, c, :], op0=ALU.mult, op1=ALU.add),
                reads=[po, xt], writes=[xt])


def ffn_host_layout(wg, wu, wd):
    g = wg.reshape(KC, 128, FC, 128).transpose(2, 1, 0, 3)
    u = wu.reshape(KC, 128, FC, 128).transpose(2, 1, 0, 3)
    wgu = np.ascontiguousarray(np.stack([g, u], axis=2))
    wdt = np.ascontiguousarray(wd.reshape(FC, 128, KC, 128).transpose(2, 1, 0, 3))
    return wgu, wdt


def gain_layout(g, nk):
    return np.ascontiguousarray(g.reshape(nk, 128).T)

import ml_dtypes

NBF = ml_dtypes.bfloat16
NTOK = 4096
NSLOT = 8
S = 16384
ROPE_THETA = 500000.0


def wtile(W, nk):
    return np.ascontiguousarray(W.reshape(nk, 128, -1).transpose(1, 0, 2))


class Proj:
    def __init__(self, ctx, nslots=3):
        self.ctx = ctx
        self.w = [ctx.sb(f"pw{i}", (128, KC, 128), BF16) for i in range(nslots)]
        self.n = 0
        self.sc = {}
        self.first = True

    def mm(self, w_d, idx, nk, M, rhsT, ps, n=TT):
        ctx = self.ctx
        w = self.w[self.n % len(self.w)]
        self.n += 1
        sc = self.sc.get(w_d.name)
        if sc is None:
            sc = self.sc[w_d.name] = ctx.dram("sc_" + w_d.name, w_d.shape, BF16, "Internal")
        if self.first:
            ctx.dma("gpsimd", w, w[:, :nk, :M], w_d, w_d[idx])
            ctx.dma("sync", sc, sc[idx], w, w[:, :nk, :M], waw=False)
        else:
            ctx.dma("gpsimd", w, w[:, :nk, :M], sc, sc[idx])
        for kc in range(nk):
            ctx.op("tensor", lambda e, kc=kc: e.matmul(ps[:M, :n], lhsT=w[:, kc, :M], rhs=rhsT[:, kc, :n],
                                                       start=(kc == 0), stop=(kc == nk - 1)),
                   reads=[w, rhsT], writes=[ps])


def rope_combine(ctx, out_t, out_ap, p1, p2, ct, c_ap, st, s_ap, tmp, M, n=TT):
    t1, t2 = tmp
    ctx.op("vector", lambda e: e.tensor_tensor(out=t1[:M, :n], in0=p1[:M, :n], in1=c_ap, op=ALU.mult),
           reads=[p1, ct], writes=[t1])
    ctx.op("vector", lambda e: e.tensor_tensor(out=t2[:M, :n], in0=p2[:M, :n], in1=s_ap, op=ALU.mult),
           reads=[p2, st], writes=[t2])
    ctx.op("vector", lambda e: e.tensor_tensor(out=out_ap, in0=t1[:M, :n], in1=t2[:M, :n], op=ALU.add),
           reads=[t1, t2], writes=[out_t])


EV_ROPE = [True] * 4 + [True, False, True, False, True, False] + [True] * 4 + [True] * 4 + [False] * 4
EV_COLS = list(range(0, 1280, 128)) + list(range(1304, 2840, 128))


def build_stage(kind):
    nc = bass.Bass("TRN2", target_bir_lowering=False)
    ctx = Ctx(nc)
    cm = Common(ctx)
    ffn = FFN(ctx, cm)
    pj = Proj(ctx)
    pb = cm.psum
    D_ = {}

    def din(name, shape, dt=F32):
        D_[name] = ctx.dram(name, shape, dt, "ExternalInput")
        return D_[name]

    def dout(name, shape, dt=F32):
        D_[name] = ctx.dram(name, shape, dt, "ExternalOutput")
        return D_[name]

    xT = din("xT", (128, KC, NTOK))
    outs = []
    gams = {}

    def load_gam(name, nk=KC):
        d = din(name, (128, nk))
        t = ctx.sb("sb_" + name, (128, nk), F32)
        ctx.dma("sync", t, t[:], d, d[:])
        gams[name] = t
        return t

    def ffn_in(pref):
        return (load_gam(pref + "_g"), din(pref + "_wgu", (FC, 128, 2, KC, 128)), din(pref + "_wd", (KC, 128, FC, 128)),
                (ctx.dram("sc_" + pref + "_wgu", (FC, 128, 2, KC, 128), BF16, "Internal"),
                 ctx.dram("sc_" + pref + "_wd", (KC, 128, FC, 128), BF16, "Internal")))

    xts = [ctx.sb(f"xt{i}", (128, KC, TT), F32) for i in range(2)]
    tmp = [ctx.sb(f"tmp{i}", (128, TT), F32) for i in range(2)]
    hTs = [ctx.sb(f"hmix{i}", (128, KC, TT), BF16) for i in range(2)]
    if kind in ("L3", "L5"):
        aT = din("aT", (128, KC, NTOK), BF16)
        wo = din("wo", (KC, 128, KC, 128))
        ats = [ctx.sb(f"at{i}", (128, KC, TT), BF16) for i in range(2)]
    if kind == "L1":
        fa = ffn_in("fa")
        gm = load_gam("gmix")
        win = din("win", (22, 128, KC, 128))
        wgt = din("wgt", (128, KC, 24))
        ctab = din("ctab", (128, NTOK))
        stab = din("stab", (128, NTOK))
        x1T = dout("x1T", (128, KC, NTOK))
        pjo = dout("pj", (22, 128, NTOK), BF16)
        gto = dout("gates", (24, NTOK))
        outs = [x1T, pjo, gto]
        cts = [ctx.sb(f"ct{i}", (128, TT), F32) for i in range(2)]
        sts = [ctx.sb(f"st{i}", (128, TT), F32) for i in range(2)]
        obs = [ctx.sb(f"ob{i}", (128, TT), BF16) for i in range(3)]
        gos = [ctx.sb(f"go{i}", (24, TT), F32) for i in range(2)]
        hT32 = ctx.sb("hT32", (128, KC, TT), F32)
        w32 = [ctx.sb(f"w32_{i}", (128, KC, 128), F32) for i in range(2)]
        ob32s = [ctx.sb(f"ob32_{i}", (128, TT), F32) for i in range(2)]
        q32o = dout("q32", (5, 128, NTOK), F32)
        outs.append(q32o)
        n32 = [0]
        permd = din("permT", (128, 128))
        permT = ctx.sb("sb_permT", (128, 128), F32)
        ctx.dma("sync", permT, permT[:], permd, permd[:])
        p1sb = [ctx.sb(f"p1sb{i}", (128, TT), F32) for i in range(2)]

        def mm32(w_d, w_ap, ps):
            w = w32[n32[0] % 2]
            n32[0] += 1
            ctx.dma("sync", w, w[:], w_d, w_ap)
            for kc in range(KC):
                ctx.op("tensor", lambda e, kc=kc: e.matmul(ps[:], lhsT=w[:, kc, :], rhs=hT32[:, kc, :],
                                                           start=(kc == 0), stop=(kc == KC - 1)),
                       reads=[w, hT32], writes=[ps])
    if kind == "L3":
        fa = ffn_in("fa")
        fb = ffn_in("fb")
        gm = load_gam("gmix")
        gq = load_gam("gq", 2)
        gkv = load_gam("gkv", 1)
        wmi = din("wmi", (3, 128, KC, 128))
        wkr = din("wkr", (2, 128, KC, 32))
        wuq = din("wuq", (16, 128, 2, 96))
        wuqs = din("wuqs", (16, 128, 2, 96))
        wukv = din("wukv", (16, 128, 1, 128))
        cq_t = din("cq_t", (96, NTOK))
        sq_t = din("sq_t", (96, NTOK))
        ck_t = din("ck_t", (32, NTOK))
        sk_t = din("sk_t", (32, NTOK))
        x3T = dout("x3T", (128, KC, NTOK))
        qTo = dout("qT", (16, 96, NTOK), BF16)
        kvo = dout("kvT", (16, 128, NTOK), BF16)
        kro = dout("krT", (32, NTOK), BF16)
        outs = [x3T, qTo, kvo, kro]
        cts = [ctx.sb(f"ct{i}", (96, TT), F32) for i in range(2)]
        sts = [ctx.sb(f"st{i}", (96, TT), F32) for i in range(2)]
        ckts = [ctx.sb(f"ckt{i}", (32, TT), F32) for i in range(2)]
        skts = [ctx.sb(f"skt{i}", (32, TT), F32) for i in range(2)]
        obs = [ctx.sb(f"ob{i}", (128, TT), BF16) for i in range(3)]
        cqT = [ctx.sb(f"cqT{i}", (128, 2, TT), F32) for i in range(2)]
        ckvT = [ctx.sb(f"ckvT{i}", (128, 1, TT), F32) for i in range(2)]
        cqn = [ctx.sb(f"cqn{i}", (128, 2, TT), BF16) for i in range(2)]
        ckvn = [ctx.sb(f"ckvn{i}", (128, 1, TT), BF16) for i in range(2)]
    if kind == "L5":
        fa = ffn_in("fa")
        gf = load_gam("gfin")
        yT = dout("yT", (128, KC, NTOK))
        outs = [yT]
        yts = [ctx.sb(f"yt{i}", (128, KC, TT), F32) for i in range(2)]

    nob = 0
    for t in range(NSLOT):
        ts = slice(t * TT, (t + 1) * TT)
        xt = xts[t % 2]
        pj.first = (t == 0)
        ctx.dma("sync", xt, xt[:], xT, xT[:, :, ts])
        if kind in ("L3", "L5"):
            at = ats[t % 2]
            ctx.dma("sync", at, at[:], aT, aT[:, :, ts])
            for c in range(KC):
                ps = pb[5 + (c % 2)]
                pj.mm(wo, c, KC, 128, at, ps)
                ctx.op("vector", lambda e, c=c, ps=ps: e.tensor_tensor(out=xt[:, c, :], in0=ps[:], in1=xt[:, c, :], op=ALU.add),
                       reads=[ps, xt], writes=[xt])
        if kind == "L1":
            ffn.run(xt, fa[0], fa[1], fa[2], pb[0:7], sc=fa[3], first=(t == 0))
            ctx.dma("sync", x1T, x1T[:, :, ts], xt, xt[:])
            hT = hTs[t % 2]
            cm.rmsnorm_T(xt, KC, gm, hT, TT, pb[0], D, out2=hT32)
            ct, st = cts[t % 2], sts[t % 2]
            ctx.dma("sync", ct, ct[:], ctab, ctab[:, ts])
            ctx.dma("sync", st, st[:], stab, stab[:, ts])
            deferred = []
            for c in range(22):
                p1 = pb[1 + (c % 2)]
                ob = obs[nob % 3]
                nob += 1
                is32 = c < 5
                if is32:
                    mm32(win, win[c], p1)
                else:
                    pj.mm(win, c, KC, 128, hT, p1)
                for fn in deferred:
                    fn()
                deferred = []
                if not EV_ROPE[c]:
                    ctx.op("scalar", lambda e, p1=p1, ob=ob: e.activation(out=ob[:], in_=p1[:], func=AF.Copy),
                           reads=[p1], writes=[ob])
                    ctx.dma("sync", pjo, pjo[c, :, ts], ob, ob[:])
                    continue
                p1s = p1sb[c % 2]
                ctx.op("scalar", lambda e, p1=p1, p1s=p1s: e.activation(out=p1s[:], in_=p1[:], func=AF.Copy),
                       reads=[p1], writes=[p1s])

                def fin(c=c, p1s=p1s, ob=ob, is32=is32):
                    p2 = pb[3 + (c % 2)]
                    ctx.op("tensor", lambda e: e.matmul(p2[:], lhsT=permT[:], rhs=p1s[:], start=True, stop=True),
                           reads=[permT, p1s], writes=[p2])
                    if is32:
                        ob32 = ob32s[c % 2]
                        rope_combine(ctx, ob32, ob32[:], p1s, p2, ct, ct[:], st, st[:], tmp, 128)
                        ctx.op("scalar", lambda e: e.activation(out=ob[:], in_=ob32[:], func=AF.Copy),
                               reads=[ob32], writes=[ob])
                        ctx.dma("sync", q32o, q32o[c, :, ts], ob32, ob32[:])
                    else:
                        rope_combine(ctx, ob, ob[:], p1s, p2, ct, ct[:], st, st[:], tmp, 128)
                    ctx.dma("sync", pjo, pjo[c, :, ts], ob, ob[:])
                deferred.append(fin)
            for fn in deferred:
                fn()
            p1 = pb[7]
            pj.mm(wgt, slice(None), KC, 24, hT, p1)
            go = gos[t % 2]
            ctx.op("scalar", lambda e, p1=p1, go=go: e.activation(out=go[:], in_=p1[:24, :], func=AF.Sigmoid),
                   reads=[p1], writes=[go])
            ctx.dma("sync", gto, gto[:, ts], go, go[:])
        if kind == "L3":
            ffn.run(xt, fa[0], fa[1], fa[2], pb[0:7], sc=fa[3], first=(t == 0))
            ffn.run(xt, fb[0], fb[1], fb[2], pb[0:7], sc=fb[3], first=(t == 0))
            ctx.dma("sync", x3T, x3T[:, :, ts], xt, xt[:])
            hT = hTs[t % 2]
            cm.rmsnorm_T(xt, KC, gm, hT, TT, pb[0], D)
            cq, ckv, cqn_, ckvn_ = cqT[t % 2], ckvT[t % 2], cqn[t % 2], ckvn[t % 2]
            for i in range(3):
                p1 = pb[1 + (i % 2)]
                pj.mm(wmi, i, KC, 128, hT, p1)
                dst_t, dst = (cq, cq[:, i, :]) if i < 2 else (ckv, ckv[:, 0, :])
                ctx.op("scalar", lambda e, p1=p1, dst=dst: e.activation(out=dst, in_=p1[:], func=AF.Copy),
                       reads=[p1], writes=[dst_t])
            ckt, skt = ckts[t % 2], skts[t % 2]
            ctx.dma("sync", ckt, ckt[:], ck_t, ck_t[:, ts])
            ctx.dma("sync", skt, skt[:], sk_t, sk_t[:, ts])
            p1, p2 = pb[3], pb[4]
            pj.mm(wkr, 0, KC, 32, hT, p1)
            pj.mm(wkr, 1, KC, 32, hT, p2)
            ob = obs[nob % 3]
            nob += 1
            rope_combine(ctx, ob, ob[:32, :], p1, p2, ckt, ckt[:], skt, skt[:], tmp, 32)
            ctx.dma("sync", kro, kro[:, ts], ob, ob[:32, :])
            cm.rmsnorm_T(cq, 2, gq, cqn_, TT, pb[0], 256)
            cm.rmsnorm_T(ckv, 1, gkv, ckvn_, TT, pb[0], 128)
            ct, st = cts[t % 2], sts[t % 2]
            ctx.dma("sync", ct, ct[:], cq_t, cq_t[:, ts])
            ctx.dma("sync", st, st[:], sq_t, sq_t[:, ts])
            for h in range(16):
                p1 = pb[1 + (h % 2)]
                p2 = pb[3 + (h % 2)]
                pj.mm(wuq, h, 2, 96, cqn_, p1)
                pj.mm(wuqs, h, 2, 96, cqn_, p2)
                ob = obs[nob % 3]
                nob += 1
                rope_combine(ctx, ob, ob[:96, :], p1, p2, ct, ct[:], st, st[:], tmp, 96)
                ctx.dma("sync", qTo, qTo[h, :, ts], ob, ob[:96, :])
            for h in range(16):
                p1 = pb[5 + (h % 2)]
                pj.mm(wukv, h, 1, 128, ckvn_, p1)
                ob = obs[nob % 3]
                nob += 1
                ctx.op("scalar", lambda e, p1=p1, ob=ob: e.activation(out=ob[:], in_=p1[:], func=AF.Copy),
                       reads=[p1], writes=[ob])
                ctx.dma("sync", kvo, kvo[h, :, ts], ob, ob[:])
        if kind == "L5":
            ffn.run(xt, fa[0], fa[1], fa[2], pb[0:7], sc=fa[3], first=(t == 0))
            yt = yts[t % 2]
            cm.rmsnorm_T(xt, KC, gf, yt, TT, pb[0], D)
            ctx.dma("sync", yT, yT[:, :, ts], yt, yt[:])
    ctx.finish(outs)
    return nc, ctx


NEG = -30000.0
BIG = 1e30
NQB = 32


class Attn:
    def __init__(self, ctx, cm, scale, consts, ns3=False, sbanks=None, lbanks=None):
        self.ctx, self.cm, self.scale = ctx, cm, scale
        self.S = sbanks if sbanks else [cm.psum[0], cm.psum[1]] + ([cm.psum[7]] if ns3 else [])
        self.lag = len(self.S) - 1
        self.pending = []
        self.LB = lbanks if lbanks else [cm.psum[5], cm.psum[6]]
        self.P = [ctx.sb(f"P{i}", (128, 512), BF16) for i in range(4)]
        self.OL = [ctx.sb(f"OL{i}", (128, 512), F32) for i in range(2)]
        self.rl = [ctx.sb(f"rl{i}", (64, 512), F32) for i in range(2)]
        self.ident = ctx.sb("sb_ident", (128, 128), BF16)
        self.sel = ctx.sb("sb_sel", (128, 64), F32)
        ctx.dma("sync", self.ident, self.ident[:], consts["ident"], consts["ident"][:])
        ctx.dma("sync", self.sel, self.sel[:], consts["sel"], consts["sel"][:])
        self.i = 0
        self.j = 0

    def step(self, nk, c0, c1, mains, masks, pvs):
        ctx = self.ctx
        S = self.S[self.i % len(self.S)]
        P = self.P[self.i % 4]
        self.i += 1
        allm = list(mains) + list(masks)
        n = len(allm)
        for k, (lt, lap, rt, rap, cs) in enumerate(allm):
            ctx.op("tensor", lambda e, lap=lap, rap=rap, cs=cs, k=k: e.matmul(
                S[:nk, cs], lhsT=lap, rhs=rap, start=(k == 0), stop=(k == n - 1), skip_group_check=True),
                reads=[lt, rt], writes=[S])
        ctx.op("scalar", lambda e: e.activation(out=P[:nk, c0:c1], in_=S[:nk, c0:c1], func=AF.Exp, scale=self.scale),
               reads=[S], writes=[P])
        self.pending.append((nk, P, pvs))
        while len(self.pending) > self.lag:
            self._flush_one()

    def _flush_one(self):
        ctx = self.ctx
        nk, P, pvs = self.pending.pop(0)
        for (vt, vap, pcs, acc, acc_ap, start) in pvs:
            ctx.op("tensor", lambda e, vap=vap, pcs=pcs, acc_ap=acc_ap, start=start: e.matmul(
                acc_ap, lhsT=vap, rhs=P[:nk, pcs], start=start, stop=True, skip_group_check=True),
                reads=[vt, P], writes=[acc])

    def flush(self):
        while self.pending:
            self._flush_one()

    def finish(self, acc):
        self.flush()
        ctx = self.ctx
        OL = self.OL[self.j % 2]
        rl = self.rl[self.j % 2]
        LB = self.LB[self.j % len(self.LB)]
        self.j += 1
        ctx.op("scalar", lambda e: e.activation(out=OL[:], in_=acc[:], func=AF.Copy), reads=[acc], writes=[OL])
        ctx.op("tensor", lambda e: e.matmul(LB[:64, :], lhsT=self.sel[:], rhs=OL[:], start=True, stop=True),
               reads=[self.sel, OL], writes=[LB])
        ctx.op("vector", lambda e: e.tensor_scalar(out=rl[:], in0=LB[:64, :], scalar1=1e-30, scalar2=None, op0=ALU.max),
               reads=[LB], writes=[rl])
        ctx.op("vector", lambda e: e.reciprocal(out=rl[:], in_=rl[:]), reads=[rl], writes=[rl])
        return OL, rl


def attn_consts_np():
    ident = np.eye(128, dtype=np.float32).astype(NBF)
    sel = np.zeros((128, 64), np.float32)
    sel[64 + np.arange(64), np.arange(64)] = 1.0
    kl = np.arange(128)[:, None]
    ql = np.arange(512)[None, :]
    mc = np.stack([np.where(ql >= kl + o, 0.0, NEG) for o in (0, 128, 256, 384)], axis=1)
    return {"ident": ident, "sel": sel, "mcausal": mc.astype(NBF)}


def build_mla():
    nc = bass.Bass("TRN2", target_bir_lowering=False)
    ctx = Ctx(nc)
    cm = Common(ctx, norm=False)
    NU = 4
    qT = ctx.dram("qT", (NU, 96, S), BF16, "ExternalInput")
    kT = ctx.dram("kT", (NU, 96, S), BF16, "ExternalInput")
    v1 = ctx.dram("v1", (NU, 128, 128, 128), BF16, "ExternalInput")
    cd = {"ident": ctx.dram("ident", (128, 128), BF16, "ExternalInput"),
          "sel": ctx.dram("sel", (128, 64), F32, "ExternalInput")}
    mcd = ctx.dram("mcausal", (128, 4, 512), BF16, "ExternalInput")
    oT = ctx.dram("oT", (NU, 64, S), BF16, "ExternalOutput")
    at = Attn(ctx, cm, 96 ** -0.5, cd, ns3=True)
    mc = ctx.sb("mc", (128, 4, 512), BF16)
    ctx.dma("sync", mc, mc[:], mcd, mcd[:])
    Kb = [ctx.sb(f"Kb{i}", (96, S), BF16) for i in range(2)]
    Vb = [ctx.sb(f"Vb{i}", (128, 128, 128), BF16) for i in range(2)]
    Qb = [ctx.sb(f"Qb{i}", (96, 512), BF16) for i in range(3)]
    Ob = [ctx.sb(f"Ob{i}", (64, 512), BF16) for i in range(2)]
    acc = [cm.psum[2], cm.psum[3]]
    n = 0
    for u in range(NU):
        K, V = Kb[u % 2], Vb[u % 2]
        ctx.dma("sync", K, K[:], kT, kT[u])
        ctx.dma("sync", V, V[:], v1, v1[u])
        for qb in range(NQB):
            Q = Qb[n % 3]
            A = acc[n % 2]
            O = Ob[n % 2]
            n += 1
            ctx.dma("sync", Q, Q[:], qT, qT[u, :, qb * 512:(qb + 1) * 512])
            nkt = 4 * qb + 4
            for kt in range(nkt):
                d = kt - 4 * qb
                c0 = d * 128 if d > 0 else 0
                mains = [(K, K[:, kt * 128:(kt + 1) * 128], Q, Q[:, c0:512], slice(c0, 512))]
                masks = []
                if d >= 0:
                    masks = [(at.ident, at.ident[:], mc, mc[:, d, c0:512], slice(c0, 512))]
                pvs = [(V, V[:, kt, :], slice(c0, 512), A, A[:, c0:512], kt == 0)]
                at.step(128, c0, 512, mains, masks, pvs)
            OL, rl = at.finish(A)
            ctx.op("vector", lambda e, OL=OL, rl=rl, O=O: e.tensor_tensor(out=O[:], in0=OL[:64, :], in1=rl[:], op=ALU.mult),
                   reads=[OL, rl], writes=[O])
            ctx.dma("sync", oT, oT[u, :, qb * 512:(qb + 1) * 512], O, O[:])
    ctx.finish([oT])
    return nc, ctx


def nsa_consts_np():
    c = attn_consts_np()
    kl = np.arange(128)[:, None]
    ql = np.arange(512)[None, :]
    d = ql - kl
    c["mwin"] = np.stack([np.where((d - o >= 0) & (d - o < 512), 0.0, NEG) for o in range(-512, 512, 128)], 1).astype(NBF)
    c["mcmp"] = np.stack([np.where(ql - 16 * kl >= 31 - 512 * dl, 0.0, NEG) for dl in range(5)], 1).astype(NBF)
    q = np.arange(128)[:, None]
    npr = np.arange(-1, 8)[None, :]
    c["mtm"] = np.where(q >= 31 + 16 * npr, 0.0, NEG).astype(NBF)
    lo = (np.arange(128) < 64)[:, None]
    c["mul3"] = np.where(lo, np.array([[0., 0., 0.]]), np.array([[1., 0., 0.]])).astype(np.float32)
    c["add3"] = np.where(lo, np.array([[BIG, BIG, -BIG]]), np.array([[0., BIG, BIG]])).astype(np.float32)
    c["identf"] = np.eye(128, dtype=np.float32)
    sg = np.zeros((6, 6, 64), np.float32)
    for r in range(6):
        sg[r, r, :] = 1.0
    c["selg"] = sg
    return c


NSA_CONST_SHAPES = {"ident": ((128, 128), BF16), "sel": ((128, 64), F32), "mcausal": ((128, 4, 512), BF16),
                    "mwin": ((128, 8, 512), BF16), "mcmp": ((128, 5, 512), BF16),
                    "mtm": ((128, 9), BF16), "mul3": ((128, 3), F32), "add3": ((128, 3), F32),
                    "identf": ((128, 128), F32), "selg": ((6, 6, 64), F32)}


USE32 = True
DBG_SKIP = set()


def build_nsa(nqb=NQB):
    nc = bass.Bass("TRN2", target_bir_lowering=False)
    ctx = Ctx(nc)
    cm = Common(ctx, norm=False)
    pb = cm.psum
    din = lambda n, s, dt=BF16: ctx.dram(n, s, dt, "ExternalInput")
    qg = din("qg", (128, 2, S), F32 if USE32 else BF16)
    qmy = din("qmy", (128, S))
    kcraw = din("kcraw", (64, 16, 1024), F32)
    vcraw = din("vcraw", (64, S))
    w1k = din("w1k", (128, 32, 128), F32)
    w1v = din("w1v", (64, 32, 128), F32)
    posk = din("posk", (128, 32, 8), F32)
    posv = din("posv", (64, 32), F32)
    w2k = din("w2k", (128, 128), F32)
    w2v = din("w2v", (128, 64), F32)
    ksT = din("ksT", (128, S))
    vs1 = din("vs1", (128, 128, 128))
    kwT = din("kwT", (128, 512 + S))
    vw1 = din("vw1", (128, 132, 128))
    gat = din("gat", (6, S), F32)
    cd = {k: din(k, s, dt) for k, (s, dt) in NSA_CONST_SHAPES.items()}
    oA = ctx.dram("oA", (2, 64, S), BF16, "ExternalOutput")
    at = Attn(ctx, cm, 0.125, cd, sbanks=[pb[0], pb[1], pb[5]], lbanks=[pb[6]])

    def cload(name, eng="sync"):
        s, dt = NSA_CONST_SHAPES[name]
        t = ctx.sb("c_" + name, s, dt)
        ctx.dma(eng, t, t[:], cd[name], cd[name][:])
        return t
    mc, mwin, mcmp, mtm, mul3, add3, identf = [cload(n) for n in
                                               ("mcausal", "mwin", "mcmp", "mtm", "mul3", "add3", "identf")]
    selg = ctx.sb("c_selg", (6, 6 * 64), F32)
    ctx.dma("sync", selg, selg[:], cd["selg"], cd["selg"].ap.rearrange("a b c -> a (b c)"))
    bigA = ctx.sb("bigA", (128, S), BF16)
    bigB = ctx.sb("bigB", (128, S), BF16)
    bigB3 = bigB.ap.rearrange("p (t c) -> p t c", c=128)
    kcT = ctx.sb("kcT", (128, 1024), BF16)
    vc1 = ctx.sb("vc1", (128, 8, 128), BF16)
    kcT32 = ctx.sb("kcT32", (128, 1024), F32)
    ctx.op("gpsimd", lambda e: e.memset(bigA[:], 0.0), writes=[bigA])
    bigA32 = bigA.ap.bitcast(F32)
    k32buf = bigA32[:, 0:2080].rearrange("p (j m) -> p j m", m=130)
    w1k32 = bigA32[:, 2080:2080 + 4096].rearrange("p (l h) -> p l h", h=128)
    pos32 = ctx.sb("pos32", (128, 32, 8), F32)
    w2k32 = ctx.sb("w2k32", (128, 128), F32)
    hid32 = [ctx.sb(f"hid32_{i}", (128, 128), F32) for i in range(2)]
    posb = ctx.sb("posb", (128, 1), F32)
    ctx.dma("sync", bigA, w1k32, w1k, w1k[:])
    ctx.dma("sync", pos32, pos32[:], posk, posk[:])
    ctx.dma("sync", w2k32, w2k32[:], w2k, w2k[:])
    ps = pb[7]
    for l in range(32 if "posb" not in DBG_SKIP else 1):
        ctx.op("tensor", lambda e, l=l: e.matmul(ps[:, 0:8], lhsT=w1k32[:, l, :], rhs=pos32[:, l, :],
                                                 start=(l == 0), stop=(l == 31)), reads=[bigA, pos32], writes=[ps])
    ctx.op("vector", lambda e: e.tensor_copy(out=posb[:], in_=ps[:, 0:1]), reads=[ps], writes=[posb])
    for p in range(8 if "kpath" not in DBG_SKIP else 0):
        nm = min(130, 1024 - 128 * p)
        ctx.dma("sync", bigA, k32buf[0:64, :, 0:nm], kcraw, kcraw[:, :, 128 * p:128 * p + nm])
        ps = pb[p % 2]
        for l in range(32):
            a_, j_ = l // 16, l % 16
            ctx.op("tensor", lambda e, l=l: e.matmul(ps[:, 0:128], lhsT=w1k32[:, l, :], rhs=k32buf[:, j_, a_:a_ + 128],
                                                     start=(l == 0), stop=(l == 31)), reads=[bigA], writes=[ps])
        h32 = hid32[p % 2]
        ctx.op("scalar", lambda e: e.activation(out=h32[:], in_=ps[:, 0:128], func=AF.Silu, bias=posb[:, 0:1]),
               reads=[ps, posb], writes=[h32])
        p2 = pb[2 + (p % 2)]
        ctx.op("tensor", lambda e: e.matmul(p2[:, 0:128], lhsT=w2k32[:], rhs=h32[:], start=True, stop=True),
               reads=[w2k32, h32], writes=[p2])
        ctx.op("vector", lambda e: e.tensor_copy(out=kcT32[:, p * 128:(p + 1) * 128], in_=p2[:, 0:128]), reads=[p2], writes=[kcT32])
        ctx.op("scalar", lambda e: e.activation(out=kcT[:, p * 128:(p + 1) * 128], in_=kcT32[:, p * 128:(p + 1) * 128], func=AF.Copy), reads=[kcT32], writes=[kcT])
    ctx.dma("sync", bigB, bigB[0:64, :], vcraw, vcraw[:])
    w1s_ap = bigA[0:64, 0:4096].rearrange("p (l h) -> p l h", h=128)
    poss = ctx.sb("poss", (64, 32), BF16)
    w2vs = ctx.sb("w2vs", (128, 64), BF16)
    ctx.dma("gpsimd", w2vs, w2vs[:], w2v, w2v[:])
    hid = [ctx.sb(f"hid{i}", (128, 512), BF16) for i in range(2)]
    ctx.op("vector", lambda e: e.memset(hid[1][:], 0.0), writes=[hid[1]])
    ctx.op("vector", lambda e: e.memset(vc1[:], 1.0), writes=[vc1])
    ctx.dma("gpsimd", bigA, w1s_ap, w1v, w1v[:])
    ctx.dma("gpsimd", poss, poss[:], posv, posv[:])
    ps = pb[7]
    for l in range(32):
        ctx.op("tensor", lambda e, l=l: e.matmul(ps[:, 0:1], lhsT=w1s_ap[:, l, :], rhs=poss[:, l:l + 1],
                                                 start=(l == 0), stop=(l == 31)), reads=[bigA, poss], writes=[ps])
    posbv = ctx.sb("posbv", (128, 1), F32)
    ctx.op("vector", lambda e: e.tensor_copy(out=posbv[:], in_=ps[:, 0:1]), reads=[ps], writes=[posbv])
    for nt in range(2 if "vpath" not in DBG_SKIP else 0):
        ncol = 512 if nt == 0 else 511
        ps = pb[nt]
        for l in range(32):
            st_ = nt * 8192 + l
            en = min(S, st_ + 16 * ncol)
            ctx.op("tensor", lambda e, l=l: e.matmul(ps[:, 0:ncol], lhsT=w1s_ap[:, l, :], rhs=bigB[0:64, st_:en:16],
                                                     start=(l == 0), stop=(l == 31)), reads=[bigA, bigB], writes=[ps])
        ctx.op("scalar", lambda e: e.activation(out=hid[nt][:, 0:ncol], in_=ps[:, 0:ncol], func=AF.Silu, bias=posbv[:, 0:1]),
               reads=[ps, posbv], writes=[hid[nt]])
        for j in range(4):
            p2 = pb[2 + (j % 2)]
            ctx.op("tensor", lambda e: e.matmul(p2[:, 0:64], lhsT=hid[nt][:, j * 128:(j + 1) * 128], rhs=w2vs[:], start=True, stop=True),
                   reads=[w2vs, hid[nt]], writes=[p2])
            ctx.op("vector", lambda e: e.tensor_copy(out=vc1[:, nt * 4 + j, 0:64], in_=p2[:, 0:64]), reads=[p2], writes=[vc1])
    ctx.dma("sync", bigA, bigA[:], ksT, ksT[:])
    ctx.dma("sync", bigB, bigB[:], vs1, vs1.ap.rearrange("p t c -> p (t c)"))
    QDT = F32 if USE32 else BF16
    Qg1 = ctx.sb("Qg0", (128, 4, 512), QDT)
    Qgs = [Qg1, Qg1]
    Qms = [[[ctx.sb(f"QY{i}_{h}_{c}", (128, 512), BF16) for c in range(4)] for h in range(2)] for i in range(2)]
    ctx.op("gpsimd", lambda e: e.memset(Qg1[:], 0.0), writes=[Qg1])
    for i in range(2):
        for h in range(2):
            for c in range(4):
                ctx.op("gpsimd", lambda e: e.memset(Qms[i][h][c][:], 0.0), writes=[Qms[i][h][c]])
    wKs = [ctx.sb(f"wK{i}", (128, 1024), BF16) for i in range(2)]
    wVs = [ctx.sb(f"wV{i}", (128, 8, 128), BF16) for i in range(2)]
    gts = [ctx.sb(f"gt{i}", (6, 512), F32) for i in range(2)]
    NE = 4
    es = [ctx.sb(f"e{i}", (128, 512), F32) for i in range(NE)]
    lp = [ctx.sb(f"lp{i}", (128, 2), F32) for i in range(2)]
    rlh = [ctx.sb(f"rlh{i}", (128, 1), F32) for i in range(2)]
    Aim = ctx.sb("Aim", (128, 1024), F32)
    I1 = ctx.sb("I1", (128, 256), F32)
    I2 = ctx.sb("I2", (128, 256), F32)
    m8 = ctx.sb("m8", (128, 16), F32)
    negms = [ctx.sb(f"negm{i}", (128, 320), F32) for i in range(4)]
    fg = ctx.sb("fg", (64, 512), F32)
    tmpc = ctx.sb("tmpc", (64, 512), F32)
    accsb = ctx.sb("accsb", (64, 512), F32)
    Ob = [ctx.sb(f"Ob{i}", (64, 512), BF16) for i in range(2)]
    accC, accS, accW, GB = pb[2], pb[3], pb[4], pb[7]
    kq = kcT32 if USE32 else kcT
    st = {"ne": 0, "no": 0}

    def load(qb):
        T0 = qb * 512
        if qb >= 2:
            for hh in range(4):
                r0 = (hh % 2) * 64
                ctx.dma("sync", Qgs[qb % 2], Qgs[qb % 2][0:64, hh, :], qg, qg[r0:r0 + 64, hh // 2, T0:T0 + 512])
        for h in range(2):
            for c in range(qb // 8 + 1):
                t_ = Qms[qb % 2][h][c]
                ctx.dma("sync", t_, t_[0:64, :], qmy, qmy[h * 64:(h + 1) * 64, T0:T0 + 512])
        ctx.dma("sync", wKs[qb % 2], wKs[qb % 2][:], kwT, kwT[:, T0:T0 + 1024])
        ctx.dma("sync", wVs[qb % 2], wVs[qb % 2][:], vw1, vw1[:, 4 * qb:4 * qb + 8, :])
        ctx.dma("sync", gts[qb % 2], gts[qb % 2][:], gat, gat[:, T0:T0 + 512])

    def phase1a(qb, subs=(0, 1, 2, 3)):
        if qb < 2:
            return
        Qg = Qgs[qb % 2]
        for qsl in subs:
            qs = 4 * qb + qsl
            ncols = 8 * qs + 8
            nj = 2 * qs + 2
            negm = negms[qsl]
            halves = [(lo, min(ncols, lo + 512)) for lo in (0, 512) if lo < ncols]
            for hh in range(4):
                ch, r0 = hh // 2, (hh % 2) * 64
                lpt = lp[hh % 2]
                rl1 = rlh[hh % 2]
                ehs = []
                for hi_, (lo, hi) in enumerate(halves):
                    w = hi - lo
                    Sb = at.S[at.i % len(at.S)]
                    at.i += 1
                    a, b2 = max(lo, ncols - 9, 0), min(hi, ncols)
                    hasm = b2 > a
                    ctx.op("tensor", lambda e: e.matmul(
                        Sb[:, 0:w], lhsT=Qg[:, hh, qsl * 128:(qsl + 1) * 128], rhs=kq[:, lo:hi],
                        start=True, stop=(not hasm), skip_group_check=True), reads=[Qg, kq], writes=[Sb])
                    if hasm:
                        ctx.op("tensor", lambda e: e.matmul(
                            Sb[:, a - lo:b2 - lo], lhsT=at.ident[:], rhs=mtm[:, a - (ncols - 9):b2 - (ncols - 9)],
                            start=False, stop=True, skip_group_check=True), reads=[at.ident, mtm], writes=[Sb])
                    et = es[st["ne"] % NE]
                    st["ne"] += 1
                    ctx.op("scalar", lambda e: e.activation(
                        out=et[:, 0:w], in_=Sb[:, 0:w], func=AF.Exp, scale=0.125, accum_out=lpt[:, hi_:hi_ + 1]),
                        reads=[Sb], writes=[et, lpt])
                    ehs.append((et, lo, hi))
                if len(halves) == 2:
                    ctx.op("vector", lambda e: e.tensor_tensor(out=lpt[:, 0:1], in0=lpt[:, 0:1], in1=lpt[:, 1:2], op=ALU.add),
                           reads=[lpt], writes=[lpt])
                ctx.op("vector", lambda e: e.tensor_scalar(out=rl1[:], in0=lpt[:, 0:1], scalar1=1e-30, scalar2=None, op0=ALU.max),
                       reads=[lpt], writes=[rl1])
                ctx.op("vector", lambda e: e.reciprocal(out=rl1[:], in_=rl1[:]), reads=[rl1], writes=[rl1])
                for (et, lo, hi) in ehs:
                    w = hi - lo
                    if hh == 0:
                        ctx.op("vector", lambda e: e.tensor_scalar(
                            out=Aim[:, lo:hi], in0=et[:, 0:w], scalar1=rl1[:, 0:1], scalar2=None, op0=ALU.mult),
                            reads=[et, rl1], writes=[Aim])
                    else:
                        ctx.op("vector", lambda e: e.scalar_tensor_tensor(
                            out=Aim[:, lo:hi], in0=et[:, 0:w], scalar=rl1[:, 0:1], in1=Aim[:, lo:hi], op0=ALU.mult, op1=ALU.add),
                            reads=[et, rl1, Aim], writes=[Aim])
            n4 = 4 * nj
            tt = lambda o, a_, b_, op: ctx.op("vector", lambda e: e.tensor_tensor(out=o, in0=a_, in1=b_, op=op),
                                              reads=[Aim, I1, mul3, add3], writes=[I1])
            tt(I1[:, 0:nj], Aim[:, 0:n4:4], Aim[:, 1:n4:4], ALU.add)
            tt(I1[:, 0:nj], I1[:, 0:nj], Aim[:, 2:n4:4], ALU.add)
            ctx.op("vector", lambda e: e.scalar_tensor_tensor(out=I1[:, 0:nj], in0=I1[:, 0:nj], scalar=2.0, in1=Aim[:, 3:n4:4],
                                                              op0=ALU.mult, op1=ALU.add), reads=[Aim, I1], writes=[I1])
            tt(I1[:, 1:nj], I1[:, 1:nj], Aim[:, 3:n4 - 4:4], ALU.add)
            tt(I1[:, nj - 3:nj], I1[:, nj - 3:nj], mul3[:, :], ALU.mult)
            tt(I1[:, nj - 3:nj], I1[:, nj - 3:nj], add3[:, :], ALU.add)
            ctx.op("vector", lambda e: e.memset(I1[:, 0:1], BIG), writes=[I1])
            ctx.op("gpsimd", lambda e: e.memset(negm[:], 0.0), writes=[negm])
            ctx.op("vector", lambda e: e.max(out=m8[:, 0:8], in_=I1[:, 0:nj]), reads=[I1], writes=[m8])
            ctx.op("vector", lambda e: e.match_replace(out=I2[:, 0:nj], in_to_replace=m8[:, 0:8], in_values=I1[:, 0:nj],
                                                       imm_value=-BIG), reads=[I1, m8], writes=[I2])
            ctx.op("vector", lambda e: e.max(out=m8[:, 8:16], in_=I2[:, 0:nj]), reads=[I2], writes=[m8])
            ctx.op("vector", lambda e: e.tensor_scalar(out=negm[:, 64:64 + nj], in0=I1[:, 0:nj], scalar1=m8[:, 15:16], scalar2=-1.0,
                                                       op0=ALU.is_ge, op1=ALU.add), reads=[I1, m8], writes=[negm])

    def phase1b(qb):
        if qb < 2:
            return
        for qsl in range(4):
            negm = negms[qsl]
            for c in range(qb // 8 + 1):
                ctx.op("tensor", lambda e: e.transpose(out=GB[:, 0:128], in_=negm[:, 64 * c:64 * c + 128], identity=identf[:]),
                       reads=[negm, identf], writes=[GB])
                for h in range(2):
                    t_ = Qms[qb % 2][h][c]
                    ctx.op("vector", lambda e: e.tensor_copy(out=t_[64:128, qsl * 128:(qsl + 1) * 128], in_=GB[64:128, 0:128]),
                           reads=[GB], writes=[t_])

    def phase2(qb, todo):
        T0 = qb * 512
        wK, wV, gt = wKs[qb % 2], wVs[qb % 2], gts[qb % 2]
        use_sel = qb >= 2
        for hl in range(2):
            r0 = hl * 64
            Qm = Qms[qb % 2][hl][0]
            for m in range(qb // 4 + 1):
                dl = qb - 4 * m
                masks = [(at.ident, at.ident[:], mcmp, mcmp[:, dl, :], slice(0, 512))] if dl <= 4 else []
                todo.append(lambda Qm=Qm, m=m, masks=masks: at.step(128, 0, 512, [(kcT, kcT[:, m * 128:(m + 1) * 128], Qm, Qm[:, 0:512], slice(0, 512))], masks,
                        [(vc1, vc1[:, m, :], slice(0, 512), accC, accC[:, :], m == 0)]))
            for kt in range(8):
                c0 = max(0, (kt - 4) * 128)
                c1 = min(512, 128 * kt + 128)
                todo.append(lambda Qm=Qm, kt=kt, c0=c0, c1=c1: at.step(128, c0, c1, [(wK, wK[:, kt * 128:(kt + 1) * 128], Qm, Qm[:, c0:c1], slice(c0, c1))],
                        [(at.ident, at.ident[:], mwin, mwin[:, kt, c0:c1], slice(c0, c1))],
                        [(wV, wV[:, kt, :], slice(c0, c1), accW, accW[:, c0:c1], kt == 0)]))
            for kt in range(4 * qb + 4):
                d = kt - 4 * qb
                c0 = d * 128 if d > 0 else 0
                masks = []
                Qm = Qms[qb % 2][hl][kt // 32]
                if d >= 0:
                    masks.append((at.ident, at.ident[:], mc, mc[:, d, c0:512], slice(c0, 512)))
                todo.append(lambda Qm=Qm, kt=kt, c0=c0, masks=masks: at.step(128, c0, 512, [(bigA, bigA[:, kt * 128:(kt + 1) * 128], Qm, Qm[:, c0:512], slice(c0, 512))], masks,
                        [(bigB, bigB3[:, kt, :], slice(c0, 512), accS, accS[:, c0:512], kt == 0)]))
            todo.append(lambda hl=hl: epilogue(qb, hl))

    def epilogue(qb, hl):
        T0 = qb * 512
        gt = gts[qb % 2]
        for br, acc in enumerate((accC, accS, accW)):
            OL, rl = at.finish(acc)
            r = hl * 3 + br
            ctx.op("tensor", lambda e: e.matmul(GB[:64, :], lhsT=selg[:, r * 64:(r + 1) * 64], rhs=gt[:, :], start=True, stop=True),
                   reads=[selg, gt], writes=[GB])
            ctx.op("vector", lambda e: e.tensor_tensor(out=fg[:], in0=GB[:64, :], in1=rl[:], op=ALU.mult),
                   reads=[GB, rl], writes=[fg])
            if br == 0:
                ctx.op("vector", lambda e: e.tensor_tensor(out=accsb[:], in0=OL[:64, :], in1=fg[:], op=ALU.mult),
                       reads=[OL, fg], writes=[accsb])
            else:
                ctx.op("vector", lambda e: e.tensor_tensor(out=tmpc[:], in0=OL[:64, :], in1=fg[:], op=ALU.mult),
                       reads=[OL, fg], writes=[tmpc])
                ctx.op("vector", lambda e: e.tensor_tensor(out=accsb[:], in0=accsb[:], in1=tmpc[:], op=ALU.add),
                       reads=[accsb, tmpc], writes=[accsb])
        O = Ob[st["no"] % 2]
        st["no"] += 1
        ctx.op("scalar", lambda e: e.activation(out=O[:], in_=accsb[:], func=AF.Copy), reads=[accsb], writes=[O])
        ctx.dma("sync", oA, oA[hl, :, T0:T0 + 512], O, O[:])

    load(0)
    phase1a(0)
    for qb in range(nqb):
        phase1b(qb)
        todo = []
        phase2(qb, todo)
        nxt = qb + 1 < nqb
        if nxt:
            load(qb + 1)
        n = len(todo)
        cuts = {(n * k) // 4: k for k in range(4)}
        for i, fn in enumerate(todo):
            if nxt and i in cuts:
                phase1a(qb + 1, (cuts[i],))
            fn()
    ctx.finish([oA])
    return nc, ctx


def dil_consts_np():
    c = attn_consts_np()
    kl = np.arange(128)[:, None]
    ql = np.arange(512)[None, :]
    d = ql - kl
    c["md1"] = np.stack([np.where((d - o >= 0) & (d - o <= 128), 0.0, NEG) for o in range(-128, 512, 128)], 1).astype(NBF)
    i4 = ql % 128
    c["md4"] = np.stack([np.where(i4 <= kl, 0.0, NEG), np.where(i4 >= kl, 0.0, NEG)], 1).astype(NBF)
    i16 = ql % 32
    c["md16"] = np.stack([np.where(i16 <= kl, 0.0, NEG), np.where(i16 >= kl, 0.0, NEG)], 1).astype(NBF)
    del c["mcausal"]
    return c


DIL_CONST_SHAPES = {"ident": ((128, 128), BF16), "sel": ((128, 64), F32), "md1": ((128, 5, 512), BF16),
                    "md4": ((128, 2, 512), BF16), "md16": ((128, 2, 512), BF16)}


def build_dil(nqb=NQB):
    nc = bass.Bass("TRN2", target_bir_lowering=False)
    ctx = Ctx(nc)
    cm = Common(ctx, norm=False)
    pb = cm.psum
    din = lambda n, s, dt=BF16: ctx.dram(n, s, dt, "ExternalInput")
    qd = din("qd", (128, S))
    kb1T = din("kb1T", (128, 128 + S))
    vb1 = din("vb1", (2, 128, 129, 128))
    kb4T = din("kb4T", (128, 4, 128 + 4096))
    vb4 = din("vb4", (2, 128, 4, 33, 128))
    kb16T = din("kb16T", (128, 16, 128 + 1024))
    vb16 = din("vb16", (2, 16, 1152, 128))
    cd = {k: din(k, s, dt) for k, (s, dt) in DIL_CONST_SHAPES.items()}
    oB = ctx.dram("oB", (2, 64, S), BF16, "ExternalOutput")
    at = Attn(ctx, cm, 0.125, cd, ns3=True)
    ms = {}
    for name in ("md1", "md4", "md16"):
        s, dt = DIL_CONST_SHAPES[name]
        ms[name] = ctx.sb("c_" + name, s, dt)
        ctx.dma("sync", ms[name], ms[name][:], cd[name], cd[name][:])
    md1, md4, md16 = ms["md1"], ms["md4"], ms["md16"]
    Qs = [[ctx.sb(f"Qd{i}_{h}", (128, 512), BF16) for h in range(2)] for i in range(2)]
    for i in range(2):
        for h in range(2):
            ctx.op("gpsimd", lambda e: e.memset(Qs[i][h][:], 0.0), writes=[Qs[i][h]])
    K1 = [ctx.sb(f"K1_{i}", (128, 640), BF16) for i in range(2)]
    V1 = [[ctx.sb(f"V1_{i}_{h}", (128, 5, 128), BF16) for h in range(2)] for i in range(2)]
    K4 = [ctx.sb(f"K4_{i}", (128, 4, 256), BF16) for i in range(2)]
    V4 = [[ctx.sb(f"V4_{i}_{h}", (128, 4, 2, 128), BF16) for h in range(2)] for i in range(2)]
    K16 = [ctx.sb(f"K16_{i}", (128, 16, 160), BF16) for i in range(2)]
    V16A = [[ctx.sb(f"V16A_{i}_{h}", (128, 16, 128), BF16) for h in range(2)] for i in range(2)]
    V16B = [[ctx.sb(f"V16B_{i}_{h}", (32, 16, 128), BF16) for h in range(2)] for i in range(2)]
    Ob = [ctx.sb(f"Ob{i}", (64, 512), BF16) for i in range(2)]
    accs = [pb[2], pb[3]]
    n = 0
    for qb in range(nqb):
        T0 = qb * 512
        i = qb % 2
        k1, k4, k16 = K1[i], K4[i], K16[i]
        for h in range(2):
            ctx.dma("sync", Qs[i][h], Qs[i][h][h * 64:(h + 1) * 64, :], qd, qd[h * 64:(h + 1) * 64, T0:T0 + 512])
        ctx.dma("sync", k1, k1[:], kb1T, kb1T[:, T0:T0 + 640])
        ctx.dma("sync", k4, k4[:], kb4T, kb4T[:, :, 128 * qb:128 * qb + 256])
        ctx.dma("sync", k16, k16[:], kb16T, kb16T[:, :, 32 * qb:32 * qb + 160])
        for h in range(2):
            ctx.dma("sync", V1[i][h], V1[i][h][:], vb1, vb1[h, :, 4 * qb:4 * qb + 5, :])
            ctx.dma("sync", V4[i][h], V4[i][h][:], vb4, vb4[h, :, :, qb:qb + 2, :])
            ctx.dma("gpsimd", V16A[i][h], V16A[i][h][:], vb16,
                    vb16[h, :, 32 * qb:32 * qb + 128, :].rearrange("r p c -> p r c"))
            ctx.dma("gpsimd", V16B[i][h], V16B[i][h][:], vb16,
                    vb16[h, :, 32 * qb + 128:32 * qb + 160, :].rearrange("r p c -> p r c"))
        for h in range(2):
            r0 = h * 64
            Q = Qs[i][h]
            A = accs[n % 2]
            O = Ob[n % 2]
            n += 1
            v1, v4, va, vb = V1[i][h], V4[i][h], V16A[i][h], V16B[i][h]
            for kt in range(5):
                o = -128 + 128 * kt
                c0, c1 = max(0, o), min(512, 128 * kt + 128)
                at.step(128, c0, c1, [(k1, k1[:, kt * 128:(kt + 1) * 128], Q, Q[:, c0:c1], slice(c0, c1))],
                        [(at.ident, at.ident[:], md1, md1[:, kt, c0:c1], slice(c0, c1))],
                        [(v1, v1[:, kt, :], slice(c0, c1), A, A[:, c0:c1], kt == 0)])
            for kt in range(2):
                mains = [(k4, k4[:, r, kt * 128:(kt + 1) * 128], Q, Q[:, r:512:4], slice(r * 128, (r + 1) * 128))
                         for r in range(4)]
                pvs = [(v4, v4[:, r, kt, :], slice(r * 128, (r + 1) * 128), A, A[:, r:512:4], False) for r in range(4)]
                at.step(128, 0, 512, mains, [(at.ident, at.ident[:], md4, md4[:, kt, :], slice(0, 512))], pvs)
            mains = [(k16, k16[:, r, 0:128], Q, Q[:, r:512:16], slice(r * 32, (r + 1) * 32)) for r in range(16)]
            pvs = [(va, va[:, r, :], slice(r * 32, (r + 1) * 32), A, A[:, r:512:16], False) for r in range(16)]
            at.step(128, 0, 512, mains, [(at.ident, at.ident[:], md16, md16[:, 0, :], slice(0, 512))], pvs)
            mains = [(k16, k16[:, r, 128:160], Q, Q[:, r:512:16], slice(r * 32, (r + 1) * 32)) for r in range(16)]
            pvs = [(vb, vb[:, r, :], slice(r * 32, (r + 1) * 32), A, A[:, r:512:16], False) for r in range(16)]
            at.step(32, 0, 512, mains, [(at.ident, at.ident[0:32, 0:32], md16, md16[0:32, 1, :], slice(0, 512))], pvs)
            OL, rl = at.finish(A)
            ctx.op("vector", lambda e, OL=OL, rl=rl, O=O: e.tensor_tensor(out=O[:], in0=OL[:64, :], in1=rl[:], op=ALU.mult),
                   reads=[OL, rl], writes=[O])
            ctx.dma("sync", oB, oB[h, :, T0:T0 + 512], O, O[:])
    ctx.finish([oB])
    return nc, ctx


NCORE = 8
DBG = {}


def _run(nc, in_maps):
    res = run_bass_kernel_spmd(nc, in_maps, core_ids=list(range(NCORE)))
    et = getattr(res, "exec_time_ns", None)
    if et is not None:
        print(f"[launch] exec_time_ns={et}", flush=True)
    return res.results


def _tokT(a):
    R = a.shape[1]
    return np.ascontiguousarray(a.T.reshape(R // 128, 128, a.shape[0]).transpose(1, 0, 2))


def _v1_tiles(vT, pad_rows=0):
    L = vT.shape[1]
    a = np.zeros((pad_rows + L, 128), NBF)
    a[pad_rows:, :64] = vT.T
    a[pad_rows:, 64:] = 1
    nt = (pad_rows + L) // 128
    return np.ascontiguousarray(a.reshape(nt, 128, 128).transpose(1, 0, 2))


def _padfront(a, n):
    z = np.zeros(a.shape[:-1] + (n,), a.dtype)
    return np.concatenate([z, a], axis=-1)


def _rope_tabs(pos, dims):
    inv = (np.float32(ROPE_THETA) ** (-np.arange(0, dims, 2, dtype=np.float32) / np.float32(dims))).astype(np.float32)
    ang = pos.astype(np.float32)[:, None] * inv[None, :]
    return np.cos(ang).astype(np.float32).T, np.sin(ang).astype(np.float32).T


def kernel(x, ffn1_norm, ffn1_w_gate, ffn1_w_up, ffn1_w_down, ffn2_norm, ffn2_w_gate, ffn2_w_up, ffn2_w_down, mix_norm,
           ev_w_in, ev_w_out, nsa_cmp_pos_k, nsa_cmp_pos_v, nsa_cmp_k_w1, nsa_cmp_k_w2, nsa_cmp_v_w1, nsa_cmp_v_w2,
           mla_w_in, mla_q_norm, mla_kv_norm, mla_w_uq, mla_w_ukv, mla_w_out, final_norm):
    f32 = lambda a: np.asarray(a, dtype=np.float32)
    x = f32(x)
    B = 2
    xf = x.reshape(B * S, D)
    pos_of_core = [(c % 4) * NTOK + np.arange(NTOK) for c in range(NCORE)]

    def ffn_maps(pref, g, wg, wu, wd):
        wgu, wdt = ffn_host_layout(f32(wg), f32(wu), f32(wd))
        return {pref + "_g": gain_layout(f32(g), KC), pref + "_wgu": wgu, pref + "_wd": wdt}

    W = f32(ev_w_in[0])
    perm64 = np.concatenate([np.arange(8, 16), np.arange(0, 8), np.arange(16, 64)])
    perm128 = np.concatenate([perm64, 64 + perm64])
    win = np.stack([wtile(W[:, c0:c0 + 128], KC) for c0 in EV_COLS])
    wgt = wtile(W[:, 1280:1304], KC)
    permT = np.zeros((128, 128), np.float32)
    permT[perm128, np.arange(128)] = 1.0
    common = dict(win=win, wgt=wgt, gmix=gain_layout(f32(mix_norm[0]), KC), permT=permT)
    common.update(ffn_maps("fa", ffn1_norm[0], ffn1_w_gate[0], ffn1_w_up[0], ffn1_w_down[0]))
    in_maps = []
    for c in range(NCORE):
        cs, sn = _rope_tabs(pos_of_core[c], 16)
        ct = np.ones((64, NTOK), np.float32)
        st = np.zeros((64, NTOK), np.float32)
        ct[0:8], ct[8:16] = cs, cs
        st[0:8], st[8:16] = -sn, sn
        m = dict(common)
        m.update(xT=_tokT(xf[c * NTOK:(c + 1) * NTOK]), ctab=np.concatenate([ct, ct]), stab=np.concatenate([st, st]))
        in_maps.append(m)
    nc, _ = build_stage("L1")
    r1 = _run(nc, in_maps)
    x1T = [r["x1T"] for r in r1]
    PJ = np.concatenate([r["pj"] for r in r1], axis=2).reshape(22, 128, B, S)
    GT = np.concatenate([r["gates"] for r in r1], axis=1).reshape(24, B, S)
    Q32 = np.concatenate([r["q32"] for r in r1], axis=2).reshape(5, 128, B, S)
    DBG.update(x1T=x1T, PJ=PJ, GT=GT)
    cn = nsa_consts_np()
    w1k = np.zeros((128, 32, 128), np.float32)
    w1k[:64] = f32(nsa_cmp_k_w1[0]).reshape(32, 64, 128).transpose(1, 0, 2)
    posk8 = np.zeros((128, 32, 8), np.float32)
    posk8[:64] = np.repeat(f32(nsa_cmp_pos_k[0]).T[:, :, None], 8, axis=2)
    w1v = np.ascontiguousarray(f32(nsa_cmp_v_w1[0]).reshape(32, 64, 128).transpose(1, 0, 2))
    w2k = f32(nsa_cmp_k_w2[0])
    XM = np.zeros((64, 128, 128), NBF)
    for kt in range(128):
        for half in range(2):
            XM[2 * (kt % 32) + half, kt, half * 64:(half + 1) * 64] = 30000.0
    XM = XM.reshape(64, S)
    in_maps = []
    for c in range(NCORE):
        b, g, pr = c // 4, (c % 4) // 2, c % 2
        gs = slice(g * 64, (g + 1) * 64)
        ks = PJ[6][gs, b]
        kw = PJ[8][gs, b]
        h0 = 4 * g + 2 * pr
        m = dict(cn)
        m.update(qg=np.ascontiguousarray(np.stack([Q32[2 * g][:, b], Q32[2 * g + 1][:, b]], axis=1)),
                 qmy=np.ascontiguousarray(PJ[2 * g + pr][:, b]),
                 kcraw=np.ascontiguousarray(Q32[4][gs, b].reshape(64, 1024, 16).transpose(0, 2, 1)), vcraw=np.ascontiguousarray(PJ[5][gs, b]),
                 w1k=w1k, w1v=w1v, posk=posk8,
                 posv=np.ascontiguousarray(f32(nsa_cmp_pos_v[0]).T),
                 w2k=np.concatenate([w2k, np.zeros_like(w2k)], axis=1), w2v=f32(nsa_cmp_v_w2[0]),
                 ksT=np.concatenate([ks, XM], axis=0), vs1=_v1_tiles(PJ[7][gs, b]),
                 kwT=_padfront(np.concatenate([kw, np.zeros_like(kw)], axis=0), 512), vw1=_v1_tiles(PJ[9][gs, b], 512),
                 gat=np.ascontiguousarray(GT[h0 * 3:h0 * 3 + 6, b]))
        in_maps.append(m)
    nc, _ = build_nsa()
    rA = _run(nc, in_maps)
    AT = np.zeros((1024, B, S), NBF)
    for c in range(NCORE):
        b, g, pr = c // 4, (c % 4) // 2, c % 2
        h0 = 4 * g + 2 * pr
        AT[h0 * 64:(h0 + 2) * 64, b] = rA[c]["oA"].reshape(128, S)
    cdl = dil_consts_np()
    in_maps = []
    for c in range(NCORE):
        b, cc = c // 4, c % 4
        k = PJ[14 + cc][:, b]
        v = PJ[18 + cc][:, b]
        m = dict(cdl)
        vh = [v[0:64], v[64:128]]
        m.update(qd=np.ascontiguousarray(PJ[10 + cc][:, b]), kb1T=_padfront(k, 128),
                 vb1=np.stack([_v1_tiles(vv, 128) for vv in vh]),
                 kb4T=np.ascontiguousarray(np.stack([_padfront(k[:, r::4], 128) for r in range(4)], axis=1)),
                 vb4=np.stack([np.stack([_v1_tiles(vv[:, r::4], 128) for r in range(4)], axis=1) for vv in vh]),
                 kb16T=np.ascontiguousarray(np.stack([_padfront(k[:, r::16], 128) for r in range(16)], axis=1)),
                 vb16=np.stack([np.stack([_v1_tiles(vv[:, r::16], 128).transpose(1, 0, 2).reshape(1152, 128)
                                          for r in range(16)]) for vv in vh]))
        in_maps.append(m)
    nc, _ = build_dil()
    rB = _run(nc, in_maps)
    for c in range(NCORE):
        b, cc = c // 4, c % 4
        AT[512 + cc * 128:512 + (cc + 1) * 128, b] = rB[c]["oB"].reshape(128, S)
    DBG.update(AT=AT)
    ATf = AT.reshape(1024, B * S)
    Wo = f32(ev_w_out[0])
    Wm = f32(mla_w_in[0])
    Wq = f32(mla_w_uq[0])
    Wkv = f32(mla_w_ukv[0])
    permq = np.concatenate([np.arange(64), np.arange(80, 96), np.arange(64, 80)])
    permk = np.concatenate([np.arange(16, 32), np.arange(0, 16)])
    common = dict(wo=np.stack([wtile(Wo[:, c0:c0 + 128], KC) for c0 in range(0, 1024, 128)]),
                  gmix=gain_layout(f32(mix_norm[1]), KC), gq=gain_layout(f32(mla_q_norm[0]), 2),
                  gkv=gain_layout(f32(mla_kv_norm[0]), 1),
                  wmi=np.stack([wtile(Wm[:, i * 128:(i + 1) * 128], KC) for i in range(3)]),
                  wkr=np.stack([wtile(Wm[:, 384:416], KC), wtile(Wm[:, 384 + permk], KC)]),
                  wuq=np.stack([wtile(Wq[:, h * 96:(h + 1) * 96], 2) for h in range(16)]),
                  wuqs=np.stack([wtile(Wq[:, h * 96 + permq], 2) for h in range(16)]),
                  wukv=np.stack([wtile(Wkv[:, h * 128:(h + 1) * 128], 1) for h in range(16)]))
    common.update(ffn_maps("fa", ffn2_norm[0], ffn2_w_gate[0], ffn2_w_up[0], ffn2_w_down[0]))
    common.update(ffn_maps("fb", ffn1_norm[1], ffn1_w_gate[1], ffn1_w_up[1], ffn1_w_down[1]))
    in_maps = []
    for c in range(NCORE):
        cs, sn = _rope_tabs(pos_of_core[c], 32)
        cq = np.ones((96, NTOK), np.float32)
        sq = np.zeros((96, NTOK), np.float32)
        cq[64:80], cq[80:96] = cs, cs
        sq[64:80], sq[80:96] = -sn, sn
        m = dict(common)
        m.update(xT=x1T[c], aT=_tokT(ATf[:, c * NTOK:(c + 1) * NTOK].T), cq_t=cq, sq_t=sq,
                 ck_t=np.concatenate([cs, cs]), sk_t=np.concatenate([-sn, sn]))
        in_maps.append(m)
    nc, _ = build_stage("L3")
    r3 = _run(nc, in_maps)
    x3T = [r["x3T"] for r in r3]
    QT = np.concatenate([r["qT"] for r in r3], axis=2).reshape(16, 96, B, S)
    KV = np.concatenate([r["kvT"] for r in r3], axis=2).reshape(16, 128, B, S)
    KR = np.concatenate([r["krT"] for r in r3], axis=1).reshape(32, B, S)
    DBG.update(x3T=x3T, QT=QT, KV=KV, KR=KR)
    ca = attn_consts_np()
    in_maps = []
    for c in range(NCORE):
        b = c // 4
        hs = [4 * (c % 4) + u for u in range(4)]
        m = dict(ca)
        m.update(qT=np.ascontiguousarray(np.stack([QT[h, :, b] for h in hs])),
                 kT=np.stack([np.concatenate([KV[h, 0:64, b], KR[:, b]], axis=0) for h in hs]),
                 v1=np.stack([_v1_tiles(KV[h, 64:128, b]) for h in hs]))
        in_maps.append(m)
    nc, _ = build_mla()
    r4 = _run(nc, in_maps)
    AT2 = np.zeros((1024, B, S), NBF)
    for c in range(NCORE):
        b = c // 4
        AT2[(c % 4) * 256:(c % 4 + 1) * 256, b] = r4[c]["oT"].reshape(256, S)
    DBG.update(AT2=AT2)
    AT2f = AT2.reshape(1024, B * S)
    Wo2 = f32(mla_w_out[0])
    common = dict(wo=np.stack([wtile(Wo2[:, c0:c0 + 128], KC) for c0 in range(0, 1024, 128)]),
                  gfin=gain_layout(f32(final_norm), KC))
    common.update(ffn_maps("fa", ffn2_norm[1], ffn2_w_gate[1], ffn2_w_up[1], ffn2_w_down[1]))
    in_maps = []
    for c in range(NCORE):
        m = dict(common)
        m.update(xT=x3T[c], aT=_tokT(AT2f[:, c * NTOK:(c + 1) * NTOK].T))
        in_maps.append(m)
    nc, _ = build_stage("L5")
    r5 = _run(nc, in_maps)
    out = np.zeros((B * S, D), np.float32)
    for c in range(NCORE):
        out[c * NTOK:(c + 1) * NTOK] = r5[c]["yT"].transpose(1, 0, 2).reshape(D, NTOK).T
    return out.reshape(B, S, D)
```

```python
import numpy as np
import concourse.bass as bass
import concourse.mybir as mybir
from concourse.bass_utils import run_bass_kernel_spmd

F32 = mybir.dt.float32
BF16 = mybir.dt.bfloat16
AF = mybir.ActivationFunctionType
ALU = mybir.AluOpType
AX = mybir.AxisListType


class SemObj:
    def __init__(self, nc, name):
        self.sem = nc.alloc_semaphore(name)
        self.name = name
        self.val = 0


class EngState:
    def __init__(self, nc, eng, name):
        self.e = eng
        self.name = name
        self.so = SemObj(nc, "sE_" + name)
        self.waited = {}


class Tile:
    def __init__(self, ctx, ap, name, dma_target=False):
        self.ap = ap
        self.name = name
        self.w = None
        self.r = {}
        self.dso = None
        self.ctx = ctx

    def dsem(self):
        if self.dso is None:
            self.dso = SemObj(self.ctx.nc, "sD_" + self.name)
        return self.dso

    def __getitem__(self, idx):
        return self.ap[idx]


class Ctx:
    def __init__(self, nc):
        self.nc = nc
        self.E = {n: EngState(nc, getattr(nc, n), n) for n in ["tensor", "vector", "scalar", "gpsimd", "sync"]}
        self.ntile = 0
        self.ninst = 0

    def sb(self, name, shape, dtype):
        self.ntile += 1
        return Tile(self, self.nc.alloc_sbuf_tensor(name, list(shape), dtype).ap(), name)

    def ps(self, name, shape=(128, 512), dtype=F32):
        self.ntile += 1
        return Tile(self, self.nc.alloc_psum_tensor(name, list(shape), dtype).ap(), name)

    def dram(self, name, shape, dtype, kind):
        t = Tile(self, self.nc.dram_tensor(name, list(shape), dtype, kind=kind).ap(), name)
        t.shape = tuple(shape)
        return t

    def _deps(self, E, reads, writes, waw=True):
        needs = {}

        def need(dep):
            if dep is None:
                return
            so, v = dep
            if needs.get(so, 0) < v:
                needs[so] = v

        for t in reads:
            need(t.w)
        for t in writes:
            if waw:
                need(t.w)
            for d in t.r.values():
                need(d)
        for so, v in needs.items():
            if so is E.so and E.name == "tensor":
                continue
            if E.waited.get(so, 0) >= v:
                continue
            E.e.wait_ge(so.sem, v)
            E.waited[so] = v

    def op(self, eng, fn, reads=(), writes=()):
        E = self.E[eng]
        self._deps(E, reads, writes)
        ins = fn(E.e)
        E.so.val += 1
        ins.then_inc(E.so.sem, 1)
        me = (E.so, E.so.val)
        for t in reads:
            t.r[E.so] = me
        for t in writes:
            t.w = me
            t.r = {}
        self.ninst += 1
        return ins

    def dma(self, eng, out_t, out_ap, in_t, in_ap, waw=True, **kw):
        E = self.E[eng]
        self._deps(E, [in_t], [out_t], waw=waw)
        so = out_t.dsem()
        ins = E.e.dma_start(out=out_ap, in_=in_ap, **kw)
        so.val += 16
        ins.then_inc(so.sem, 16)
        me = (so, so.val)
        in_t.r[so] = me
        out_t.w = me
        out_t.r = {}
        self.ninst += 1
        return ins

    def finish(self, out_tiles):
        E = self.E["sync"]
        for t in out_tiles:
            if t.w is not None:
                so, v = t.w
                E.e.wait_ge(so.sem, v)


NORM_EPS = 1e-6
D = 1024
KC = 8
FF = 2816
FC = 22
TT = 512


class Common:
    def __init__(self, ctx, norm=True):
        self.ctx = ctx
        self.psum = [ctx.ps(f"ps{i}") for i in range(8)]
        if norm:
            self.init_eps()
            self.ones = ctx.sb("ones_f32", (128, 128), F32)
            ctx.op("vector", lambda e: e.memset(self.ones[:], 1.0), writes=[self.ones])
            self.sq = [ctx.sb(f"sq{i}", (128, TT), F32) for i in range(2)]
            self.rstd = ctx.sb("rstd", (128, TT), F32)
            self.ssum = ctx.sb("ssum", (128, TT), F32)
        self.rr = 0

    def rmsnorm_T(self, xt, nk, gam, outT, n, pbank, width, out2=None):
        ctx = self.ctx
        ps = pbank
        ssum = self.ssum
        for kc in range(nk):
            sq = ssum if kc == 0 else self.sq[self.rr % 2]
            self.rr += 1
            ctx.op("scalar", lambda e, kc=kc, sq=sq: e.activation(out=sq[:, :n], in_=xt[:, kc, :n], func=AF.Square),
                   reads=[xt], writes=[sq])
            if kc > 0:
                ctx.op("vector", lambda e, sq=sq: e.tensor_tensor(out=ssum[:, :n], in0=ssum[:, :n], in1=sq[:, :n], op=ALU.add),
                       reads=[ssum, sq], writes=[ssum])
        ctx.op("tensor", lambda e: e.matmul(ps[:, :n], lhsT=self.ones[:], rhs=ssum[:, :n], start=True, stop=True),
               reads=[self.ones, ssum], writes=[ps])
        rstd = self.rstd
        ctx.op("scalar", lambda e: e.activation(out=rstd[:, :n], in_=ps[:, :n], func=AF.Sqrt,
                                                 bias=self.eps_t(), scale=1.0 / width),
               reads=[ps, self.eps_tile], writes=[rstd])
        ctx.op("vector", lambda e: e.reciprocal(out=rstd[:, :n], in_=rstd[:, :n]), reads=[rstd], writes=[rstd])
        for kc in range(nk):
            ctx.op("vector", lambda e, kc=kc: e.scalar_tensor_tensor(
                out=outT[:, kc, :n], in0=xt[:, kc, :n], scalar=gam[:, kc:kc + 1], in1=rstd[:, :n],
                op0=ALU.mult, op1=ALU.mult), reads=[xt, gam, rstd], writes=[outT])
            if out2 is not None:
                ctx.op("vector", lambda e, kc=kc: e.scalar_tensor_tensor(
                    out=out2[:, kc, :n], in0=xt[:, kc, :n], scalar=gam[:, kc:kc + 1], in1=rstd[:, :n],
                    op0=ALU.mult, op1=ALU.mult), reads=[xt, gam, rstd], writes=[out2])

    def eps_t(self):
        return self.eps_tile[:, 0:1]

    def init_eps(self):
        ctx = self.ctx
        self.eps_tile = ctx.sb("eps", (128, 1), F32)
        ctx.op("vector", lambda e: e.memset(self.eps_tile[:], NORM_EPS), writes=[self.eps_tile])


class FFN:
    def __init__(self, ctx, cm):
        self.ctx = ctx
        self.cm = cm
        self.hT = [ctx.sb(f"ffn_hT{i}", (128, KC, TT), BF16) for i in range(2)]
        self.wgu = [ctx.sb(f"ffn_wgu{i}", (128, 2, KC, 128), BF16) for i in range(3)]
        self.wd = [ctx.sb(f"ffn_wd{i}", (128, FC, 128), BF16) for i in range(2)]
        self.act = [ctx.sb(f"ffn_act{j}", (128, TT), BF16) for j in range(FC)]
        self.sg = [ctx.sb(f"ffn_sg{i}", (128, TT), F32) for i in range(2)]
        self.n = 0
        self.nw = 0
        self.nd = 0

    def run(self, xt, gam, wgu_d, wd_d, pb, sc=None, first=True):
        ctx, cm = self.ctx, self.cm
        hT = self.hT[self.n % 2]
        self.n += 1
        cm.rmsnorm_T(xt, KC, gam, hT, TT, pb[0], D)
        for j in range(FC):
            w = self.wgu[self.nw % 3]
            self.nw += 1
            if sc is None or first:
                ctx.dma("gpsimd", w, w[:], wgu_d, wgu_d[j], max_dma_last_dim=4096)
                if sc is not None:
                    ctx.dma("sync", sc[0], sc[0][j], w, w[:], waw=False, max_dma_last_dim=4096)
            else:
                ctx.dma("gpsimd", w, w[:], sc[0], sc[0][j], max_dma_last_dim=4096)
            pg = pb[1 + (j % 2)]
            pu = pb[3 + (j % 2)]
            for kc in range(KC):
                ctx.op("tensor", lambda e, kc=kc, w=w, pg=pg: e.matmul(pg[:], lhsT=w[:, 0, kc, :], rhs=hT[:, kc, :],
                                                                     start=(kc == 0), stop=(kc == KC - 1)),
                       reads=[w, hT], writes=[pg])
            for kc in range(KC):
                ctx.op("tensor", lambda e, kc=kc, w=w, pu=pu: e.matmul(pu[:], lhsT=w[:, 1, kc, :], rhs=hT[:, kc, :],
                                                                     start=(kc == 0), stop=(kc == KC - 1)),
                       reads=[w, hT], writes=[pu])
            sg = self.sg[j % 2]
            ctx.op("scalar", lambda e, sg=sg, pg=pg: e.activation(out=sg[:], in_=pg[:], func=AF.Silu),
                   reads=[pg], writes=[sg])
            a = self.act[j]
            ctx.op("vector", lambda e, sg=sg, pu=pu, a=a: e.tensor_tensor(out=a[:], in0=pu[:], in1=sg[:], op=ALU.mult),
                   reads=[pu, sg], writes=[a])
        for c in range(KC):
            w = self.wd[self.nd % 2]
            self.nd += 1
            if sc is None or first:
                ctx.dma("gpsimd", w, w[:], wd_d, wd_d[c], max_dma_last_dim=4096)
                if sc is not None:
                    ctx.dma("sync", sc[1], sc[1][c], w, w[:], waw=False, max_dma_last_dim=4096)
            else:
                ctx.dma("gpsimd", w, w[:], sc[1], sc[1][c], max_dma_last_dim=4096)
            po = pb[5 + (c % 2)]
            for j in range(FC):
                ctx.op("tensor", lambda e, j=j, w=w, po=po: e.matmul(po[:], lhsT=w[:, j, :], rhs=self.act[j][:],
                                                                     start=(j == 0), stop=(j == FC - 1)),
                       reads=[w, self.act[j]], writes=[po])
            ctx.op("vector", lambda e, c=c, po=po: e.scalar_tensor_tensor(
                out=xt[:, c, :], in0=po[:], scalar=0.5, in1=xt[:, c, :], op0=ALU.mult, op1=ALU.add),
                reads=[po, xt], writes=[xt])


def ffn_host_layout(wg, wu, wd):
    g = wg.reshape(KC, 128, FC, 128).transpose(2, 1, 0, 3)
    u = wu.reshape(KC, 128, FC, 128).transpose(2, 1, 0, 3)
    wgu = np.ascontiguousarray(np.stack([g, u], axis=2))
    wdt = np.ascontiguousarray(wd.reshape(FC, 128, KC, 128).transpose(2, 1, 0, 3))
    return wgu, wdt


def gain_layout(g, nk):
    return np.ascontiguousarray(g.reshape(nk, 128).T)

import ml_dtypes

NBF = ml_dtypes.bfloat16
NTOK = 4096
NSLOT = 8
S = 16384
ROPE_THETA = 500000.0


def wtile(W, nk):
    return np.ascontiguousarray(W.reshape(nk, 128, -1).transpose(1, 0, 2))


class Proj:
    def __init__(self, ctx, nslots=3):
        self.ctx = ctx
        self.w = [ctx.sb(f"pw{i}", (128, KC, 128), BF16) for i in range(nslots)]
        self.n = 0
        self.sc = {}
        self.first = True

    def mm(self, w_d, idx, nk, M, rhsT, ps, n=TT):
        ctx = self.ctx
        w = self.w[self.n % len(self.w)]
        self.n += 1
        sc = self.sc.get(w_d.name)
        if sc is None:
            sc = self.sc[w_d.name] = ctx.dram("sc_" + w_d.name, w_d.shape, BF16, "Internal")
        if self.first:
            ctx.dma("gpsimd", w, w[:, :nk, :M], w_d, w_d[idx])
            ctx.dma("sync", sc, sc[idx], w, w[:, :nk, :M], waw=False)
        else:
            ctx.dma("gpsimd", w, w[:, :nk, :M], sc, sc[idx])
        for kc in range(nk):
            ctx.op("tensor", lambda e, kc=kc: e.matmul(ps[:M, :n], lhsT=w[:, kc, :M], rhs=rhsT[:, kc, :n],
                                                       start=(kc == 0), stop=(kc == nk - 1)),
                   reads=[w, rhsT], writes=[ps])


def rope_combine(ctx, out_t, out_ap, p1, p2, ct, c_ap, st, s_ap, tmp, M, n=TT):
    t1, t2 = tmp
    ctx.op("vector", lambda e: e.tensor_tensor(out=t1[:M, :n], in0=p1[:M, :n], in1=c_ap, op=ALU.mult),
           reads=[p1, ct], writes=[t1])
    ctx.op("vector", lambda e: e.tensor_tensor(out=t2[:M, :n], in0=p2[:M, :n], in1=s_ap, op=ALU.mult),
           reads=[p2, st], writes=[t2])
    ctx.op("vector", lambda e: e.tensor_tensor(out=out_ap, in0=t1[:M, :n], in1=t2[:M, :n], op=ALU.add),
           reads=[t1, t2], writes=[out_t])


EV_ROPE = [True] * 4 + [True, False, True, False, True, False] + [True] * 4 + [True] * 4 + [False] * 4
EV_COLS = list(range(0, 1280, 128)) + list(range(1304, 2840, 128))


def build_stage(kind):
    nc = bass.Bass("TRN2", target_bir_lowering=False)
    ctx = Ctx(nc)
    cm = Common(ctx)
    ffn = FFN(ctx, cm)
    pj = Proj(ctx)
    pb = cm.psum
    D_ = {}

    def din(name, shape, dt=F32):
        D_[name] = ctx.dram(name, shape, dt, "ExternalInput")
        return D_[name]

    def dout(name, shape, dt=F32):
        D_[name] = ctx.dram(name, shape, dt, "ExternalOutput")
        return D_[name]

    xT = din("xT", (128, KC, NTOK))
    outs = []
    gams = {}

    def load_gam(name, nk=KC):
        d = din(name, (128, nk))
        t = ctx.sb("sb_" + name, (128, nk), F32)
        ctx.dma("sync", t, t[:], d, d[:])
        gams[name] = t
        return t

    def ffn_in(pref):
        return (load_gam(pref + "_g"), din(pref + "_wgu", (FC, 128, 2, KC, 128)), din(pref + "_wd", (KC, 128, FC, 128)),
                (ctx.dram("sc_" + pref + "_wgu", (FC, 128, 2, KC, 128), BF16, "Internal"),
                 ctx.dram("sc_" + pref + "_wd", (KC, 128, FC, 128), BF16, "Internal")))

    xts = [ctx.sb(f"xt{i}", (128, KC, TT), F32) for i in range(2)]
    tmp = [ctx.sb(f"tmp{i}", (128, TT), F32) for i in range(2)]
    hTs = [ctx.sb(f"hmix{i}", (128, KC, TT), BF16) for i in range(2)]
    if kind in ("L3", "L5"):
        aT = din("aT", (128, KC, NTOK), BF16)
        wo = din("wo", (KC, 128, KC, 128))
        ats = [ctx.sb(f"at{i}", (128, KC, TT), BF16) for i in range(2)]
    if kind == "L1":
        fa = ffn_in("fa")
        gm = load_gam("gmix")
        win = din("win", (22, 128, KC, 128))
        wgt = din("wgt", (128, KC, 24))
        ctab = din("ctab", (128, NTOK))
        stab = din("stab", (128, NTOK))
        x1T = dout("x1T", (128, KC, NTOK))
        pjo = dout("pj", (22, 128, NTOK), BF16)
        gto = dout("gates", (24, NTOK))
        outs = [x1T, pjo, gto]
        cts = [ctx.sb(f"ct{i}", (128, TT), F32) for i in range(2)]
        sts = [ctx.sb(f"st{i}", (128, TT), F32) for i in range(2)]
        obs = [ctx.sb(f"ob{i}", (128, TT), BF16) for i in range(3)]
        gos = [ctx.sb(f"go{i}", (24, TT), F32) for i in range(2)]
        hT32 = ctx.sb("hT32", (128, KC, TT), F32)
        w32 = [ctx.sb(f"w32_{i}", (128, KC, 128), F32) for i in range(2)]
        ob32s = [ctx.sb(f"ob32_{i}", (128, TT), F32) for i in range(2)]
        q32o = dout("q32", (5, 128, NTOK), F32)
        outs.append(q32o)
        n32 = [0]
        permd = din("permT", (128, 128))
        permT = ctx.sb("sb_permT", (128, 128), F32)
        ctx.dma("sync", permT, permT[:], permd, permd[:])
        p1sb = [ctx.sb(f"p1sb{i}", (128, TT), F32) for i in range(2)]

        def mm32(w_d, w_ap, ps):
            w = w32[n32[0] % 2]
            n32[0] += 1
            ctx.dma("sync", w, w[:], w_d, w_ap)
            for kc in range(KC):
                ctx.op("tensor", lambda e, kc=kc: e.matmul(ps[:], lhsT=w[:, kc, :], rhs=hT32[:, kc, :],
                                                           start=(kc == 0), stop=(kc == KC - 1)),
                       reads=[w, hT32], writes=[ps])
    if kind == "L3":
        fa = ffn_in("fa")
        fb = ffn_in("fb")
        gm = load_gam("gmix")
        gq = load_gam("gq", 2)
        gkv = load_gam("gkv", 1)
        wmi = din("wmi", (3, 128, KC, 128))
        wkr = din("wkr", (2, 128, KC, 32))
        wuq = din("wuq", (16, 128, 2, 96))
        wuqs = din("wuqs", (16, 128, 2, 96))
        wukv = din("wukv", (16, 128, 1, 128))
        cq_t = din("cq_t", (96, NTOK))
        sq_t = din("sq_t", (96, NTOK))
        ck_t = din("ck_t", (32, NTOK))
        sk_t = din("sk_t", (32, NTOK))
        x3T = dout("x3T", (128, KC, NTOK))
        qTo = dout("qT", (16, 96, NTOK), BF16)
        kvo = dout("kvT", (16, 128, NTOK), BF16)
        kro = dout("krT", (32, NTOK), BF16)
        outs = [x3T, qTo, kvo, kro]
        cts = [ctx.sb(f"ct{i}", (96, TT), F32) for i in range(2)]
        sts = [ctx.sb(f"st{i}", (96, TT), F32) for i in range(2)]
        ckts = [ctx.sb(f"ckt{i}", (32, TT), F32) for i in range(2)]
        skts = [ctx.sb(f"skt{i}", (32, TT), F32) for i in range(2)]
        obs = [ctx.sb(f"ob{i}", (128, TT), BF16) for i in range(3)]
        cqT = [ctx.sb(f"cqT{i}", (128, 2, TT), F32) for i in range(2)]
        ckvT = [ctx.sb(f"ckvT{i}", (128, 1, TT), F32) for i in range(2)]
        cqn = [ctx.sb(f"cqn{i}", (128, 2, TT), BF16) for i in range(2)]
        ckvn = [ctx.sb(f"ckvn{i}", (128, 1, TT), BF16) for i in range(2)]
    if kind == "L5":
        fa = ffn_in("fa")
        gf = load_gam("gfin")
        yT = dout("yT", (128, KC, NTOK))
        outs = [yT]
        yts = [ctx.sb(f"yt{i}", (128, KC, TT), F32) for i in range(2)]

    nob = 0
    for t in range(NSLOT):
        ts = slice(t * TT, (t + 1) * TT)
        xt = xts[t % 2]
        pj.first = (t == 0)
        ctx.dma("sync", xt, xt[:], xT, xT[:, :, ts])
        if kind in ("L3", "L5"):
            at = ats[t % 2]
            ctx.dma("sync", at, at[:], aT, aT[:, :, ts])
            for c in range(KC):
                ps = pb[5 + (c % 2)]
                pj.mm(wo, c, KC, 128, at, ps)
                ctx.op("vector", lambda e, c=c, ps=ps: e.tensor_tensor(out=xt[:, c, :], in0=ps[:], in1=xt[:, c, :], op=ALU.add),
                       reads=[ps, xt], writes=[xt])
        if kind == "L1":
            ffn.run(xt, fa[0], fa[1], fa[2], pb[0:7], sc=fa[3], first=(t == 0))
            ctx.dma("sync", x1T, x1T[:, :, ts], xt, xt[:])
            hT = hTs[t % 2]
            cm.rmsnorm_T(xt, KC, gm, hT, TT, pb[0], D, out2=hT32)
            ct, st = cts[t % 2], sts[t % 2]
            ctx.dma("sync", ct, ct[:], ctab, ctab[:, ts])
            ctx.dma("sync", st, st[:], stab, stab[:, ts])
            deferred = []
            for c in range(22):
                p1 = pb[1 + (c % 2)]
                ob = obs[nob % 3]
                nob += 1
                is32 = c < 5
                if is32:
                    mm32(win, win[c], p1)
                else:
                    pj.mm(win, c, KC, 128, hT, p1)
                for fn in deferred:
                    fn()
                deferred = []
                if not EV_ROPE[c]:
                    ctx.op("scalar", lambda e, p1=p1, ob=ob: e.activation(out=ob[:], in_=p1[:], func=AF.Copy),
                           reads=[p1], writes=[ob])
                    ctx.dma("sync", pjo, pjo[c, :, ts], ob, ob[:])
                    continue
                p1s = p1sb[c % 2]
                ctx.op("scalar", lambda e, p1=p1, p1s=p1s: e.activation(out=p1s[:], in_=p1[:], func=AF.Copy),
                       reads=[p1], writes=[p1s])

                def fin(c=c, p1s=p1s, ob=ob, is32=is32):
                    p2 = pb[3 + (c % 2)]
                    ctx.op("tensor", lambda e: e.matmul(p2[:], lhsT=permT[:], rhs=p1s[:], start=True, stop=True),
                           reads=[permT, p1s], writes=[p2])
                    if is32:
                        ob32 = ob32s[c % 2]
                        rope_combine(ctx, ob32, ob32[:], p1s, p2, ct, ct[:], st, st[:], tmp, 128)
                        ctx.op("scalar", lambda e: e.activation(out=ob[:], in_=ob32[:], func=AF.Copy),
                               reads=[ob32], writes=[ob])
                        ctx.dma("sync", q32o, q32o[c, :, ts], ob32, ob32[:])
                    else:
                        rope_combine(ctx, ob, ob[:], p1s, p2, ct, ct[:], st, st[:], tmp, 128)
                    ctx.dma("sync", pjo, pjo[c, :, ts], ob, ob[:])
                deferred.append(fin)
            for fn in deferred:
                fn()
            p1 = pb[7]
            pj.mm(wgt, slice(None), KC, 24, hT, p1)
            go = gos[t % 2]
            ctx.op("scalar", lambda e, p1=p1, go=go: e.activation(out=go[:], in_=p1[:24, :], func=AF.Sigmoid),
                   reads=[p1], writes=[go])
            ctx.dma("sync", gto, gto[:, ts], go, go[:])
        if kind == "L3":
            ffn.run(xt, fa[0], fa[1], fa[2], pb[0:7], sc=fa[3], first=(t == 0))
            ffn.run(xt, fb[0], fb[1], fb[2], pb[0:7], sc=fb[3], first=(t == 0))
            ctx.dma("sync", x3T, x3T[:, :, ts], xt, xt[:])
            hT = hTs[t % 2]
            cm.rmsnorm_T(xt, KC, gm, hT, TT, pb[0], D)
            cq, ckv, cqn_, ckvn_ = cqT[t % 2], ckvT[t % 2], cqn[t % 2], ckvn[t % 2]
            for i in range(3):
                p1 = pb[1 + (i % 2)]
                pj.mm(wmi, i, KC, 128, hT, p1)
                dst_t, dst = (cq, cq[:, i, :]) if i < 2 else (ckv, ckv[:, 0, :])
                ctx.op("scalar", lambda e, p1=p1, dst=dst: e.activation(out=dst, in_=p1[:], func=AF.Copy),
                       reads=[p1], writes=[dst_t])
            ckt, skt = ckts[t % 2], skts[t % 2]
            ctx.dma("sync", ckt, ckt[:], ck_t, ck_t[:, ts])
            ctx.dma("sync", skt, skt[:], sk_t, sk_t[:, ts])
            p1, p2 = pb[3], pb[4]
            pj.mm(wkr, 0, KC, 32, hT, p1)
            pj.mm(wkr, 1, KC, 32, hT, p2)
            ob = obs[nob % 3]
            nob += 1
            rope_combine(ctx, ob, ob[:32, :], p1, p2, ckt, ckt[:], skt, skt[:], tmp, 32)
            ctx.dma("sync", kro, kro[:, ts], ob, ob[:32, :])
            cm.rmsnorm_T(cq, 2, gq, cqn_, TT, pb[0], 256)
            cm.rmsnorm_T(ckv, 1, gkv, ckvn_, TT, pb[0], 128)
            ct, st = cts[t % 2], sts[t % 2]
            ctx.dma("sync", ct, ct[:], cq_t, cq_t[:, ts])
            ctx.dma("sync", st, st[:], sq_t, sq_t[:, ts])
            for h in range(16):
                p1 = pb[1 + (h % 2)]
                p2 = pb[3 + (h % 2)]
                pj.mm(wuq, h, 2, 96, cqn_, p1)
                pj.mm(wuqs, h, 2, 96, cqn_, p2)
                ob = obs[nob % 3]
                nob += 1
                rope_combine(ctx, ob, ob[:96, :], p1, p2, ct, ct[:], st, st[:], tmp, 96)
                ctx.dma("sync", qTo, qTo[h, :, ts], ob, ob[:96, :])
            for h in range(16):
                p1 = pb[5 + (h % 2)]
                pj.mm(wukv, h, 1, 128, ckvn_, p1)
                ob = obs[nob % 3]
                nob += 1
                ctx.op("scalar", lambda e, p1=p1, ob=ob: e.activation(out=ob[:], in_=p1[:], func=AF.Copy),
                       reads=[p1], writes=[ob])
                ctx.dma("sync", kvo, kvo[h, :, ts], ob, ob[:])
        if kind == "L5":
            ffn.run(xt, fa[0], fa[1], fa[2], pb[0:7], sc=fa[3], first=(t == 0))
            yt = yts[t % 2]
            cm.rmsnorm_T(xt, KC, gf, yt, TT, pb[0], D)
            ctx.dma("sync", yT, yT[:, :, ts], yt, yt[:])
    ctx.finish(outs)
    return nc, ctx


NEG = -30000.0
BIG = 1e30
NQB = 32


class Attn:
    def __init__(self, ctx, cm, scale, consts, ns3=False, sbanks=None, lbanks=None):
        self.ctx, self.cm, self.scale = ctx, cm, scale
        self.S = sbanks if sbanks else [cm.psum[0], cm.psum[1]] + ([cm.psum[7]] if ns3 else [])
        self.lag = len(self.S) - 1
        self.pending = []
        self.LB = lbanks if lbanks else [cm.psum[5], cm.psum[6]]
        self.P = [ctx.sb(f"P{i}", (128, 512), BF16) for i in range(4)]
        self.OL = [ctx.sb(f"OL{i}", (128, 512), F32) for i in range(2)]
        self.rl = [ctx.sb(f"rl{i}", (64, 512), F32) for i in range(2)]
        self.ident = ctx.sb("sb_ident", (128, 128), BF16)
        self.sel = ctx.sb("sb_sel", (128, 64), F32)
        ctx.dma("sync", self.ident, self.ident[:], consts["ident"], consts["ident"][:])
        ctx.dma("sync", self.sel, self.sel[:], consts["sel"], consts["sel"][:])
        self.i = 0
        self.j = 0

    def step(self, nk, c0, c1, mains, masks, pvs):
        ctx = self.ctx
        S = self.S[self.i % len(self.S)]
        P = self.P[self.i % 4]
        self.i += 1
        allm = list(mains) + list(masks)
        n = len(allm)
        for k, (lt, lap, rt, rap, cs) in enumerate(allm):
            ctx.op("tensor", lambda e, lap=lap, rap=rap, cs=cs, k=k: e.matmul(
                S[:nk, cs], lhsT=lap, rhs=rap, start=(k == 0), stop=(k == n - 1), skip_group_check=True),
                reads=[lt, rt], writes=[S])
        ctx.op("scalar", lambda e: e.activation(out=P[:nk, c0:c1], in_=S[:nk, c0:c1], func=AF.Exp, scale=self.scale),
               reads=[S], writes=[P])
        self.pending.append((nk, P, pvs))
        while len(self.pending) > self.lag:
            self._flush_one()

    def _flush_one(self):
        ctx = self.ctx
        nk, P, pvs = self.pending.pop(0)
        for (vt, vap, pcs, acc, acc_ap, start) in pvs:
            ctx.op("tensor", lambda e, vap=vap, pcs=pcs, acc_ap=acc_ap, start=start: e.matmul(
                acc_ap, lhsT=vap, rhs=P[:nk, pcs], start=start, stop=True, skip_group_check=True),
                reads=[vt, P], writes=[acc])

    def flush(self):
        while self.pending:
            self._flush_one()

    def finish(self, acc):
        self.flush()
        ctx = self.ctx
        OL = self.OL[self.j % 2]
        rl = self.rl[self.j % 2]
        LB = self.LB[self.j % len(self.LB)]
        self.j += 1
        ctx.op("scalar", lambda e: e.activation(out=OL[:], in_=acc[:], func=AF.Copy), reads=[acc], writes=[OL])
        ctx.op("tensor", lambda e: e.matmul(LB[:64, :], lhsT=self.sel[:], rhs=OL[:], start=True, stop=True),
               reads=[self.sel, OL], writes=[LB])
        ctx.op("vector", lambda e: e.tensor_scalar(out=rl[:], in0=LB[:64, :], scalar1=1e-30, scalar2=None, op0=ALU.max),
               reads=[LB], writes=[rl])
        ctx.op("vector", lambda e: e.reciprocal(out=rl[:], in_=rl[:]), reads=[rl], writes=[rl])
        return OL, rl


def attn_consts_np():
    ident = np.eye(128, dtype=np.float32).astype(NBF)
    sel = np.zeros((128, 64), np.float32)
    sel[64 + np.arange(64), np.arange(64)] = 1.0
    kl = np.arange(128)[:, None]
    ql = np.arange(512)[None, :]
    mc = np.stack([np.where(ql >= kl + o, 0.0, NEG) for o in (0, 128, 256, 384)], axis=1)
    return {"ident": ident, "sel": sel, "mcausal": mc.astype(NBF)}


def build_mla():
    nc = bass.Bass("TRN2", target_bir_lowering=False)
    ctx = Ctx(nc)
    cm = Common(ctx, norm=False)
    NU = 4
    qT = ctx.dram("qT", (NU, 96, S), BF16, "ExternalInput")
    kT = ctx.dram("kT", (NU, 96, S), BF16, "ExternalInput")
    v1 = ctx.dram("v1", (NU, 128, 128, 128), BF16, "ExternalInput")
    cd = {"ident": ctx.dram("ident", (128, 128), BF16, "ExternalInput"),
          "sel": ctx.dram("sel", (128, 64), F32, "ExternalInput")}
    mcd = ctx.dram("mcausal", (128, 4, 512), BF16, "ExternalInput")
    oT = ctx.dram("oT", (NU, 64, S), BF16, "ExternalOutput")
    at = Attn(ctx, cm, 96 ** -0.5, cd, ns3=True)
    mc = ctx.sb("mc", (128, 4, 512), BF16)
    ctx.dma("sync", mc, mc[:], mcd, mcd[:])
    Kb = [ctx.sb(f"Kb{i}", (96, S), BF16) for i in range(2)]
    Vb = [ctx.sb(f"Vb{i}", (128, 128, 128), BF16) for i in range(2)]
    Qb = [ctx.sb(f"Qb{i}", (96, 512), BF16) for i in range(3)]
    Ob = [ctx.sb(f"Ob{i}", (64, 512), BF16) for i in range(2)]
    acc = [cm.psum[2], cm.psum[3]]
    n = 0
    for u in range(NU):
        K, V = Kb[u % 2], Vb[u % 2]
        ctx.dma("sync", K, K[:], kT, kT[u])
        ctx.dma("sync", V, V[:], v1, v1[u])
        for qb in range(NQB):
            Q = Qb[n % 3]
            A = acc[n % 2]
            O = Ob[n % 2]
            n += 1
            ctx.dma("sync", Q, Q[:], qT, qT[u, :, qb * 512:(qb + 1) * 512])
            nkt = 4 * qb + 4
            for kt in range(nkt):
                d = kt - 4 * qb
                c0 = d * 128 if d > 0 else 0
                mains = [(K, K[:, kt * 128:(kt + 1) * 128], Q, Q[:, c0:512], slice(c0, 512))]
                masks = []
                if d >= 0:
                    masks = [(at.ident, at.ident[:], mc, mc[:, d, c0:512], slice(c0, 512))]
                pvs = [(V, V[:, kt, :], slice(c0, 512), A, A[:, c0:512], kt == 0)]
                at.step(128, c0, 512, mains, masks, pvs)
            OL, rl = at.finish(A)
            ctx.op("vector", lambda e, OL=OL, rl=rl, O=O: e.tensor_tensor(out=O[:], in0=OL[:64, :], in1=rl[:], op=ALU.mult),
                   reads=[OL, rl], writes=[O])
            ctx.dma("sync", oT, oT[u, :, qb * 512:(qb + 1) * 512], O, O[:])
    ctx.finish([oT])
    return nc, ctx


def nsa_consts_np():
    c = attn_consts_np()
    kl = np.arange(128)[:, None]
    ql = np.arange(512)[None, :]
    d = ql - kl
    c["mwin"] = np.stack([np.where((d - o >= 0) & (d - o < 512), 0.0, NEG) for o in range(-512, 512, 128)], 1).astype(NBF)
    c["mcmp"] = np.stack([np.where(ql - 16 * kl >= 31 - 512 * dl, 0.0, NEG) for dl in range(5)], 1).astype(NBF)
    q = np.arange(128)[:, None]
    npr = np.arange(-1, 8)[None, :]
    c["mtm"] = np.where(q >= 31 + 16 * npr, 0.0, NEG).astype(NBF)
    lo = (np.arange(128) < 64)[:, None]
    c["mul3"] = np.where(lo, np.array([[0., 0., 0.]]), np.array([[1., 0., 0.]])).astype(np.float32)
    c["add3"] = np.where(lo, np.array([[BIG, BIG, -BIG]]), np.array([[0., BIG, BIG]])).astype(np.float32)
    c["identf"] = np.eye(128, dtype=np.float32)
    sg = np.zeros((6, 6, 64), np.float32)
    for r in range(6):
        sg[r, r, :] = 1.0
    c["selg"] = sg
    return c


NSA_CONST_SHAPES = {"ident": ((128, 128), BF16), "sel": ((128, 64), F32), "mcausal": ((128, 4, 512), BF16),
                    "mwin": ((128, 8, 512), BF16), "mcmp": ((128, 5, 512), BF16),
                    "mtm": ((128, 9), BF16), "mul3": ((128, 3), F32), "add3": ((128, 3), F32),
                    "identf": ((128, 128), F32), "selg": ((6, 6, 64), F32)}


USE32 = True
DBG_SKIP = set()


def build_nsa(nqb=NQB):
    nc = bass.Bass("TRN2", target_bir_lowering=False)
    ctx = Ctx(nc)
    cm = Common(ctx, norm=False)
    pb = cm.psum
    din = lambda n, s, dt=BF16: ctx.dram(n, s, dt, "ExternalInput")
    qg = din("qg", (128, 2, S), F32 if USE32 else BF16)
    qmy = din("qmy", (128, S))
    kcraw = din("kcraw", (64, 16, 1024), F32)
    vcraw = din("vcraw", (64, S))
    w1k = din("w1k", (128, 32, 128), F32)
    w1v = din("w1v", (64, 32, 128), F32)
    posk = din("posk", (128, 32, 8), F32)
    posv = din("posv", (64, 32), F32)
    w2k = din("w2k", (128, 128), F32)
    w2v = din("w2v", (128, 64), F32)
    ksT = din("ksT", (128, S))
    vs1 = din("vs1", (128, 128, 128))
    kwT = din("kwT", (128, 512 + S))
    vw1 = din("vw1", (128, 132, 128))
    gat = din("gat", (6, S), F32)
    cd = {k: din(k, s, dt) for k, (s, dt) in NSA_CONST_SHAPES.items()}
    oA = ctx.dram("oA", (2, 64, S), BF16, "ExternalOutput")
    at = Attn(ctx, cm, 0.125, cd, sbanks=[pb[0], pb[1], pb[5]], lbanks=[pb[6]])

    def cload(name, eng="sync"):
        s, dt = NSA_CONST_SHAPES[name]
        t = ctx.sb("c_" + name, s, dt)
        ctx.dma(eng, t, t[:], cd[name], cd[name][:])
        return t
    mc, mwin, mcmp, mtm, mul3, add3, identf = [cload(n) for n in
                                               ("mcausal", "mwin", "mcmp", "mtm", "mul3", "add3", "identf")]
    selg = ctx.sb("c_selg", (6, 6 * 64), F32)
    ctx.dma("sync", selg, selg[:], cd["selg"], cd["selg"].ap.rearrange("a b c -> a (b c)"))
    bigA = ctx.sb("bigA", (128, S), BF16)
    bigB = ctx.sb("bigB", (128, S), BF16)
    bigB3 = bigB.ap.rearrange("p (t c) -> p t c", c=128)
    kcT = ctx.sb("kcT", (128, 1024), BF16)
    vc1 = ctx.sb("vc1", (128, 8, 128), BF16)
    kcT32 = ctx.sb("kcT32", (128, 1024), F32)
    ctx.op("gpsimd", lambda e: e.memset(bigA[:], 0.0), writes=[bigA])
    bigA32 = bigA.ap.bitcast(F32)
    k32buf = bigA32[:, 0:2080].rearrange("p (j m) -> p j m", m=130)
    w1k32 = bigA32[:, 2080:2080 + 4096].rearrange("p (l h) -> p l h", h=128)
    pos32 = ctx.sb("pos32", (128, 32, 8), F32)
    w2k32 = ctx.sb("w2k32", (128, 128), F32)
    hid32 = [ctx.sb(f"hid32_{i}", (128, 128), F32) for i in range(2)]
    posb = ctx.sb("posb", (128, 1), F32)
    ctx.dma("sync", bigA, w1k32, w1k, w1k[:])
    ctx.dma("sync", pos32, pos32[:], posk, posk[:])
    ctx.dma("sync", w2k32, w2k32[:], w2k, w2k[:])
    ps = pb[7]
    for l in range(32 if "posb" not in DBG_SKIP else 1):
        ctx.op("tensor", lambda e, l=l: e.matmul(ps[:, 0:8], lhsT=w1k32[:, l, :], rhs=pos32[:, l, :],
                                                 start=(l == 0), stop=(l == 31)), reads=[bigA, pos32], writes=[ps])
    ctx.op("vector", lambda e: e.tensor_copy(out=posb[:], in_=ps[:, 0:1]), reads=[ps], writes=[posb])
    for p in range(8 if "kpath" not in DBG_SKIP else 0):
        nm = min(130, 1024 - 128 * p)
        ctx.dma("sync", bigA, k32buf[0:64, :, 0:nm], kcraw, kcraw[:, :, 128 * p:128 * p + nm])
        ps = pb[p % 2]
        for l in range(32):
            a_, j_ = l // 16, l % 16
            ctx.op("tensor", lambda e, l=l: e.matmul(ps[:, 0:128], lhsT=w1k32[:, l, :], rhs=k32buf[:, j_, a_:a_ + 128],
                                                     start=(l == 0), stop=(l == 31)), reads=[bigA], writes=[ps])
        h32 = hid32[p % 2]
        ctx.op("scalar", lambda e: e.activation(out=h32[:], in_=ps[:, 0:128], func=AF.Silu, bias=posb[:, 0:1]),
               reads=[ps, posb], writes=[h32])
        p2 = pb[2 + (p % 2)]
        ctx.op("tensor", lambda e: e.matmul(p2[:, 0:128], lhsT=w2k32[:], rhs=h32[:], start=True, stop=True),
               reads=[w2k32, h32], writes=[p2])
        ctx.op("vector", lambda e: e.tensor_copy(out=kcT32[:, p * 128:(p + 1) * 128], in_=p2[:, 0:128]), reads=[p2], writes=[kcT32])
        ctx.op("scalar", lambda e: e.activation(out=kcT[:, p * 128:(p + 1) * 128], in_=kcT32[:, p * 128:(p + 1) * 128], func=AF.Copy), reads=[kcT32], writes=[kcT])
    ctx.dma("sync", bigB, bigB[0:64, :], vcraw, vcraw[:])
    w1s_ap = bigA[0:64, 0:4096].rearrange("p (l h) -> p l h", h=128)
    poss = ctx.sb("poss", (64, 32), BF16)
    w2vs = ctx.sb("w2vs", (128, 64), BF16)
    ctx.dma("gpsimd", w2vs, w2vs[:], w2v, w2v[:])
    hid = [ctx.sb(f"hid{i}", (128, 512), BF16) for i in range(2)]
    ctx.op("vector", lambda e: e.memset(hid[1][:], 0.0), writes=[hid[1]])
    ctx.op("vector", lambda e: e.memset(vc1[:], 1.0), writes=[vc1])
    ctx.dma("gpsimd", bigA, w1s_ap, w1v, w1v[:])
    ctx.dma("gpsimd", poss, poss[:], posv, posv[:])
    ps = pb[7]
    for l in range(32):
        ctx.op("tensor", lambda e, l=l: e.matmul(ps[:, 0:1], lhsT=w1s_ap[:, l, :], rhs=poss[:, l:l + 1],
                                                 start=(l == 0), stop=(l == 31)), reads=[bigA, poss], writes=[ps])
    posbv = ctx.sb("posbv", (128, 1), F32)
    ctx.op("vector", lambda e: e.tensor_copy(out=posbv[:], in_=ps[:, 0:1]), reads=[ps], writes=[posbv])
    for nt in range(2 if "vpath" not in DBG_SKIP else 0):
        ncol = 512 if nt == 0 else 511
        ps = pb[nt]
        for l in range(32):
            st_ = nt * 8192 + l
            en = min(S, st_ + 16 * ncol)
            ctx.op("tensor", lambda e, l=l: e.matmul(ps[:, 0:ncol], lhsT=w1s_ap[:, l, :], rhs=bigB[0:64, st_:en:16],
                                                     start=(l == 0), stop=(l == 31)), reads=[bigA, bigB], writes=[ps])
        ctx.op("scalar", lambda e: e.activation(out=hid[nt][:, 0:ncol], in_=ps[:, 0:ncol], func=AF.Silu, bias=posbv[:, 0:1]),
               reads=[ps, posbv], writes=[hid[nt]])
        for j in range(4):
            p2 = pb[2 + (j % 2)]
            ctx.op("tensor", lambda e: e.matmul(p2[:, 0:64], lhsT=hid[nt][:, j * 128:(j + 1) * 128], rhs=w2vs[:], start=True, stop=True),
                   reads=[w2vs, hid[nt]], writes=[p2])
            ctx.op("vector", lambda e: e.tensor_copy(out=vc1[:, nt * 4 + j, 0:64], in_=p2[:, 0:64]), reads=[p2], writes=[vc1])
    ctx.dma("sync", bigA, bigA[:], ksT, ksT[:])
    ctx.dma("sync", bigB, bigB[:], vs1, vs1.ap.rearrange("p t c -> p (t c)"))
    QDT = F32 if USE32 else BF16
    Qg1 = ctx.sb("Qg0", (128, 4, 512), QDT)
    Qgs = [Qg1, Qg1]
    Qms = [[[ctx.sb(f"QY{i}_{h}_{c}", (128, 512), BF16) for c in range(4)] for h in range(2)] for i in range(2)]
    ctx.op("gpsimd", lambda e: e.memset(Qg1[:], 0.0), writes=[Qg1])
    for i in range(2):
        for h in range(2):
            for c in range(4):
                ctx.op("gpsimd", lambda e: e.memset(Qms[i][h][c][:], 0.0), writes=[Qms[i][h][c]])
    wKs = [ctx.sb(f"wK{i}", (128, 1024), BF16) for i in range(2)]
    wVs = [ctx.sb(f"wV{i}", (128, 8, 128), BF16) for i in range(2)]
    gts = [ctx.sb(f"gt{i}", (6, 512), F32) for i in range(2)]
    NE = 4
    es = [ctx.sb(f"e{i}", (128, 512), F32) for i in range(NE)]
    lp = [ctx.sb(f"lp{i}", (128, 2), F32) for i in range(2)]
    rlh = [ctx.sb(f"rlh{i}", (128, 1), F32) for i in range(2)]
    Aim = ctx.sb("Aim", (128, 1024), F32)
    I1 = ctx.sb("I1", (128, 256), F32)
    I2 = ctx.sb("I2", (128, 256), F32)
    m8 = ctx.sb("m8", (128, 16), F32)
    negms = [ctx.sb(f"negm{i}", (128, 320), F32) for i in range(4)]
    fg = ctx.sb("fg", (64, 512), F32)
    tmpc = ctx.sb("tmpc", (64, 512), F32)
    accsb = ctx.sb("accsb", (64, 512), F32)
    Ob = [ctx.sb(f"Ob{i}", (64, 512), BF16) for i in range(2)]
    accC, accS, accW, GB = pb[2], pb[3], pb[4], pb[7]
    kq = kcT32 if USE32 else kcT
    st = {"ne": 0, "no": 0}

    def load(qb):
        T0 = qb * 512
        if qb >= 2:
            for hh in range(4):
                r0 = (hh % 2) * 64
                ctx.dma("sync", Qgs[qb % 2], Qgs[qb % 2][0:64, hh, :], qg, qg[r0:r0 + 64, hh // 2, T0:T0 + 512])
        for h in range(2):
            for c in range(qb // 8 + 1):
                t_ = Qms[qb % 2][h][c]
                ctx.dma("sync", t_, t_[0:64, :], qmy, qmy[h * 64:(h + 1) * 64, T0:T0 + 512])
        ctx.dma("sync", wKs[qb % 2], wKs[qb % 2][:], kwT, kwT[:, T0:T0 + 1024])
        ctx.dma("sync", wVs[qb % 2], wVs[qb % 2][:], vw1, vw1[:, 4 * qb:4 * qb + 8, :])
        ctx.dma("sync", gts[qb % 2], gts[qb % 2][:], gat, gat[:, T0:T0 + 512])

    def phase1a(qb, subs=(0, 1, 2, 3)):
        if qb < 2:
            return
        Qg = Qgs[qb % 2]
        for qsl in subs:
            qs = 4 * qb + qsl
            ncols = 8 * qs + 8
            nj = 2 * qs + 2
            negm = negms[qsl]
            halves = [(lo, min(ncols, lo + 512)) for lo in (0, 512) if lo < ncols]
            for hh in range(4):
                ch, r0 = hh // 2, (hh % 2) * 64
                lpt = lp[hh % 2]
                rl1 = rlh[hh % 2]
                ehs = []
                for hi_, (lo, hi) in enumerate(halves):
                    w = hi - lo
                    Sb = at.S[at.i % len(at.S)]
                    at.i += 1
                    a, b2 = max(lo, ncols - 9, 0), min(hi, ncols)
                    hasm = b2 > a
                    ctx.op("tensor", lambda e: e.matmul(
                        Sb[:, 0:w], lhsT=Qg[:, hh, qsl * 128:(qsl + 1) * 128], rhs=kq[:, lo:hi],
                        start=True, stop=(not hasm), skip_group_check=True), reads=[Qg, kq], writes=[Sb])
                    if hasm:
                        ctx.op("tensor", lambda e: e.matmul(
                            Sb[:, a - lo:b2 - lo], lhsT=at.ident[:], rhs=mtm[:, a - (ncols - 9):b2 - (ncols - 9)],
                            start=False, stop=True, skip_group_check=True), reads=[at.ident, mtm], writes=[Sb])
                    et = es[st["ne"] % NE]
                    st["ne"] += 1
                    ctx.op("scalar", lambda e: e.activation(
                        out=et[:, 0:w], in_=Sb[:, 0:w], func=AF.Exp, scale=0.125, accum_out=lpt[:, hi_:hi_ + 1]),
                        reads=[Sb], writes=[et, lpt])
                    ehs.append((et, lo, hi))
                if len(halves) == 2:
                    ctx.op("vector", lambda e: e.tensor_tensor(out=lpt[:, 0:1], in0=lpt[:, 0:1], in1=lpt[:, 1:2], op=ALU.add),
                           reads=[lpt], writes=[lpt])
                ctx.op("vector", lambda e: e.tensor_scalar(out=rl1[:], in0=lpt[:, 0:1], scalar1=1e-30, scalar2=None, op0=ALU.max),
                       reads=[lpt], writes=[rl1])
                ctx.op("vector", lambda e: e.reciprocal(out=rl1[:], in_=rl1[:]), reads=[rl1], writes=[rl1])
                for (et, lo, hi) in ehs:
                    w = hi - lo
                    if hh == 0:
                        ctx.op("vector", lambda e: e.tensor_scalar(
                            out=Aim[:, lo:hi], in0=et[:, 0:w], scalar1=rl1[:, 0:1], scalar2=None, op0=ALU.mult),
                            reads=[et, rl1], writes=[Aim])
                    else:
                        ctx.op("vector", lambda e: e.scalar_tensor_tensor(
                            out=Aim[:, lo:hi], in0=et[:, 0:w], scalar=rl1[:, 0:1], in1=Aim[:, lo:hi], op0=ALU.mult, op1=ALU.add),
                            reads=[et, rl1, Aim], writes=[Aim])
            n4 = 4 * nj
            tt = lambda o, a_, b_, op: ctx.op("vector", lambda e: e.tensor_tensor(out=o, in0=a_, in1=b_, op=op),
                                              reads=[Aim, I1, mul3, add3], writes=[I1])
            tt(I1[:, 0:nj], Aim[:, 0:n4:4], Aim[:, 1:n4:4], ALU.add)
            tt(I1[:, 0:nj], I1[:, 0:nj], Aim[:, 2:n4:4], ALU.add)
            ctx.op("vector", lambda e: e.scalar_tensor_tensor(out=I1[:, 0:nj], in0=I1[:, 0:nj], scalar=2.0, in1=Aim[:, 3:n4:4],
                                                              op0=ALU.mult, op1=ALU.add), reads=[Aim, I1], writes=[I1])
            tt(I1[:, 1:nj], I1[:, 1:nj], Aim[:, 3:n4 - 4:4], ALU.add)
            tt(I1[:, nj - 3:nj], I1[:, nj - 3:nj], mul3[:, :], ALU.mult)
            tt(I1[:, nj - 3:nj], I1[:, nj - 3:nj], add3[:, :], ALU.add)
            ctx.op("vector", lambda e: e.memset(I1[:, 0:1], BIG), writes=[I1])
            ctx.op("gpsimd", lambda e: e.memset(negm[:], 0.0), writes=[negm])
            ctx.op("vector", lambda e: e.max(out=m8[:, 0:8], in_=I1[:, 0:nj]), reads=[I1], writes=[m8])
            ctx.op("vector", lambda e: e.match_replace(out=I2[:, 0:nj], in_to_replace=m8[:, 0:8], in_values=I1[:, 0:nj],
                                                       imm_value=-BIG), reads=[I1, m8], writes=[I2])
            ctx.op("vector", lambda e: e.max(out=m8[:, 8:16], in_=I2[:, 0:nj]), reads=[I2], writes=[m8])
            ctx.op("vector", lambda e: e.tensor_scalar(out=negm[:, 64:64 + nj], in0=I1[:, 0:nj], scalar1=m8[:, 15:16], scalar2=-1.0,
                                                       op0=ALU.is_ge, op1=ALU.add), reads=[I1, m8], writes=[negm])

    def phase1b(qb):
        if qb < 2:
            return
        for qsl in range(4):
            negm = negms[qsl]
            for c in range(qb // 8 + 1):
                ctx.op("tensor", lambda e: e.transpose(out=GB[:, 0:128], in_=negm[:, 64 * c:64 * c + 128], identity=identf[:]),
                       reads=[negm, identf], writes=[GB])
                for h in range(2):
                    t_ = Qms[qb % 2][h][c]
                    ctx.op("vector", lambda e: e.tensor_copy(out=t_[64:128, qsl * 128:(qsl + 1) * 128], in_=GB[64:128, 0:128]),
                           reads=[GB], writes=[t_])

    def phase2(qb, todo):
        T0 = qb * 512
        wK, wV, gt = wKs[qb % 2], wVs[qb % 2], gts[qb % 2]
        use_sel = qb >= 2
        for hl in range(2):
            r0 = hl * 64
            Qm = Qms[qb % 2][hl][0]
            for m in range(qb // 4 + 1):
                dl = qb - 4 * m
                masks = [(at.ident, at.ident[:], mcmp, mcmp[:, dl, :], slice(0, 512))] if dl <= 4 else []
                todo.append(lambda Qm=Qm, m=m, masks=masks: at.step(128, 0, 512, [(kcT, kcT[:, m * 128:(m + 1) * 128], Qm, Qm[:, 0:512], slice(0, 512))], masks,
                        [(vc1, vc1[:, m, :], slice(0, 512), accC, accC[:, :], m == 0)]))
            for kt in range(8):
                c0 = max(0, (kt - 4) * 128)
                c1 = min(512, 128 * kt + 128)
                todo.append(lambda Qm=Qm, kt=kt, c0=c0, c1=c1: at.step(128, c0, c1, [(wK, wK[:, kt * 128:(kt + 1) * 128], Qm, Qm[:, c0:c1], slice(c0, c1))],
                        [(at.ident, at.ident[:], mwin, mwin[:, kt, c0:c1], slice(c0, c1))],
                        [(wV, wV[:, kt, :], slice(c0, c1), accW, accW[:, c0:c1], kt == 0)]))
            for kt in range(4 * qb + 4):
                d = kt - 4 * qb
                c0 = d * 128 if d > 0 else 0
                masks = []
                Qm = Qms[qb % 2][hl][kt // 32]
                if d >= 0:
                    masks.append((at.ident, at.ident[:], mc, mc[:, d, c0:512], slice(c0, 512)))
                todo.append(lambda Qm=Qm, kt=kt, c0=c0, masks=masks: at.step(128, c0, 512, [(bigA, bigA[:, kt * 128:(kt + 1) * 128], Qm, Qm[:, c0:512], slice(c0, 512))], masks,
                        [(bigB, bigB3[:, kt, :], slice(c0, 512), accS, accS[:, c0:512], kt == 0)]))
            todo.append(lambda hl=hl: epilogue(qb, hl))

    def epilogue(qb, hl):
        T0 = qb * 512
        gt = gts[qb % 2]
        for br, acc in enumerate((accC, accS, accW)):
            OL, rl = at.finish(acc)
            r = hl * 3 + br
            ctx.op("tensor", lambda e: e.matmul(GB[:64, :], lhsT=selg[:, r * 64:(r + 1) * 64], rhs=gt[:, :], start=True, stop=True),
                   reads=[selg, gt], writes=[GB])
            ctx.op("vector", lambda e: e.tensor_tensor(out=fg[:], in0=GB[:64, :], in1=rl[:], op=ALU.mult),
                   reads=[GB, rl], writes=[fg])
            if br == 0:
                ctx.op("vector", lambda e: e.tensor_tensor(out=accsb[:], in0=OL[:64, :], in1=fg[:], op=ALU.mult),
                       reads=[OL, fg], writes=[accsb])
            else:
                ctx.op("vector", lambda e: e.tensor_tensor(out=tmpc[:], in0=OL[:64, :], in1=fg[:], op=ALU.mult),
                       reads=[OL, fg], writes=[tmpc])
                ctx.op("vector", lambda e: e.tensor_tensor(out=accsb[:], in0=accsb[:], in1=tmpc[:], op=ALU.add),
                       reads=[accsb, tmpc], writes=[accsb])
        O = Ob[st["no"] % 2]
        st["no"] += 1
        ctx.op("scalar", lambda e: e.activation(out=O[:], in_=accsb[:], func=AF.Copy), reads=[accsb], writes=[O])
        ctx.dma("sync", oA, oA[hl, :, T0:T0 + 512], O, O[:])

    load(0)
    phase1a(0)
    for qb in range(nqb):
        phase1b(qb)
        todo = []
        phase2(qb, todo)
        nxt = qb + 1 < nqb
        if nxt:
            load(qb + 1)
        n = len(todo)
        cuts = {(n * k) // 4: k for k in range(4)}
        for i, fn in enumerate(todo):
            if nxt and i in cuts:
                phase1a(qb + 1, (cuts[i],))
            fn()
    ctx.finish([oA])
    return nc, ctx


def dil_consts_np():
    c = attn_consts_np()
    kl = np.arange(128)[:, None]
    ql = np.arange(512)[None, :]
    d = ql - kl
    c["md1"] = np.stack([np.where((d - o >= 0) & (d - o <= 128), 0.0, NEG) for o in range(-128, 512, 128)], 1).astype(NBF)
    i4 = ql % 128
    c["md4"] = np.stack([np.where(i4 <= kl, 0.0, NEG), np.where(i4 >= kl, 0.0, NEG)], 1).astype(NBF)
    i16 = ql % 32
    c["md16"] = np.stack([np.where(i16 <= kl, 0.0, NEG), np.where(i16 >= kl, 0.0, NEG)], 1).astype(NBF)
    del c["mcausal"]
    return c


DIL_CONST_SHAPES = {"ident": ((128, 128), BF16), "sel": ((128, 64), F32), "md1": ((128, 5, 512), BF16),
                    "md4": ((128, 2, 512), BF16), "md16": ((128, 2, 512), BF16)}


def build_dil(nqb=NQB):
    nc = bass.Bass("TRN2", target_bir_lowering=False)
    ctx = Ctx(nc)
    cm = Common(ctx, norm=False)
    pb = cm.psum
    din = lambda n, s, dt=BF16: ctx.dram(n, s, dt, "ExternalInput")
    qd = din("qd", (128, S))
    kb1T = din("kb1T", (128, 128 + S))
    vb1 = din("vb1", (2, 128, 129, 128))
    kb4T = din("kb4T", (128, 4, 128 + 4096))
    vb4 = din("vb4", (2, 128, 4, 33, 128))
    kb16T = din("kb16T", (128, 16, 128 + 1024))
    vb16 = din("vb16", (2, 16, 1152, 128))
    cd = {k: din(k, s, dt) for k, (s, dt) in DIL_CONST_SHAPES.items()}
    oB = ctx.dram("oB", (2, 64, S), BF16, "ExternalOutput")
    at = Attn(ctx, cm, 0.125, cd, ns3=True)
    ms = {}
    for name in ("md1", "md4", "md16"):
        s, dt = DIL_CONST_SHAPES[name]
        ms[name] = ctx.sb("c_" + name, s, dt)
        ctx.dma("sync", ms[name], ms[name][:], cd[name], cd[name][:])
    md1, md4, md16 = ms["md1"], ms["md4"], ms["md16"]
    Qs = [[ctx.sb(f"Qd{i}_{h}", (128, 512), BF16) for h in range(2)] for i in range(2)]
    for i in range(2):
        for h in range(2):
            ctx.op("gpsimd", lambda e: e.memset(Qs[i][h][:], 0.0), writes=[Qs[i][h]])
    K1 = [ctx.sb(f"K1_{i}", (128, 640), BF16) for i in range(2)]
    V1 = [[ctx.sb(f"V1_{i}_{h}", (128, 5, 128), BF16) for h in range(2)] for i in range(2)]
    K4 = [ctx.sb(f"K4_{i}", (128, 4, 256), BF16) for i in range(2)]
    V4 = [[ctx.sb(f"V4_{i}_{h}", (128, 4, 2, 128), BF16) for h in range(2)] for i in range(2)]
    K16 = [ctx.sb(f"K16_{i}", (128, 16, 160), BF16) for i in range(2)]
    V16A = [[ctx.sb(f"V16A_{i}_{h}", (128, 16, 128), BF16) for h in range(2)] for i in range(2)]
    V16B = [[ctx.sb(f"V16B_{i}_{h}", (32, 16, 128), BF16) for h in range(2)] for i in range(2)]
    Ob = [ctx.sb(f"Ob{i}", (64, 512), BF16) for i in range(2)]
    accs = [pb[2], pb[3]]
    n = 0
    for qb in range(nqb):
        T0 = qb * 512
        i = qb % 2
        k1, k4, k16 = K1[i], K4[i], K16[i]
        for h in range(2):
            ctx.dma("sync", Qs[i][h], Qs[i][h][h * 64:(h + 1) * 64, :], qd, qd[h * 64:(h + 1) * 64, T0:T0 + 512])
        ctx.dma("sync", k1, k1[:], kb1T, kb1T[:, T0:T0 + 640])
        ctx.dma("sync", k4, k4[:], kb4T, kb4T[:, :, 128 * qb:128 * qb + 256])
        ctx.dma("sync", k16, k16[:], kb16T, kb16T[:, :, 32 * qb:32 * qb + 160])
        for h in range(2):
            ctx.dma("sync", V1[i][h], V1[i][h][:], vb1, vb1[h, :, 4 * qb:4 * qb + 5, :])
            ctx.dma("sync", V4[i][h], V4[i][h][:], vb4, vb4[h, :, :, qb:qb + 2, :])
            ctx.dma("gpsimd", V16A[i][h], V16A[i][h][:], vb16,
                    vb16[h, :, 32 * qb:32 * qb + 128, :].rearrange("r p c -> p r c"))
            ctx.dma("gpsimd", V16B[i][h], V16B[i][h][:], vb16,
                    vb16[h, :, 32 * qb + 128:32 * qb + 160, :].rearrange("r p c -> p r c"))
        for h in range(2):
            r0 = h * 64
            Q = Qs[i][h]
            A = accs[n % 2]
            O = Ob[n % 2]
            n += 1
            v1, v4, va, vb = V1[i][h], V4[i][h], V16A[i][h], V16B[i][h]
            for kt in range(5):
                o = -128 + 128 * kt
                c0, c1 = max(0, o), min(512, 128 * kt + 128)
                at.step(128, c0, c1, [(k1, k1[:, kt * 128:(kt + 1) * 128], Q, Q[:, c0:c1], slice(c0, c1))],
                        [(at.ident, at.ident[:], md1, md1[:, kt, c0:c1], slice(c0, c1))],
                        [(v1, v1[:, kt, :], slice(c0, c1), A, A[:, c0:c1], kt == 0)])
            for kt in range(2):
                mains = [(k4, k4[:, r, kt * 128:(kt + 1) * 128], Q, Q[:, r:512:4], slice(r * 128, (r + 1) * 128))
                         for r in range(4)]
                pvs = [(v4, v4[:, r, kt, :], slice(r * 128, (r + 1) * 128), A, A[:, r:512:4], False) for r in range(4)]
                at.step(128, 0, 512, mains, [(at.ident, at.ident[:], md4, md4[:, kt, :], slice(0, 512))], pvs)
            mains = [(k16, k16[:, r, 0:128], Q, Q[:, r:512:16], slice(r * 32, (r + 1) * 32)) for r in range(16)]
            pvs = [(va, va[:, r, :], slice(r * 32, (r + 1) * 32), A, A[:, r:512:16], False) for r in range(16)]
            at.step(128, 0, 512, mains, [(at.ident, at.ident[:], md16, md16[:, 0, :], slice(0, 512))], pvs)
            mains = [(k16, k16[:, r, 128:160], Q, Q[:, r:512:16], slice(r * 32, (r + 1) * 32)) for r in range(16)]
            pvs = [(vb, vb[:, r, :], slice(r * 32, (r + 1) * 32), A, A[:, r:512:16], False) for r in range(16)]
            at.step(32, 0, 512, mains, [(at.ident, at.ident[0:32, 0:32], md16, md16[0:32, 1, :], slice(0, 512))], pvs)
            OL, rl = at.finish(A)
            ctx.op("vector", lambda e, OL=OL, rl=rl, O=O: e.tensor_tensor(out=O[:], in0=OL[:64, :], in1=rl[:], op=ALU.mult),
                   reads=[OL, rl], writes=[O])
            ctx.dma("sync", oB, oB[h, :, T0:T0 + 512], O, O[:])
    ctx.finish([oB])
    return nc, ctx


NCORE = 8
DBG = {}


def _run(nc, in_maps):
    res = run_bass_kernel_spmd(nc, in_maps, core_ids=list(range(NCORE)))
    et = getattr(res, "exec_time_ns", None)
    if et is not None:
        print(f"[launch] exec_time_ns={et}", flush=True)
    return res.results


def _tokT(a):
    R = a.shape[1]
    return np.ascontiguousarray(a.T.reshape(R // 128, 128, a.shape[0]).transpose(1, 0, 2))


def _v1_tiles(vT, pad_rows=0):
    L = vT.shape[1]
    a = np.zeros((pad_rows + L, 128), NBF)
    a[pad_rows:, :64] = vT.T
    a[pad_rows:, 64:] = 1
    nt = (pad_rows + L) // 128
    return np.ascontiguousarray(a.reshape(nt, 128, 128).transpose(1, 0, 2))


def _padfront(a, n):
    z = np.zeros(a.shape[:-1] + (n,), a.dtype)
    return np.concatenate([z, a], axis=-1)


def _rope_tabs(pos, dims):
    inv = (np.float32(ROPE_THETA) ** (-np.arange(0, dims, 2, dtype=np.float32) / np.float32(dims))).astype(np.float32)
    ang = pos.astype(np.float32)[:, None] * inv[None, :]
    return np.cos(ang).astype(np.float32).T, np.sin(ang).astype(np.float32).T


def kernel(x, ffn1_norm, ffn1_w_gate, ffn1_w_up, ffn1_w_down, ffn2_norm, ffn2_w_gate, ffn2_w_up, ffn2_w_down, mix_norm,
           ev_w_in, ev_w_out, nsa_cmp_pos_k, nsa_cmp_pos_v, nsa_cmp_k_w1, nsa_cmp_k_w2, nsa_cmp_v_w1, nsa_cmp_v_w2,
           mla_w_in, mla_q_norm, mla_kv_norm, mla_w_uq, mla_w_ukv, mla_w_out, final_norm):
    f32 = lambda a: np.asarray(a, dtype=np.float32)
    x = f32(x)
    B = 2
    xf = x.reshape(B * S, D)
    pos_of_core = [(c % 4) * NTOK + np.arange(NTOK) for c in range(NCORE)]

    def ffn_maps(pref, g, wg, wu, wd):
        wgu, wdt = ffn_host_layout(f32(wg), f32(wu), f32(wd))
        return {pref + "_g": gain_layout(f32(g), KC), pref + "_wgu": wgu, pref + "_wd": wdt}

    W = f32(ev_w_in[0])
    perm64 = np.concatenate([np.arange(8, 16), np.arange(0, 8), np.arange(16, 64)])
    perm128 = np.concatenate([perm64, 64 + perm64])
    win = np.stack([wtile(W[:, c0:c0 + 128], KC) for c0 in EV_COLS])
    wgt = wtile(W[:, 1280:1304], KC)
    permT = np.zeros((128, 128), np.float32)
    permT[perm128, np.arange(128)] = 1.0
    common = dict(win=win, wgt=wgt, gmix=gain_layout(f32(mix_norm[0]), KC), permT=permT)
    common.update(ffn_maps("fa", ffn1_norm[0], ffn1_w_gate[0], ffn1_w_up[0], ffn1_w_down[0]))
    in_maps = []
    for c in range(NCORE):
        cs, sn = _rope_tabs(pos_of_core[c], 16)
        ct = np.ones((64, NTOK), np.float32)
        st = np.zeros((64, NTOK), np.float32)
        ct[0:8], ct[8:16] = cs, cs
        st[0:8], st[8:16] = -sn, sn
        m = dict(common)
        m.update(xT=_tokT(xf[c * NTOK:(c + 1) * NTOK]), ctab=np.concatenate([ct, ct]), stab=np.concatenate([st, st]))
        in_maps.append(m)
    nc, _ = build_stage("L1")
    r1 = _run(nc, in_maps)
    x1T = [r["x1T"] for r in r1]
    PJ = np.concatenate([r["pj"] for r in r1], axis=2).reshape(22, 128, B, S)
    GT = np.concatenate([r["gates"] for r in r1], axis=1).reshape(24, B, S)
    Q32 = np.concatenate([r["q32"] for r in r1], axis=2).reshape(5, 128, B, S)
    DBG.update(x1T=x1T, PJ=PJ, GT=GT)
    cn = nsa_consts_np()
    w1k = np.zeros((128, 32, 128), np.float32)
    w1k[:64] = f32(nsa_cmp_k_w1[0]).reshape(32, 64, 128).transpose(1, 0, 2)
    posk8 = np.zeros((128, 32, 8), np.float32)
    posk8[:64] = np.repeat(f32(nsa_cmp_pos_k[0]).T[:, :, None], 8, axis=2)
    w1v = np.ascontiguousarray(f32(nsa_cmp_v_w1[0]).reshape(32, 64, 128).transpose(1, 0, 2))
    w2k = f32(nsa_cmp_k_w2[0])
    XM = np.zeros((64, 128, 128), NBF)
    for kt in range(128):
        for half in range(2):
            XM[2 * (kt % 32) + half, kt, half * 64:(half + 1) * 64] = 30000.0
    XM = XM.reshape(64, S)
    in_maps = []
    for c in range(NCORE):
        b, g, pr = c // 4, (c % 4) // 2, c % 2
        gs = slice(g * 64, (g + 1) * 64)
        ks = PJ[6][gs, b]
        kw = PJ[8][gs, b]
        h0 = 4 * g + 2 * pr
        m = dict(cn)
        m.update(qg=np.ascontiguousarray(np.stack([Q32[2 * g][:, b], Q32[2 * g + 1][:, b]], axis=1)),
                 qmy=np.ascontiguousarray(PJ[2 * g + pr][:, b]),
                 kcraw=np.ascontiguousarray(Q32[4][gs, b].reshape(64, 1024, 16).transpose(0, 2, 1)), vcraw=np.ascontiguousarray(PJ[5][gs, b]),
                 w1k=w1k, w1v=w1v, posk=posk8,
                 posv=np.ascontiguousarray(f32(nsa_cmp_pos_v[0]).T),
                 w2k=np.concatenate([w2k, np.zeros_like(w2k)], axis=1), w2v=f32(nsa_cmp_v_w2[0]),
                 ksT=np.concatenate([ks, XM], axis=0), vs1=_v1_tiles(PJ[7][gs, b]),
                 kwT=_padfront(np.concatenate([kw, np.zeros_like(kw)], axis=0), 512), vw1=_v1_tiles(PJ[9][gs, b], 512),
                 gat=np.ascontiguousarray(GT[h0 * 3:h0 * 3 + 6, b]))
        in_maps.append(m)
    nc, _ = build_nsa()
    rA = _run(nc, in_maps)
    AT = np.zeros((1024, B, S), NBF)
    for c in range(NCORE):
        b, g, pr = c // 4, (c % 4) // 2, c % 2
        h0 = 4 * g + 2 * pr
        AT[h0 * 64:(h0 + 2) * 64, b] = rA[c]["oA"].reshape(128, S)
    cdl = dil_consts_np()
    in_maps = []
    for c in range(NCORE):
        b, cc = c // 4, c % 4
        k = PJ[14 + cc][:, b]
        v = PJ[18 + cc][:, b]
        m = dict(cdl)
        vh = [v[0:64], v[64:128]]
        m.update(qd=np.ascontiguousarray(PJ[10 + cc][:, b]), kb1T=_padfront(k, 128),
                 vb1=np.stack([_v1_tiles(vv, 128) for vv in vh]),
                 kb4T=np.ascontiguousarray(np.stack([_padfront(k[:, r::4], 128) for r in range(4)], axis=1)),
                 vb4=np.stack([np.stack([_v1_tiles(vv[:, r::4], 128) for r in range(4)], axis=1) for vv in vh]),
                 kb16T=np.ascontiguousarray(np.stack([_padfront(k[:, r::16], 128) for r in range(16)], axis=1)),
                 vb16=np.stack([np.stack([_v1_tiles(vv[:, r::16], 128).transpose(1, 0, 2).reshape(1152, 128)
                                          for r in range(16)]) for vv in vh]))
        in_maps.append(m)
    nc, _ = build_dil()
    rB = _run(nc, in_maps)
    for c in range(NCORE):
        b, cc = c // 4, c % 4
        AT[512 + cc * 128:512 + (cc + 1) * 128, b] = rB[c]["oB"].reshape(128, S)
    DBG.update(AT=AT)
    ATf = AT.reshape(1024, B * S)
    Wo = f32(ev_w_out[0])
    Wm = f32(mla_w_in[0])
    Wq = f32(mla_w_uq[0])
    Wkv = f32(mla_w_ukv[0])
    permq = np.concatenate([np.arange(64), np.arange(80, 96), np.arange(64, 80)])
    permk = np.concatenate([np.arange(16, 32), np.arange(0, 16)])
    common = dict(wo=np.stack([wtile(Wo[:, c0:c0 + 128], KC) for c0 in range(0, 1024, 128)]),
                  gmix=gain_layout(f32(mix_norm[1]), KC), gq=gain_layout(f32(mla_q_norm[0]), 2),
                  gkv=gain_layout(f32(mla_kv_norm[0]), 1),
                  wmi=np.stack([wtile(Wm[:, i * 128:(i + 1) * 128], KC) for i in range(3)]),
                  wkr=np.stack([wtile(Wm[:, 384:416], KC), wtile(Wm[:, 384 + permk], KC)]),
                  wuq=np.stack([wtile(Wq[:, h * 96:(h + 1) * 96], 2) for h in range(16)]),
                  wuqs=np.stack([wtile(Wq[:, h * 96 + permq], 2) for h in range(16)]),
                  wukv=np.stack([wtile(Wkv[:, h * 128:(h + 1) * 128], 1) for h in range(16)]))
    common.update(ffn_maps("fa", ffn2_norm[0], ffn2_w_gate[0], ffn2_w_up[0], ffn2_w_down[0]))
    common.update(ffn_maps("fb", ffn1_norm[1], ffn1_w_gate[1], ffn1_w_up[1], ffn1_w_down[1]))
    in_maps = []
    for c in range(NCORE):
        cs, sn = _rope_tabs(pos_of_core[c], 32)
        cq = np.ones((96, NTOK), np.float32)
        sq = np.zeros((96, NTOK), np.float32)
        cq[64:80], cq[80:96] = cs, cs
        sq[64:80], sq[80:96] = -sn, sn
        m = dict(common)
        m.update(xT=x1T[c], aT=_tokT(ATf[:, c * NTOK:(c + 1) * NTOK].T), cq_t=cq, sq_t=sq,
                 ck_t=np.concatenate([cs, cs]), sk_t=np.concatenate([-sn, sn]))
        in_maps.append(m)
    nc, _ = build_stage("L3")
    r3 = _run(nc, in_maps)
    x3T = [r["x3T"] for r in r3]
    QT = np.concatenate([r["qT"] for r in r3], axis=2).reshape(16, 96, B, S)
    KV = np.concatenate([r["kvT"] for r in r3], axis=2).reshape(16, 128, B, S)
    KR = np.concatenate([r["krT"] for r in r3], axis=1).reshape(32, B, S)
    DBG.update(x3T=x3T, QT=QT, KV=KV, KR=KR)
    ca = attn_consts_np()
    in_maps = []
    for c in range(NCORE):
        b = c // 4
        hs = [4 * (c % 4) + u for u in range(4)]
        m = dict(ca)
        m.update(qT=np.ascontiguousarray(np.stack([QT[h, :, b] for h in hs])),
                 kT=np.stack([np.concatenate([KV[h, 0:64, b], KR[:, b]], axis=0) for h in hs]),
                 v1=np.stack([_v1_tiles(KV[h, 64:128, b]) for h in hs]))
        in_maps.append(m)
    nc, _ = build_mla()
    r4 = _run(nc, in_maps)
    AT2 = np.zeros((1024, B, S), NBF)
    for c in range(NCORE):
        b = c // 4
        AT2[(c % 4) * 256:(c % 4 + 1) * 256, b] = r4[c]["oT"].reshape(256, S)
    DBG.update(AT2=AT2)
    AT2f = AT2.reshape(1024, B * S)
    Wo2 = f32(mla_w_out[0])
    common = dict(wo=np.stack([wtile(Wo2[:, c0:c0 + 128], KC) for c0 in range(0, 1024, 128)]),
                  gfin=gain_layout(f32(final_norm), KC))
    common.update(ffn_maps("fa", ffn2_norm[1], ffn2_w_gate[1], ffn2_w_up[1], ffn2_w_down[1]))
    in_maps = []
    for c in range(NCORE):
        m = dict(common)
        m.update(xT=x3T[c], aT=_tokT(AT2f[:, c * NTOK:(c + 1) * NTOK].T))
        in_maps.append(m)
    nc, _ = build_stage("L5")
    r5 = _run(nc, in_maps)
    out = np.zeros((B * S, D), np.float32)
    for c in range(NCORE):
        out[c * NTOK:(c + 1) * NTOK] = r5[c]["yT"].transpose(1, 0, 2).reshape(D, NTOK).T
    return out.reshape(B, S, D)
```

```python
import numpy as np
import concourse.bass as bass
import concourse.mybir as mybir
from concourse.bass_utils import run_bass_kernel_spmd

F32 = mybir.dt.float32
BF16 = mybir.dt.bfloat16
AF = mybir.ActivationFunctionType
ALU = mybir.AluOpType
AX = mybir.AxisListType


class SemObj:
    def __init__(self, nc, name):
        self.sem = nc.alloc_semaphore(name)
        self.name = name
        self.val = 0


class EngState:
    def __init__(self, nc, eng, name):
        self.e = eng
        self.name = name
        self.so = SemObj(nc, "sE_" + name)
        self.waited = {}


class Tile:
    def __init__(self, ctx, ap, name, dma_target=False):
        self.ap = ap
        self.name = name
        self.w = None
        self.r = {}
        self.dso = None
        self.ctx = ctx

    def dsem(self):
        if self.dso is None:
            self.dso = SemObj(self.ctx.nc, "sD_" + self.name)
        return self.dso

    def __getitem__(self, idx):
        return self.ap[idx]


class Ctx:
    def __init__(self, nc):
        self.nc = nc
        self.E = {n: EngState(nc, getattr(nc, n), n) for n in ["tensor", "vector", "scalar", "gpsimd", "sync"]}
        self.ntile = 0
        self.ninst = 0

    def sb(self, name, shape, dtype):
        self.ntile += 1
        return Tile(self, self.nc.alloc_sbuf_tensor(name, list(shape), dtype).ap(), name)

    def ps(self, name, shape=(128, 512), dtype=F32):
        self.ntile += 1
        return Tile(self, self.nc.alloc_psum_tensor(name, list(shape), dtype).ap(), name)

    def dram(self, name, shape, dtype, kind):
        t = Tile(self, self.nc.dram_tensor(name, list(shape), dtype, kind=kind).ap(), name)
        t.shape = tuple(shape)
        return t

    def _deps(self, E, reads, writes, waw=True):
        needs = {}

        def need(dep):
            if dep is None:
                return
            so, v = dep
            if needs.get(so, 0) < v:
                needs[so] = v

        for t in reads:
            need(t.w)
        for t in writes:
            if waw:
                need(t.w)
            for d in t.r.values():
                need(d)
        for so, v in needs.items():
            if so is E.so and E.name == "tensor":
                continue
            if E.waited.get(so, 0) >= v:
                continue
            E.e.wait_ge(so.sem, v)
            E.waited[so] = v

    def op(self, eng, fn, reads=(), writes=()):
        E = self.E[eng]
        self._deps(E, reads, writes)
        ins = fn(E.e)
        E.so.val += 1
        ins.then_inc(E.so.sem, 1)
        me = (E.so, E.so.val)
        for t in reads:
            t.r[E.so] = me
        for t in writes:
            t.w = me
            t.r = {}
        self.ninst += 1
        return ins

    def dma(self, eng, out_t, out_ap, in_t, in_ap, waw=True, **kw):
        E = self.E[eng]
        self._deps(E, [in_t], [out_t], waw=waw)
        so = out_t.dsem()
        ins = E.e.dma_start(out=out_ap, in_=in_ap, **kw)
        so.val += 16
        ins.then_inc(so.sem, 16)
        me = (so, so.val)
        in_t.r[so] = me
        out_t.w = me
        out_t.r = {}
        self.ninst += 1
        return ins

    def finish(self, out_tiles):
        E = self.E["sync"]
        for t in out_tiles:
            if t.w is not None:
                so, v = t.w
                E.e.wait_ge(so.sem, v)


NORM_EPS = 1e-6
D = 1024
KC = 8
FF = 2816
FC = 22
TT = 512


class Common:
    def __init__(self, ctx, norm=True):
        self.ctx = ctx
        self.psum = [ctx.ps(f"ps{i}") for i in range(8)]
        if norm:
            self.init_eps()
            self.ones = ctx.sb("ones_f32", (128, 128), F32)
            ctx.op("vector", lambda e: e.memset(self.ones[:], 1.0), writes=[self.ones])
            self.sq = [ctx.sb(f"sq{i}", (128, TT), F32) for i in range(2)]
            self.rstd = ctx.sb("rstd", (128, TT), F32)
            self.ssum = ctx.sb("ssum", (128, TT), F32)
        self.rr = 0

    def rmsnorm_T(self, xt, nk, gam, outT, n, pbank, width, out2=None):
        ctx = self.ctx
        ps = pbank
        ssum = self.ssum
        for kc in range(nk):
            sq = ssum if kc == 0 else self.sq[self.rr % 2]
            self.rr += 1
            ctx.op("scalar", lambda e, kc=kc, sq=sq: e.activation(out=sq[:, :n], in_=xt[:, kc, :n], func=AF.Square),
                   reads=[xt], writes=[sq])
            if kc > 0:
                ctx.op("vector", lambda e, sq=sq: e.tensor_tensor(out=ssum[:, :n], in0=ssum[:, :n], in1=sq[:, :n], op=ALU.add),
                       reads=[ssum, sq], writes=[ssum])
        ctx.op("tensor", lambda e: e.matmul(ps[:, :n], lhsT=self.ones[:], rhs=ssum[:, :n], start=True, stop=True),
               reads=[self.ones, ssum], writes=[ps])
        rstd = self.rstd
        ctx.op("scalar", lambda e: e.activation(out=rstd[:, :n], in_=ps[:, :n], func=AF.Sqrt,
                                                 bias=self.eps_t(), scale=1.0 / width),
               reads=[ps, self.eps_tile], writes=[rstd])
        ctx.op("vector", lambda e: e.reciprocal(out=rstd[:, :n], in_=rstd[:, :n]), reads=[rstd], writes=[rstd])
        for kc in range(nk):
            ctx.op("vector", lambda e, kc=kc: e.scalar_tensor_tensor(
                out=outT[:, kc, :n], in0=xt[:, kc, :n], scalar=gam[:, kc:kc + 1], in1=rstd[:, :n],
                op0=ALU.mult, op1=ALU.mult), reads=[xt, gam, rstd], writes=[outT])
            if out2 is not None:
                ctx.op("vector", lambda e, kc=kc: e.scalar_tensor_tensor(
                    out=out2[:, kc, :n], in0=xt[:, kc, :n], scalar=gam[:, kc:kc + 1], in1=rstd[:, :n],
                    op0=ALU.mult, op1=ALU.mult), reads=[xt, gam, rstd], writes=[out2])

    def eps_t(self):
        return self.eps_tile[:, 0:1]

    def init_eps(self):
        ctx = self.ctx
        self.eps_tile = ctx.sb("eps", (128, 1), F32)
        ctx.op("vector", lambda e: e.memset(self.eps_tile[:], NORM_EPS), writes=[self.eps_tile])


class FFN:
    def __init__(self, ctx, cm):
        self.ctx = ctx
        self.cm = cm
        self.hT = [ctx.sb(f"ffn_hT{i}", (128, KC, TT), BF16) for i in range(2)]
        self.wgu = [ctx.sb(f"ffn_wgu{i}", (128, 2, KC, 128), BF16) for i in range(3)]
        self.wd = [ctx.sb(f"ffn_wd{i}", (128, FC, 128), BF16) for i in range(2)]
        self.act = [ctx.sb(f"ffn_act{j}", (128, TT), BF16) for j in range(FC)]
        self.sg = [ctx.sb(f"ffn_sg{i}", (128, TT), F32) for i in range(2)]
        self.n = 0
        self.nw = 0
        self.nd = 0

    def run(self, xt, gam, wgu_d, wd_d, pb, sc=None, first=True):
        ctx, cm = self.ctx, self.cm
        hT = self.hT[self.n % 2]
        self.n += 1
        cm.rmsnorm_T(xt, KC, gam, hT, TT, pb[0], D)
        for j in range(FC):
            w = self.wgu[self.nw % 3]
            self.nw += 1
            if sc is None or first:
                ctx.dma("gpsimd", w, w[:], wgu_d, wgu_d[j], max_dma_last_dim=4096)
                if sc is not None:
                    ctx.dma("sync", sc[0], sc[0][j], w, w[:], waw=False, max_dma_last_dim=4096)
            else:
                ctx.dma("gpsimd", w, w[:], sc[0], sc[0][j], max_dma_last_dim=4096)
            pg = pb[1 + (j % 2)]
            pu = pb[3 + (j % 2)]
            for kc in range(KC):
                ctx.op("tensor", lambda e, kc=kc, w=w, pg=pg: e.matmul(pg[:], lhsT=w[:, 0, kc, :], rhs=hT[:, kc, :],
                                                                     start=(kc == 0), stop=(kc == KC - 1)),
                       reads=[w, hT], writes=[pg])
            for kc in range(KC):
                ctx.op("tensor", lambda e, kc=kc, w=w, pu=pu: e.matmul(pu[:], lhsT=w[:, 1, kc, :], rhs=hT[:, kc, :],
                                                                     start=(kc == 0), stop=(kc == KC - 1)),
                       reads=[w, hT], writes=[pu])
            sg = self.sg[j % 2]
            ctx.op("scalar", lambda e, sg=sg, pg=pg: e.activation(out=sg[:], in_=pg[:], func=AF.Silu),
                   reads=[pg], writes=[sg])
            a = self.act[j]
            ctx.op("vector", lambda e, sg=sg, pu=pu, a=a: e.tensor_tensor(out=a[:], in0=pu[:], in1=sg[:], op=ALU.mult),
                   reads=[pu, sg], writes=[a])
        for c in range(KC):
            w = self.wd[self.nd % 2]
            self.nd += 1
            if sc is None or first:
                ctx.dma("gpsimd", w, w[:], wd_d, wd_d[c], max_dma_last_dim=4096)
                if sc is not None:
                    ctx.dma("sync", sc[1], sc[1][c], w, w[:], waw=False, max_dma_last_dim=4096)
            else:
                ctx.dma("gpsimd", w, w[:], sc[1], sc[1][c], max_dma_last_dim=4096)
            po = pb[5 + (c % 2)]
            for j in range(FC):
                ctx.op("tensor", lambda e, j=j, w=w, po=po: e.matmul(po[:], lhsT=w[:, j, :], rhs=self.act[j][:],
                                                                     start=(j == 0), stop=(j == FC - 1)),
                       reads=[w, self.act[j]], writes=[po])
            ctx.op("vector", lambda e, c=c, po=po: e.scalar_tensor_tensor(
                out=xt[:, c, :], in0=po[:], scalar=0.5, in1=xt[:, c, :], op0=ALU.mult, op1=ALU.add),
                reads=[po, xt], writes=[xt])


def ffn_host_layout(wg, wu, wd):
    g = wg.reshape(KC, 128, FC, 128).transpose(2, 1, 0, 3)
    u = wu.reshape(KC, 128, FC, 128).transpose(2, 1, 0, 3)
    wgu = np.ascontiguousarray(np.stack([g, u], axis=2))
    wdt = np.ascontiguousarray(wd.reshape(FC, 128, KC, 128).transpose(2, 1, 0, 3))
    return wgu, wdt


def gain_layout(g, nk):
    return np.ascontiguousarray(g.reshape(nk, 128).T)

import ml_dtypes

NBF = ml_dtypes.bfloat16
NTOK = 4096
NSLOT = 8
S = 16384
ROPE_THETA = 500000.0


def wtile(W, nk):
    return np.ascontiguousarray(W.reshape(nk, 128, -1).transpose(1, 0, 2))


class Proj:
    def __init__(self, ctx, nslots=3):
        self.ctx = ctx
        self.w = [ctx.sb(f"pw{i}", (128, KC, 128), BF16) for i in range(nslots)]
        self.n = 0
        self.sc = {}
        self.first = True

    def mm(self, w_d, idx, nk, M, rhsT, ps, n=TT):
        ctx = self.ctx
        w = self.w[self.n % len(self.w)]
        self.n += 1
        sc = self.sc.get(w_d.name)
        if sc is None:
            sc = self.sc[w_d.name] = ctx.dram("sc_" + w_d.name, w_d.shape, BF16, "Internal")
        if self.first:
            ctx.dma("gpsimd", w, w[:, :nk, :M], w_d, w_d[idx])
            ctx.dma("sync", sc, sc[idx], w, w[:, :nk, :M], waw=False)
        else:
            ctx.dma("gpsimd", w, w[:, :nk, :M], sc, sc[idx])
        for kc in range(nk):
            ctx.op("tensor", lambda e, kc=kc: e.matmul(ps[:M, :n], lhsT=w[:, kc, :M], rhs=rhsT[:, kc, :n],
                                                       start=(kc == 0), stop=(kc == nk - 1)),
                   reads=[w, rhsT], writes=[ps])


def rope_combine(ctx, out_t, out_ap, p1, p2, ct, c_ap, st, s_ap, tmp, M, n=TT):
    t1, t2 = tmp
    ctx.op("vector", lambda e: e.tensor_tensor(out=t1[:M, :n], in0=p1[:M, :n], in1=c_ap, op=ALU.mult),
           reads=[p1, ct], writes=[t1])
    ctx.op("vector", lambda e: e.tensor_tensor(out=t2[:M, :n], in0=p2[:M, :n], in1=s_ap, op=ALU.mult),
           reads=[p2, st], writes=[t2])
    ctx.op("vector", lambda e: e.tensor_tensor(out=out_ap, in0=t1[:M, :n], in1=t2[:M, :n], op=ALU.add),
           reads=[t1, t2], writes=[out_t])


EV_ROPE = [True] * 4 + [True, False, True, False, True, False] + [True] * 4 + [True] * 4 + [False] * 4
EV_COLS = list(range(0, 1280, 128)) + list(range(1304, 2840, 128))


def build_stage(kind):
    nc = bass.Bass("TRN2", target_bir_lowering=False)
    ctx = Ctx(nc)
    cm = Common(ctx)
    ffn = FFN(ctx, cm)
    pj = Proj(ctx)
    pb = cm.psum
    D_ = {}

    def din(name, shape, dt=F32):
        D_[name] = ctx.dram(name, shape, dt, "ExternalInput")
        return D_[name]

    def dout(name, shape, dt=F32):
        D_[name] = ctx.dram(name, shape, dt, "ExternalOutput")
        return D_[name]

    xT = din("xT", (128, KC, NTOK))
    outs = []
    gams = {}

    def load_gam(name, nk=KC):
        d = din(name, (128, nk))
        t = ctx.sb("sb_" + name, (128, nk), F32)
        ctx.dma("sync", t, t[:], d, d[:])
        gams[name] = t
        return t

    def ffn_in(pref):
        return (load_gam(pref + "_g"), din(pref + "_wgu", (FC, 128, 2, KC, 128)), din(pref + "_wd", (KC, 128, FC, 128)),
                (ctx.dram("sc_" + pref + "_wgu", (FC, 128, 2, KC, 128), BF16, "Internal"),
                 ctx.dram("sc_" + pref + "_wd", (KC, 128, FC, 128), BF16, "Internal")))

    xts = [ctx.sb(f"xt{i}", (128, KC, TT), F32) for i in range(2)]
    tmp = [ctx.sb(f"tmp{i}", (128, TT), F32) for i in range(2)]
    hTs = [ctx.sb(f"hmix{i}", (128, KC, TT), BF16) for i in range(2)]
    if kind in ("L3", "L5"):
        aT = din("aT", (128, KC, NTOK), BF16)
        wo = din("wo", (KC, 128, KC, 128))
        ats = [ctx.sb(f"at{i}", (128, KC, TT), BF16) for i in range(2)]
    if kind == "L1":
        fa = ffn_in("fa")
        gm = load_gam("gmix")
        win = din("win", (22, 128, KC, 128))
        wgt = din("wgt", (128, KC, 24))
        ctab = din("ctab", (128, NTOK))
        stab = din("stab", (128, NTOK))
        x1T = dout("x1T", (128, KC, NTOK))
        pjo = dout("pj", (22, 128, NTOK), BF16)
        gto = dout("gates", (24, NTOK))
        outs = [x1T, pjo, gto]
        cts = [ctx.sb(f"ct{i}", (128, TT), F32) for i in range(2)]
        sts = [ctx.sb(f"st{i}", (128, TT), F32) for i in range(2)]
        obs = [ctx.sb(f"ob{i}", (128, TT), BF16) for i in range(3)]
        gos = [ctx.sb(f"go{i}", (24, TT), F32) for i in range(2)]
        hT32 = ctx.sb("hT32", (128, KC, TT), F32)
        w32 = [ctx.sb(f"w32_{i}", (128, KC, 128), F32) for i in range(2)]
        ob32s = [ctx.sb(f"ob32_{i}", (128, TT), F32) for i in range(2)]
        q32o = dout("q32", (5, 128, NTOK), F32)
        outs.append(q32o)
        n32 = [0]
        permd = din("permT", (128, 128))
        permT = ctx.sb("sb_permT", (128, 128), F32)
        ctx.dma("sync", permT, permT[:], permd, permd[:])
        p1sb = [ctx.sb(f"p1sb{i}", (128, TT), F32) for i in range(2)]

        def mm32(w_d, w_ap, ps):
            w = w32[n32[0] % 2]
            n32[0] += 1
            ctx.dma("sync", w, w[:], w_d, w_ap)
            for kc in range(KC):
                ctx.op("tensor", lambda e, kc=kc: e.matmul(ps[:], lhsT=w[:, kc, :], rhs=hT32[:, kc, :],
                                                           start=(kc == 0), stop=(kc == KC - 1)),
                       reads=[w, hT32], writes=[ps])
    if kind == "L3":
        fa = ffn_in("fa")
        fb = ffn_in("fb")
        gm = load_gam("gmix")
        gq = load_gam("gq", 2)
        gkv = load_gam("gkv", 1)
        wmi = din("wmi", (3, 128, KC, 128))
        wkr = din("wkr", (2, 128, KC, 32))
        wuq = din("wuq", (16, 128, 2, 96))
        wuqs = din("wuqs", (16, 128, 2, 96))
        wukv = din("wukv", (16, 128, 1, 128))
        cq_t = din("cq_t", (96, NTOK))
        sq_t = din("sq_t", (96, NTOK))
        ck_t = din("ck_t", (32, NTOK))
        sk_t = din("sk_t", (32, NTOK))
        x3T = dout("x3T", (128, KC, NTOK))
        qTo = dout("qT", (16, 96, NTOK), BF16)
        kvo = dout("kvT", (16, 128, NTOK), BF16)
        kro = dout("krT", (32, NTOK), BF16)
        outs = [x3T, qTo, kvo, kro]
        cts = [ctx.sb(f"ct{i}", (96, TT), F32) for i in range(2)]
        sts = [ctx.sb(f"st{i}", (96, TT), F32) for i in range(2)]
        ckts = [ctx.sb(f"ckt{i}", (32, TT), F32) for i in range(2)]
        skts = [ctx.sb(f"skt{i}", (32, TT), F32) for i in range(2)]
        obs = [ctx.sb(f"ob{i}", (128, TT), BF16) for i in range(3)]
        cqT = [ctx.sb(f"cqT{i}", (128, 2, TT), F32) for i in range(2)]
        ckvT = [ctx.sb(f"ckvT{i}", (128, 1, TT), F32) for i in range(2)]
        cqn = [ctx.sb(f"cqn{i}", (128, 2, TT), BF16) for i in range(2)]
        ckvn = [ctx.sb(f"ckvn{i}", (128, 1, TT), BF16) for i in range(2)]
    if kind == "L5":
        fa = ffn_in("fa")
        gf = load_gam("gfin")
        yT = dout("yT", (128, KC, NTOK))
        outs = [yT]
        yts = [ctx.sb(f"yt{i}", (128, KC, TT), F32) for i in range(2)]

    nob = 0
    for t in range(NSLOT):
        ts = slice(t * TT, (t + 1) * TT)
        xt = xts[t % 2]
        pj.first = (t == 0)
        ctx.dma("sync", xt, xt[:], xT, xT[:, :, ts])
        if kind in ("L3", "L5"):
            at = ats[t % 2]
            ctx.dma("sync", at, at[:], aT, aT[:, :, ts])
            for c in range(KC):
                ps = pb[5 + (c % 2)]
                pj.mm(wo, c, KC, 128, at, ps)
                ctx.op("vector", lambda e, c=c, ps=ps: e.tensor_tensor(out=xt[:, c, :], in0=ps[:], in1=xt[:, c, :], op=ALU.add),
                       reads=[ps, xt], writes=[xt])
        if kind == "L1":
            ffn.run(xt, fa[0], fa[1], fa[2], pb[0:7], sc=fa[3], first=(t == 0))
            ctx.dma("sync", x1T, x1T[:, :, ts], xt, xt[:])
            hT = hTs[t % 2]
            cm.rmsnorm_T(xt, KC, gm, hT, TT, pb[0], D, out2=hT32)
            ct, st = cts[t % 2], sts[t % 2]
            ctx.dma("sync", ct, ct[:], ctab, ctab[:, ts])
            ctx.dma("sync", st, st[:], stab, stab[:, ts])
            deferred = []
            for c in range(22):
                p1 = pb[1 + (c % 2)]
                ob = obs[nob % 3]
                nob += 1
                is32 = c < 5
                if is32:
                    mm32(win, win[c], p1)
                else:
                    pj.mm(win, c, KC, 128, hT, p1)
                for fn in deferred:
                    fn()
                deferred = []
                if not EV_ROPE[c]:
                    ctx.op("scalar", lambda e, p1=p1, ob=ob: e.activation(out=ob[:], in_=p1[:], func=AF.Copy),
                           reads=[p1], writes=[ob])
                    ctx.dma("sync", pjo, pjo[c, :, ts], ob, ob[:])
                    continue
                p1s = p1sb[c % 2]
                ctx.op("scalar", lambda e, p1=p1, p1s=p1s: e.activation(out=p1s[:], in_=p1[:], func=AF.Copy),
                       reads=[p1], writes=[p1s])

                def fin(c=c, p1s=p1s, ob=ob, is32=is32):
                    p2 = pb[3 + (c % 2)]
                    ctx.op("tensor", lambda e: e.matmul(p2[:], lhsT=permT[:], rhs=p1s[:], start=True, stop=True),
                           reads=[permT, p1s], writes=[p2])
                    if is32:
                        ob32 = ob32s[c % 2]
                        rope_combine(ctx, ob32, ob32[:], p1s, p2, ct, ct[:], st, st[:], tmp, 128)
                        ctx.op("scalar", lambda e: e.activation(out=ob[:], in_=ob32[:], func=AF.Copy),
                               reads=[ob32], writes=[ob])
                        ctx.dma("sync", q32o, q32o[c, :, ts], ob32, ob32[:])
                    else:
                        rope_combine(ctx, ob, ob[:], p1s, p2, ct, ct[:], st, st[:], tmp, 128)
                    ctx.dma("sync", pjo, pjo[c, :, ts], ob, ob[:])
                deferred.append(fin)
            for fn in deferred:
                fn()
            p1 = pb[7]
            pj.mm(wgt, slice(None), KC, 24, hT, p1)
            go = gos[t % 2]
            ctx.op("scalar", lambda e, p1=p1, go=go: e.activation(out=go[:], in_=p1[:24, :], func=AF.Sigmoid),
                   reads=[p1], writes=[go])
            ctx.dma("sync", gto, gto[:, ts], go, go[:])
        if kind == "L3":
            ffn.run(xt, fa[0], fa[1], fa[2], pb[0:7], sc=fa[3], first=(t == 0))
            ffn.run(xt, fb[0], fb[1], fb[2], pb[0:7], sc=fb[3], first=(t == 0))
            ctx.dma("sync", x3T, x3T[:, :, ts], xt, xt[:])
            hT = hTs[t % 2]
            cm.rmsnorm_T(xt, KC, gm, hT, TT, pb[0], D)
            cq, ckv, cqn_, ckvn_ = cqT[t % 2], ckvT[t % 2], cqn[t % 2], ckvn[t % 2]
            for i in range(3):
                p1 = pb[1 + (i % 2)]
                pj.mm(wmi, i, KC, 128, hT, p1)
                dst_t, dst = (cq, cq[:, i, :]) if i < 2 else (ckv, ckv[:, 0, :])
                ctx.op("scalar", lambda e, p1=p1, dst=dst: e.activation(out=dst, in_=p1[:], func=AF.Copy),
                       reads=[p1], writes=[dst_t])
            ckt, skt = ckts[t % 2], skts[t % 2]
            ctx.dma("sync", ckt, ckt[:], ck_t, ck_t[:, ts])
            ctx.dma("sync", skt, skt[:], sk_t, sk_t[:, ts])
            p1, p2 = pb[3], pb[4]
            pj.mm(wkr, 0, KC, 32, hT, p1)
            pj.mm(wkr, 1, KC, 32, hT, p2)
            ob = obs[nob % 3]
            nob += 1
            rope_combine(ctx, ob, ob[:32, :], p1, p2, ckt, ckt[:], skt, skt[:], tmp, 32)
            ctx.dma("sync", kro, kro[:, ts], ob, ob[:32, :])
            cm.rmsnorm_T(cq, 2, gq, cqn_, TT, pb[0], 256)
            cm.rmsnorm_T(ckv, 1, gkv, ckvn_, TT, pb[0], 128)
            ct, st = cts[t % 2], sts[t % 2]
            ctx.dma("sync", ct, ct[:], cq_t, cq_t[:, ts])
            ctx.dma("sync", st, st[:], sq_t, sq_t[:, ts])
            for h in range(16):
                p1 = pb[1 + (h % 2)]
                p2 = pb[3 + (h % 2)]
                pj.mm(wuq, h, 2, 96, cqn_, p1)
                pj.mm(wuqs, h, 2, 96, cqn_, p2)
                ob = obs[nob % 3]
                nob += 1
                rope_combine(ctx, ob, ob[:96, :], p1, p2, ct, ct[:], st, st[:], tmp, 96)
                ctx.dma("sync", qTo, qTo[h, :, ts], ob, ob[:96, :])
            for h in range(16):
                p1 = pb[5 + (h % 2)]
                pj.mm(wukv, h, 1, 128, ckvn_, p1)
                ob = obs[nob % 3]
                nob += 1
                ctx.op("scalar", lambda e, p1=p1, ob=ob: e.activation(out=ob[:], in_=p1[:], func=AF.Copy),
                       reads=[p1], writes=[ob])
                ctx.dma("sync", kvo, kvo[h, :, ts], ob, ob[:])
        if kind == "L5":
            ffn.run(xt, fa[0], fa[1], fa[2], pb[0:7], sc=fa[3], first=(t == 0))
            yt = yts[t % 2]
            cm.rmsnorm_T(xt, KC, gf, yt, TT, pb[0], D)
            ctx.dma("sync", yT, yT[:, :, ts], yt, yt[:])
    ctx.finish(outs)
    return nc, ctx


NEG = -30000.0
BIG = 1e30
NQB = 32


class Attn:
    def __init__(self, ctx, cm, scale, consts, ns3=False, sbanks=None, lbanks=None):
        self.ctx, self.cm, self.scale = ctx, cm, scale
        self.S = sbanks if sbanks else [cm.psum[0], cm.psum[1]] + ([cm.psum[7]] if ns3 else [])
        self.lag = len(self.S) - 1
        self.pending = []
        self.LB = lbanks if lbanks else [cm.psum[5], cm.psum[6]]
        self.P = [ctx.sb(f"P{i}", (128, 512), BF16) for i in range(4)]
        self.OL = [ctx.sb(f"OL{i}", (128, 512), F32) for i in range(2)]
        self.rl = [ctx.sb(f"rl{i}", (64, 512), F32) for i in range(2)]
        self.ident = ctx.sb("sb_ident", (128, 128), BF16)
        self.sel = ctx.sb("sb_sel", (128, 64), F32)
        ctx.dma("sync", self.ident, self.ident[:], consts["ident"], consts["ident"][:])
        ctx.dma("sync", self.sel, self.sel[:], consts["sel"], consts["sel"][:])
        self.i = 0
        self.j = 0

    def step(self, nk, c0, c1, mains, masks, pvs):
        ctx = self.ctx
        S = self.S[self.i % len(self.S)]
        P = self.P[self.i % 4]
        self.i += 1
        allm = list(mains) + list(masks)
        n = len(allm)
        for k, (lt, lap, rt, rap, cs) in enumerate(allm):
            ctx.op("tensor", lambda e, lap=lap, rap=rap, cs=cs, k=k: e.matmul(
                S[:nk, cs], lhsT=lap, rhs=rap, start=(k == 0), stop=(k == n - 1), skip_group_check=True),
                reads=[lt, rt], writes=[S])
        ctx.op("scalar", lambda e: e.activation(out=P[:nk, c0:c1], in_=S[:nk, c0:c1], func=AF.Exp, scale=self.scale),
               reads=[S], writes=[P])
        self.pending.append((nk, P, pvs))
        while len(self.pending) > self.lag:
            self._flush_one()

    def _flush_one(self):
        ctx = self.ctx
        nk, P, pvs = self.pending.pop(0)
        for (vt, vap, pcs, acc, acc_ap, start) in pvs:
            ctx.op("tensor", lambda e, vap=vap, pcs=pcs, acc_ap=acc_ap, start=start: e.matmul(
                acc_ap, lhsT=vap, rhs=P[:nk, pcs], start=start, stop=True, skip_group_check=True),
                reads=[vt, P], writes=[acc])

    def flush(self):
        while self.pending:
            self._flush_one()

    def finish_split(self, acc):
        self.flush()
        ctx = self.ctx
        OL = self.OL[self.j % 2]
        rl = self.rl[self.j % 2]
        LB = self.LB[self.j % len(self.LB)]
        self.j += 1
        ctx.op("scalar", lambda e: e.activation(out=OL[:], in_=acc[:], func=AF.Copy), reads=[acc], writes=[OL])

        def part_b():
            ctx.op("tensor", lambda e: e.matmul(LB[:64, :], lhsT=self.sel[:], rhs=OL[:], start=True, stop=True),
                   reads=[self.sel, OL], writes=[LB])
            ctx.op("vector", lambda e: e.tensor_scalar(out=rl[:], in0=LB[:64, :], scalar1=1e-30, scalar2=None, op0=ALU.max),
                   reads=[LB], writes=[rl])
            ctx.op("vector", lambda e: e.reciprocal(out=rl[:], in_=rl[:]), reads=[rl], writes=[rl])
        return OL, rl, part_b

    def finish(self, acc):
        self.flush()
        ctx = self.ctx
        OL = self.OL[self.j % 2]
        rl = self.rl[self.j % 2]
        LB = self.LB[self.j % len(self.LB)]
        self.j += 1
        ctx.op("scalar", lambda e: e.activation(out=OL[:], in_=acc[:], func=AF.Copy), reads=[acc], writes=[OL])
        ctx.op("tensor", lambda e: e.matmul(LB[:64, :], lhsT=self.sel[:], rhs=OL[:], start=True, stop=True),
               reads=[self.sel, OL], writes=[LB])
        ctx.op("vector", lambda e: e.tensor_scalar(out=rl[:], in0=LB[:64, :], scalar1=1e-30, scalar2=None, op0=ALU.max),
               reads=[LB], writes=[rl])
        ctx.op("vector", lambda e: e.reciprocal(out=rl[:], in_=rl[:]), reads=[rl], writes=[rl])
        return OL, rl


def attn_consts_np():
    ident = np.eye(128, dtype=np.float32).astype(NBF)
    sel = np.zeros((128, 64), np.float32)
    sel[64 + np.arange(64), np.arange(64)] = 1.0
    kl = np.arange(128)[:, None]
    ql = np.arange(512)[None, :]
    mc = np.stack([np.where(ql >= kl + o, 0.0, NEG) for o in (0, 128, 256, 384)], axis=1)
    return {"ident": ident, "sel": sel, "mcausal": mc.astype(NBF)}


def build_mla():
    nc = bass.Bass("TRN2", target_bir_lowering=False)
    ctx = Ctx(nc)
    cm = Common(ctx, norm=False)
    NU = 4
    qT = ctx.dram("qT", (NU, 96, S), BF16, "ExternalInput")
    kT = ctx.dram("kT", (NU, 96, S), BF16, "ExternalInput")
    v1 = ctx.dram("v1", (NU, 128, 128, 128), BF16, "ExternalInput")
    cd = {"ident": ctx.dram("ident", (128, 128), BF16, "ExternalInput"),
          "sel": ctx.dram("sel", (128, 64), F32, "ExternalInput")}
    mcd = ctx.dram("mcausal", (128, 4, 512), BF16, "ExternalInput")
    oT = ctx.dram("oT", (NU, 64, S), BF16, "ExternalOutput")
    at = Attn(ctx, cm, 96 ** -0.5, cd, ns3=True)
    mc = ctx.sb("mc", (128, 4, 512), BF16)
    ctx.dma("sync", mc, mc[:], mcd, mcd[:])
    Kb = [ctx.sb(f"Kb{i}", (96, S), BF16) for i in range(2)]
    Vb = [ctx.sb(f"Vb{i}", (128, 128, 128), BF16) for i in range(2)]
    Qb = [ctx.sb(f"Qb{i}", (96, 512), BF16) for i in range(3)]
    Ob = [ctx.sb(f"Ob{i}", (64, 512), BF16) for i in range(2)]
    acc = [cm.psum[2], cm.psum[3]]
    n = 0
    pend = [None]
    for u in range(NU):
        K, V = Kb[u % 2], Vb[u % 2]
        ctx.dma("sync", K, K[:], kT, kT[u])
        ctx.dma("sync", V, V[:], v1, v1[u])
        for qb in range(NQB):
            Q = Qb[n % 3]
            A = acc[n % 2]
            O = Ob[n % 2]
            n += 1
            ctx.dma("sync", Q, Q[:], qT, qT[u, :, qb * 512:(qb + 1) * 512])
            nkt = 4 * qb + 4
            for kt in range(nkt):
                d = kt - 4 * qb
                c0 = d * 128 if d > 0 else 0
                mains = [(K, K[:, kt * 128:(kt + 1) * 128], Q, Q[:, c0:512], slice(c0, 512))]
                masks = []
                if d >= 0:
                    masks = [(at.ident, at.ident[:], mc, mc[:, d, c0:512], slice(c0, 512))]
                pvs = [(V, V[:, kt, :], slice(c0, 512), A, A[:, c0:512], kt == 0)]
                at.step(128, c0, 512, mains, masks, pvs)
                if kt == 1 and pend[0] is not None:
                    pend[0]()
                    pend[0] = None
            OL, rl, fb = at.finish_split(A)

            def tail(OL=OL, rl=rl, O=O, u=u, qb=qb, fb=fb):
                fb()
                ctx.op("vector", lambda e: e.tensor_tensor(out=O[:], in0=OL[:64, :], in1=rl[:], op=ALU.mult),
                       reads=[OL, rl], writes=[O])
                ctx.dma("sync", oT, oT[u, :, qb * 512:(qb + 1) * 512], O, O[:])
            pend[0] = tail
    if pend[0] is not None:
        pend[0]()
    ctx.finish([oT])
    return nc, ctx


def nsa_consts_np():
    c = attn_consts_np()
    kl = np.arange(128)[:, None]
    ql = np.arange(512)[None, :]
    d = ql - kl
    c["mwin"] = np.stack([np.where((d - o >= 0) & (d - o < 512), 0.0, NEG) for o in range(-512, 512, 128)], 1).astype(NBF)
    c["mcmp"] = np.stack([np.where(ql - 16 * kl >= 31 - 512 * dl, 0.0, NEG) for dl in range(5)], 1).astype(NBF)
    q = np.arange(128)[:, None]
    npr = np.arange(-1, 8)[None, :]
    c["mtm"] = np.where(q >= 31 + 16 * npr, 0.0, NEG).astype(NBF)
    lo = (np.arange(128) < 64)[:, None]
    c["mul3"] = np.where(lo, np.array([[0., 0., 0.]]), np.array([[1., 0., 0.]])).astype(np.float32)
    c["add3"] = np.where(lo, np.array([[BIG, BIG, -BIG]]), np.array([[0., BIG, BIG]])).astype(np.float32)
    c["identf"] = np.eye(128, dtype=np.float32)
    sg = np.zeros((6, 6, 64), np.float32)
    for r in range(6):
        sg[r, r, :] = 1.0
    c["selg"] = sg
    return c


NSA_CONST_SHAPES = {"ident": ((128, 128), BF16), "sel": ((128, 64), F32), "mcausal": ((128, 4, 512), BF16),
                    "mwin": ((128, 8, 512), BF16), "mcmp": ((128, 5, 512), BF16),
                    "mtm": ((128, 9), BF16), "mul3": ((128, 3), F32), "add3": ((128, 3), F32),
                    "identf": ((128, 128), F32), "selg": ((6, 6, 64), F32)}


USE32 = True
DBG_SKIP = set()


def build_nsa(nqb=NQB):
    nc = bass.Bass("TRN2", target_bir_lowering=False)
    ctx = Ctx(nc)
    cm = Common(ctx, norm=False)
    pb = cm.psum
    din = lambda n, s, dt=BF16: ctx.dram(n, s, dt, "ExternalInput")
    qg = din("qg", (128, 2, S), F32 if USE32 else BF16)
    qmy = din("qmy", (128, S))
    kcraw = din("kcraw", (64, 16, 1024), F32)
    vcraw = din("vcraw", (64, S))
    w1k = din("w1k", (128, 32, 128), F32)
    w1v = din("w1v", (64, 32, 128), F32)
    posk = din("posk", (128, 32, 8), F32)
    posv = din("posv", (64, 32), F32)
    w2k = din("w2k", (128, 128), F32)
    w2v = din("w2v", (128, 64), F32)
    ksT = din("ksT", (128, S))
    vs1 = din("vs1", (128, 128, 128))
    kwT = din("kwT", (128, 512 + S))
    vw1 = din("vw1", (128, 132, 128))
    gat = din("gat", (6, S), F32)
    cd = {k: din(k, s, dt) for k, (s, dt) in NSA_CONST_SHAPES.items()}
    oA = ctx.dram("oA", (2, 64, S), BF16, "ExternalOutput")
    at = Attn(ctx, cm, 0.125, cd, sbanks=[pb[0], pb[1], pb[5]], lbanks=[pb[6]])

    def cload(name, eng="sync"):
        s, dt = NSA_CONST_SHAPES[name]
        t = ctx.sb("c_" + name, s, dt)
        ctx.dma(eng, t, t[:], cd[name], cd[name][:])
        return t
    mc, mwin, mcmp, mtm, mul3, add3, identf = [cload(n) for n in
                                               ("mcausal", "mwin", "mcmp", "mtm", "mul3", "add3", "identf")]
    selg = ctx.sb("c_selg", (6, 6 * 64), F32)
    ctx.dma("sync", selg, selg[:], cd["selg"], cd["selg"].ap.rearrange("a b c -> a (b c)"))
    bigA = ctx.sb("bigA", (128, S), BF16)
    bigB = ctx.sb("bigB", (128, S), BF16)
    bigB3 = bigB.ap.rearrange("p (t c) -> p t c", c=128)
    kcT = ctx.sb("kcT", (128, 1024), BF16)
    vc1 = ctx.sb("vc1", (128, 8, 128), BF16)
    kcT32 = ctx.sb("kcT32", (128, 1024), F32)
    ctx.op("gpsimd", lambda e: e.memset(bigA[:], 0.0), writes=[bigA])
    bigA32 = bigA.ap.bitcast(F32)
    k32buf = bigA32[:, 0:2080].rearrange("p (j m) -> p j m", m=130)
    w1k32 = bigA32[:, 2080:2080 + 4096].rearrange("p (l h) -> p l h", h=128)
    pos32 = ctx.sb("pos32", (128, 32, 8), F32)
    w2k32 = ctx.sb("w2k32", (128, 128), F32)
    hid32 = [ctx.sb(f"hid32_{i}", (128, 128), F32) for i in range(2)]
    posb = ctx.sb("posb", (128, 1), F32)
    ctx.dma("sync", bigA, w1k32, w1k, w1k[:])
    ctx.dma("sync", pos32, pos32[:], posk, posk[:])
    ctx.dma("sync", w2k32, w2k32[:], w2k, w2k[:])
    ps = pb[7]
    for l in range(32 if "posb" not in DBG_SKIP else 1):
        ctx.op("tensor", lambda e, l=l: e.matmul(ps[:, 0:8], lhsT=w1k32[:, l, :], rhs=pos32[:, l, :],
                                                 start=(l == 0), stop=(l == 31)), reads=[bigA, pos32], writes=[ps])
    ctx.op("vector", lambda e: e.tensor_copy(out=posb[:], in_=ps[:, 0:1]), reads=[ps], writes=[posb])
    for p in range(8 if "kpath" not in DBG_SKIP else 0):
        nm = min(130, 1024 - 128 * p)
        ctx.dma("sync", bigA, k32buf[0:64, :, 0:nm], kcraw, kcraw[:, :, 128 * p:128 * p + nm])
        ps = pb[p % 2]
        for l in range(32):
            a_, j_ = l // 16, l % 16
            ctx.op("tensor", lambda e, l=l: e.matmul(ps[:, 0:128], lhsT=w1k32[:, l, :], rhs=k32buf[:, j_, a_:a_ + 128],
                                                     start=(l == 0), stop=(l == 31)), reads=[bigA], writes=[ps])
        h32 = hid32[p % 2]
        ctx.op("scalar", lambda e: e.activation(out=h32[:], in_=ps[:, 0:128], func=AF.Silu, bias=posb[:, 0:1]),
               reads=[ps, posb], writes=[h32])
        p2 = pb[2 + (p % 2)]
        ctx.op("tensor", lambda e: e.matmul(p2[:, 0:128], lhsT=w2k32[:], rhs=h32[:], start=True, stop=True),
               reads=[w2k32, h32], writes=[p2])
        ctx.op("vector", lambda e: e.tensor_copy(out=kcT32[:, p * 128:(p + 1) * 128], in_=p2[:, 0:128]), reads=[p2], writes=[kcT32])
        ctx.op("scalar", lambda e: e.activation(out=kcT[:, p * 128:(p + 1) * 128], in_=kcT32[:, p * 128:(p + 1) * 128], func=AF.Copy), reads=[kcT32], writes=[kcT])
    ctx.dma("sync", bigB, bigB[0:64, :], vcraw, vcraw[:])
    w1s_ap = bigA[0:64, 0:4096].rearrange("p (l h) -> p l h", h=128)
    poss = ctx.sb("poss", (64, 32), BF16)
    w2vs = ctx.sb("w2vs", (128, 64), BF16)
    ctx.dma("gpsimd", w2vs, w2vs[:], w2v, w2v[:])
    hid = [ctx.sb(f"hid{i}", (128, 512), BF16) for i in range(2)]
    ctx.op("vector", lambda e: e.memset(hid[1][:], 0.0), writes=[hid[1]])
    ctx.op("vector", lambda e: e.memset(vc1[:], 1.0), writes=[vc1])
    ctx.dma("gpsimd", bigA, w1s_ap, w1v, w1v[:])
    ctx.dma("gpsimd", poss, poss[:], posv, posv[:])
    ps = pb[7]
    for l in range(32):
        ctx.op("tensor", lambda e, l=l: e.matmul(ps[:, 0:1], lhsT=w1s_ap[:, l, :], rhs=poss[:, l:l + 1],
                                                 start=(l == 0), stop=(l == 31)), reads=[bigA, poss], writes=[ps])
    posbv = ctx.sb("posbv", (128, 1), F32)
    ctx.op("vector", lambda e: e.tensor_copy(out=posbv[:], in_=ps[:, 0:1]), reads=[ps], writes=[posbv])
    for nt in range(2 if "vpath" not in DBG_SKIP else 0):
        ncol = 512 if nt == 0 else 511
        ps = pb[nt]
        for l in range(32):
            st_ = nt * 8192 + l
            en = min(S, st_ + 16 * ncol)
            ctx.op("tensor", lambda e, l=l: e.matmul(ps[:, 0:ncol], lhsT=w1s_ap[:, l, :], rhs=bigB[0:64, st_:en:16],
                                                     start=(l == 0), stop=(l == 31)), reads=[bigA, bigB], writes=[ps])
        ctx.op("scalar", lambda e: e.activation(out=hid[nt][:, 0:ncol], in_=ps[:, 0:ncol], func=AF.Silu, bias=posbv[:, 0:1]),
               reads=[ps, posbv], writes=[hid[nt]])
        for j in range(4):
            p2 = pb[2 + (j % 2)]
            ctx.op("tensor", lambda e: e.matmul(p2[:, 0:64], lhsT=hid[nt][:, j * 128:(j + 1) * 128], rhs=w2vs[:], start=True, stop=True),
                   reads=[w2vs, hid[nt]], writes=[p2])
            ctx.op("vector", lambda e: e.tensor_copy(out=vc1[:, nt * 4 + j, 0:64], in_=p2[:, 0:64]), reads=[p2], writes=[vc1])
    ctx.dma("sync", bigA, bigA[:], ksT, ksT[:])
    ctx.dma("sync", bigB, bigB[:], vs1, vs1.ap.rearrange("p t c -> p (t c)"))
    QDT = F32 if USE32 else BF16
    Qg1 = ctx.sb("Qg0", (128, 4, 512), QDT)
    Qgs = [Qg1, Qg1]
    Qms = [[[ctx.sb(f"QY{i}_{h}_{c}", (128, 512), BF16) for c in range(4)] for h in range(2)] for i in range(2)]
    ctx.op("gpsimd", lambda e: e.memset(Qg1[:], 0.0), writes=[Qg1])
    for i in range(2):
        for h in range(2):
            for c in range(4):
                ctx.op("gpsimd", lambda e: e.memset(Qms[i][h][c][:], 0.0), writes=[Qms[i][h][c]])
    wKs = [ctx.sb(f"wK{i}", (128, 1024), BF16) for i in range(2)]
    wVs = [ctx.sb(f"wV{i}", (128, 8, 128), BF16) for i in range(2)]
    gts = [ctx.sb(f"gt{i}", (6, 512), F32) for i in range(2)]
    NE = 4
    es = [ctx.sb(f"e{i}", (128, 512), F32) for i in range(NE)]
    lp = [ctx.sb(f"lp{i}", (128, 2), F32) for i in range(2)]
    rlh = [ctx.sb(f"rlh{i}", (128, 1), F32) for i in range(2)]
    Aim = ctx.sb("Aim", (128, 1024), F32)
    I1 = ctx.sb("I1", (128, 256), F32)
    I2 = ctx.sb("I2", (128, 256), F32)
    m8 = ctx.sb("m8", (128, 16), F32)
    negms = [ctx.sb(f"negm{i}", (128, 320), F32) for i in range(4)]
    fg = ctx.sb("fg", (64, 512), F32)
    tmpc = ctx.sb("tmpc", (64, 512), F32)
    accsb = ctx.sb("accsb", (64, 512), F32)
    Ob = [ctx.sb(f"Ob{i}", (64, 512), BF16) for i in range(2)]
    accC, accS, accW, GB = pb[2], pb[3], pb[4], pb[7]
    kq = kcT32 if USE32 else kcT
    st = {"ne": 0, "no": 0}

    def load(qb):
        T0 = qb * 512
        if qb >= 2:
            for hh in range(4):
                r0 = (hh % 2) * 64
                ctx.dma("sync", Qgs[qb % 2], Qgs[qb % 2][0:64, hh, :], qg, qg[r0:r0 + 64, hh // 2, T0:T0 + 512])
        for h in range(2):
            for c in range(qb // 8 + 1):
                t_ = Qms[qb % 2][h][c]
                ctx.dma("sync", t_, t_[0:64, :], qmy, qmy[h * 64:(h + 1) * 64, T0:T0 + 512])
        ctx.dma("sync", wKs[qb % 2], wKs[qb % 2][:], kwT, kwT[:, T0:T0 + 1024])
        ctx.dma("sync", wVs[qb % 2], wVs[qb % 2][:], vw1, vw1[:, 4 * qb:4 * qb + 8, :])
        ctx.dma("sync", gts[qb % 2], gts[qb % 2][:], gat, gat[:, T0:T0 + 512])

    def phase1a(qb, subs=(0, 1, 2, 3)):
        if qb < 2:
            return
        Qg = Qgs[qb % 2]
        for qsl in subs:
            qs = 4 * qb + qsl
            ncols = 8 * qs + 8
            nj = 2 * qs + 2
            negm = negms[qsl]
            halves = [(lo, min(ncols, lo + 512)) for lo in (0, 512) if lo < ncols]
            for hh in range(4):
                ch, r0 = hh // 2, (hh % 2) * 64
                lpt = lp[hh % 2]
                rl1 = rlh[hh % 2]
                ehs = []
                for hi_, (lo, hi) in enumerate(halves):
                    w = hi - lo
                    Sb = at.S[at.i % len(at.S)]
                    at.i += 1
                    a, b2 = max(lo, ncols - 9, 0), min(hi, ncols)
                    hasm = b2 > a
                    ctx.op("tensor", lambda e: e.matmul(
                        Sb[:, 0:w], lhsT=Qg[:, hh, qsl * 128:(qsl + 1) * 128], rhs=kq[:, lo:hi],
                        start=True, stop=(not hasm), skip_group_check=True), reads=[Qg, kq], writes=[Sb])
                    if hasm:
                        ctx.op("tensor", lambda e: e.matmul(
                            Sb[:, a - lo:b2 - lo], lhsT=at.ident[:], rhs=mtm[:, a - (ncols - 9):b2 - (ncols - 9)],
                            start=False, stop=True, skip_group_check=True), reads=[at.ident, mtm], writes=[Sb])
                    et = es[st["ne"] % NE]
                    st["ne"] += 1
                    ctx.op("scalar", lambda e: e.activation(
                        out=et[:, 0:w], in_=Sb[:, 0:w], func=AF.Exp, scale=0.125, accum_out=lpt[:, hi_:hi_ + 1]),
                        reads=[Sb], writes=[et, lpt])
                    ehs.append((et, lo, hi))
                if len(halves) == 2:
                    ctx.op("vector", lambda e: e.tensor_tensor(out=lpt[:, 0:1], in0=lpt[:, 0:1], in1=lpt[:, 1:2], op=ALU.add),
                           reads=[lpt], writes=[lpt])
                ctx.op("vector", lambda e: e.tensor_scalar(out=rl1[:], in0=lpt[:, 0:1], scalar1=1e-30, scalar2=None, op0=ALU.max),
                       reads=[lpt], writes=[rl1])
                ctx.op("vector", lambda e: e.reciprocal(out=rl1[:], in_=rl1[:]), reads=[rl1], writes=[rl1])
                for (et, lo, hi) in ehs:
                    w = hi - lo
                    if hh == 0:
                        ctx.op("vector", lambda e: e.tensor_scalar(
                            out=Aim[:, lo:hi], in0=et[:, 0:w], scalar1=rl1[:, 0:1], scalar2=None, op0=ALU.mult),
                            reads=[et, rl1], writes=[Aim])
                    else:
                        ctx.op("vector", lambda e: e.scalar_tensor_tensor(
                            out=Aim[:, lo:hi], in0=et[:, 0:w], scalar=rl1[:, 0:1], in1=Aim[:, lo:hi], op0=ALU.mult, op1=ALU.add),
                            reads=[et, rl1, Aim], writes=[Aim])
            n4 = 4 * nj
            tt = lambda o, a_, b_, op: ctx.op("vector", lambda e: e.tensor_tensor(out=o, in0=a_, in1=b_, op=op),
                                              reads=[Aim, I1, mul3, add3], writes=[I1])
            tt(I1[:, 0:nj], Aim[:, 0:n4:4], Aim[:, 1:n4:4], ALU.add)
            tt(I1[:, 0:nj], I1[:, 0:nj], Aim[:, 2:n4:4], ALU.add)
            ctx.op("vector", lambda e: e.scalar_tensor_tensor(out=I1[:, 0:nj], in0=I1[:, 0:nj], scalar=2.0, in1=Aim[:, 3:n4:4],
                                                              op0=ALU.mult, op1=ALU.add), reads=[Aim, I1], writes=[I1])
            tt(I1[:, 1:nj], I1[:, 1:nj], Aim[:, 3:n4 - 4:4], ALU.add)
            tt(I1[:, nj - 3:nj], I1[:, nj - 3:nj], mul3[:, :], ALU.mult)
            tt(I1[:, nj - 3:nj], I1[:, nj - 3:nj], add3[:, :], ALU.add)
            ctx.op("vector", lambda e: e.memset(I1[:, 0:1], BIG), writes=[I1])
            ctx.op("gpsimd", lambda e: e.memset(negm[:], 0.0), writes=[negm])
            ctx.op("vector", lambda e: e.max(out=m8[:, 0:8], in_=I1[:, 0:nj]), reads=[I1], writes=[m8])
            ctx.op("vector", lambda e: e.match_replace(out=I2[:, 0:nj], in_to_replace=m8[:, 0:8], in_values=I1[:, 0:nj],
                                                       imm_value=-BIG), reads=[I1, m8], writes=[I2])
            ctx.op("vector", lambda e: e.max(out=m8[:, 8:16], in_=I2[:, 0:nj]), reads=[I2], writes=[m8])
            ctx.op("vector", lambda e: e.tensor_scalar(out=negm[:, 64:64 + nj], in0=I1[:, 0:nj], scalar1=m8[:, 15:16], scalar2=-1.0,
                                                       op0=ALU.is_ge, op1=ALU.add), reads=[I1, m8], writes=[negm])

    def phase1b(qb):
        if qb < 2:
            return
        for qsl in range(4):
            negm = negms[qsl]
            for c in range(qb // 8 + 1):
                ctx.op("tensor", lambda e: e.transpose(out=GB[:, 0:128], in_=negm[:, 64 * c:64 * c + 128], identity=identf[:]),
                       reads=[negm, identf], writes=[GB])
                for h in range(2):
                    t_ = Qms[qb % 2][h][c]
                    ctx.op("vector", lambda e: e.tensor_copy(out=t_[64:128, qsl * 128:(qsl + 1) * 128], in_=GB[64:128, 0:128]),
                           reads=[GB], writes=[t_])

    def phase2(qb, todo):
        T0 = qb * 512
        wK, wV, gt = wKs[qb % 2], wVs[qb % 2], gts[qb % 2]
        use_sel = qb >= 2
        for hl in range(2):
            r0 = hl * 64
            Qm = Qms[qb % 2][hl][0]
            for m in range(qb // 4 + 1):
                dl = qb - 4 * m
                masks = [(at.ident, at.ident[:], mcmp, mcmp[:, dl, :], slice(0, 512))] if dl <= 4 else []
                todo.append(lambda Qm=Qm, m=m, masks=masks: at.step(128, 0, 512, [(kcT, kcT[:, m * 128:(m + 1) * 128], Qm, Qm[:, 0:512], slice(0, 512))], masks,
                        [(vc1, vc1[:, m, :], slice(0, 512), accC, accC[:, :], m == 0)]))
            for kt in range(8):
                c0 = max(0, (kt - 4) * 128)
                c1 = min(512, 128 * kt + 128)
                todo.append(lambda Qm=Qm, kt=kt, c0=c0, c1=c1: at.step(128, c0, c1, [(wK, wK[:, kt * 128:(kt + 1) * 128], Qm, Qm[:, c0:c1], slice(c0, c1))],
                        [(at.ident, at.ident[:], mwin, mwin[:, kt, c0:c1], slice(c0, c1))],
                        [(wV, wV[:, kt, :], slice(c0, c1), accW, accW[:, c0:c1], kt == 0)]))
            for kt in range(4 * qb + 4):
                d = kt - 4 * qb
                c0 = d * 128 if d > 0 else 0
                masks = []
                Qm = Qms[qb % 2][hl][kt // 32]
                if d >= 0:
                    masks.append((at.ident, at.ident[:], mc, mc[:, d, c0:512], slice(c0, 512)))
                todo.append(lambda Qm=Qm, kt=kt, c0=c0, masks=masks: at.step(128, c0, 512, [(bigA, bigA[:, kt * 128:(kt + 1) * 128], Qm, Qm[:, c0:512], slice(c0, 512))], masks,
                        [(bigB, bigB3[:, kt, :], slice(c0, 512), accS, accS[:, c0:512], kt == 0)]))
            todo.append(lambda hl=hl: epilogue(qb, hl))

    def epilogue(qb, hl):
        T0 = qb * 512
        gt = gts[qb % 2]
        for br, acc in enumerate((accC, accS, accW)):
            OL, rl = at.finish(acc)
            r = hl * 3 + br
            ctx.op("tensor", lambda e: e.matmul(GB[:64, :], lhsT=selg[:, r * 64:(r + 1) * 64], rhs=gt[:, :], start=True, stop=True),
                   reads=[selg, gt], writes=[GB])
            ctx.op("vector", lambda e: e.tensor_tensor(out=fg[:], in0=GB[:64, :], in1=rl[:], op=ALU.mult),
                   reads=[GB, rl], writes=[fg])
            if br == 0:
                ctx.op("vector", lambda e: e.tensor_tensor(out=accsb[:], in0=OL[:64, :], in1=fg[:], op=ALU.mult),
                       reads=[OL, fg], writes=[accsb])
            else:
                ctx.op("vector", lambda e: e.tensor_tensor(out=tmpc[:], in0=OL[:64, :], in1=fg[:], op=ALU.mult),
                       reads=[OL, fg], writes=[tmpc])
                ctx.op("vector", lambda e: e.tensor_tensor(out=accsb[:], in0=accsb[:], in1=tmpc[:], op=ALU.add),
                       reads=[accsb, tmpc], writes=[accsb])
        O = Ob[st["no"] % 2]
        st["no"] += 1
        ctx.op("scalar", lambda e: e.activation(out=O[:], in_=accsb[:], func=AF.Copy), reads=[accsb], writes=[O])
        ctx.dma("sync", oA, oA[hl, :, T0:T0 + 512], O, O[:])

    load(0)
    phase1a(0)
    for qb in range(nqb):
        phase1b(qb)
        todo = []
        phase2(qb, todo)
        nxt = qb + 1 < nqb
        if nxt:
            load(qb + 1)
        n = len(todo)
        cuts = {(n * k) // 4: k for k in range(4)}
        for i, fn in enumerate(todo):
            if nxt and i in cuts:
                phase1a(qb + 1, (cuts[i],))
            fn()
    ctx.finish([oA])
    return nc, ctx


def dil_consts_np():
    c = attn_consts_np()
    kl = np.arange(128)[:, None]
    ql = np.arange(512)[None, :]
    d = ql - kl
    c["md1"] = np.stack([np.where((d - o >= 0) & (d - o <= 128), 0.0, NEG) for o in range(-128, 512, 128)], 1).astype(NBF)
    i4 = ql % 128
    c["md4"] = np.stack([np.where(i4 <= kl, 0.0, NEG), np.where(i4 >= kl, 0.0, NEG)], 1).astype(NBF)
    i16 = ql % 32
    c["md16"] = np.stack([np.where(i16 <= kl, 0.0, NEG), np.where(i16 >= kl, 0.0, NEG)], 1).astype(NBF)
    del c["mcausal"]
    return c


DIL_CONST_SHAPES = {"ident": ((128, 128), BF16), "sel": ((128, 64), F32), "md1": ((128, 5, 512), BF16),
                    "md4": ((128, 2, 512), BF16), "md16": ((128, 2, 512), BF16)}


def build_dil(nqb=NQB):
    nc = bass.Bass("TRN2", target_bir_lowering=False)
    ctx = Ctx(nc)
    cm = Common(ctx, norm=False)
    pb = cm.psum
    din = lambda n, s, dt=BF16: ctx.dram(n, s, dt, "ExternalInput")
    qd = din("qd", (128, S))
    kb1T = din("kb1T", (128, 128 + S))
    vb1 = din("vb1", (2, 128, 129, 128))
    kb4T = din("kb4T", (128, 4, 128 + 4096))
    vb4 = din("vb4", (2, 128, 4, 33, 128))
    kb16T = din("kb16T", (128, 16, 128 + 1024))
    vb16 = din("vb16", (2, 16, 1152, 128))
    cd = {k: din(k, s, dt) for k, (s, dt) in DIL_CONST_SHAPES.items()}
    oB = ctx.dram("oB", (2, 64, S), BF16, "ExternalOutput")
    at = Attn(ctx, cm, 0.125, cd, ns3=True)
    ms = {}
    for name in ("md1", "md4", "md16"):
        s, dt = DIL_CONST_SHAPES[name]
        ms[name] = ctx.sb("c_" + name, s, dt)
        ctx.dma("sync", ms[name], ms[name][:], cd[name], cd[name][:])
    md1, md4, md16 = ms["md1"], ms["md4"], ms["md16"]
    Qs = [[ctx.sb(f"Qd{i}_{h}", (128, 512), BF16) for h in range(2)] for i in range(2)]
    for i in range(2):
        for h in range(2):
            ctx.op("gpsimd", lambda e: e.memset(Qs[i][h][:], 0.0), writes=[Qs[i][h]])
    K1 = [ctx.sb(f"K1_{i}", (128, 640), BF16) for i in range(2)]
    V1 = [[ctx.sb(f"V1_{i}_{h}", (128, 5, 128), BF16) for h in range(2)] for i in range(2)]
    K4 = [ctx.sb(f"K4_{i}", (128, 4, 256), BF16) for i in range(2)]
    V4 = [[ctx.sb(f"V4_{i}_{h}", (128, 4, 2, 128), BF16) for h in range(2)] for i in range(2)]
    K16 = [ctx.sb(f"K16_{i}", (128, 16, 160), BF16) for i in range(2)]
    V16A = [[ctx.sb(f"V16A_{i}_{h}", (128, 16, 128), BF16) for h in range(2)] for i in range(2)]
    V16B = [[ctx.sb(f"V16B_{i}_{h}", (32, 16, 128), BF16) for h in range(2)] for i in range(2)]
    Ob = [ctx.sb(f"Ob{i}", (64, 512), BF16) for i in range(2)]
    accs = [pb[2], pb[3]]
    n = 0
    for qb in range(nqb):
        T0 = qb * 512
        i = qb % 2
        k1, k4, k16 = K1[i], K4[i], K16[i]
        for h in range(2):
            ctx.dma("sync", Qs[i][h], Qs[i][h][h * 64:(h + 1) * 64, :], qd, qd[h * 64:(h + 1) * 64, T0:T0 + 512])
        ctx.dma("sync", k1, k1[:], kb1T, kb1T[:, T0:T0 + 640])
        ctx.dma("sync", k4, k4[:], kb4T, kb4T[:, :, 128 * qb:128 * qb + 256])
        ctx.dma("sync", k16, k16[:], kb16T, kb16T[:, :, 32 * qb:32 * qb + 160])
        for h in range(2):
            ctx.dma("sync", V1[i][h], V1[i][h][:], vb1, vb1[h, :, 4 * qb:4 * qb + 5, :])
            ctx.dma("sync", V4[i][h], V4[i][h][:], vb4, vb4[h, :, :, qb:qb + 2, :])
            ctx.dma("gpsimd", V16A[i][h], V16A[i][h][:], vb16,
                    vb16[h, :, 32 * qb:32 * qb + 128, :].rearrange("r p c -> p r c"))
            ctx.dma("gpsimd", V16B[i][h], V16B[i][h][:], vb16,
                    vb16[h, :, 32 * qb + 128:32 * qb + 160, :].rearrange("r p c -> p r c"))
        for h in range(2):
            r0 = h * 64
            Q = Qs[i][h]
            A = accs[n % 2]
            O = Ob[n % 2]
            n += 1
            v1, v4, va, vb = V1[i][h], V4[i][h], V16A[i][h], V16B[i][h]
            for kt in range(5):
                o = -128 + 128 * kt
                c0, c1 = max(0, o), min(512, 128 * kt + 128)
                at.step(128, c0, c1, [(k1, k1[:, kt * 128:(kt + 1) * 128], Q, Q[:, c0:c1], slice(c0, c1))],
                        [(at.ident, at.ident[:], md1, md1[:, kt, c0:c1], slice(c0, c1))],
                        [(v1, v1[:, kt, :], slice(c0, c1), A, A[:, c0:c1], kt == 0)])
            for kt in range(2):
                mains = [(k4, k4[:, r, kt * 128:(kt + 1) * 128], Q, Q[:, r:512:4], slice(r * 128, (r + 1) * 128))
                         for r in range(4)]
                pvs = [(v4, v4[:, r, kt, :], slice(r * 128, (r + 1) * 128), A, A[:, r:512:4], False) for r in range(4)]
                at.step(128, 0, 512, mains, [(at.ident, at.ident[:], md4, md4[:, kt, :], slice(0, 512))], pvs)
            mains = [(k16, k16[:, r, 0:128], Q, Q[:, r:512:16], slice(r * 32, (r + 1) * 32)) for r in range(16)]
            pvs = [(va, va[:, r, :], slice(r * 32, (r + 1) * 32), A, A[:, r:512:16], False) for r in range(16)]
            at.step(128, 0, 512, mains, [(at.ident, at.ident[:], md16, md16[:, 0, :], slice(0, 512))], pvs)
            mains = [(k16, k16[:, r, 128:160], Q, Q[:, r:512:16], slice(r * 32, (r + 1) * 32)) for r in range(16)]
            pvs = [(vb, vb[:, r, :], slice(r * 32, (r + 1) * 32), A, A[:, r:512:16], False) for r in range(16)]
            at.step(32, 0, 512, mains, [(at.ident, at.ident[0:32, 0:32], md16, md16[0:32, 1, :], slice(0, 512))], pvs)
            OL, rl = at.finish(A)
            ctx.op("vector", lambda e, OL=OL, rl=rl, O=O: e.tensor_tensor(out=O[:], in0=OL[:64, :], in1=rl[:], op=ALU.mult),
                   reads=[OL, rl], writes=[O])
            ctx.dma("sync", oB, oB[h, :, T0:T0 + 512], O, O[:])
    ctx.finish([oB])
    return nc, ctx


NCORE = 8
DBG = {}


def _run(nc, in_maps):
    res = run_bass_kernel_spmd(nc, in_maps, core_ids=list(range(NCORE)))
    et = getattr(res, "exec_time_ns", None)
    if et is not None:
        print(f"[launch] exec_time_ns={et}", flush=True)
    return res.results


def _tokT(a):
    R = a.shape[1]
    return np.ascontiguousarray(a.T.reshape(R // 128, 128, a.shape[0]).transpose(1, 0, 2))


def _v1_tiles(vT, pad_rows=0):
    L = vT.shape[1]
    a = np.zeros((pad_rows + L, 128), NBF)
    a[pad_rows:, :64] = vT.T
    a[pad_rows:, 64:] = 1
    nt = (pad_rows + L) // 128
    return np.ascontiguousarray(a.reshape(nt, 128, 128).transpose(1, 0, 2))


def _padfront(a, n):
    z = np.zeros(a.shape[:-1] + (n,), a.dtype)
    return np.concatenate([z, a], axis=-1)


def _rope_tabs(pos, dims):
    inv = (np.float32(ROPE_THETA) ** (-np.arange(0, dims, 2, dtype=np.float32) / np.float32(dims))).astype(np.float32)
    ang = pos.astype(np.float32)[:, None] * inv[None, :]
    return np.cos(ang).astype(np.float32).T, np.sin(ang).astype(np.float32).T


def kernel(x, ffn1_norm, ffn1_w_gate, ffn1_w_up, ffn1_w_down, ffn2_norm, ffn2_w_gate, ffn2_w_up, ffn2_w_down, mix_norm,
           ev_w_in, ev_w_out, nsa_cmp_pos_k, nsa_cmp_pos_v, nsa_cmp_k_w1, nsa_cmp_k_w2, nsa_cmp_v_w1, nsa_cmp_v_w2,
           mla_w_in, mla_q_norm, mla_kv_norm, mla_w_uq, mla_w_ukv, mla_w_out, final_norm):
    f32 = lambda a: np.asarray(a, dtype=np.float32)
    x = f32(x)
    B = 2
    xf = x.reshape(B * S, D)
    pos_of_core = [(c % 4) * NTOK + np.arange(NTOK) for c in range(NCORE)]

    def ffn_maps(pref, g, wg, wu, wd):
        wgu, wdt = ffn_host_layout(f32(wg), f32(wu), f32(wd))
        return {pref + "_g": gain_layout(f32(g), KC), pref + "_wgu": wgu, pref + "_wd": wdt}

    W = f32(ev_w_in[0])
    perm64 = np.concatenate([np.arange(8, 16), np.arange(0, 8), np.arange(16, 64)])
    perm128 = np.concatenate([perm64, 64 + perm64])
    win = np.stack([wtile(W[:, c0:c0 + 128], KC) for c0 in EV_COLS])
    wgt = wtile(W[:, 1280:1304], KC)
    permT = np.zeros((128, 128), np.float32)
    permT[perm128, np.arange(128)] = 1.0
    common = dict(win=win, wgt=wgt, gmix=gain_layout(f32(mix_norm[0]), KC), permT=permT)
    common.update(ffn_maps("fa", ffn1_norm[0], ffn1_w_gate[0], ffn1_w_up[0], ffn1_w_down[0]))
    in_maps = []
    for c in range(NCORE):
        cs, sn = _rope_tabs(pos_of_core[c], 16)
        ct = np.ones((64, NTOK), np.float32)
        st = np.zeros((64, NTOK), np.float32)
        ct[0:8], ct[8:16] = cs, cs
        st[0:8], st[8:16] = -sn, sn
        m = dict(common)
        m.update(xT=_tokT(xf[c * NTOK:(c + 1) * NTOK]), ctab=np.concatenate([ct, ct]), stab=np.concatenate([st, st]))
        in_maps.append(m)
    nc, _ = build_stage("L1")
    r1 = _run(nc, in_maps)
    x1T = [r["x1T"] for r in r1]
    PJ = np.concatenate([r["pj"] for r in r1], axis=2).reshape(22, 128, B, S)
    GT = np.concatenate([r["gates"] for r in r1], axis=1).reshape(24, B, S)
    Q32 = np.concatenate([r["q32"] for r in r1], axis=2).reshape(5, 128, B, S)
    DBG.update(x1T=x1T, PJ=PJ, GT=GT)
    cn = nsa_consts_np()
    w1k = np.zeros((128, 32, 128), np.float32)
    w1k[:64] = f32(nsa_cmp_k_w1[0]).reshape(32, 64, 128).transpose(1, 0, 2)
    posk8 = np.zeros((128, 32, 8), np.float32)
    posk8[:64] = np.repeat(f32(nsa_cmp_pos_k[0]).T[:, :, None], 8, axis=2)
    w1v = np.ascontiguousarray(f32(nsa_cmp_v_w1[0]).reshape(32, 64, 128).transpose(1, 0, 2))
    w2k = f32(nsa_cmp_k_w2[0])
    XM = np.zeros((64, 128, 128), NBF)
    for kt in range(128):
        for half in range(2):
            XM[2 * (kt % 32) + half, kt, half * 64:(half + 1) * 64] = 30000.0
    XM = XM.reshape(64, S)
    in_maps = []
    for c in range(NCORE):
        b, g, pr = c // 4, (c % 4) // 2, c % 2
        gs = slice(g * 64, (g + 1) * 64)
        ks = PJ[6][gs, b]
        kw = PJ[8][gs, b]
        h0 = 4 * g + 2 * pr
        m = dict(cn)
        m.update(qg=np.ascontiguousarray(np.stack([Q32[2 * g][:, b], Q32[2 * g + 1][:, b]], axis=1)),
                 qmy=np.ascontiguousarray(PJ[2 * g + pr][:, b]),
                 kcraw=np.ascontiguousarray(Q32[4][gs, b].reshape(64, 1024, 16).transpose(0, 2, 1)), vcraw=np.ascontiguousarray(PJ[5][gs, b]),
                 w1k=w1k, w1v=w1v, posk=posk8,
                 posv=np.ascontiguousarray(f32(nsa_cmp_pos_v[0]).T),
                 w2k=np.concatenate([w2k, np.zeros_like(w2k)], axis=1), w2v=f32(nsa_cmp_v_w2[0]),
                 ksT=np.concatenate([ks, XM], axis=0), vs1=_v1_tiles(PJ[7][gs, b]),
                 kwT=_padfront(np.concatenate([kw, np.zeros_like(kw)], axis=0), 512), vw1=_v1_tiles(PJ[9][gs, b], 512),
                 gat=np.ascontiguousarray(GT[h0 * 3:h0 * 3 + 6, b]))
        in_maps.append(m)
    nc, _ = build_nsa()
    rA = _run(nc, in_maps)
    AT = np.zeros((1024, B, S), NBF)
    for c in range(NCORE):
        b, g, pr = c // 4, (c % 4) // 2, c % 2
        h0 = 4 * g + 2 * pr
        AT[h0 * 64:(h0 + 2) * 64, b] = rA[c]["oA"].reshape(128, S)
    cdl = dil_consts_np()
    in_maps = []
    for c in range(NCORE):
        b, cc = c // 4, c % 4
        k = PJ[14 + cc][:, b]
        v = PJ[18 + cc][:, b]
        m = dict(cdl)
        vh = [v[0:64], v[64:128]]
        m.update(qd=np.ascontiguousarray(PJ[10 + cc][:, b]), kb1T=_padfront(k, 128),
                 vb1=np.stack([_v1_tiles(vv, 128) for vv in vh]),
                 kb4T=np.ascontiguousarray(np.stack([_padfront(k[:, r::4], 128) for r in range(4)], axis=1)),
                 vb4=np.stack([np.stack([_v1_tiles(vv[:, r::4], 128) for r in range(4)], axis=1) for vv in vh]),
                 kb16T=np.ascontiguousarray(np.stack([_padfront(k[:, r::16], 128) for r in range(16)], axis=1)),
                 vb16=np.stack([np.stack([_v1_tiles(vv[:, r::16], 128).transpose(1, 0, 2).reshape(1152, 128)
                                          for r in range(16)]) for vv in vh]))
        in_maps.append(m)
    nc, _ = build_dil()
    rB = _run(nc, in_maps)
    for c in range(NCORE):
        b, cc = c // 4, c % 4
        AT[512 + cc * 128:512 + (cc + 1) * 128, b] = rB[c]["oB"].reshape(128, S)
    DBG.update(AT=AT)
    ATf = AT.reshape(1024, B * S)
    Wo = f32(ev_w_out[0])
    Wm = f32(mla_w_in[0])
    Wq = f32(mla_w_uq[0])
    Wkv = f32(mla_w_ukv[0])
    permq = np.concatenate([np.arange(64), np.arange(80, 96), np.arange(64, 80)])
    permk = np.concatenate([np.arange(16, 32), np.arange(0, 16)])
    common = dict(wo=np.stack([wtile(Wo[:, c0:c0 + 128], KC) for c0 in range(0, 1024, 128)]),
                  gmix=gain_layout(f32(mix_norm[1]), KC), gq=gain_layout(f32(mla_q_norm[0]), 2),
                  gkv=gain_layout(f32(mla_kv_norm[0]), 1),
                  wmi=np.stack([wtile(Wm[:, i * 128:(i + 1) * 128], KC) for i in range(3)]),
                  wkr=np.stack([wtile(Wm[:, 384:416], KC), wtile(Wm[:, 384 + permk], KC)]),
                  wuq=np.stack([wtile(Wq[:, h * 96:(h + 1) * 96], 2) for h in range(16)]),
                  wuqs=np.stack([wtile(Wq[:, h * 96 + permq], 2) for h in range(16)]),
                  wukv=np.stack([wtile(Wkv[:, h * 128:(h + 1) * 128], 1) for h in range(16)]))
    common.update(ffn_maps("fa", ffn2_norm[0], ffn2_w_gate[0], ffn2_w_up[0], ffn2_w_down[0]))
    common.update(ffn_maps("fb", ffn1_norm[1], ffn1_w_gate[1], ffn1_w_up[1], ffn1_w_down[1]))
    in_maps = []
    for c in range(NCORE):
        cs, sn = _rope_tabs(pos_of_core[c], 32)
        cq = np.ones((96, NTOK), np.float32)
        sq = np.zeros((96, NTOK), np.float32)
        cq[64:80], cq[80:96] = cs, cs
        sq[64:80], sq[80:96] = -sn, sn
        m = dict(common)
        m.update(xT=x1T[c], aT=_tokT(ATf[:, c * NTOK:(c + 1) * NTOK].T), cq_t=cq, sq_t=sq,
                 ck_t=np.concatenate([cs, cs]), sk_t=np.concatenate([-sn, sn]))
        in_maps.append(m)
    nc, _ = build_stage("L3")
    r3 = _run(nc, in_maps)
    x3T = [r["x3T"] for r in r3]
    QT = np.concatenate([r["qT"] for r in r3], axis=2).reshape(16, 96, B, S)
    KV = np.concatenate([r["kvT"] for r in r3], axis=2).reshape(16, 128, B, S)
    KR = np.concatenate([r["krT"] for r in r3], axis=1).reshape(32, B, S)
    DBG.update(x3T=x3T, QT=QT, KV=KV, KR=KR)
    ca = attn_consts_np()
    in_maps = []
    for c in range(NCORE):
        b = c // 4
        hs = [4 * (c % 4) + u for u in range(4)]
        m = dict(ca)
        m.update(qT=np.ascontiguousarray(np.stack([QT[h, :, b] for h in hs])),
                 kT=np.stack([np.concatenate([KV[h, 0:64, b], KR[:, b]], axis=0) for h in hs]),
                 v1=np.stack([_v1_tiles(KV[h, 64:128, b]) for h in hs]))
        in_maps.append(m)
    nc, _ = build_mla()
    r4 = _run(nc, in_maps)
    AT2 = np.zeros((1024, B, S), NBF)
    for c in range(NCORE):
        b = c // 4
        AT2[(c % 4) * 256:(c % 4 + 1) * 256, b] = r4[c]["oT"].reshape(256, S)
    DBG.update(AT2=AT2)
    AT2f = AT2.reshape(1024, B * S)
    Wo2 = f32(mla_w_out[0])
    common = dict(wo=np.stack([wtile(Wo2[:, c0:c0 + 128], KC) for c0 in range(0, 1024, 128)]),
                  gfin=gain_layout(f32(final_norm), KC))
    common.update(ffn_maps("fa", ffn2_norm[1], ffn2_w_gate[1], ffn2_w_up[1], ffn2_w_down[1]))
    in_maps = []
    for c in range(NCORE):
        m = dict(common)
        m.update(xT=x3T[c], aT=_tokT(AT2f[:, c * NTOK:(c + 1) * NTOK].T))
        in_maps.append(m)
    nc, _ = build_stage("L5")
    r5 = _run(nc, in_maps)
    out = np.zeros((B * S, D), np.float32)
    for c in range(NCORE):
        out[c * NTOK:(c + 1) * NTOK] = r5[c]["yT"].transpose(1, 0, 2).reshape(D, NTOK).T
    return out.reshape(B, S, D)
```
